# Optimizing a Trainium2 kernel written in Bass

```python
import math
import jax
import jax.numpy as jnp
from jax import lax
import numpy as np

D_MODEL = 1024
BATCH = 16
SEQ = 2048
DEPTH = 2

CTX_LEN = 256
GRID_W = 64
NORM_EPS = 1e-6

D_MIX = D_MODEL
RG_WIDTH = D_MIX // 4
SSD_WIDTH = D_MIX // 4
DA_WIDTH = D_MIX - RG_WIDTH - SSD_WIDTH

RG_HEADS = 4
RG_BLOCK = RG_WIDTH // RG_HEADS
RG_CONV = 4
RG_C = 8.0

SSD_HEADDIM = 64
SSD_HEADS = SSD_WIDTH // SSD_HEADDIM
SSD_GROUPS = 2
SSD_STATE = 64
SSD_CONV = 4
SSD_CHUNK = 128
SSD_XBC = SSD_WIDTH + 2 * SSD_GROUPS * SSD_STATE

DA_HEADS = 4
DA_HEAD_DIM = DA_WIDTH // (2 * DA_HEADS)
DA_V_DIM = 2 * DA_HEAD_DIM
ROPE_BASE = 10000.0
Q_BLOCK = 128

D_FF = ((8 * D_MODEL + 3 * 256 - 1) // (3 * 256)) * 256

IN_SIZES = (RG_WIDTH, RG_WIDTH, SSD_WIDTH, SSD_XBC, 2 * SSD_HEADS, DA_WIDTH, DA_WIDTH, DA_HEADS * DA_V_DIM)
D_IN = sum(IN_SIZES)
IN_OFFSETS = tuple(int(v) for v in np.cumsum(IN_SIZES)[:-1])

kernel_name = "hybrid_rglru_ssd_diffattn_dit_block"

F32 = jnp.float32


def rmsnorm(x, g):
    xf = x.astype(F32)
    y = xf * lax.rsqrt(jnp.mean(xf * xf, axis=-1, keepdims=True) + NORM_EPS)
    return (y * g.astype(F32)).astype(x.dtype)


def modulate(x, shift, scale):
    return x * (1 + scale) + shift


def dwconv_centred(x, w, b):
    k = w.shape[0]
    y = lax.conv_general_dilated(x, w[:, None, :], window_strides=(1,),
                                 padding=[(k // 2, k - 1 - k // 2)],
                                 dimension_numbers=("NWC", "WIO", "NWC"),
                                 feature_group_count=x.shape[-1])
    return y + b


def swiglu(x, w_gate, w_up, w_down):
    return (jax.nn.silu(x @ w_gate) * (x @ w_up)) @ w_down


def blockdiag(x, w, b):
    xs = x.reshape(x.shape[:-1] + (RG_HEADS, RG_BLOCK))
    return jnp.einsum("blhi,hij->blhj", xs, w).reshape(x.shape) + b


def rglru_coeffs(xc, w_a, b_a, w_x, b_x, lam):
    r = jax.nn.sigmoid(blockdiag(xc, w_a, b_a).astype(F32))
    i = jax.nn.sigmoid(blockdiag(xc, w_x, b_x).astype(F32))
    log_a = -RG_C * r * jax.nn.softplus(-lam.astype(F32))
    a = jnp.exp(log_a)
    bx = jnp.sqrt(-jnp.expm1(2.0 * log_a)) * (i * xc.astype(F32))
    return a, bx


def linear_scan(a, bx, h0, reverse):
    def combine(left, right):
        a_l, b_l = left
        a_r, b_r = right
        return a_l * a_r, a_r * b_l + b_r
    a_cum, h = lax.associative_scan(combine, (a, bx), axis=1, reverse=reverse)
    return h + a_cum * h0[:, None, :]


def rglru_mixer(x_ctx, g_ctx, x_lat, g_lat, conv_w, conv_b, w_a, b_a, w_x, b_x, lam, need_ctx):
    xc_ctx = dwconv_centred(x_ctx, conv_w, conv_b)
    xc_lat = dwconv_centred(x_lat, conv_w, conv_b)
    h_ctx_sum = 0.0
    h_lat_sum = 0.0
    for direction, rev in enumerate((False, True)):
        a_c, b_c = rglru_coeffs(xc_ctx, w_a[direction], b_a[direction], w_x[direction], b_x[direction], lam[direction])
        h_c = linear_scan(a_c, b_c, jnp.zeros_like(b_c[:, 0]), rev)
        h_end = h_c[:, 0] if rev else h_c[:, -1]
        a_l, b_l = rglru_coeffs(xc_lat, w_a[direction], b_a[direction], w_x[direction], b_x[direction], lam[direction])
        h_lat_sum = h_lat_sum + linear_scan(a_l, b_l, h_end, rev)
        if need_ctx:
            h_ctx_sum = h_ctx_sum + h_c
    y_lat = (h_lat_sum * jax.nn.gelu(g_lat.astype(F32))).astype(x_lat.dtype)
    y_ctx = (h_ctx_sum * jax.nn.gelu(g_ctx.astype(F32))).astype(x_ctx.dtype) if need_ctx else None
    return y_ctx, y_lat


def segsum(a):
    t = a.shape[-1]
    a_rep = jnp.broadcast_to(a[..., :, None], a.shape + (t,))
    strict = jnp.tril(jnp.ones((t, t), dtype=bool), -1)
    cs = jnp.cumsum(jnp.where(strict, a_rep, 0.0), axis=-2)
    return jnp.where(jnp.tril(jnp.ones((t, t), dtype=bool)), cs, -jnp.inf)


def ssd_scan(x, dt, a_neg, bm, cm, h0):
    b, L, H, P = x.shape
    n_state = bm.shape[-1]
    nc = L // SSD_CHUNK
    xd = (x * dt[..., None]).reshape(b, nc, SSD_CHUNK, H, P)
    bc = bm.reshape(b, nc, SSD_CHUNK, H, n_state)
    cc = cm.reshape(b, nc, SSD_CHUNK, H, n_state)
    a_dt = (dt * a_neg).reshape(b, nc, SSD_CHUNK, H).transpose(0, 3, 1, 2)
    a_cs = jnp.cumsum(a_dt, axis=-1)
    decay = jnp.exp(segsum(a_dt))
    scores = jnp.einsum("bclhn,bcshn->bhcls", cc, bc) * decay
    y_diag = jnp.einsum("bhcls,bcshp->bclhp", scores, xd)
    decay_states = jnp.exp(a_cs[..., -1:] - a_cs)
    states = jnp.einsum("bclhn,bhcl,bclhp->bchpn", bc, decay_states, xd)
    states = jnp.concatenate([h0[:, None], states], axis=1)
    chunk_a = jnp.pad(a_cs[..., -1], ((0, 0), (0, 0), (1, 0)))
    decay_chunk = jnp.exp(segsum(chunk_a))
    new_states = jnp.einsum("bhzc,bchpn->bzhpn", decay_chunk, states)
    entry_states, final_state = new_states[:, :-1], new_states[:, -1]
    y_off = jnp.einsum("bclhn,bchpn,bhcl->bclhp", cc, entry_states, jnp.exp(a_cs))
    return (y_diag + y_off).reshape(b, L, H, P), final_state


def gated_group_rmsnorm(y, z, g):
    b, n = z.shape[:2]
    v = y.reshape(b, n, SSD_WIDTH) * jax.nn.silu(z.astype(F32))
    v = v.reshape(b, n, SSD_GROUPS, SSD_WIDTH // SSD_GROUPS)
    v = v * lax.rsqrt(jnp.mean(v * v, axis=-1, keepdims=True) + NORM_EPS)
    return (v.reshape(b, n, SSD_WIDTH) * g.astype(F32)).astype(z.dtype)


def ssd_mixer(z_ctx, xbc_ctx, dt_ctx, z_lat, xbc_lat, dt_lat, conv_w, conv_b, dt_bias, a_log, d_skip, norm_g, need_ctx):
    def prep(xbc, dt_raw):
        xbc = jax.nn.silu(dwconv_centred(xbc, conv_w, conv_b)).astype(F32)
        b, n = xbc.shape[:2]
        xs, bm, cm = jnp.split(xbc, (SSD_WIDTH, SSD_WIDTH + SSD_GROUPS * SSD_STATE), axis=-1)
        xs = xs.reshape(b, n, SSD_HEADS, SSD_HEADDIM)
        rep = SSD_HEADS // SSD_GROUPS
        bm = jnp.repeat(bm.reshape(b, n, SSD_GROUPS, SSD_STATE), rep, axis=2)
        cm = jnp.repeat(cm.reshape(b, n, SSD_GROUPS, SSD_STATE), rep, axis=2)
        dts = jax.nn.softplus(dt_raw.astype(F32).reshape(b, n, 2, SSD_HEADS) + dt_bias.astype(F32))
        return xs, bm, cm, dts

    xs_c, b_c, c_c, dt_c = prep(xbc_ctx, dt_ctx)
    xs_l, b_l, c_l, dt_l = prep(xbc_lat, dt_lat)
    dsk = d_skip.astype(F32)[:, None]
    y_c = xs_c * dsk
    y_l = xs_l * dsk
    bsz = xs_c.shape[0]
    for direction in range(2):
        if direction == 0:
            flip = lambda t: t
        else:
            flip = lambda t: jnp.flip(t, axis=1)
        a_neg = -jnp.exp(a_log[direction].astype(F32))
        h0 = jnp.zeros((bsz, SSD_HEADS, SSD_HEADDIM, SSD_STATE), F32)
        yc_dir, h_ctx_end = ssd_scan(flip(xs_c), flip(dt_c[:, :, direction]), a_neg, flip(b_c), flip(c_c), h0)
        yl_dir, _ = ssd_scan(flip(xs_l), flip(dt_l[:, :, direction]), a_neg, flip(b_l), flip(c_l), h_ctx_end)
        y_l = y_l + flip(yl_dir)
        if need_ctx:
            y_c = y_c + flip(yc_dir)
    out_l = gated_group_rmsnorm(y_l, z_lat, norm_g)
    out_c = gated_group_rmsnorm(y_c, z_ctx, norm_g) if need_ctx else None
    return out_c, out_l


def axial_rope_tables(row, col):
    half = DA_HEAD_DIM // 2
    inv_freq = jnp.power(ROPE_BASE, -jnp.arange(0, half, 2, dtype=F32) / half)
    ang_r = row.astype(F32)[:, None] * inv_freq
    ang_c = col.astype(F32)[:, None] * inv_freq
    return jnp.cos(ang_r), jnp.sin(ang_r), jnp.cos(ang_c), jnp.sin(ang_c)


def rotate(x, cos, sin):
    x1, x2 = jnp.split(x, 2, axis=-1)
    return jnp.concatenate([x1 * cos - x2 * sin, x1 * sin + x2 * cos], axis=-1)


def apply_axial_rope(x, rope):
    cos_r, sin_r, cos_c, sin_c = (t[:, None, None, :] for t in rope)
    xf = x.astype(F32)
    half = DA_HEAD_DIM // 2
    out = jnp.concatenate([rotate(xf[..., :half], cos_r, sin_r), rotate(xf[..., half:], cos_c, sin_c)], axis=-1)
    return out.astype(x.dtype)


def diff_attn_core(q, k, v, lam):
    s = jnp.einsum("bqhcd,bkhcd->bhcqk", q, k).astype(F32)
    p = jax.nn.softmax(s, axis=-1)
    w = (p[:, :, 0] - lam * p[:, :, 1]).astype(v.dtype)
    return jnp.einsum("bhqk,bkhd->bqhd", w, v)


def diff_attn_mixer(q_ctx, k_ctx, v_ctx, q_lat, k_lat, v_lat, rope, lam_vec, subln_g, layer_idx, need_ctx):
    b, n = q_lat.shape[:2]
    m = q_ctx.shape[1]
    scale = DA_HEAD_DIM ** -0.5
    kc = k_ctx.reshape(b, m, DA_HEADS, 2, DA_HEAD_DIM)
    vc = v_ctx.reshape(b, m, DA_HEADS, DA_V_DIM)
    ql = apply_axial_rope(q_lat.reshape(b, n, DA_HEADS, 2, DA_HEAD_DIM), rope) * scale
    kl = apply_axial_rope(k_lat.reshape(b, n, DA_HEADS, 2, DA_HEAD_DIM), rope)
    vl = v_lat.reshape(b, n, DA_HEADS, DA_V_DIM)
    lam_init = 0.8 - 0.6 * math.exp(-0.3 * layer_idx)
    lv = lam_vec.astype(F32)
    lam = jnp.exp(jnp.sum(lv[0] * lv[1])) - jnp.exp(jnp.sum(lv[2] * lv[3])) + lam_init
    k_all = jnp.concatenate([kl, kc], axis=1)
    v_all = jnp.concatenate([vl, vc], axis=1)
    nb = n // Q_BLOCK
    q_blocks = ql.reshape(b, nb, Q_BLOCK, DA_HEADS, 2, DA_HEAD_DIM).swapaxes(0, 1)
    o_blocks = lax.map(lambda qb: diff_attn_core(qb, k_all, v_all, lam), q_blocks)
    o_lat = o_blocks.swapaxes(0, 1).reshape(b, n, DA_HEADS, DA_V_DIM)

    def finish(o):
        return (rmsnorm(o, subln_g) * (1.0 - lam_init)).reshape(o.shape[0], o.shape[1], DA_WIDTH)

    y_lat = finish(o_lat)
    if need_ctx:
        qc = q_ctx.reshape(b, m, DA_HEADS, 2, DA_HEAD_DIM) * scale
        y_ctx = finish(diff_attn_core(qc, kc, vc, lam))
    else:
        y_ctx = None
    return y_ctx, y_lat


def hybrid_layer(h_ctx, h_lat, mod_ctx, mod_lat, rope, layer_idx, need_ctx,
                 norm1_g, w_in, rg_conv_w, rg_conv_b, rg_w_a, rg_b_a, rg_w_x, rg_b_x, rg_lambda,
                 ssd_conv_w, ssd_conv_b, ssd_dt_bias, ssd_a_log, ssd_d, ssd_norm_g,
                 da_lambda, da_subln_g, w_out, norm2_g, w_gate, w_up, w_down):
    csh1, csc1, cg1, csh2, csc2, cg2 = mod_ctx
    sh1, sc1, g1, sh2, sc2, g2 = mod_lat
    u_ctx = modulate(rmsnorm(h_ctx, norm1_g), csh1, csc1) @ w_in
    u_lat = modulate(rmsnorm(h_lat, norm1_g), sh1, sc1) @ w_in
    rgx_c, rgg_c, sz_c, sxbc_c, sdt_c, q_c, k_c, v_c = jnp.split(u_ctx, IN_OFFSETS, axis=-1)
    rgx_l, rgg_l, sz_l, sxbc_l, sdt_l, q_l, k_l, v_l = jnp.split(u_lat, IN_OFFSETS, axis=-1)

    rg_c, rg_l = rglru_mixer(rgx_c, rgg_c, rgx_l, rgg_l, rg_conv_w, rg_conv_b,
                             rg_w_a, rg_b_a, rg_w_x, rg_b_x, rg_lambda, need_ctx)
    ssd_c, ssd_l = ssd_mixer(sz_c, sxbc_c, sdt_c, sz_l, sxbc_l, sdt_l, ssd_conv_w, ssd_conv_b,
                             ssd_dt_bias, ssd_a_log, ssd_d, ssd_norm_g, need_ctx)
    da_c, da_l = diff_attn_mixer(q_c, k_c, v_c, q_l, k_l, v_l, rope, da_lambda, da_subln_g,
                                 layer_idx, need_ctx)

    mix_lat = jnp.concatenate([rg_l, ssd_l, da_l], axis=-1) @ w_out
    h_lat = h_lat + g1 * mix_lat
    h_lat = h_lat + g2 * swiglu(modulate(rmsnorm(h_lat, norm2_g), sh2, sc2), w_gate, w_up, w_down)
    if need_ctx:
        mix_ctx = jnp.concatenate([rg_c, ssd_c, da_c], axis=-1) @ w_out
        h_ctx = h_ctx + cg1 * mix_ctx
        h_ctx = h_ctx + cg2 * swiglu(modulate(rmsnorm(h_ctx, norm2_g), csh2, csc2), w_gate, w_up, w_down)
    return h_ctx, h_lat


def setup_inputs(seed: int = 0) -> dict:
    key = jax.random.key(seed)
    ks = iter(jax.random.split(key, 48))

    def nrm(shape, scale):
        return jax.random.normal(next(ks), shape, F32) * scale

    def gain(shape):
        return 1.0 + nrm(shape, 0.02)

    a0 = jax.random.uniform(next(ks), (DEPTH, 2, RG_WIDTH), F32, 0.9, 0.999)
    s0 = a0 ** (1.0 / RG_C)
    rg_lambda = jnp.log(s0) - jnp.log1p(-s0)
    dt0 = jnp.exp(jax.random.uniform(next(ks), (DEPTH, 2, SSD_HEADS), F32, math.log(1e-3), math.log(1e-1)))
    ssd_dt_bias = dt0 + jnp.log(-jnp.expm1(-dt0))
    ssd_a_log = jnp.log(jax.random.uniform(next(ks), (DEPTH, 2, SSD_HEADS), F32, 1.0, 16.0))

    return {
        "x": nrm((BATCH, SEQ, D_MODEL), 1.0),
        "c": nrm((BATCH, D_MODEL), 1.0),
        "ctx": nrm((BATCH, CTX_LEN, D_MODEL), 1.0),
        "c_ctx": nrm((D_MODEL,), 1.0),
        "w_mod": nrm((DEPTH, D_MODEL, 6 * D_MODEL), 0.3 * D_MODEL ** -0.5),
        "b_mod": nrm((DEPTH, 6 * D_MODEL), 0.02),
        "norm1_g": gain((DEPTH, D_MODEL)),
        "w_in": nrm((DEPTH, D_MODEL, D_IN), D_MODEL ** -0.5),
        "rg_conv_w": nrm((DEPTH, RG_CONV, RG_WIDTH), RG_CONV ** -0.5),
        "rg_conv_b": nrm((DEPTH, RG_WIDTH), 0.02),
        "rg_w_a": nrm((DEPTH, 2, RG_HEADS, RG_BLOCK, RG_BLOCK), RG_BLOCK ** -0.5),
        "rg_b_a": nrm((DEPTH, 2, RG_WIDTH), 0.02),
        "rg_w_x": nrm((DEPTH, 2, RG_HEADS, RG_BLOCK, RG_BLOCK), RG_BLOCK ** -0.5),
        "rg_b_x": nrm((DEPTH, 2, RG_WIDTH), 0.02),
        "rg_lambda": rg_lambda,
        "ssd_conv_w": nrm((DEPTH, SSD_CONV, SSD_XBC), SSD_CONV ** -0.5),
        "ssd_conv_b": nrm((DEPTH, SSD_XBC), 0.02),
        "ssd_dt_bias": ssd_dt_bias,
        "ssd_a_log": ssd_a_log,
        "ssd_d": gain((DEPTH, SSD_HEADS)),
        "ssd_norm_g": gain((DEPTH, SSD_WIDTH)),
        "da_lambda": nrm((DEPTH, 4, DA_HEAD_DIM), 0.1),
        "da_subln_g": gain((DEPTH, DA_V_DIM)),
        "w_out": nrm((DEPTH, D_MIX, D_MODEL), D_MIX ** -0.5),
        "norm2_g": gain((DEPTH, D_MODEL)),
        "w_gate": nrm((DEPTH, D_MODEL, D_FF), D_MODEL ** -0.5),
        "w_up": nrm((DEPTH, D_MODEL, D_FF), D_MODEL ** -0.5),
        "w_down": nrm((DEPTH, D_FF, D_MODEL), D_FF ** -0.5),
        "final_norm_g": gain((D_MODEL,)),
    }


def reference(x, c, ctx, c_ctx, w_mod, b_mod, norm1_g, w_in, rg_conv_w, rg_conv_b, rg_w_a, rg_b_a,
              rg_w_x, rg_b_x, rg_lambda, ssd_conv_w, ssd_conv_b, ssd_dt_bias, ssd_a_log, ssd_d,
              ssd_norm_g, da_lambda, da_subln_g, w_out, norm2_g, w_gate, w_up, w_down, final_norm_g):
    n = x.shape[1]
    rows = n // GRID_W
    row = jnp.repeat(jnp.arange(rows, dtype=jnp.int32), GRID_W)
    col = jnp.tile(jnp.arange(GRID_W, dtype=jnp.int32), rows)
    rope = axial_rope_tables(row, col)
    h_lat, h_ctx = x, ctx
    for l in range(DEPTH):
        need_ctx = l < DEPTH - 1
        mod_lat = jnp.split((jax.nn.silu(c) @ w_mod[l] + b_mod[l])[:, None, :], 6, axis=-1)
        mod_ctx = jnp.split(jax.nn.silu(c_ctx) @ w_mod[l] + b_mod[l], 6, axis=-1)
        h_ctx, h_lat = hybrid_layer(
            h_ctx, h_lat, mod_ctx, mod_lat, rope, l, need_ctx,
            norm1_g[l], w_in[l], rg_conv_w[l], rg_conv_b[l], rg_w_a[l], rg_b_a[l], rg_w_x[l], rg_b_x[l],
            rg_lambda[l], ssd_conv_w[l], ssd_conv_b[l], ssd_dt_bias[l], ssd_a_log[l], ssd_d[l],
            ssd_norm_g[l], da_lambda[l], da_subln_g[l], w_out[l], norm2_g[l], w_gate[l], w_up[l], w_down[l])
    return rmsnorm(h_lat, final_norm_g)
```

```python
import contextlib
import math
import numpy as np
import concourse.bass as bass
import concourse.mybir as mybir
from concourse.bass_utils import run_bass_kernel_spmd

F32 = mybir.dt.float32
BF16 = mybir.dt.bfloat16
ALU = mybir.AluOpType
AF = mybir.ActivationFunctionType

D = 1024
NLAT = 2048
NCTX = 256
T = NLAT + NCTX
NBLK = T // 128
TILES = [(0, 512), (512, 512), (1024, 512), (1536, 512), (2048, 256)]
DFF = 2816
DIN = 2824
DINX = DIN + 1024
EPS = 1e-6
OFF_RGX, OFF_RGG, OFF_SZ, OFF_XBC, OFF_DT, OFF_Q, OFF_K, OFF_V = 0, 256, 512, 768, 1280, 1288, 1800, 2312
OFF_QS, OFF_KS = DIN, DIN + 512
SCRW = 14848


class _Op:
    __slots__ = ("eng", "fn", "waits", "tok", "clock", "dma", "idx")


class Prog:
    ENGS = ("pe", "act", "dve", "pool", "sp")

    def __init__(self, nc, stack, n_dma_sems=12):
        self.nc = nc
        self.stack = stack
        self.ops = {e: [] for e in self.ENGS}
        self.sem = {e: stack.enter_context(nc.semaphore("S_" + e)) for e in self.ENGS}
        self.cnt = {e: 0 for e in self.ENGS}
        self.know = {e: {} for e in self.ENGS}
        self.last_op = {e: None for e in self.ENGS}
        self.dma_sems = {}
        for q in ("sp", "pool"):
            self.dma_sems[q] = [[stack.enter_context(nc.semaphore("D_%s%d" % (q, i))), 0, None]
                                for i in range(n_dma_sems)]
        self.dma_rr = {q: 0 for q in self.dma_sems}
        self.last_w = {}
        self.readers = {}
        self.nops = 0
        self.nwaits = 0

    def sb(self, name, shape, dt=F32):
        return self.stack.enter_context(self.nc.sbuf_tensor(name, list(shape), dt))

    def ps(self, name, shape, dt=F32):
        return self.stack.enter_context(self.nc.psum_tensor(name, list(shape), dt))

    def _need(self, e, d, waits):
        if d is None:
            return
        if e == "pe" and d.eng == "pe" and not d.dma:
            return
        sem, val = d.tok
        k = self.know[e]
        if k.get(sem, 0) >= val:
            return
        waits.append((sem, val))
        for s, v in d.clock.items():
            if k.get(s, 0) < v:
                k[s] = v

    def op(self, eng, fn, reads=(), writes=(), dma=False, extra_deps=()):
        o = _Op()
        o.eng = eng
        o.fn = fn
        o.dma = dma
        o.idx = self.nops
        self.nops += 1
        deps = {}
        for r in reads:
            d = self.last_w.get(r)
            if d is not None:
                deps[d.idx] = d
        for w in writes:
            d = self.last_w.get(w)
            if d is not None:
                deps[d.idx] = d
            for d in self.readers.get(w, ()):
                deps[d.idx] = d
        for d in extra_deps:
            if d is not None:
                deps[d.idx] = d
        waits = []
        for i in sorted(deps, reverse=True):
            self._need(eng, deps[i], waits)
        if dma:
            pool = self.dma_sems[eng]
            slot = pool[self.dma_rr[eng] % len(pool)]
            self.dma_rr[eng] += 1
            if slot[2] is not None:
                self._need(eng, slot[2], waits)
            slot[1] += 16
            o.tok = (slot[0], slot[1])
            slot[2] = o
        elif fn is not None:
            self.cnt[eng] += 1
            o.tok = (self.sem[eng], self.cnt[eng])
        else:
            o.tok = None
        o.waits = waits
        self.nwaits += len(waits)
        o.clock = dict(self.know[eng])
        if o.tok is not None:
            o.clock[o.tok[0]] = o.tok[1]
            if not dma:
                self.last_op[eng] = o
        for w in writes:
            self.last_w[w] = o
            self.readers[w] = []
        for r in reads:
            if r not in writes:
                self.readers.setdefault(r, []).append(o)
        self.ops[eng].append(o)
        return o

    def pe(self, fn, reads=(), writes=()):
        return self.op("pe", fn, reads, writes)

    def act(self, fn, reads=(), writes=()):
        return self.op("act", fn, reads, writes)

    def dve(self, fn, reads=(), writes=()):
        return self.op("dve", fn, reads, writes)

    def pool(self, fn, reads=(), writes=()):
        return self.op("pool", fn, reads, writes)

    def dma(self, fn, reads=(), writes=(), q="sp"):
        return self.op(q, fn, reads, writes, dma=True)

    def barrier(self):
        pend = [self.last_op[e] for e in self.ENGS if self.last_op[e] is not None]
        for q in self.dma_sems:
            for slot in self.dma_sems[q]:
                if slot[2] is not None:
                    pend.append(slot[2])
        for e in self.ENGS:
            self.op(e, None, extra_deps=pend)

    def emit(self, final_reads=()):
        nc = self.nc
        fin_waits = []
        for r in final_reads:
            self._need("sp", self.last_w.get(r), fin_waits)

        def run(e, engine, extra=()):
            for o in self.ops[e]:
                for (s, v) in o.waits:
                    engine.wait_ge(s, v)
                if o.fn is None:
                    continue
                ins = o.fn(engine)
                ins.then_inc(o.tok[0], 16 if o.dma else 1)
            for (s, v) in extra:
                engine.wait_ge(s, v)

        with nc.Block() as block:
            @block.sync
            def _(eng):
                run("sp", eng, fin_waits)

            @block.gpsimd
            def _(eng):
                run("pool", eng)

            @block.scalar
            def _(eng):
                run("act", eng)

            @block.vector
            def _(eng):
                run("dve", eng)

            @block.tensor
            def _(eng):
                run("pe", eng)


def MMS(lst):
    def f(e):
        ins = None
        for (o, l, r, st, sp) in lst:
            ins = e.matmul(o, lhsT=l, rhs=r, start=st, stop=sp)
        return ins
    return f


def TRS(lst, ident):
    def f(e):
        ins = None
        for (o, i) in lst:
            ins = e.transpose(out=o, in_=i, identity=ident)
        return ins
    return f


def ACTF(out, in_, func, bias=0.0, scale=1.0, accum_out=None):
    if accum_out is None:
        return lambda e: e.activation(out=out, in_=in_, func=func, bias=bias, scale=scale)
    return lambda e: e.activation(out=out, in_=in_, func=func, bias=bias, scale=scale, accum_out=accum_out)


def TT(out, a, b, op):
    return lambda e: e.tensor_tensor(out=out, in0=a, in1=b, op=op)


def TS(out, a, s1, op0, s2=None, op1=None):
    if op1 is None:
        return lambda e: e.tensor_scalar(out=out, in0=a, scalar1=s1, scalar2=None, op0=op0)
    return lambda e: e.tensor_scalar(out=out, in0=a, scalar1=s1, scalar2=s2, op0=op0, op1=op1)


def STT(out, in0, scalar, in1, op0, op1):
    return lambda e: e.scalar_tensor_tensor(out=out, in0=in0, scalar=scalar, in1=in1, op0=op0, op1=op1)


def CP(out, in_):
    return lambda e: e.tensor_copy(out=out, in_=in_)


def RECIP(out, in_):
    return lambda e: e.reciprocal(out=out, in_=in_)


def MSET(ap, v):
    return lambda e: e.memset(ap, v)


def DMA(out, in_):
    return lambda e: e.dma_start(out=out, in_=in_)


def SCAN(out, a, b, init):
    return lambda e: e.tensor_tensor_scan(out=out, data0=a, data1=b, initial=init, op0=ALU.mult, op1=ALU.add)


PRM_SPEC = [("n1g", 16), ("n2g", 16), ("fng", 8), ("bmod", 96), ("rgcw", 16), ("rgcb", 4), ("rgba", 8),
            ("rgbx", 8), ("rglam", 8), ("scw", 32), ("scb", 8), ("subg", 2)]
PRM_OFF = {}
_o = 0
for _n, _w in PRM_SPEC:
    PRM_OFF[_n] = _o
    _o += _w
NPRM = _o
ROW_SPEC = [("dtb", 16), ("alog", 16), ("dsk", 8), ("sng", 512), ("dal", 512)]
ROW_OFF = {}
_o = 0
for _n, _w in ROW_SPEC:
    ROW_OFF[_n] = _o
    _o += _w
NROW = _o


def _fm(v):
    v = np.asarray(v, np.float32)
    lead = v.shape[:-1]
    n = v.shape[-1] // 128
    v = v.reshape(lead + (n, 128))
    return np.moveaxis(v, -1, 0).reshape(128, -1)


def pack_shared(inp):
    f = lambda k: np.asarray(inp[k], np.float32)
    prm = np.zeros((128, NPRM), np.float32)

    def put(name, arr):
        o = PRM_OFF[name]
        prm[:, o:o + arr.shape[1]] = arr

    put("n1g", _fm(f("norm1_g")))
    put("n2g", _fm(f("norm2_g")))
    put("fng", _fm(f("final_norm_g")))
    put("bmod", _fm(f("b_mod")))
    w = f("rg_conv_w").reshape(2, 4, 2, 128).transpose(3, 0, 2, 1).reshape(128, 16)
    put("rgcw", w)
    put("rgcb", _fm(f("rg_conv_b")))
    put("rgba", _fm(f("rg_b_a")))
    put("rgbx", _fm(f("rg_b_x")))
    put("rglam", _fm(f("rg_lambda")))
    w = f("ssd_conv_w").reshape(2, 4, 4, 128).transpose(3, 0, 2, 1).reshape(128, 32)
    put("scw", w)
    put("scb", _fm(f("ssd_conv_b")))
    put("subg", _fm(f("da_subln_g")))
    row = np.zeros((NROW,), np.float32)
    row[ROW_OFF["dtb"]:ROW_OFF["dtb"] + 16] = f("ssd_dt_bias").reshape(-1)
    row[ROW_OFF["alog"]:ROW_OFF["alog"] + 16] = f("ssd_a_log").reshape(-1)
    row[ROW_OFF["dsk"]:ROW_OFF["dsk"] + 8] = f("ssd_d").reshape(-1)
    row[ROW_OFF["sng"]:ROW_OFF["sng"] + 512] = f("ssd_norm_g").reshape(-1)
    row[ROW_OFF["dal"]:ROW_OFF["dal"] + 512] = f("da_lambda").reshape(-1)
    rowp = np.ascontiguousarray(np.broadcast_to(row[None, :], (128, NROW)))
    bd = np.zeros((128, 16, 128), np.float32)
    wa, wx = f("rg_w_a"), f("rg_w_x")
    for l in range(2):
        for d in range(2):
            for cc in range(2):
                for ax, wsrc in enumerate((wa, wx)):
                    idx = ((l * 2 + d) * 2 + cc) * 2 + ax
                    for blk in range(2):
                        bd[blk * 64:(blk + 1) * 64, idx, blk * 64:(blk + 1) * 64] = wsrc[l, d, 2 * cc + blk]
    j = np.arange(64)
    sw = np.where((j % 32) < 16, j + 16, j - 16)
    win = f("w_in")
    qidx = np.concatenate([OFF_Q + h * 128 + c * 64 + sw for h in range(4) for c in range(2)])
    kidx = np.concatenate([OFF_K + h * 128 + c * 64 + sw for h in range(4) for c in range(2)])
    winx = np.ascontiguousarray(np.concatenate([win, win[:, :, qidx], win[:, :, kidx]], axis=2))
    ident = np.eye(128, dtype=np.float32)
    tt = np.arange(128)
    uf = (tt[:, None] <= tt[None, :]).astype(np.float32)
    ub = (tt[:, None] >= tt[None, :]).astype(np.float32)
    t = np.arange(NLAT)
    row_i = (t // 64).astype(np.float32)
    col_i = (t % 64).astype(np.float32)
    half = 32
    inv_freq = np.power(np.float32(10000.0), -np.arange(0, half, 2, dtype=np.float32) / np.float32(half)).astype(np.float32)
    ang_r = row_i[:, None] * inv_freq[None, :]
    ang_c = col_i[:, None] * inv_freq[None, :]
    rope = np.zeros((128, 2, NLAT), np.float32)
    for p in range(128):
        jj = p % 64
        ang = ang_r if jj < 32 else ang_c
        i = jj % 16
        rope[p, 0] = np.cos(ang[:, i])
        sgn = -1.0 if (jj % 32) < 16 else 1.0
        rope[p, 1] = sgn * np.sin(ang[:, i])
    return dict(prm=prm, rowp=rowp, bd=bd, w_in_x=winx, ident=ident, uf=uf, ub=ub, rope=rope,
                w_mod=f("w_mod"), w_out=f("w_out"), w_gate=f("w_gate"), w_up=f("w_up"), w_down=f("w_down"))


class Kern:
    def __init__(self, nc, P, NB, opts):
        self.nc = nc
        self.P = P
        self.NB = NB
        self.o = opts
        dt_in = lambda n, s: nc.dram_tensor(n, list(s), F32, kind="ExternalInput").ap()
        self.x = dt_in("x", [NB, NLAT, D])
        self.ctx = dt_in("ctx", [NB, NCTX, D])
        self.ct = dt_in("ct", [128, 8, 3])
        self.prm_d = dt_in("prm", [128, NPRM])
        self.rowp_d = dt_in("rowp", [128, NROW])
        self.bd_d = dt_in("bd", [128, 16, 128])
        self.w_in = dt_in("w_in_x", [2, D, DINX])
        self.ident_d = dt_in("ident", [128, 128])
        self.uf_d = dt_in("uf", [128, 128])
        self.ub_d = dt_in("ub", [128, 128])
        self.rope_d = dt_in("rope", [128, 2, NLAT])
        self.w_mod = dt_in("w_mod", [2, D, 6 * D])
        self.w_out = dt_in("w_out", [2, D, D])
        self.w_gate = dt_in("w_gate", [2, D, DFF])
        self.w_up = dt_in("w_up", [2, D, DFF])
        self.w_down = dt_in("w_down", [2, DFF, D])
        self.out = nc.dram_tensor("out", [NB, NLAT, D], F32, kind="ExternalOutput").ap()
        sb = P.sb
        self.resid = sb("resid", [128, 8 * T])
        self.R3 = self.resid[:].rearrange("p (c t) -> p c t", c=8)
        self.xn = sb("xn", [128, 8 * T], BF16)
        self.XN = self.xn[:].rearrange("p (c t) -> p c t", c=8)
        self.mixb = sb("mixb", [128, 4 * T], BF16)
        self.MX = self.mixb[:].rearrange("p (c t) -> p c t", c=4)
        self.scr = sb("scr", [128, SCRW])
        self.WB = [sb("wb%d" % i, [128, 1024], BF16) for i in range(4)]
        self.wrr = 0
        self.ident = sb("ident_s", [128, 128])
        self.uf = sb("uf_s", [128, 128])
        self.ub = sb("ub_s", [128, 128])
        self.ones_f = sb("ones_f", [128, 128])
        self.ones_b = sb("ones_b", [128, 128], BF16)
        self.prm = sb("prm_s", [128, NPRM])
        self.rowp = sb("rowp_s", [128, NROW])
        self.bd = sb("bd_s", [128, 16 * 128], BF16)
        self.BD = self.bd[:].rearrange("p (i j) -> p i j", i=16)
        self.modt = sb("modt", [128, 2 * 48 * 3])
        self.MT = self.modt[:].rearrange("p (l j n) -> p l j n", l=2, j=48)
        self.amod = sb("amod", [128, 2 * 2 * 3 * 8])
        self.AM = self.amod[:].rearrange("p (l w n c) -> p l w n c", l=2, w=2, n=3)
        self.small = sb("small", [128, 256])
        self.sml_o = 0
        self.pb = [P.ps("pb%d" % i, [128, 512]) for i in range(8)]
        self.rot = [0, 1, 2, 3]
        self.rr = 0

    def prmcol(self, name, i):
        o = PRM_OFF[name] + i
        return self.prm[:, o:o + 1]

    def salloc(self, n):
        o = self.sml_o
        self.sml_o += n
        assert self.sml_o <= 256
        return self.small[:, o:o + n]

    def scr_f(self, off, n):
        return self.scr[:, off:off + n]

    def scr_b(self, off, n):
        assert n % 2 == 0
        return self.scr[:, off:off + n // 2].bitcast(BF16)

    def next_bank(self):
        i = self.rot[self.rr % len(self.rot)]
        self.rr += 1
        return self.pb[i], "pb%d" % i

    def load_w(self, wsrc, k0, kch, col0, ncols=128):
        i = self.wrr % len(self.WB)
        self.wrr += 1
        wb = self.WB[i][:, 0:kch * ncols].rearrange("p (c n) -> p c n", c=kch)
        src = wsrc.rearrange("(c p) n -> p c n", p=128)[:, k0:k0 + kch, col0:col0 + ncols]
        tok = "wb%d" % i
        self.P.dma(DMA(wb, src), writes=[tok], q="pool")
        return wb, tok

    def linear_fm(self, wsrc, col0, tiles, evac, kch=8, k0=0, rhs=None, rtok=None, ncols=128):
        wb, wtok = self.load_w(wsrc, k0, kch, col0, ncols)
        if rhs is None:
            rhs = lambda kc, t0, n: self.XN[:, kc, t0:t0 + n]
            rtok = lambda t: ["xn.%d" % t]
        for t in tiles:
            t0, n = TILES[t]
            bank, btok = self.next_bank()
            out = bank[0:ncols, 0:n]
            mms = [(out, wb[:, kc, :], rhs(kc, t0, n), kc == 0, kc == kch - 1) for kc in range(kch)]
            self.P.pe(MMS(mms), reads=[wtok] + rtok(t), writes=[btok])
            evac(t, out, btok)

    def linear_fm2(self, wsrc, colA, colB, tiles, evac2, extraA=(), evacA=None):
        wa, atok = self.load_w(wsrc, 0, 8, colA)
        wb2, btok2 = self.load_w(wsrc, 0, 8, colB)
        for t in tiles:
            t0, n = TILES[t]
            ba, bat = self.next_bank()
            bb, bbt = self.next_bank()
            self.P.pe(MMS([(ba[:, 0:n], wa[:, kc, :], self.XN[:, kc, t0:t0 + n], kc == 0, kc == 7) for kc in range(8)]
                          + [(bb[:, 0:n], wb2[:, kc, :], self.XN[:, kc, t0:t0 + n], kc == 0, kc == 7) for kc in range(8)]),
                      reads=[atok, btok2, "xn.%d" % t], writes=[bat, bbt])
            evac2(t, ba[:, 0:n], bat, bb[:, 0:n], bbt)
        for t in extraA:
            t0, n = TILES[t]
            ba, bat = self.next_bank()
            self.P.pe(MMS([(ba[:, 0:n], wa[:, kc, :], self.XN[:, kc, t0:t0 + n], kc == 0, kc == 7) for kc in range(8)]),
                      reads=[atok, "xn.%d" % t], writes=[bat])
            evacA(t, ba[:, 0:n], bat)

    def linear_tm(self, wsrc, col0, ncols, blocks, evac, grp):
        wb, wtok = self.load_w(wsrc, 0, 8, col0, ncols)
        for g0 in range(0, len(blocks), grp):
            blks = blocks[g0:g0 + grp]
            bank, btok = self.next_bank()
            mms = []
            rt = set()
            for gi, i in enumerate(blks):
                out = bank[:, gi * ncols:(gi + 1) * ncols]
                rt.add("xn.%d" % min(i // 4, 4))
                for kc in range(8):
                    mms.append((out, self.XN[:, kc, i * 128:(i + 1) * 128], wb[:, kc, :], kc == 0, kc == 7))
            self.P.pe(MMS(mms), reads=[wtok] + sorted(rt), writes=[btok])
            evac(blks, bank[:, 0:len(blks) * ncols], btok)

    def softplus(self, out, x, n, s0, toks):
        P = self.P
        ax, e, z, p = (s0[:, i * n:(i + 1) * n] for i in range(4))
        rd, wr = toks
        tk = ["spl.tmp"]
        P.dve(STT(ax, x, -1.0, x, ALU.mult, ALU.max), reads=rd, writes=tk)
        P.act(ACTF(e, ax, AF.Exp, scale=-1.0), reads=tk, writes=tk)
        P.dve(TS(z, e, 2.0, ALU.add), reads=tk, writes=tk)
        P.dve(RECIP(z, z), reads=tk, writes=tk)
        P.dve(TT(z, z, e, ALU.mult), reads=tk, writes=tk)
        P.dve(TT(e, z, z, ALU.mult), reads=tk, writes=tk)
        P.dve(TS(p, e, 1.0 / 13.0, ALU.mult, 1.0 / 11.0, ALU.add), reads=tk, writes=tk)
        for cst in (1.0 / 9.0, 1.0 / 7.0, 1.0 / 5.0, 1.0 / 3.0, 1.0):
            P.dve(TT(p, p, e, ALU.mult), reads=tk, writes=tk)
            P.dve(TS(p, p, cst, ALU.add), reads=tk, writes=tk)
        P.dve(TT(p, p, z, ALU.mult), reads=tk, writes=tk)
        P.dve(TS(ax, x, 0.0, ALU.max), reads=rd + tk, writes=tk)
        P.dve(STT(out, p, 2.0, ax, ALU.mult, ALU.add), reads=tk, writes=wr)

    def setup(self):
        P = self.P
        P.dma(DMA(self.ident[:], self.ident_d), writes=["ident"])
        P.dma(DMA(self.uf[:], self.uf_d), writes=["uf"])
        P.dma(DMA(self.ub[:], self.ub_d), writes=["ub"])
        P.dma(DMA(self.prm[:], self.prm_d), writes=["prm"])
        P.dma(DMA(self.rowp[:], self.rowp_d), writes=["rowp"])
        P.dma(DMA(self.BD, self.bd_d), writes=["bd"], q="pool")
        P.pool(MSET(self.ones_f[:], 1.0), writes=["ones_f"])
        P.pool(MSET(self.ones_b[:], 1.0), writes=["ones_b"])
        ctile = self.scr_f(0, 24).rearrange("p (k n) -> p k n", k=8)
        sig = self.scr_f(24, 24).rearrange("p (k n) -> p k n", k=8)
        P.dma(DMA(ctile, self.ct), writes=["ct"])
        P.act(ACTF(sig, ctile, AF.Sigmoid), reads=["ct"], writes=["sig"])
        P.dve(TT(ctile, ctile, sig, ALU.mult), reads=["ct", "sig"], writes=["ct"])
        WM = [self.scr_f(64 + i * 4096, 4096).rearrange("p (k n) -> p k n", k=8) for i in range(2)]
        for l in range(2):
            bank = self.pb[4 + l]
            btok = "pb%d" % (4 + l)
            for j in range(12):
                wm = WM[j % 2]
                wtok = "wm%d" % (j % 2)
                src = self.w_mod[l].rearrange("(c p) n -> p c n", p=128)[:, :, j * 512:(j + 1) * 512]
                P.dma(DMA(wm, src), writes=[wtok])
                mms = []
                for i in range(4):
                    jj = j * 4 + i
                    for kc in range(8):
                        mms.append((bank[:, jj * 3:jj * 3 + 3], wm[:, kc, i * 128:(i + 1) * 128], ctile[:, kc, :],
                                    kc == 0, kc == 7))
                P.pe(MMS(mms), reads=[wtok, "ct"], writes=[btok])
            bo = PRM_OFF["bmod"] + l * 48
            P.dve(TT(self.MT[:, l], bank[:, 0:144].rearrange("p (j n) -> p j n", j=48),
                     self.prm[:, bo:bo + 48].unsqueeze(2).to_broadcast([128, 48, 3]), ALU.add),
                  reads=[btok, "prm"], writes=["modt"])
            for w in range(2):
                gname = "n1g" if w == 0 else "n2g"
                go = PRM_OFF[gname] + l * 8
                for n in range(3):
                    jj0 = 8 + 24 * w
                    P.dve(STT(self.AM[:, l, w, n, :], self.MT[:, l, jj0:jj0 + 8, n], 1.0, self.prm[:, go:go + 8],
                              ALU.add, ALU.mult), reads=["modt", "prm"], writes=["amod"])
        self.m8sp = self.salloc(8)
        nl = self.salloc(8)
        lo = PRM_OFF["rglam"]
        P.dve(TS(nl, self.prm[:, lo:lo + 8], -1.0, ALU.mult), reads=["prm"], writes=["nl"])
        self.softplus(self.m8sp, nl, 8, self.scr_f(8300, 32), (["nl"], ["m8sp"]))
        P.dve(TS(self.m8sp, self.m8sp, -8.0, ALU.mult), reads=["m8sp"], writes=["m8sp"])
        self.hm = self.salloc(8)
        self.hba = self.salloc(8)
        self.hbx = self.salloc(8)
        P.dve(TS(self.hm, self.m8sp, 0.5, ALU.mult), reads=["m8sp"], writes=["hm"])
        oa, ox = PRM_OFF["rgba"], PRM_OFF["rgbx"]
        P.dve(TS(self.hba, self.prm[:, oa:oa + 8], 0.5, ALU.mult), reads=["prm"], writes=["hba"])
        P.dve(TS(self.hbx, self.prm[:, ox:ox + 8], 0.5, ALU.mult), reads=["prm"], writes=["hbx"])
        self.aneg = self.salloc(16)
        ao = ROW_OFF["alog"]
        P.act(ACTF(self.aneg, self.rowp[:, ao:ao + 16], AF.Exp), reads=["rowp"], writes=["aneg"])
        P.dve(TS(self.aneg, self.aneg, -1.0, ALU.mult), reads=["aneg"], writes=["aneg"])
        self.nlam = self.salloc(2)
        self.gsc = self.salloc(2)
        pr = self.scr_f(8400, 128)
        sm = self.salloc(4)
        for l in range(2):
            do = ROW_OFF["dal"] + l * 256
            lam_init = 0.8 - 0.6 * math.exp(-0.3 * l)
            for i in range(2):
                P.dve(TT(pr[:, 0:64], self.rowp[:, do + 128 * i:do + 128 * i + 64],
                         self.rowp[:, do + 128 * i + 64:do + 128 * i + 128], ALU.mult), reads=["rowp"], writes=["pr"])
                P.dve(lambda e, o_=sm[:, 2 * l + i:2 * l + i + 1], i_=pr[:, 0:64]: e.reduce_sum(
                    out=o_, in_=i_, axis=mybir.AxisListType.X), reads=["pr"], writes=["sm"])
            P.act(ACTF(sm[:, 2 * l:2 * l + 2], sm[:, 2 * l:2 * l + 2], AF.Exp), reads=["sm"], writes=["sm"])
            P.dve(TT(self.nlam[:, l:l + 1], sm[:, 2 * l + 1:2 * l + 2], sm[:, 2 * l:2 * l + 1], ALU.subtract),
                  reads=["sm"], writes=["nlam"])
            P.dve(TS(self.nlam[:, l:l + 1], self.nlam[:, l:l + 1], -lam_init, ALU.add), reads=["nlam"], writes=["nlam"])
            P.dve(TS(self.gsc[:, l:l + 1], self.prmcol("subg", l), 1.0 - lam_init, ALU.mult), reads=["prm"],
                  writes=["gsc"])
        P.barrier()

    def load_x(self, b):
        P = self.P
        STG = [self.scr_f(i * 1024, 1024) for i in range(2)]
        self.rot = [0, 1, 2, 3]
        for i in range(NBLK):
            src = self.x[b, i * 128:(i + 1) * 128, :] if i < 16 else self.ctx[b, (i - 16) * 128:(i - 15) * 128, :]
            stg = STG[i % 2]
            stok = "stg%d" % (i % 2)
            P.dma(DMA(stg, src), writes=[stok])
            t = min(i // 4, 4)
            for half in range(2):
                bank, btok = self.next_bank()
                P.pe(TRS([(bank[:, j * 128:(j + 1) * 128], stg[:, (half * 4 + j) * 128:(half * 4 + j + 1) * 128])
                          for j in range(4)], self.ident[:]), reads=[stok, "ident"], writes=[btok])
                dst = self.R3[:, half * 4:half * 4 + 4, i * 128:(i + 1) * 128]
                srcp = bank[:, 0:512].rearrange("p (c t) -> p c t", c=4)
                wr = ["r.%d.%d" % (c, t) for c in range(half * 4, half * 4 + 4)]
                if half == 0:
                    P.act(ACTF(dst, srcp, AF.Identity), reads=[btok], writes=wr)
                else:
                    P.dve(CP(dst, srcp), reads=[btok], writes=wr)

    def rstd_tile(self, t, RS, SQ):
        P = self.P
        t0, n = TILES[t]
        bank = self.pb[7]
        for c in range(8):
            sq = SQ[c % 2]
            P.dve(TT(sq[:, :n], self.R3[:, c, t0:t0 + n], self.R3[:, c, t0:t0 + n], ALU.mult),
                  reads=["r.%d.%d" % (c, t)], writes=["sq%d" % (c % 2)])
            P.pe(MMS([(bank[:, :n], self.ones_b[:], sq[:, :n], c == 0, c == 7)]), reads=["sq%d" % (c % 2), "ones_b"],
                 writes=["pb7"])
        P.act(ACTF(RS[:, :n], bank[:, :n], AF.Sqrt, bias=EPS, scale=1.0 / D), reads=["pb7"], writes=["rs"])
        P.dve(RECIP(RS[:, :n], RS[:, :n]), reads=["rs"], writes=["rs"])

    def norm(self, b, l, w, tiles):
        P = self.P
        SQ = [self.scr_b(i * 256, 512) for i in range(2)]
        RS = self.scr_f(512, 512)
        TM = [self.scr_f(1024 + i * 512, 512) for i in range(2)]
        for t in tiles:
            t0, n = TILES[t]
            nn = b if t < 4 else 2
            self.rstd_tile(t, RS, SQ)
            for c in range(8):
                tm = TM[c % 2]
                P.dve(TT(tm[:, :n], self.R3[:, c, t0:t0 + n], RS[:, :n], ALU.mult), reads=["r.%d.%d" % (c, t), "rs"],
                      writes=["tm%d" % (c % 2)])
                P.act(ACTF(self.XN[:, c, t0:t0 + n], tm[:, :n], AF.Identity,
                           bias=self.MT[:, l, 24 * w + c, nn:nn + 1], scale=self.AM[:, l, w, nn, c:c + 1]),
                      reads=["tm%d" % (c % 2), "modt", "amod"], writes=["xn.%d" % t])

    def wout_partial(self, b, l, rows0, tiles):
        P = self.P
        self.rot = [0, 1, 2, 3]
        for c in range(8):
            def evac(t, ps, btok, c=c):
                t0, n = TILES[t]
                nn = b if t < 4 else 2
                dst = self.R3[:, c, t0:t0 + n]
                P.dve(STT(dst, ps, self.MT[:, l, 16 + c, nn:nn + 1], dst, ALU.mult, ALU.add),
                      reads=[btok, "modt", "r.%d.%d" % (c, t)], writes=["r.%d.%d" % (c, t)])
            self.linear_fm(self.w_out[l], c * 128, tiles, evac, kch=4, k0=rows0 // 128,
                           rhs=lambda kc, t0, n: self.MX[:, kc, t0:t0 + n], rtok=lambda t: ["mx.%d" % t])

    def ffn(self, b, l, tiles):
        P = self.P
        HH = self.scr_b(0, 6 * T).rearrange("p (j t) -> p j t", j=6)
        SG = [self.scr_f(7000 + i * 512, 512) for i in range(2)]
        groups = [(0, 6), (6, 6), (12, 5), (17, 5)]
        it = 0
        for (j0, J) in groups:
            for jl in range(J):
                j = j0 + jl
                wg, gtok = self.load_w(self.w_gate[l], 0, 8, j * 128)
                wu, utok = self.load_w(self.w_up[l], 0, 8, j * 128)
                for t in tiles:
                    t0, n = TILES[t]
                    bg, bgt = self.pb[it % 2], "pb%d" % (it % 2)
                    bu, but = self.pb[2 + it % 2], "pb%d" % (2 + it % 2)
                    sg = SG[it % 2]
                    sgt = "sg%d" % (it % 2)
                    it += 1
                    P.pe(MMS([(bg[:, :n], wg[:, kc, :], self.XN[:, kc, t0:t0 + n], kc == 0, kc == 7) for kc in range(8)]),
                         reads=[gtok, "xn.%d" % t], writes=[bgt])
                    P.pe(MMS([(bu[:, :n], wu[:, kc, :], self.XN[:, kc, t0:t0 + n], kc == 0, kc == 7) for kc in range(8)]),
                         reads=[utok, "xn.%d" % t], writes=[but])
                    P.act(ACTF(sg[:, :n], bg[:, :n], AF.Silu), reads=[bgt], writes=[sgt])
                    P.dve(TT(HH[:, jl, t0:t0 + n], sg[:, :n], bu[:, :n], ALU.mult), reads=[sgt, but],
                          writes=["hh.%d.%d" % (jl, t)])
            for c in range(8):
                wd, dtok = self.load_w(self.w_down[l], j0, J, c * 128)
                for t in tiles:
                    t0, n = TILES[t]
                    nn = b if t < 4 else 2
                    bi = 4 + (it % 3)
                    it += 1
                    bank, btok = self.pb[bi], "pb%d" % bi
                    P.pe(MMS([(bank[:, :n], wd[:, jl, :], HH[:, jl, t0:t0 + n], jl == 0, jl == J - 1) for jl in range(J)]),
                         reads=[dtok] + ["hh.%d.%d" % (jl, t) for jl in range(J)], writes=[btok])
                    dst = self.R3[:, c, t0:t0 + n]
                    P.dve(STT(dst, bank[:, :n], self.MT[:, l, 40 + c, nn:nn + 1], dst, ALU.mult, ALU.add),
                          reads=[btok, "modt", "r.%d.%d" % (c, t)], writes=["r.%d.%d" % (c, t)])

    def final(self, b):
        P = self.P
        SQ = [self.scr_b(i * 256, 512) for i in range(2)]
        RS = self.scr_f(512, 512)
        TM = [self.scr_f(1024 + i * 512, 512) for i in range(2)]
        YN = self.scr_f(2048, 4096).rearrange("p (c t) -> p c t", c=8)
        OST = [self.scr_f(6144 + i * 1024, 1024) for i in range(2)]
        self.rot = [0, 1, 2, 3]
        oi = 0
        for t in range(4):
            t0, n = TILES[t]
            self.rstd_tile(t, RS, SQ)
            for c in range(8):
                tm = TM[c % 2]
                P.dve(TT(tm[:, :n], self.R3[:, c, t0:t0 + n], RS[:, :n], ALU.mult), reads=["r.%d.%d" % (c, t), "rs"],
                      writes=["tm%d" % (c % 2)])
                P.act(ACTF(YN[:, c, :], tm[:, :n], AF.Identity, scale=self.prmcol("fng", c)),
                      reads=["tm%d" % (c % 2), "prm"], writes=["yn.%d" % c])
            for j in range(4):
                ost = OST[oi % 2]
                otok = "ost%d" % (oi % 2)
                oi += 1
                for half in range(2):
                    bank, btok = self.next_bank()
                    P.pe(TRS([(bank[:, cc * 128:(cc + 1) * 128], YN[:, half * 4 + cc, j * 128:(j + 1) * 128])
                              for cc in range(4)], self.ident[:]),
                         reads=["yn.%d" % (half * 4 + cc) for cc in range(4)] + ["ident"], writes=[btok])
                    if half == 0:
                        P.act(ACTF(ost[:, 0:512], bank[:, 0:512], AF.Identity), reads=[btok], writes=[otok])
                    else:
                        P.dve(CP(ost[:, 512:1024], bank[:, 0:512]), reads=[btok], writes=[otok])
                r0 = t0 + j * 128
                P.dma(DMA(self.out[b, r0:r0 + 128, :], ost), reads=[otok], writes=["out"])

    def da_mixer(self, b, l):
        P = self.P
        need_ctx = (l == 0)
        o = 0
        ROPE = self.scr_f(o, 4096).rearrange("p (a t) -> p a t", a=2); o += 4096
        QR = self.scr_b(o, T); o += T // 2
        KC = [self.scr_b(o + i * (T // 2), T) for i in range(2)]; o += T
        VT = self.scr_b(o, T).rearrange("p (i d) -> p i d", i=NBLK); o += T // 2
        T1 = [self.scr_f(o + i * 512, 512) for i in range(2)]; o += 1024
        T2 = [self.scr_f(o + i * 512, 512) for i in range(2)]; o += 1024
        PT = [self.scr_b(o + i * 256, 512) for i in range(3)]; o += 768
        RC = self.scr_f(o, 512); o += 512
        TC = [self.scr_f(o + i * 512, 512) for i in range(2)]; o += 1024
        O2 = self.scr_f(o, 512); o += 512
        OV = [self.scr_f(o + i * 512, 512) for i in range(2)]; o += 1024
        assert o <= SCRW, o
        P.dma(DMA(ROPE, self.rope_d), writes=["rope"])
        P.pool(MSET(KC[0][64:128, :], 0.0), writes=["kc0z"])
        P.pool(MSET(KC[1][0:64, :], 0.0), writes=["kc1z"])
        lat = [0, 1, 2, 3]
        W = self.w_in[l]
        ti = [0]
        pend = {"A": None, "B": None}
        fcount = [0]

        def emitA():
            f = pend["A"]
            pend["A"] = None
            if f is not None:
                f()

        def emitB():
            f = pend["B"]
            pend["B"] = None
            if f is not None:
                f()

        def make_final(h, qt, q0, qn):
            fi = fcount[0] % 2
            fcount[0] += 1
            ov = OV[fi][:, :qn]
            ovt = "ov%d" % fi

            def fa():
                P.dve(STT(ov, TC[1][:, :qn], self.nlam[:, l:l + 1], TC[0][:, :qn], ALU.mult, ALU.add),
                      reads=["tc0", "tc1", "nlam"], writes=[ovt])
                P.dve(TT(O2[:, :qn], ov, ov, ALU.mult), reads=[ovt], writes=["o2"])

            def fb():
                P.pe(MMS([(self.pb[7][:, :qn], self.ones_f[:], O2[:, :qn], True, True)]), reads=["o2", "ones_f"],
                     writes=["pb7"])
                P.act(ACTF(O2[:, :qn], self.pb[7][:, :qn], AF.Ln, bias=EPS, scale=1.0 / 128.0), reads=["pb7"],
                      writes=["o2"])
                P.act(ACTF(O2[:, :qn], O2[:, :qn], AF.Exp, scale=-0.5), reads=["o2"], writes=["o2"])
                P.dve(TT(ov, ov, O2[:, :qn], ALU.mult), reads=[ovt, "o2"], writes=[ovt])
                P.act(ACTF(self.MX[:, h, q0:q0 + qn], ov, AF.Identity, scale=self.gsc[:, l:l + 1]),
                      reads=[ovt, "gsc"], writes=["mx.%d" % qt])
            return fa, fb

        for h in range(4):
            self.rot = [0, 1, 2, 3]

            def ev_q(t, pa, ta, pb_, tb):
                t0, n = TILES[t]
                i = ti[0] % 2
                ti[0] += 1
                P.dve(TT(T1[i][:, :n], pa, ROPE[:, 0, t0:t0 + n], ALU.mult), reads=[ta, "rope"], writes=["t1.%d" % i])
                P.dve(TT(T2[i][:, :n], pb_, ROPE[:, 1, t0:t0 + n], ALU.mult), reads=[tb, "rope"], writes=["t2.%d" % i])
                P.pool(TT(QR[:, t0:t0 + n], T1[i][:, :n], T2[i][:, :n], ALU.add), reads=["t1.%d" % i, "t2.%d" % i],
                       writes=["qr.%d" % t])

            def ev_k(t, pa, ta, pb_, tb):
                t0, n = TILES[t]
                i = ti[0] % 2
                ti[0] += 1
                P.dve(TT(T1[i][:, :n], pa, ROPE[:, 0, t0:t0 + n], ALU.mult), reads=[ta, "rope"], writes=["t1.%d" % i])
                P.dve(TT(T2[i][:, :n], pb_, ROPE[:, 1, t0:t0 + n], ALU.mult), reads=[tb, "rope"], writes=["t2.%d" % i])
                P.pool(TT(KC[0][0:64, t0:t0 + n], T1[i][0:64, :n], T2[i][0:64, :n], ALU.add),
                       reads=["t1.%d" % i, "t2.%d" % i, "kc0z"], writes=["kc0.%d" % t])
                P.pool(TT(KC[1][64:128, t0:t0 + n], T1[i][64:128, :n], T2[i][64:128, :n], ALU.add),
                       reads=["t1.%d" % i, "t2.%d" % i, "kc1z"], writes=["kc1.%d" % t])

            def ev_qc(t, ps, btok):
                t0, n = TILES[t]
                P.act(ACTF(QR[:, t0:t0 + n], ps, AF.Identity), reads=[btok], writes=["qr.%d" % t])

            def ev_kc(t, ps, btok):
                t0, n = TILES[t]
                P.act(ACTF(KC[0][0:64, t0:t0 + n], ps[0:64, :], AF.Identity), reads=[btok, "kc0z"], writes=["kc0.%d" % t])
                P.act(ACTF(KC[1][64:128, t0:t0 + n], ps[64:128, :], AF.Identity), reads=[btok, "kc1z"],
                      writes=["kc1.%d" % t])
            self.linear_fm2(W, OFF_Q + h * 128, OFF_QS + h * 128, lat, ev_q, [4] if need_ctx else [], ev_qc)
            emitA()
            self.linear_fm2(W, OFF_K + h * 128, OFF_KS + h * 128, lat, ev_k, [4], ev_kc)
            emitB()

            def ev_v(blks, ps, btok):
                i0 = blks[0]
                P.act(ACTF(VT[:, i0:i0 + len(blks), :], ps.rearrange("p (i d) -> p i d", i=len(blks)), AF.Identity),
                      reads=[btok], writes=["vt.%d" % i for i in blks])
            self.linear_tm(W, OFF_V + h * 128, 128, list(range(NBLK)), ev_v, 4)

            for qt in (lat + ([4] if need_ctx else [])):
                q0, qn = TILES[qt]
                kts = list(range(NBLK)) if qt < 4 else [16, 17]
                steps = [(c, kt) for c in range(2) for kt in kts]

                def emit_s(i):
                    c, kt = steps[i]
                    bi = i % 3
                    P.pe(MMS([(self.pb[bi][:, :qn], KC[c][:, kt * 128:(kt + 1) * 128], QR[:, q0:q0 + qn], True, True)]),
                         reads=["kc%d.%d" % (c, min(kt // 4, 4)), "kc%dz" % c, "qr.%d" % qt], writes=["pb%d" % bi])
                ACCD = [T1[0], T1[1]]
                ACCP = [T2[0], T2[1]]
                accd_t = ["t1.0", "t1.1"]
                accp_t = ["t2.0", "t2.1"]
                psum_pend = [None]

                def make_sumfin(c, have_d, have_p):
                    def f():
                        mm = []
                        if have_d:
                            mm.append((self.pb[5 + c][:, :qn], self.ones_f[:], ACCD[c][:, :qn], False, not have_p))
                        if have_p:
                            mm.append((self.pb[5 + c][:, :qn], self.ones_f[:], ACCP[c][:, :qn], False, True))
                        if mm:
                            P.pe(MMS(mm), reads=["ones_f", accd_t[c], accp_t[c]], writes=["pb%d" % (5 + c)])
                        P.dve(RECIP(RC[:, :qn], self.pb[5 + c][:, :qn]), reads=["pb%d" % (5 + c)], writes=["rc"])
                        P.dve(TT(TC[c][:, :qn], self.pb[3 + c][:, :qn], RC[:, :qn], ALU.mult),
                              reads=["pb%d" % (3 + c), "rc"], writes=["tc%d" % c])
                    return f

                def flush_sum():
                    f = psum_pend[0]
                    psum_pend[0] = None
                    if f is not None:
                        f()
                emit_s(0)
                nk = len(kts)
                for i, (c, kt) in enumerate(steps):
                    j = i % nk
                    if i == 0:
                        emitA()
                    if i == min(8, len(steps) - 1):
                        emitB()
                    if i + 1 < len(steps):
                        emit_s(i + 1)
                    bi = i % 3
                    P.act(ACTF(PT[bi][:, :qn], self.pb[bi][:, :qn], AF.Exp, scale=0.125), reads=["pb%d" % bi],
                          writes=["pt%d" % bi])
                    first, last = (j == 0), (j == nk - 1)
                    mode = j % 3
                    only_pe = (nk < 2)
                    mm = [(self.pb[3 + c][:, :qn], VT[:, kt, :], PT[bi][:, :qn], first, last)]
                    wr = ["pb%d" % (3 + c)]
                    if mode == 0:
                        mm.append((self.pb[5 + c][:, :qn], self.ones_b[:], PT[bi][:, :qn], first, only_pe))
                        wr.append("pb%d" % (5 + c))
                    P.pe(MMS(mm), reads=["pt%d" % bi, "vt.%d" % kt, "ones_b"], writes=wr)
                    if mode == 1:
                        if j == 1:
                            P.dve(CP(ACCD[c][:, :qn], PT[bi][:, :qn]), reads=["pt%d" % bi], writes=[accd_t[c]])
                        else:
                            P.dve(TT(ACCD[c][:, :qn], ACCD[c][:, :qn], PT[bi][:, :qn], ALU.add),
                                  reads=["pt%d" % bi, accd_t[c]], writes=[accd_t[c]])
                    if mode == 2:
                        if j == 2:
                            P.pool(CP(ACCP[c][:, :qn], PT[bi][:, :qn]), reads=["pt%d" % bi], writes=[accp_t[c]])
                        else:
                            P.pool(TT(ACCP[c][:, :qn], ACCP[c][:, :qn], PT[bi][:, :qn], ALU.add),
                                   reads=["pt%d" % bi, accp_t[c]], writes=[accp_t[c]])
                    if j == 1:
                        flush_sum()
                    if last:
                        flush_sum()
                        psum_pend[0] = make_sumfin(c, nk >= 2, nk >= 3)
                flush_sum()
                emitA()
                emitB()
                pend["A"], pend["B"] = make_final(h, qt, q0, qn)
        emitA()
        emitB()

    def rg_mixer(self, b, l):
        P = self.P
        need_ctx = (l == 0)
        o = 0
        RX = self.scr_f(o, 2312); o += 2312
        XC = self.scr_f(o, T); o += T
        XCB = self.scr_b(o, T); o += T // 2
        HS = self.scr_f(o, T); o += T
        B1 = self.scr_f(o, T); o += T
        B2 = self.scr_f(o, T); o += T
        TL = [self.scr_f(o + i * 512, 512) for i in range(4)]; o += 2048
        assert o <= SCRW, o
        B0 = RX[:, 0:T]
        b0t = ["rx.%d" % t for t in range(5)] + ["rxpad"]
        W = self.w_in[l]
        self.rot = [0, 1, 2, 3]
        it = 0
        nout = T if need_ctx else NLAT
        for cc in range(2):
            for (a0, a1) in ((0, 2), (2050, 2053), (2309, 2312)):
                P.pool(MSET(RX[:, a0:a1], 0.0), writes=["rxpad"])

            def ev_x(t, ps, btok):
                t0, n = TILES[t]
                off = t0 + 2 if t < 4 else t0 + 5
                P.act(ACTF(RX[:, off:off + n], ps, AF.Identity), reads=[btok], writes=["rx.%d" % t])
            self.linear_fm(W, OFF_RGX + cc * 128, [0, 1, 2, 3, 4], ev_x)
            for (x0, r0, n, rt, wt) in ((0, 0, NLAT, ["rx.0", "rx.1", "rx.2", "rx.3"], ["xc.0", "xc.1", "xc.2", "xc.3"]),
                                        (NLAT, 2051, NCTX, ["rx.4"], ["xc.4"])):
                wo = PRM_OFF["rgcw"] + (l * 2 + cc) * 4
                P.act(ACTF(XC[:, x0:x0 + n], RX[:, r0:r0 + n], AF.Identity, bias=self.prmcol("rgcb", l * 2 + cc),
                           scale=self.prm[:, wo:wo + 1]), reads=rt + ["rxpad", "prm"], writes=wt)
                for k in range(1, 4):
                    P.dve(STT(XC[:, x0:x0 + n], RX[:, r0 + k:r0 + k + n], self.prm[:, wo + k:wo + k + 1], XC[:, x0:x0 + n],
                              ALU.mult, ALU.add), reads=rt + wt + ["rxpad", "prm"], writes=wt)
                P.pool(CP(XCB[:, x0:x0 + n], XC[:, x0:x0 + n]), reads=wt, writes=["xcb.%d" % t for t in
                                                                                  ([0, 1, 2, 3] if x0 == 0 else [4])])
            xct = ["xc.%d" % t for t in range(5)]
            for d in range(2):
                pi = (l * 2 + d) * 2 + cc
                for t in range(5):
                    t0, n = TILES[t]
                    TA, TX = TL[(it % 2) * 2], TL[(it % 2) * 2 + 1]
                    sfx = "%d" % (it % 2)
                    it += 1
                    ba, bat = self.next_bank()
                    bx, bxt = self.next_bank()
                    P.pe(MMS([(ba[:, :n], self.BD[:, pi * 2, :], XCB[:, t0:t0 + n], True, True),
                              (bx[:, :n], self.BD[:, pi * 2 + 1, :], XCB[:, t0:t0 + n], True, True)]),
                         reads=["bd", "xcb.%d" % t], writes=[bat, bxt])
                    P.act(ACTF(TA[:, :n], ba[:, :n], AF.Tanh, bias=self.hba[:, pi:pi + 1], scale=0.5),
                          reads=[bat, "hba"], writes=["ta" + sfx])
                    P.act(ACTF(B0[:, t0:t0 + n], TA[:, :n], AF.Exp, bias=self.hm[:, pi:pi + 1],
                               scale=self.hm[:, pi:pi + 1]), reads=["ta" + sfx, "hm"], writes=b0t)
                    P.act(ACTF(B1[:, t0:t0 + n], B0[:, t0:t0 + n], AF.Square), reads=b0t, writes=["b1"])
                    P.act(ACTF(TX[:, :n], bx[:, :n], AF.Tanh, bias=self.hbx[:, pi:pi + 1], scale=0.5),
                          reads=[bxt, "hbx"], writes=["tx" + sfx])
                    P.dve(STT(B2[:, t0:t0 + n], TX[:, :n], 1.0, XC[:, t0:t0 + n], ALU.add, ALU.mult),
                          reads=["tx" + sfx, "xc.%d" % t], writes=["b2"])
                P.act(ACTF(B1, B1, AF.Sqrt, bias=1.0, scale=-(1.0 - 2.0 ** -23)), reads=["b1"], writes=["b1"])
                P.dve(STT(B1, B2, 0.5, B1, ALU.mult, ALU.mult), reads=["b1", "b2"], writes=["b1"])
                if d == 0:
                    P.dve(SCAN(B2[:, NLAT:T], B0[:, NLAT:T], B1[:, NLAT:T], 0.0), reads=b0t + ["b1"], writes=["b2"])
                    P.dve(SCAN(B2[:, 0:NLAT], B0[:, 0:NLAT], B1[:, 0:NLAT], B2[:, T - 1:T]), reads=b0t + ["b1", "b2"],
                          writes=["b2"])
                    P.pool(CP(HS[:, 0:nout], B2[:, 0:nout]), reads=["b2"], writes=["hs"])
                else:
                    P.dve(SCAN(B2[:, NLAT:T][:, ::-1], B0[:, NLAT:T][:, ::-1], B1[:, NLAT:T][:, ::-1], 0.0),
                          reads=b0t + ["b1"], writes=["b2"])
                    P.dve(SCAN(B2[:, 0:NLAT][:, ::-1], B0[:, 0:NLAT][:, ::-1], B1[:, 0:NLAT][:, ::-1], B2[:, NLAT:NLAT + 1]),
                          reads=b0t + ["b1", "b2"], writes=["b2"])
                    P.pool(TT(HS[:, 0:nout], HS[:, 0:nout], B2[:, 0:nout], ALU.add), reads=["b2", "hs"], writes=["hs"])
            def ev_g(t, ps, btok):
                t0, n = TILES[t]
                P.act(ACTF(B0[:, t0:t0 + n], ps, AF.Identity), reads=[btok], writes=b0t)
            self.linear_fm(W, OFF_RGG + cc * 128, [0, 1, 2, 3] + ([4] if need_ctx else []), ev_g)
            G, Q = B0[:, 0:nout], B1[:, 0:nout]
            P.dve(TT(Q, G, G, ALU.mult), reads=b0t, writes=["b1"])
            P.dve(TS(Q, Q, 0.044715, ALU.mult, 1.0, ALU.add), reads=["b1"], writes=["b1"])
            P.dve(TT(Q, Q, G, ALU.mult), reads=["b1"] + b0t, writes=["b1"])
            P.act(ACTF(Q, Q, AF.Tanh, scale=0.7978845608028654), reads=["b1"], writes=["b1"])
            P.dve(STT(Q, Q, 1.0, G, ALU.add, ALU.mult), reads=["b1"] + b0t, writes=["b1"])
            P.dve(STT(self.MX[:, cc, 0:nout], Q, 0.5, HS[:, 0:nout], ALU.mult, ALU.mult), reads=["b1", "hs"],
                  writes=["mx.%d" % t for t in range(5 if need_ctx else 4)])

    def ssd_mixer(self, b, l):
        P = self.P
        need_ctx = (l == 0)
        W = self.w_in[l]
        o = 0
        RAW = self.scr_f(o, 2312); o += 2312
        TMP = self.scr_f(o, T); o += T
        XS = self.scr_b(o, NBLK * 256).rearrange("p (i d) -> p i d", i=NBLK); o += NBLK * 128
        BF = self.scr_b(o, T); o += T // 2
        CF = self.scr_b(o, T); o += T // 2
        BT = self.scr_b(o, NBLK * 128).rearrange("p (i d) -> p i d", i=NBLK); o += NBLK * 64
        SZ = self.scr_b(o, NBLK * 256).rearrange("p (i d) -> p i d", i=NBLK); o += NBLK * 128
        DTR = self.scr_f(o, 144); o += 144
        DTv = self.scr_f(o, 144); o += 144
        ADT = self.scr_f(o, 144); o += 144
        SPS = self.scr_f(o, 576); o += 576
        assert o <= SCRW, o
        DT3 = DTv.rearrange("p (i e) -> p i e", i=NBLK)
        ADT3 = ADT.rearrange("p (i e) -> p i e", i=NBLK)
        self.rot = [0, 1, 2, 3]
        for ch in range(4):
            for (a0, a1) in ((0, 2), (2050, 2053), (2309, 2312)):
                P.pool(MSET(RAW[:, a0:a1], 0.0), writes=["rxpad"])

            def ev_x(t, ps, btok):
                t0, n = TILES[t]
                off = t0 + 2 if t < 4 else t0 + 5
                P.act(ACTF(RAW[:, off:off + n], ps, AF.Identity), reads=[btok], writes=["rx.%d" % t])
            self.linear_fm(W, OFF_XBC + ch * 128, [0, 1, 2, 3, 4], ev_x)
            for (x0, r0, n, rt, wt) in ((0, 0, NLAT, ["rx.0", "rx.1", "rx.2", "rx.3"], ["xc.0", "xc.1", "xc.2", "xc.3"]),
                                        (NLAT, 2051, NCTX, ["rx.4"], ["xc.4"])):
                wo = PRM_OFF["scw"] + (l * 4 + ch) * 4
                P.act(ACTF(TMP[:, x0:x0 + n], RAW[:, r0:r0 + n], AF.Identity, bias=self.prmcol("scb", l * 4 + ch),
                           scale=self.prm[:, wo:wo + 1]), reads=rt + ["rxpad", "prm"], writes=wt)
                for k in range(1, 4):
                    P.dve(STT(TMP[:, x0:x0 + n], RAW[:, r0 + k:r0 + k + n], self.prm[:, wo + k:wo + k + 1],
                              TMP[:, x0:x0 + n], ALU.mult, ALU.add), reads=rt + wt + ["rxpad", "prm"], writes=wt)
                if ch == 3:
                    P.act(ACTF(CF[:, x0:x0 + n], TMP[:, x0:x0 + n], AF.Silu), reads=wt, writes=["cf"])
                else:
                    P.act(ACTF(TMP[:, x0:x0 + n], TMP[:, x0:x0 + n], AF.Silu), reads=wt, writes=wt)
                    if ch == 2:
                        P.pool(CP(BF[:, x0:x0 + n], TMP[:, x0:x0 + n]), reads=wt, writes=["bf"])
            if ch < 3:
                for i0 in range(0, NBLK, 4):
                    blks = list(range(i0, min(i0 + 4, NBLK)))
                    bank, btok = self.next_bank()
                    P.pe(TRS([(bank[:, gi * 128:(gi + 1) * 128], TMP[:, i * 128:(i + 1) * 128]) for gi, i in
                              enumerate(blks)], self.ident[:]),
                         reads=["xc.%d" % min(i // 4, 4) for i in blks] + ["ident"], writes=[btok])
                    src = bank[:, 0:len(blks) * 128].rearrange("p (i d) -> p i d", i=len(blks))
                    if ch < 2:
                        P.dve(CP(XS[:, i0:i0 + len(blks), ch * 128:(ch + 1) * 128], src), reads=[btok], writes=["xs"])
                    else:
                        P.dve(CP(BT[:, i0:i0 + len(blks), :], src), reads=[btok], writes=["bt"])
        for zc in range(2):
            def ev_z(blks, ps, btok, zc=zc):
                i0 = blks[0]
                P.act(ACTF(SZ[:, i0:i0 + len(blks), zc * 128:(zc + 1) * 128],
                           ps.rearrange("p (i d) -> p i d", i=len(blks)), AF.Silu), reads=[btok], writes=["sz"])
            self.linear_tm(W, OFF_SZ + zc * 128, 128, list(range(NBLK)), ev_z, 4)

        def ev_dt(blks, ps, btok):
            do = ROW_OFF["dtb"] + l * 8
            P.dve(TT(DTR.rearrange("p (i e) -> p i e", i=NBLK), ps.rearrange("p (i e) -> p i e", i=NBLK),
                     self.rowp[:, do:do + 8].unsqueeze(1).to_broadcast([128, NBLK, 8]), ALU.add),
                  reads=[btok, "rowp"], writes=["dtr"])
        self.linear_tm(W, OFF_DT, 8, list(range(NBLK)), ev_dt, NBLK)
        self.softplus(DTv, DTR, 144, SPS, (["dtr"], ["dt"]))
        P.dve(TT(ADT3, DT3, self.aneg[:, l * 8:l * 8 + 8].unsqueeze(1).to_broadcast([128, NBLK, 8]), ALU.mult),
              reads=["dt", "aneg"], writes=["adt"])
        P.barrier()
        if self.o.get("ssd_prep_only"):
            P.pool(MSET(self.MX[:, 2:4, :], 0.0), writes=["mx.%d" % t for t in range(5)])
            return
        xf = self.xn[:].bitcast(F32)
        o = 0
        Y = xf[:, o:o + NBLK * 256].rearrange("p (i d) -> p i d", i=NBLK); o += NBLK * 256
        LT = xf[:, o:o + 512]; o += 512
        D1 = xf[:, o:o + 512]; o += 512
        GM = xf[:, o:o + 256]; o += 256
        T1 = xf[:, o:o + 256]; o += 256
        T2 = xf[:, o:o + 256]; o += 256
        VV = xf[:, o:o + 256]; o += 256
        OT = xf[:, o:o + 256]; o += 256
        JK = xf[:, o:o + 256]; o += 256
        H = xf[:, o:o + 128]; o += 128
        SM = xf[:, o:o + 64]; o += 64
        Mb = self.xn[:, 2 * o:2 * o + 512].rearrange("p (h t) -> p h t", h=4); o += 256
        XD = self.xn[:, 2 * o:2 * o + 256].rearrange("p (h d) -> p h d", h=4); o += 128
        XDW = self.xn[:, 2 * o:2 * o + 256].rearrange("p (h d) -> p h d", h=4); o += 128
        HB = self.xn[:, 2 * o:2 * o + 128]; o += 64
        assert o <= 9216
        cst, ecs, wdec, dtw, dch, ss = (SM[:, i * 4:(i + 1) * 4] for i in range(6))
        rsd = SM[:, 24:26]
        ssq = SM[:, 26:28]
        PB = self.pb
        dsk = self.rowp[:, ROW_OFF["dsk"] + l * 4:ROW_OFF["dsk"] + l * 4 + 4]
        sng = self.rowp[:, ROW_OFF["sng"] + l * 256:ROW_OFF["sng"] + (l + 1) * 256]
        for d in range(2):
            U = self.uf if d == 0 else self.ub
            utok = "uf" if d == 0 else "ub"
            order = [16, 17] + list(range(16)) if d == 0 else [17, 16] + list(range(15, -1, -1))
            edge = 127 if d == 0 else 0
            P.dve(MSET(H, 0.0), writes=["H"])
            P.dve(MSET(HB, 0.0), writes=["HB"])
            if d >= self.o.get("ssd_dirs", 2):
                break
            for k in order[:self.o.get("ssd_nchunks", 18)]:
                want_y = ((k < 16) or need_ctx) and not self.o.get("ssd_noy")
                blk = slice(k * 128, (k + 1) * 128)
                adt = ADT3[:, k, d * 4:d * 4 + 4]
                mm = [(PB[3][:, 0:4], U[:], adt, True, True)]
                for h in range(4):
                    mm.append((PB[2][:, h * 128:(h + 1) * 128], ADT3[:, k, d * 4 + h:d * 4 + h + 1].to_broadcast([128, 128]),
                               U[:], True, True))
                P.pe(MMS(mm), reads=["adt", utok], writes=["pb3", "pb2"])
                P.dve(CP(cst, PB[3][:, 0:4]), reads=["pb3"], writes=["cst"])
                csb = PB[2][:, 0:512].rearrange("p (h t) -> p h t", h=4)
                if want_y:
                    P.dve(TT(D1.rearrange("p (h t) -> p h t", h=4), csb, cst.unsqueeze(2).to_broadcast([128, 4, 128]),
                             ALU.subtract), reads=["pb2", "cst"], writes=["d1"])
                    P.dve(STT(D1, D1, -1.0, D1, ALU.mult, ALU.min), reads=["d1"], writes=["d1"])
                    P.act(ACTF(LT, D1, AF.Exp), reads=["d1"], writes=["lt"])
                ys = self.o.get("ssd_ystage", 9)
                if want_y and ys >= 2:
                    P.pe(MMS([(PB[1 - g][:, 0:128], BF[64 * g:64 * g + 64, blk], CF[64 * g:64 * g + 64, blk],
                               True, True) for g in range(2)]), reads=["bf", "cf"], writes=["pb1", "pb0"])
                    for g in range(2):
                        P.dve(TT(GM[:, g * 128:(g + 1) * 128], PB[1 - g][:, 0:128], U[:], ALU.mult),
                              reads=["pb%d" % (1 - g), utok], writes=["gm"])
                    P.dve(TT(Mb.rearrange("p (g a) t -> p g a t", g=2), LT.rearrange("p (g a t) -> p g a t", g=2, a=2),
                             GM.rearrange("p (g t) -> p g t", g=2).unsqueeze(2).to_broadcast([128, 2, 2, 128]), ALU.mult),
                          reads=["lt", "gm"], writes=["mb"])
                dts = DT3[:, k, d * 4:d * 4 + 4]
                XS4 = XS[:, k, :].rearrange("p (h d) -> p h d", h=4)
                if want_y and ys >= 3:
                    P.dve(TT(XD, XS4, dts.unsqueeze(2).to_broadcast([128, 4, 64]), ALU.mult), reads=["xs", "dt"],
                          writes=["xd"])
                if want_y and ys >= 4:
                    P.pe(MMS([(PB[4][:, h * 64:(h + 1) * 64], Mb[:, h, :], XD[:, h, :], True, True) for h in range(4)]
                             + [(PB[5 + 2 * g][:, 0:128], CF[64 * g:64 * g + 64, blk], HB[64 * g:64 * g + 64, :],
                                 True, True) for g in range(2)]),
                         reads=["mb", "xd", "cf", "HB"], writes=["pb4", "pb5", "pb7"])
                if want_y and ys >= 5:
                    P.act(ACTF(ecs, cst, AF.Exp), reads=["cst"], writes=["ecs"])
                    for g in range(2):
                        P.dve(TT(T1[:, g * 128:(g + 1) * 128].rearrange("p (h d) -> p h d", h=2),
                                 PB[5 + 2 * g][:, 0:128].rearrange("p (h d) -> p h d", h=2),
                                 ecs[:, 2 * g:2 * g + 2].unsqueeze(2).to_broadcast([128, 2, 64]), ALU.mult),
                              reads=["pb%d" % (5 + 2 * g), "ecs"], writes=["t1"])
                    P.dve(TT(T1, T1, PB[4][:, 0:256], ALU.add), reads=["t1", "pb4"], writes=["t1"])
                    if d == 0:
                        P.dve(TT(T2.rearrange("p (h d) -> p h d", h=4), XS4,
                                 dsk.unsqueeze(2).to_broadcast([128, 4, 64]), ALU.mult), reads=["xs", "rowp"],
                              writes=["t2"])
                        P.dve(TT(Y[:, k, :], T1, T2, ALU.add), reads=["t1", "t2"], writes=["y.%d" % k])
                    else:
                        P.dve(TT(Y[:, k, :], Y[:, k, :], T1, ALU.add), reads=["t1", "y.%d" % k], writes=["y.%d" % k])
                P.dve(TT(wdec, csb[:, :, edge], cst, ALU.subtract), reads=["pb2", "cst"], writes=["wdec"])
                P.act(ACTF(wdec, wdec, AF.Exp), reads=["wdec"], writes=["wdec"])
                P.dve(TT(dtw, wdec, dts, ALU.mult), reads=["wdec", "dt"], writes=["dtw"])
                P.dve(TT(XDW, XS4, dtw.unsqueeze(2).to_broadcast([128, 4, 64]), ALU.mult), reads=["xs", "dtw"],
                      writes=["xdw"])
                P.pe(MMS([(PB[6][64 * g:64 * g + 64, 0:128], BT[:, k, 64 * g:64 * g + 64],
                           XDW[:, 2 * g:2 * g + 2, :].rearrange("p a d -> p (a d)"), True, True) for g in range(2)]),
                     reads=["bt", "xdw"], writes=["pb6"])
                for g in range(2):
                    P.act(ACTF(dch[64 * g:64 * g + 64, 0:2], csb[64 * g:64 * g + 64, 2 * g:2 * g + 2, edge], AF.Exp),
                          reads=["pb2"], writes=["dch"])
                P.dve(TT(H.rearrange("p (a d) -> p a d", a=2), H.rearrange("p (a d) -> p a d", a=2),
                         dch[:, 0:2].unsqueeze(2).to_broadcast([128, 2, 64]), ALU.mult), reads=["H", "dch"], writes=["H"])
                P.dve(TT(H, H, PB[6][:, 0:128], ALU.add), reads=["H", "pb6"], writes=["H"])
                P.dve(CP(HB, H), reads=["H"], writes=["HB"])
                if d == 1 and want_y and ys >= 6:
                    P.dve(TT(VV, Y[:, k, :], SZ[:, k, :], ALU.mult), reads=["y.%d" % k, "sz"], writes=["vv"])
                    for g in range(2):
                        P.act(ACTF(JK[:, g * 128:(g + 1) * 128], VV[:, g * 128:(g + 1) * 128], AF.Square,
                                   accum_out=ssq[:, g:g + 1]), reads=["vv"], writes=["jk", "ssq"])
                    P.act(ACTF(rsd, ssq, AF.Ln, bias=EPS, scale=1.0 / 128.0), reads=["ssq"], writes=["rsd"])
                    P.act(ACTF(rsd, rsd, AF.Exp, scale=-0.5), reads=["rsd"], writes=["rsd"])
                    for g in range(2):
                        P.dve(STT(OT[:, g * 128:(g + 1) * 128], VV[:, g * 128:(g + 1) * 128], rsd[:, g:g + 1],
                                  sng[:, g * 128:(g + 1) * 128], ALU.mult, ALU.mult), reads=["vv", "rsd", "rowp"],
                              writes=["ot"])
                    P.pe(TRS([(PB[7][:, g * 128:(g + 1) * 128], OT[:, g * 128:(g + 1) * 128]) for g in range(2)],
                             self.ident[:]), reads=["ot", "ident"], writes=["pb7"])
                    P.act(ACTF(self.MX[:, 2:4, k * 128:(k + 1) * 128],
                               PB[7][:, 0:256].rearrange("p (g t) -> p g t", g=2), AF.Identity), reads=["pb7"],
                          writes=["mx.%d" % min(k // 4, 4)])

    def layer(self, b, l):
        P = self.P
        o = self.o
        need_ctx = (l == 0)
        allt = [0, 1, 2, 3, 4]
        outt = allt if need_ctx else [0, 1, 2, 3]
        if o.get("mix", True):
            self.norm(b, l, 0, allt)
            P.barrier()
            if o.get("da", True):
                self.da_mixer(b, l)
                self.wout_partial(b, l, 512, outt)
                P.barrier()
            if o.get("rg", True) or o.get("ssd", True):
                if o.get("rg", True):
                    self.rg_mixer(b, l)
                else:
                    P.pool(MSET(self.MX[:, 0:2, :], 0.0), writes=["mx.%d" % t for t in range(5)])
                P.barrier()
                if o.get("ssd", True):
                    self.ssd_mixer(b, l)
                else:
                    P.pool(MSET(self.MX[:, 2:4, :], 0.0), writes=["mx.%d" % t for t in range(5)])
                self.wout_partial(b, l, 0, outt)
                P.barrier()
        if o.get("ffn", True):
            self.norm(b, l, 1, outt)
            P.barrier()
            self.ffn(b, l, outt)
            P.barrier()

    def run(self):
        self.setup()
        for b in range(self.NB):
            self.load_x(b)
            self.P.barrier()
            for l in range(self.o.get("layers", 2)):
                self.layer(b, l)
            self.final(b)
            self.P.barrier()


def build_nc(NB=2, opts=None):
    nc = bass.Bass("TRN2", target_bir_lowering=False)
    with contextlib.ExitStack() as st:
        P = Prog(nc, st)
        K = Kern(nc, P, NB, opts or {})
        K.run()
        P.emit(final_reads=["out"])
    return nc


_NC_CACHE = {}


def kernel(x, c, ctx, c_ctx, w_mod, b_mod, norm1_g, w_in, rg_conv_w, rg_conv_b, rg_w_a, rg_b_a,
           rg_w_x, rg_b_x, rg_lambda, ssd_conv_w, ssd_conv_b, ssd_dt_bias, ssd_a_log, ssd_d,
           ssd_norm_g, da_lambda, da_subln_g, w_out, norm2_g, w_gate, w_up, w_down, final_norm_g,
           _opts=None, _ncores=8):
    inp = dict(w_mod=w_mod, b_mod=b_mod, norm1_g=norm1_g, w_in=w_in, rg_conv_w=rg_conv_w, rg_conv_b=rg_conv_b,
               rg_w_a=rg_w_a, rg_b_a=rg_b_a, rg_w_x=rg_w_x, rg_b_x=rg_b_x, rg_lambda=rg_lambda,
               ssd_conv_w=ssd_conv_w, ssd_conv_b=ssd_conv_b, ssd_dt_bias=ssd_dt_bias, ssd_a_log=ssd_a_log,
               ssd_d=ssd_d, ssd_norm_g=ssd_norm_g, da_lambda=da_lambda, da_subln_g=da_subln_g, w_out=w_out,
               norm2_g=norm2_g, w_gate=w_gate, w_up=w_up, w_down=w_down, final_norm_g=final_norm_g)
    shared = pack_shared(inp)
    x = np.asarray(x, np.float32)
    ctx = np.asarray(ctx, np.float32)
    c = np.asarray(c, np.float32)
    c_ctx = np.asarray(c_ctx, np.float32)
    NB = 2
    in_maps = []
    for i in range(_ncores):
        cc = np.stack([c[2 * i], c[2 * i + 1], c_ctx], axis=0)
        ct = np.ascontiguousarray(cc.reshape(3, 8, 128).transpose(2, 1, 0))
        m = dict(shared)
        m["x"] = np.ascontiguousarray(x[2 * i:2 * i + 2])
        m["ctx"] = np.ascontiguousarray(ctx[2 * i:2 * i + 2])
        m["ct"] = ct
        in_maps.append(m)
    key = repr(sorted((_opts or {}).items()))
    if key not in _NC_CACHE:
        _NC_CACHE[key] = build_nc(NB, _opts)
    nc = _NC_CACHE[key]
    res = run_bass_kernel_spmd(nc, in_maps, core_ids=list(range(_ncores)))
    out = np.concatenate([np.asarray(r["out"], np.float32) for r in res.results], axis=0)
    return out
```

```python
import contextlib
import math
import numpy as np
import concourse.bass as bass
import concourse.mybir as mybir
from concourse.bass_utils import run_bass_kernel_spmd

F32 = mybir.dt.float32
BF16 = mybir.dt.bfloat16
ALU = mybir.AluOpType
AF = mybir.ActivationFunctionType

D = 1024
NLAT = 2048
NCTX = 256
T = NLAT + NCTX
NBLK = T // 128
TILES = [(0, 512), (512, 512), (1024, 512), (1536, 512), (2048, 256)]
DFF = 2816
DIN = 2824
DINX = DIN + 1024
EPS = 1e-6
OFF_RGX, OFF_RGG, OFF_SZ, OFF_XBC, OFF_DT, OFF_Q, OFF_K, OFF_V = 0, 256, 512, 768, 1280, 1288, 1800, 2312
OFF_QS, OFF_KS = DIN, DIN + 512
SCRW = 14848


class _Op:
    __slots__ = ("eng", "fn", "waits", "tok", "clock", "dma", "idx")


class Prog:
    ENGS = ("pe", "act", "dve", "pool", "sp")

    def __init__(self, nc, stack, n_dma_sems=12):
        self.nc = nc
        self.stack = stack
        self.ops = {e: [] for e in self.ENGS}
        self.sem = {e: stack.enter_context(nc.semaphore("S_" + e)) for e in self.ENGS}
        self.cnt = {e: 0 for e in self.ENGS}
        self.know = {e: {} for e in self.ENGS}
        self.last_op = {e: None for e in self.ENGS}
        self.dma_sems = {}
        for q in ("sp", "pool"):
            self.dma_sems[q] = [[stack.enter_context(nc.semaphore("D_%s%d" % (q, i))), 0, None]
                                for i in range(n_dma_sems)]
        self.dma_rr = {q: 0 for q in self.dma_sems}
        self.last_w = {}
        self.readers = {}
        self.nops = 0
        self.nwaits = 0

    def sb(self, name, shape, dt=F32):
        return self.stack.enter_context(self.nc.sbuf_tensor(name, list(shape), dt))

    def ps(self, name, shape, dt=F32):
        return self.stack.enter_context(self.nc.psum_tensor(name, list(shape), dt))

    def _need(self, e, d, waits):
        if d is None:
            return
        if e == "pe" and d.eng == "pe" and not d.dma:
            return
        sem, val = d.tok
        k = self.know[e]
        if k.get(sem, 0) >= val:
            return
        waits.append((sem, val))
        for s, v in d.clock.items():
            if k.get(s, 0) < v:
                k[s] = v

    def op(self, eng, fn, reads=(), writes=(), dma=False, extra_deps=()):
        o = _Op()
        o.eng = eng
        o.fn = fn
        o.dma = dma
        o.idx = self.nops
        self.nops += 1
        deps = {}
        for r in reads:
            d = self.last_w.get(r)
            if d is not None:
                deps[d.idx] = d
        for w in writes:
            d = self.last_w.get(w)
            if d is not None:
                deps[d.idx] = d
            for d in self.readers.get(w, ()):
                deps[d.idx] = d
        for d in extra_deps:
            if d is not None:
                deps[d.idx] = d
        waits = []
        for i in sorted(deps, reverse=True):
            self._need(eng, deps[i], waits)
        if dma:
            pool = self.dma_sems[eng]
            slot = pool[self.dma_rr[eng] % len(pool)]
            self.dma_rr[eng] += 1
            if slot[2] is not None:
                self._need(eng, slot[2], waits)
            slot[1] += 16
            o.tok = (slot[0], slot[1])
            slot[2] = o
        elif fn is not None:
            self.cnt[eng] += 1
            o.tok = (self.sem[eng], self.cnt[eng])
        else:
            o.tok = None
        o.waits = waits
        self.nwaits += len(waits)
        o.clock = dict(self.know[eng])
        if o.tok is not None:
            o.clock[o.tok[0]] = o.tok[1]
            if not dma:
                self.last_op[eng] = o
        for w in writes:
            self.last_w[w] = o
            self.readers[w] = []
        for r in reads:
            if r not in writes:
                self.readers.setdefault(r, []).append(o)
        self.ops[eng].append(o)
        return o

    def pe(self, fn, reads=(), writes=()):
        return self.op("pe", fn, reads, writes)

    def act(self, fn, reads=(), writes=()):
        return self.op("act", fn, reads, writes)

    def dve(self, fn, reads=(), writes=()):
        return self.op("dve", fn, reads, writes)

    def pool(self, fn, reads=(), writes=()):
        return self.op("pool", fn, reads, writes)

    def dma(self, fn, reads=(), writes=(), q="sp"):
        return self.op(q, fn, reads, writes, dma=True)

    def barrier(self):
        pend = [self.last_op[e] for e in self.ENGS if self.last_op[e] is not None]
        for q in self.dma_sems:
            for slot in self.dma_sems[q]:
                if slot[2] is not None:
                    pend.append(slot[2])
        for e in self.ENGS:
            self.op(e, None, extra_deps=pend)

    def emit(self, final_reads=()):
        nc = self.nc
        fin_waits = []
        for r in final_reads:
            self._need("sp", self.last_w.get(r), fin_waits)

        def run(e, engine, extra=()):
            for o in self.ops[e]:
                for (s, v) in o.waits:
                    engine.wait_ge(s, v)
                if o.fn is None:
                    continue
                ins = o.fn(engine)
                ins.then_inc(o.tok[0], 16 if o.dma else 1)
            for (s, v) in extra:
                engine.wait_ge(s, v)

        with nc.Block() as block:
            @block.sync
            def _(eng):
                run("sp", eng, fin_waits)

            @block.gpsimd
            def _(eng):
                run("pool", eng)

            @block.scalar
            def _(eng):
                run("act", eng)

            @block.vector
            def _(eng):
                run("dve", eng)

            @block.tensor
            def _(eng):
                run("pe", eng)


def MMS(lst):
    def f(e):
        ins = None
        for (o, l, r, st, sp) in lst:
            ins = e.matmul(o, lhsT=l, rhs=r, start=st, stop=sp)
        return ins
    return f


def TRS(lst, ident):
    def f(e):
        ins = None
        for (o, i) in lst:
            ins = e.transpose(out=o, in_=i, identity=ident)
        return ins
    return f


def ACTF(out, in_, func, bias=0.0, scale=1.0, accum_out=None):
    if accum_out is None:
        return lambda e: e.activation(out=out, in_=in_, func=func, bias=bias, scale=scale)
    return lambda e: e.activation(out=out, in_=in_, func=func, bias=bias, scale=scale, accum_out=accum_out)


def TT(out, a, b, op):
    return lambda e: e.tensor_tensor(out=out, in0=a, in1=b, op=op)


def TS(out, a, s1, op0, s2=None, op1=None):
    if op1 is None:
        return lambda e: e.tensor_scalar(out=out, in0=a, scalar1=s1, scalar2=None, op0=op0)
    return lambda e: e.tensor_scalar(out=out, in0=a, scalar1=s1, scalar2=s2, op0=op0, op1=op1)


def STT(out, in0, scalar, in1, op0, op1):
    return lambda e: e.scalar_tensor_tensor(out=out, in0=in0, scalar=scalar, in1=in1, op0=op0, op1=op1)


def CP(out, in_):
    return lambda e: e.tensor_copy(out=out, in_=in_)


def RECIP(out, in_):
    return lambda e: e.reciprocal(out=out, in_=in_)


def MSET(ap, v):
    return lambda e: e.memset(ap, v)


def DMA(out, in_):
    return lambda e: e.dma_start(out=out, in_=in_)


def SCAN(out, a, b, init):
    return lambda e: e.tensor_tensor_scan(out=out, data0=a, data1=b, initial=init, op0=ALU.mult, op1=ALU.add)


PRM_SPEC = [("n1g", 16), ("n2g", 16), ("fng", 8), ("bmod", 96), ("rgcw", 16), ("rgcb", 4), ("rgba", 8),
            ("rgbx", 8), ("rglam", 8), ("scw", 32), ("scb", 8), ("subg", 2)]
PRM_OFF = {}
_o = 0
for _n, _w in PRM_SPEC:
    PRM_OFF[_n] = _o
    _o += _w
NPRM = _o
ROW_SPEC = [("dtb", 16), ("alog", 16), ("dsk", 8), ("sng", 512), ("dal", 512)]
ROW_OFF = {}
_o = 0
for _n, _w in ROW_SPEC:
    ROW_OFF[_n] = _o
    _o += _w
NROW = _o


def _fm(v):
    v = np.asarray(v, np.float32)
    lead = v.shape[:-1]
    n = v.shape[-1] // 128
    v = v.reshape(lead + (n, 128))
    return np.moveaxis(v, -1, 0).reshape(128, -1)


def pack_shared(inp):
    f = lambda k: np.asarray(inp[k], np.float32)
    prm = np.zeros((128, NPRM), np.float32)

    def put(name, arr):
        o = PRM_OFF[name]
        prm[:, o:o + arr.shape[1]] = arr

    put("n1g", _fm(f("norm1_g")))
    put("n2g", _fm(f("norm2_g")))
    put("fng", _fm(f("final_norm_g")))
    put("bmod", _fm(f("b_mod")))
    w = f("rg_conv_w").reshape(2, 4, 2, 128).transpose(3, 0, 2, 1).reshape(128, 16)
    put("rgcw", w)
    put("rgcb", _fm(f("rg_conv_b")))
    put("rgba", _fm(f("rg_b_a")))
    put("rgbx", _fm(f("rg_b_x")))
    put("rglam", _fm(f("rg_lambda")))
    w = f("ssd_conv_w").reshape(2, 4, 4, 128).transpose(3, 0, 2, 1).reshape(128, 32)
    put("scw", w)
    put("scb", _fm(f("ssd_conv_b")))
    put("subg", _fm(f("da_subln_g")))
    row = np.zeros((NROW,), np.float32)
    row[ROW_OFF["dtb"]:ROW_OFF["dtb"] + 16] = f("ssd_dt_bias").reshape(-1)
    row[ROW_OFF["alog"]:ROW_OFF["alog"] + 16] = f("ssd_a_log").reshape(-1)
    row[ROW_OFF["dsk"]:ROW_OFF["dsk"] + 8] = f("ssd_d").reshape(-1)
    row[ROW_OFF["sng"]:ROW_OFF["sng"] + 512] = f("ssd_norm_g").reshape(-1)
    row[ROW_OFF["dal"]:ROW_OFF["dal"] + 512] = f("da_lambda").reshape(-1)
    rowp = np.ascontiguousarray(np.broadcast_to(row[None, :], (128, NROW)))
    bd = np.zeros((128, 16, 128), np.float32)
    wa, wx = f("rg_w_a"), f("rg_w_x")
    for l in range(2):
        for d in range(2):
            for cc in range(2):
                for ax, wsrc in enumerate((wa, wx)):
                    idx = ((l * 2 + d) * 2 + cc) * 2 + ax
                    for blk in range(2):
                        bd[blk * 64:(blk + 1) * 64, idx, blk * 64:(blk + 1) * 64] = wsrc[l, d, 2 * cc + blk]
    j = np.arange(64)
    sw = np.where((j % 32) < 16, j + 16, j - 16)
    win = f("w_in")
    qidx = np.concatenate([OFF_Q + h * 128 + c * 64 + sw for h in range(4) for c in range(2)])
    kidx = np.concatenate([OFF_K + h * 128 + c * 64 + sw for h in range(4) for c in range(2)])
    winx = np.ascontiguousarray(np.concatenate([win, win[:, :, qidx], win[:, :, kidx]], axis=2))
    ident = np.eye(128, dtype=np.float32)
    tt = np.arange(128)
    uf = (tt[:, None] <= tt[None, :]).astype(np.float32)
    ub = (tt[:, None] >= tt[None, :]).astype(np.float32)
    t = np.arange(NLAT)
    row_i = (t // 64).astype(np.float32)
    col_i = (t % 64).astype(np.float32)
    half = 32
    inv_freq = np.power(np.float32(10000.0), -np.arange(0, half, 2, dtype=np.float32) / np.float32(half)).astype(np.float32)
    ang_r = row_i[:, None] * inv_freq[None, :]
    ang_c = col_i[:, None] * inv_freq[None, :]
    rope = np.zeros((128, 2, NLAT), np.float32)
    for p in range(128):
        jj = p % 64
        ang = ang_r if jj < 32 else ang_c
        i = jj % 16
        rope[p, 0] = np.cos(ang[:, i])
        sgn = -1.0 if (jj % 32) < 16 else 1.0
        rope[p, 1] = sgn * np.sin(ang[:, i])
    return dict(prm=prm, rowp=rowp, bd=bd, w_in_x=winx, ident=ident, uf=uf, ub=ub, rope=rope,
                w_mod=f("w_mod"), w_out=f("w_out"), w_gate=f("w_gate"), w_up=f("w_up"), w_down=f("w_down"))


class Kern:
    def __init__(self, nc, P, NB, opts):
        self.nc = nc
        self.P = P
        self.NB = NB
        self.o = opts
        dt_in = lambda n, s: nc.dram_tensor(n, list(s), F32, kind="ExternalInput").ap()
        self.x = dt_in("x", [NB, NLAT, D])
        self.ctx = dt_in("ctx", [NB, NCTX, D])
        self.ct = dt_in("ct", [128, 8, 3])
        self.prm_d = dt_in("prm", [128, NPRM])
        self.rowp_d = dt_in("rowp", [128, NROW])
        self.bd_d = dt_in("bd", [128, 16, 128])
        self.w_in = dt_in("w_in_x", [2, D, DINX])
        self.ident_d = dt_in("ident", [128, 128])
        self.uf_d = dt_in("uf", [128, 128])
        self.ub_d = dt_in("ub", [128, 128])
        self.rope_d = dt_in("rope", [128, 2, NLAT])
        self.w_mod = dt_in("w_mod", [2, D, 6 * D])
        self.w_out = dt_in("w_out", [2, D, D])
        self.w_gate = dt_in("w_gate", [2, D, DFF])
        self.w_up = dt_in("w_up", [2, D, DFF])
        self.w_down = dt_in("w_down", [2, DFF, D])
        self.out = nc.dram_tensor("out", [NB, NLAT, D], F32, kind="ExternalOutput").ap()
        sb = P.sb
        self.resid = sb("resid", [128, 8 * T])
        self.R3 = self.resid[:].rearrange("p (c t) -> p c t", c=8)
        self.xn = sb("xn", [128, 8 * T], BF16)
        self.XN = self.xn[:].rearrange("p (c t) -> p c t", c=8)
        self.mixb = sb("mixb", [128, 4 * T], BF16)
        self.MX = self.mixb[:].rearrange("p (c t) -> p c t", c=4)
        self.scr = sb("scr", [128, SCRW])
        self.WB = [sb("wb%d" % i, [128, 1024], BF16) for i in range(4)]
        self.wrr = 0
        self.ident = sb("ident_s", [128, 128])
        self.uf = sb("uf_s", [128, 128])
        self.ub = sb("ub_s", [128, 128])
        self.ones_f = sb("ones_f", [128, 128])
        self.ones_b = sb("ones_b", [128, 128], BF16)
        self.prm = sb("prm_s", [128, NPRM])
        self.rowp = sb("rowp_s", [128, NROW])
        self.bd = sb("bd_s", [128, 16 * 128], BF16)
        self.BD = self.bd[:].rearrange("p (i j) -> p i j", i=16)
        self.modt = sb("modt", [128, 2 * 48 * 3])
        self.MT = self.modt[:].rearrange("p (l j n) -> p l j n", l=2, j=48)
        self.amod = sb("amod", [128, 2 * 2 * 3 * 8])
        self.AM = self.amod[:].rearrange("p (l w n c) -> p l w n c", l=2, w=2, n=3)
        self.small = sb("small", [128, 256])
        self.sml_o = 0
        self.pb = [P.ps("pb%d" % i, [128, 512]) for i in range(8)]
        self.rot = [0, 1, 2, 3]
        self.rr = 0

    def prmcol(self, name, i):
        o = PRM_OFF[name] + i
        return self.prm[:, o:o + 1]

    def salloc(self, n):
        o = self.sml_o
        self.sml_o += n
        assert self.sml_o <= 256
        return self.small[:, o:o + n]

    def scr_f(self, off, n):
        return self.scr[:, off:off + n]

    def scr_b(self, off, n):
        assert n % 2 == 0
        return self.scr[:, off:off + n // 2].bitcast(BF16)

    def next_bank(self):
        i = self.rot[self.rr % len(self.rot)]
        self.rr += 1
        return self.pb[i], "pb%d" % i

    def load_w(self, wsrc, k0, kch, col0, ncols=128):
        i = self.wrr % len(self.WB)
        self.wrr += 1
        wb = self.WB[i][:, 0:kch * ncols].rearrange("p (c n) -> p c n", c=kch)
        src = wsrc.rearrange("(c p) n -> p c n", p=128)[:, k0:k0 + kch, col0:col0 + ncols]
        tok = "wb%d" % i
        self.P.dma(DMA(wb, src), writes=[tok], q="pool")
        return wb, tok

    def linear_fm(self, wsrc, col0, tiles, evac, kch=8, k0=0, rhs=None, rtok=None, ncols=128):
        wb, wtok = self.load_w(wsrc, k0, kch, col0, ncols)
        if rhs is None:
            rhs = lambda kc, t0, n: self.XN[:, kc, t0:t0 + n]
            rtok = lambda t: ["xn.%d" % t]
        for t in tiles:
            t0, n = TILES[t]
            bank, btok = self.next_bank()
            out = bank[0:ncols, 0:n]
            mms = [(out, wb[:, kc, :], rhs(kc, t0, n), kc == 0, kc == kch - 1) for kc in range(kch)]
            self.P.pe(MMS(mms), reads=[wtok] + rtok(t), writes=[btok])
            evac(t, out, btok)

    def linear_fm2(self, wsrc, colA, colB, tiles, evac2, extraA=(), evacA=None):
        wa, atok = self.load_w(wsrc, 0, 8, colA)
        wb2, btok2 = self.load_w(wsrc, 0, 8, colB)
        for t in tiles:
            t0, n = TILES[t]
            ba, bat = self.next_bank()
            bb, bbt = self.next_bank()
            self.P.pe(MMS([(ba[:, 0:n], wa[:, kc, :], self.XN[:, kc, t0:t0 + n], kc == 0, kc == 7) for kc in range(8)]
                          + [(bb[:, 0:n], wb2[:, kc, :], self.XN[:, kc, t0:t0 + n], kc == 0, kc == 7) for kc in range(8)]),
                      reads=[atok, btok2, "xn.%d" % t], writes=[bat, bbt])
            evac2(t, ba[:, 0:n], bat, bb[:, 0:n], bbt)
        for t in extraA:
            t0, n = TILES[t]
            ba, bat = self.next_bank()
            self.P.pe(MMS([(ba[:, 0:n], wa[:, kc, :], self.XN[:, kc, t0:t0 + n], kc == 0, kc == 7) for kc in range(8)]),
                      reads=[atok, "xn.%d" % t], writes=[bat])
            evacA(t, ba[:, 0:n], bat)

    def linear_tm(self, wsrc, col0, ncols, blocks, evac, grp):
        wb, wtok = self.load_w(wsrc, 0, 8, col0, ncols)
        for g0 in range(0, len(blocks), grp):
            blks = blocks[g0:g0 + grp]
            bank, btok = self.next_bank()
            mms = []
            rt = set()
            for gi, i in enumerate(blks):
                out = bank[:, gi * ncols:(gi + 1) * ncols]
                rt.add("xn.%d" % min(i // 4, 4))
                for kc in range(8):
                    mms.append((out, self.XN[:, kc, i * 128:(i + 1) * 128], wb[:, kc, :], kc == 0, kc == 7))
            self.P.pe(MMS(mms), reads=[wtok] + sorted(rt), writes=[btok])
            evac(blks, bank[:, 0:len(blks) * ncols], btok)

    def softplus(self, out, x, n, s0, toks):
        P = self.P
        ax, e, z, p = (s0[:, i * n:(i + 1) * n] for i in range(4))
        rd, wr = toks
        tk = ["spl.tmp"]
        P.dve(STT(ax, x, -1.0, x, ALU.mult, ALU.max), reads=rd, writes=tk)
        P.act(ACTF(e, ax, AF.Exp, scale=-1.0), reads=tk, writes=tk)
        P.dve(TS(z, e, 2.0, ALU.add), reads=tk, writes=tk)
        P.dve(RECIP(z, z), reads=tk, writes=tk)
        P.dve(TT(z, z, e, ALU.mult), reads=tk, writes=tk)
        P.dve(TT(e, z, z, ALU.mult), reads=tk, writes=tk)
        P.dve(TS(p, e, 1.0 / 13.0, ALU.mult, 1.0 / 11.0, ALU.add), reads=tk, writes=tk)
        for cst in (1.0 / 9.0, 1.0 / 7.0, 1.0 / 5.0, 1.0 / 3.0, 1.0):
            P.dve(TT(p, p, e, ALU.mult), reads=tk, writes=tk)
            P.dve(TS(p, p, cst, ALU.add), reads=tk, writes=tk)
        P.dve(TT(p, p, z, ALU.mult), reads=tk, writes=tk)
        P.dve(TS(ax, x, 0.0, ALU.max), reads=rd + tk, writes=tk)
        P.dve(STT(out, p, 2.0, ax, ALU.mult, ALU.add), reads=tk, writes=wr)

    def mods_emit(self, l, jj):
        P = self.P
        wb, wtok = self.load_w(self.w_mod[l], 0, 8, jj * 128)
        bi = 5 + (self.mod_i % 2)
        self.mod_i += 1
        out = self.pb[bi][:, 0:3]
        P.pe(MMS([(out, wb[:, kc, :], self.ctb[:, kc, :], kc == 0, kc == 7) for kc in range(8)]),
             reads=[wtok, "ctb"], writes=["pb%d" % bi])
        P.dve(TS(self.MT[:, l, jj, :], out, self.prmcol("bmod", l * 48 + jj), ALU.add), reads=["pb%d" % bi, "prm"],
              writes=["modt"])

    def mods_finish(self, l, w):
        gname = "n1g" if w == 0 else "n2g"
        go = PRM_OFF[gname] + l * 8
        jj0 = 8 + 24 * w
        for n in range(3):
            self.P.dve(STT(self.AM[:, l, w, n, :], self.MT[:, l, jj0:jj0 + 8, n], 1.0, self.prm[:, go:go + 8],
                           ALU.add, ALU.mult), reads=["modt", "prm"], writes=["amod"])

    def mods_step(self, n):
        while n > 0 and self.mod_q:
            l, jj = self.mod_q.pop(0)
            self.mods_emit(l, jj)
            n -= 1
            if not self.mod_q:
                self.mods_finish(0, 1)
                self.mods_finish(1, 0)
                self.mods_finish(1, 1)

    def setup(self):
        P = self.P
        P.dma(DMA(self.ident[:], self.ident_d), writes=["ident"])
        P.dma(DMA(self.uf[:], self.uf_d), writes=["uf"])
        P.dma(DMA(self.ub[:], self.ub_d), writes=["ub"])
        P.dma(DMA(self.prm[:], self.prm_d), writes=["prm"])
        P.dma(DMA(self.rowp[:], self.rowp_d), writes=["rowp"])
        P.dma(DMA(self.BD, self.bd_d), writes=["bd"], q="pool")
        P.pool(MSET(self.ones_f[:], 1.0), writes=["ones_f"])
        P.pool(MSET(self.ones_b[:], 1.0), writes=["ones_b"])
        ctile = self.scr_f(0, 24).rearrange("p (k n) -> p k n", k=8)
        sig = self.scr_f(24, 24).rearrange("p (k n) -> p k n", k=8)
        P.dma(DMA(ctile, self.ct), writes=["ct"])
        P.act(ACTF(sig, ctile, AF.Sigmoid), reads=["ct"], writes=["sig"])
        P.dve(TT(ctile, ctile, sig, ALU.mult), reads=["ct", "sig"], writes=["ct"])
        self.ctb = self.salloc(12).bitcast(BF16)[:, 0:24].rearrange("p (k n) -> p k n", k=8)
        P.dve(CP(self.ctb, ctile), reads=["ct"], writes=["ctb"])
        self.mod_q = [(0, jj) for jj in range(24, 48)] + [(1, jj) for jj in range(48)]
        self.mod_i = 0
        for jj in range(24):
            self.mods_emit(0, jj)
        self.mods_finish(0, 0)
        self.m8sp = self.salloc(8)
        nl = self.salloc(8)
        lo = PRM_OFF["rglam"]
        P.dve(TS(nl, self.prm[:, lo:lo + 8], -1.0, ALU.mult), reads=["prm"], writes=["nl"])
        self.softplus(self.m8sp, nl, 8, self.scr_f(8300, 32), (["nl"], ["m8sp"]))
        P.dve(TS(self.m8sp, self.m8sp, -8.0, ALU.mult), reads=["m8sp"], writes=["m8sp"])
        self.hm = self.salloc(8)
        self.hba = self.salloc(8)
        self.hbx = self.salloc(8)
        P.dve(TS(self.hm, self.m8sp, 0.5, ALU.mult), reads=["m8sp"], writes=["hm"])
        oa, ox = PRM_OFF["rgba"], PRM_OFF["rgbx"]
        P.dve(TS(self.hba, self.prm[:, oa:oa + 8], 0.5, ALU.mult), reads=["prm"], writes=["hba"])
        P.dve(TS(self.hbx, self.prm[:, ox:ox + 8], 0.5, ALU.mult), reads=["prm"], writes=["hbx"])
        self.aneg = self.salloc(16)
        ao = ROW_OFF["alog"]
        P.act(ACTF(self.aneg, self.rowp[:, ao:ao + 16], AF.Exp), reads=["rowp"], writes=["aneg"])
        P.dve(TS(self.aneg, self.aneg, -1.0, ALU.mult), reads=["aneg"], writes=["aneg"])
        self.nlam = self.salloc(2)
        self.gsc = self.salloc(2)
        pr = self.scr_f(8400, 128)
        sm = self.salloc(4)
        for l in range(2):
            do = ROW_OFF["dal"] + l * 256
            lam_init = 0.8 - 0.6 * math.exp(-0.3 * l)
            for i in range(2):
                P.dve(TT(pr[:, 0:64], self.rowp[:, do + 128 * i:do + 128 * i + 64],
                         self.rowp[:, do + 128 * i + 64:do + 128 * i + 128], ALU.mult), reads=["rowp"], writes=["pr"])
                P.dve(lambda e, o_=sm[:, 2 * l + i:2 * l + i + 1], i_=pr[:, 0:64]: e.reduce_sum(
                    out=o_, in_=i_, axis=mybir.AxisListType.X), reads=["pr"], writes=["sm"])
            P.act(ACTF(sm[:, 2 * l:2 * l + 2], sm[:, 2 * l:2 * l + 2], AF.Exp), reads=["sm"], writes=["sm"])
            P.dve(TT(self.nlam[:, l:l + 1], sm[:, 2 * l + 1:2 * l + 2], sm[:, 2 * l:2 * l + 1], ALU.subtract),
                  reads=["sm"], writes=["nlam"])
            P.dve(TS(self.nlam[:, l:l + 1], self.nlam[:, l:l + 1], -lam_init, ALU.add), reads=["nlam"], writes=["nlam"])
            P.dve(TS(self.gsc[:, l:l + 1], self.prmcol("subg", l), 1.0 - lam_init, ALU.mult), reads=["prm"],
                  writes=["gsc"])
        P.barrier()

    def load_x(self, b):
        P = self.P
        STG = [self.scr_f(i * 1024, 1024) for i in range(2)]
        self.rot = [0, 1, 2, 3]
        for i in range(NBLK):
            src = self.x[b, i * 128:(i + 1) * 128, :] if i < 16 else self.ctx[b, (i - 16) * 128:(i - 15) * 128, :]
            stg = STG[i % 2]
            stok = "stg%d" % (i % 2)
            P.dma(DMA(stg, src), writes=[stok])
            t = min(i // 4, 4)
            for half in range(2):
                bank, btok = self.next_bank()
                P.pe(TRS([(bank[:, j * 128:(j + 1) * 128], stg[:, (half * 4 + j) * 128:(half * 4 + j + 1) * 128])
                          for j in range(4)], self.ident[:]), reads=[stok, "ident"], writes=[btok])
                dst = self.R3[:, half * 4:half * 4 + 4, i * 128:(i + 1) * 128]
                srcp = bank[:, 0:512].rearrange("p (c t) -> p c t", c=4)
                wr = ["r.%d.%d" % (c, t) for c in range(half * 4, half * 4 + 4)]
                if half == 0:
                    P.act(ACTF(dst, srcp, AF.Identity), reads=[btok], writes=wr)
                else:
                    P.dve(CP(dst, srcp), reads=[btok], writes=wr)

    def rstd_tile(self, t, RS, SQ):
        P = self.P
        t0, n = TILES[t]
        bank = self.pb[7]
        for c in range(8):
            sq = SQ[c % 2]
            P.dve(TT(sq[:, :n], self.R3[:, c, t0:t0 + n], self.R3[:, c, t0:t0 + n], ALU.mult),
                  reads=["r.%d.%d" % (c, t)], writes=["sq%d" % (c % 2)])
            P.pe(MMS([(bank[:, :n], self.ones_b[:], sq[:, :n], c == 0, c == 7)]), reads=["sq%d" % (c % 2), "ones_b"],
                 writes=["pb7"])
        P.act(ACTF(RS[:, :n], bank[:, :n], AF.Sqrt, bias=EPS, scale=1.0 / D), reads=["pb7"], writes=["rs"])
        P.dve(RECIP(RS[:, :n], RS[:, :n]), reads=["rs"], writes=["rs"])

    def norm(self, b, l, w, tiles):
        P = self.P
        SQ = [self.scr_b(i * 256, 512) for i in range(2)]
        RS = self.scr_f(512, 512)
        TM = [self.scr_f(1024 + i * 512, 512) for i in range(2)]
        for t in tiles:
            t0, n = TILES[t]
            nn = b if t < 4 else 2
            self.rstd_tile(t, RS, SQ)
            for c in range(8):
                tm = TM[c % 2]
                P.dve(TT(tm[:, :n], self.R3[:, c, t0:t0 + n], RS[:, :n], ALU.mult), reads=["r.%d.%d" % (c, t), "rs"],
                      writes=["tm%d" % (c % 2)])
                P.act(ACTF(self.XN[:, c, t0:t0 + n], tm[:, :n], AF.Identity,
                           bias=self.MT[:, l, 24 * w + c, nn:nn + 1], scale=self.AM[:, l, w, nn, c:c + 1]),
                      reads=["tm%d" % (c % 2), "modt", "amod"], writes=["xn.%d" % t])

    def wout_partial(self, b, l, rows0, tiles):
        P = self.P
        self.rot = [0, 1, 2, 3]
        for c in range(8):
            def evac(t, ps, btok, c=c):
                t0, n = TILES[t]
                nn = b if t < 4 else 2
                dst = self.R3[:, c, t0:t0 + n]
                P.dve(STT(dst, ps, self.MT[:, l, 16 + c, nn:nn + 1], dst, ALU.mult, ALU.add),
                      reads=[btok, "modt", "r.%d.%d" % (c, t)], writes=["r.%d.%d" % (c, t)])
            self.linear_fm(self.w_out[l], c * 128, tiles, evac, kch=4, k0=rows0 // 128,
                           rhs=lambda kc, t0, n: self.MX[:, kc, t0:t0 + n], rtok=lambda t: ["mx.%d" % t])

    def ffn(self, b, l, tiles):
        P = self.P
        HH = self.scr_b(0, 6 * T).rearrange("p (j t) -> p j t", j=6)
        SG = [self.scr_f(7000 + i * 512, 512) for i in range(2)]
        groups = [(0, 6), (6, 6), (12, 5), (17, 5)]
        it = 0
        for (j0, J) in groups:
            for jl in range(J):
                j = j0 + jl
                wg, gtok = self.load_w(self.w_gate[l], 0, 8, j * 128)
                wu, utok = self.load_w(self.w_up[l], 0, 8, j * 128)
                for t in tiles:
                    t0, n = TILES[t]
                    bg, bgt = self.pb[it % 2], "pb%d" % (it % 2)
                    bu, but = self.pb[2 + it % 2], "pb%d" % (2 + it % 2)
                    sg = SG[it % 2]
                    sgt = "sg%d" % (it % 2)
                    it += 1
                    P.pe(MMS([(bg[:, :n], wg[:, kc, :], self.XN[:, kc, t0:t0 + n], kc == 0, kc == 7) for kc in range(8)]),
                         reads=[gtok, "xn.%d" % t], writes=[bgt])
                    P.pe(MMS([(bu[:, :n], wu[:, kc, :], self.XN[:, kc, t0:t0 + n], kc == 0, kc == 7) for kc in range(8)]),
                         reads=[utok, "xn.%d" % t], writes=[but])
                    P.act(ACTF(sg[:, :n], bg[:, :n], AF.Silu), reads=[bgt], writes=[sgt])
                    P.dve(TT(HH[:, jl, t0:t0 + n], sg[:, :n], bu[:, :n], ALU.mult), reads=[sgt, but],
                          writes=["hh.%d.%d" % (jl, t)])
            for c in range(8):
                wd, dtok = self.load_w(self.w_down[l], j0, J, c * 128)
                for t in tiles:
                    t0, n = TILES[t]
                    nn = b if t < 4 else 2
                    bi = 4 + (it % 3)
                    it += 1
                    bank, btok = self.pb[bi], "pb%d" % bi
                    P.pe(MMS([(bank[:, :n], wd[:, jl, :], HH[:, jl, t0:t0 + n], jl == 0, jl == J - 1) for jl in range(J)]),
                         reads=[dtok] + ["hh.%d.%d" % (jl, t) for jl in range(J)], writes=[btok])
                    dst = self.R3[:, c, t0:t0 + n]
                    P.dve(STT(dst, bank[:, :n], self.MT[:, l, 40 + c, nn:nn + 1], dst, ALU.mult, ALU.add),
                          reads=[btok, "modt", "r.%d.%d" % (c, t)], writes=["r.%d.%d" % (c, t)])

    def final(self, b):
        P = self.P
        SQ = [self.scr_b(i * 256, 512) for i in range(2)]
        RS = self.scr_f(512, 512)
        TM = [self.scr_f(1024 + i * 512, 512) for i in range(2)]
        YN = self.scr_f(2048, 4096).rearrange("p (c t) -> p c t", c=8)
        OST = [self.scr_f(6144 + i * 1024, 1024) for i in range(2)]
        self.rot = [0, 1, 2, 3]
        oi = 0
        for t in range(4):
            t0, n = TILES[t]
            self.rstd_tile(t, RS, SQ)
            for c in range(8):
                tm = TM[c % 2]
                P.dve(TT(tm[:, :n], self.R3[:, c, t0:t0 + n], RS[:, :n], ALU.mult), reads=["r.%d.%d" % (c, t), "rs"],
                      writes=["tm%d" % (c % 2)])
                P.act(ACTF(YN[:, c, :], tm[:, :n], AF.Identity, scale=self.prmcol("fng", c)),
                      reads=["tm%d" % (c % 2), "prm"], writes=["yn.%d" % c])
            for j in range(4):
                ost = OST[oi % 2]
                otok = "ost%d" % (oi % 2)
                oi += 1
                for half in range(2):
                    bank, btok = self.next_bank()
                    P.pe(TRS([(bank[:, cc * 128:(cc + 1) * 128], YN[:, half * 4 + cc, j * 128:(j + 1) * 128])
                              for cc in range(4)], self.ident[:]),
                         reads=["yn.%d" % (half * 4 + cc) for cc in range(4)] + ["ident"], writes=[btok])
                    if half == 0:
                        P.act(ACTF(ost[:, 0:512], bank[:, 0:512], AF.Identity), reads=[btok], writes=[otok])
                    else:
                        P.dve(CP(ost[:, 512:1024], bank[:, 0:512]), reads=[btok], writes=[otok])
                r0 = t0 + j * 128
                P.dma(DMA(self.out[b, r0:r0 + 128, :], ost), reads=[otok], writes=["out"])

    def da_mixer(self, b, l):
        P = self.P
        need_ctx = (l == 0)
        o = 0
        ROPE = self.scr_f(o, 4096).rearrange("p (a t) -> p a t", a=2); o += 4096
        QR = self.scr_b(o, T); o += T // 2
        KC = [self.scr_b(o + i * (T // 2), T) for i in range(2)]; o += T
        VT = self.scr_b(o, T).rearrange("p (i d) -> p i d", i=NBLK); o += T // 2
        T1 = [self.scr_f(o + i * 512, 512) for i in range(2)]; o += 1024
        T2 = [self.scr_f(o + i * 512, 512) for i in range(2)]; o += 1024
        PT = [self.scr_b(o + i * 256, 512) for i in range(3)]; o += 768
        RC = self.scr_f(o, 512); o += 512
        TC = [self.scr_f(o + i * 512, 512) for i in range(2)]; o += 1024
        O2 = self.scr_f(o, 512); o += 512
        OV = [self.scr_f(o + i * 512, 512) for i in range(2)]; o += 1024
        assert o <= SCRW, o
        P.dma(DMA(ROPE, self.rope_d), writes=["rope"])
        P.pool(MSET(KC[0][64:128, :], 0.0), writes=["kc0z"])
        P.pool(MSET(KC[1][0:64, :], 0.0), writes=["kc1z"])
        lat = [0, 1, 2, 3]
        W = self.w_in[l]
        ti = [0]
        pend = {"A": None, "B": None}
        fcount = [0]

        def emitA():
            f = pend["A"]
            pend["A"] = None
            if f is not None:
                f()

        def emitB():
            f = pend["B"]
            pend["B"] = None
            if f is not None:
                f()

        def make_final(h, qt, q0, qn):
            fi = fcount[0] % 2
            fcount[0] += 1
            ov = OV[fi][:, :qn]
            ovt = "ov%d" % fi

            def fa():
                P.dve(STT(ov, TC[1][:, :qn], self.nlam[:, l:l + 1], TC[0][:, :qn], ALU.mult, ALU.add),
                      reads=["tc0", "tc1", "nlam"], writes=[ovt])
                P.dve(TT(O2[:, :qn], ov, ov, ALU.mult), reads=[ovt], writes=["o2"])

            def fb():
                P.pe(MMS([(self.pb[7][:, :qn], self.ones_f[:], O2[:, :qn], True, True)]), reads=["o2", "ones_f"],
                     writes=["pb7"])
                P.act(ACTF(O2[:, :qn], self.pb[7][:, :qn], AF.Ln, bias=EPS, scale=1.0 / 128.0), reads=["pb7"],
                      writes=["o2"])
                P.act(ACTF(O2[:, :qn], O2[:, :qn], AF.Exp, scale=-0.5), reads=["o2"], writes=["o2"])
                P.dve(TT(ov, ov, O2[:, :qn], ALU.mult), reads=[ovt, "o2"], writes=[ovt])
                P.act(ACTF(self.MX[:, h, q0:q0 + qn], ov, AF.Identity, scale=self.gsc[:, l:l + 1]),
                      reads=[ovt, "gsc"], writes=["mx.%d" % qt])
            return fa, fb

        for h in range(4):
            self.rot = [0, 1, 2, 3]

            def ev_q(t, pa, ta, pb_, tb):
                t0, n = TILES[t]
                i = ti[0] % 2
                ti[0] += 1
                P.dve(TT(T1[i][:, :n], pa, ROPE[:, 0, t0:t0 + n], ALU.mult), reads=[ta, "rope"], writes=["t1.%d" % i])
                P.dve(TT(T2[i][:, :n], pb_, ROPE[:, 1, t0:t0 + n], ALU.mult), reads=[tb, "rope"], writes=["t2.%d" % i])
                P.pool(TT(QR[:, t0:t0 + n], T1[i][:, :n], T2[i][:, :n], ALU.add), reads=["t1.%d" % i, "t2.%d" % i],
                       writes=["qr.%d" % t])

            def ev_k(t, pa, ta, pb_, tb):
                t0, n = TILES[t]
                i = ti[0] % 2
                ti[0] += 1
                P.dve(TT(T1[i][:, :n], pa, ROPE[:, 0, t0:t0 + n], ALU.mult), reads=[ta, "rope"], writes=["t1.%d" % i])
                P.dve(TT(T2[i][:, :n], pb_, ROPE[:, 1, t0:t0 + n], ALU.mult), reads=[tb, "rope"], writes=["t2.%d" % i])
                P.pool(TT(KC[0][0:64, t0:t0 + n], T1[i][0:64, :n], T2[i][0:64, :n], ALU.add),
                       reads=["t1.%d" % i, "t2.%d" % i, "kc0z"], writes=["kc0.%d" % t])
                P.pool(TT(KC[1][64:128, t0:t0 + n], T1[i][64:128, :n], T2[i][64:128, :n], ALU.add),
                       reads=["t1.%d" % i, "t2.%d" % i, "kc1z"], writes=["kc1.%d" % t])

            def ev_qc(t, ps, btok):
                t0, n = TILES[t]
                P.act(ACTF(QR[:, t0:t0 + n], ps, AF.Identity), reads=[btok], writes=["qr.%d" % t])

            def ev_kc(t, ps, btok):
                t0, n = TILES[t]
                P.act(ACTF(KC[0][0:64, t0:t0 + n], ps[0:64, :], AF.Identity), reads=[btok, "kc0z"], writes=["kc0.%d" % t])
                P.act(ACTF(KC[1][64:128, t0:t0 + n], ps[64:128, :], AF.Identity), reads=[btok, "kc1z"],
                      writes=["kc1.%d" % t])
            self.linear_fm2(W, OFF_Q + h * 128, OFF_QS + h * 128, lat, ev_q, [4] if need_ctx else [], ev_qc)
            emitA()
            self.linear_fm2(W, OFF_K + h * 128, OFF_KS + h * 128, lat, ev_k, [4], ev_kc)
            emitB()

            def ev_v(blks, ps, btok):
                i0 = blks[0]
                P.act(ACTF(VT[:, i0:i0 + len(blks), :], ps.rearrange("p (i d) -> p i d", i=len(blks)), AF.Identity),
                      reads=[btok], writes=["vt.%d" % i for i in blks])
            self.linear_tm(W, OFF_V + h * 128, 128, list(range(NBLK)), ev_v, 4)

            for qt in (lat + ([4] if need_ctx else [])):
                q0, qn = TILES[qt]
                kts = list(range(NBLK)) if qt < 4 else [16, 17]
                steps = [(c, kt) for c in range(2) for kt in kts]

                def emit_s(i):
                    c, kt = steps[i]
                    bi = i % 3
                    P.pe(MMS([(self.pb[bi][:, :qn], KC[c][:, kt * 128:(kt + 1) * 128], QR[:, q0:q0 + qn], True, True)]),
                         reads=["kc%d.%d" % (c, min(kt // 4, 4)), "kc%dz" % c, "qr.%d" % qt], writes=["pb%d" % bi])
                ACCD = [T1[0], T1[1]]
                ACCP = [T2[0], T2[1]]
                accd_t = ["t1.0", "t1.1"]
                accp_t = ["t2.0", "t2.1"]
                psum_pend = [None]

                HILO = O2.bitcast(BF16)

                def make_sumfin(c, have_d, have_p):
                    def f():
                        if have_d:
                            acc = ACCD[c][:, :qn]
                            if have_p:
                                P.dve(TT(acc, acc, ACCP[c][:, :qn], ALU.add), reads=[accd_t[c], accp_t[c]],
                                      writes=[accd_t[c]])
                            hi, lo = HILO[:, 0:qn], HILO[:, 512:512 + qn]
                            P.dve(CP(hi, acc), reads=[accd_t[c]], writes=["o2"])
                            P.dve(TT(lo, acc, hi, ALU.subtract), reads=[accd_t[c], "o2"], writes=["o2"])
                            P.pe(MMS([(self.pb[5 + c][:, :qn], self.ones_b[:], hi, False, False),
                                      (self.pb[5 + c][:, :qn], self.ones_b[:], lo, False, True)]),
                                 reads=["ones_b", "o2"], writes=["pb%d" % (5 + c)])
                        P.dve(RECIP(RC[:, :qn], self.pb[5 + c][:, :qn]), reads=["pb%d" % (5 + c)], writes=["rc"])
                        P.dve(TT(TC[c][:, :qn], self.pb[3 + c][:, :qn], RC[:, :qn], ALU.mult),
                              reads=["pb%d" % (3 + c), "rc"], writes=["tc%d" % c])
                    return f

                def flush_sum():
                    f = psum_pend[0]
                    psum_pend[0] = None
                    if f is not None:
                        f()
                emit_s(0)
                nk = len(kts)
                for i, (c, kt) in enumerate(steps):
                    j = i % nk
                    if i == 0:
                        emitA()
                    if i == min(8, len(steps) - 1):
                        emitB()
                    if i + 1 < len(steps):
                        emit_s(i + 1)
                    bi = i % 3
                    P.act(ACTF(PT[bi][:, :qn], self.pb[bi][:, :qn], AF.Exp, scale=0.125), reads=["pb%d" % bi],
                          writes=["pt%d" % bi])
                    first, last = (j == 0), (j == nk - 1)
                    mode = j % 3
                    only_pe = (nk < 2)
                    mm = [(self.pb[3 + c][:, :qn], VT[:, kt, :], PT[bi][:, :qn], first, last)]
                    wr = ["pb%d" % (3 + c)]
                    if mode == 0:
                        mm.append((self.pb[5 + c][:, :qn], self.ones_b[:], PT[bi][:, :qn], first, only_pe))
                        wr.append("pb%d" % (5 + c))
                    P.pe(MMS(mm), reads=["pt%d" % bi, "vt.%d" % kt, "ones_b"], writes=wr)
                    if mode == 1:
                        if j == 1:
                            P.dve(CP(ACCD[c][:, :qn], PT[bi][:, :qn]), reads=["pt%d" % bi], writes=[accd_t[c]])
                        else:
                            P.dve(TT(ACCD[c][:, :qn], ACCD[c][:, :qn], PT[bi][:, :qn], ALU.add),
                                  reads=["pt%d" % bi, accd_t[c]], writes=[accd_t[c]])
                    if mode == 2:
                        if j == 2:
                            P.pool(CP(ACCP[c][:, :qn], PT[bi][:, :qn]), reads=["pt%d" % bi], writes=[accp_t[c]])
                        else:
                            P.pool(TT(ACCP[c][:, :qn], ACCP[c][:, :qn], PT[bi][:, :qn], ALU.add),
                                   reads=["pt%d" % bi, accp_t[c]], writes=[accp_t[c]])
                    if j == 1:
                        flush_sum()
                    if last:
                        flush_sum()
                        psum_pend[0] = make_sumfin(c, nk >= 2, nk >= 3)
                flush_sum()
                emitA()
                emitB()
                pend["A"], pend["B"] = make_final(h, qt, q0, qn)
        emitA()
        emitB()

    def rg_mixer(self, b, l):
        P = self.P
        need_ctx = (l == 0)
        o = 0
        RX = self.scr_f(o, 2312); o += 2312
        XC = self.scr_f(o, T); o += T
        XCB = self.scr_b(o, T); o += T // 2
        HS = self.scr_f(o, T); o += T
        B1 = self.scr_f(o, T); o += T
        B2 = self.scr_f(o, T); o += T
        TL = [self.scr_f(o + i * 512, 512) for i in range(4)]; o += 2048
        assert o <= SCRW, o
        B0 = RX[:, 0:T]
        b0t = ["rx.%d" % t for t in range(5)] + ["rxpad"]
        W = self.w_in[l]
        self.rot = [0, 1, 2, 3]
        it = 0
        nout = T if need_ctx else NLAT
        for cc in range(2):
            for (a0, a1) in ((0, 2), (2050, 2053), (2309, 2312)):
                P.pool(MSET(RX[:, a0:a1], 0.0), writes=["rxpad"])

            def ev_x(t, ps, btok):
                t0, n = TILES[t]
                off = t0 + 2 if t < 4 else t0 + 5
                P.act(ACTF(RX[:, off:off + n], ps, AF.Identity), reads=[btok], writes=["rx.%d" % t])
            self.linear_fm(W, OFF_RGX + cc * 128, [0, 1, 2, 3, 4], ev_x)
            for (x0, r0, n, rt, wt) in ((0, 0, NLAT, ["rx.0", "rx.1", "rx.2", "rx.3"], ["xc.0", "xc.1", "xc.2", "xc.3"]),
                                        (NLAT, 2051, NCTX, ["rx.4"], ["xc.4"])):
                wo = PRM_OFF["rgcw"] + (l * 2 + cc) * 4
                P.act(ACTF(XC[:, x0:x0 + n], RX[:, r0:r0 + n], AF.Identity, bias=self.prmcol("rgcb", l * 2 + cc),
                           scale=self.prm[:, wo:wo + 1]), reads=rt + ["rxpad", "prm"], writes=wt)
                for k in range(1, 4):
                    P.dve(STT(XC[:, x0:x0 + n], RX[:, r0 + k:r0 + k + n], self.prm[:, wo + k:wo + k + 1], XC[:, x0:x0 + n],
                              ALU.mult, ALU.add), reads=rt + wt + ["rxpad", "prm"], writes=wt)
                P.pool(CP(XCB[:, x0:x0 + n], XC[:, x0:x0 + n]), reads=wt, writes=["xcb.%d" % t for t in
                                                                                  ([0, 1, 2, 3] if x0 == 0 else [4])])
            xct = ["xc.%d" % t for t in range(5)]
            for d in range(2):
                pi = (l * 2 + d) * 2 + cc
                for t in range(5):
                    t0, n = TILES[t]
                    TA, TX = TL[(it % 2) * 2], TL[(it % 2) * 2 + 1]
                    sfx = "%d" % (it % 2)
                    it += 1
                    ba, bat = self.next_bank()
                    bx, bxt = self.next_bank()
                    P.pe(MMS([(ba[:, :n], self.BD[:, pi * 2, :], XCB[:, t0:t0 + n], True, True),
                              (bx[:, :n], self.BD[:, pi * 2 + 1, :], XCB[:, t0:t0 + n], True, True)]),
                         reads=["bd", "xcb.%d" % t], writes=[bat, bxt])
                    P.act(ACTF(TA[:, :n], ba[:, :n], AF.Tanh, bias=self.hba[:, pi:pi + 1], scale=0.5),
                          reads=[bat, "hba"], writes=["ta" + sfx])
                    P.act(ACTF(B0[:, t0:t0 + n], TA[:, :n], AF.Exp, bias=self.hm[:, pi:pi + 1],
                               scale=self.hm[:, pi:pi + 1]), reads=["ta" + sfx, "hm"], writes=b0t)
                    P.act(ACTF(B1[:, t0:t0 + n], B0[:, t0:t0 + n], AF.Square), reads=b0t, writes=["b1"])
                    P.act(ACTF(TX[:, :n], bx[:, :n], AF.Tanh, bias=self.hbx[:, pi:pi + 1], scale=0.5),
                          reads=[bxt, "hbx"], writes=["tx" + sfx])
                    P.dve(STT(B2[:, t0:t0 + n], TX[:, :n], 1.0, XC[:, t0:t0 + n], ALU.add, ALU.mult),
                          reads=["tx" + sfx, "xc.%d" % t], writes=["b2"])
                    self.mods_step(4)
                P.act(ACTF(B1, B1, AF.Sqrt, bias=1.0, scale=-(1.0 - 2.0 ** -23)), reads=["b1"], writes=["b1"])
                P.dve(STT(B1, B2, 0.5, B1, ALU.mult, ALU.mult), reads=["b1", "b2"], writes=["b1"])
                if d == 0:
                    P.dve(SCAN(B2[:, NLAT:T], B0[:, NLAT:T], B1[:, NLAT:T], 0.0), reads=b0t + ["b1"], writes=["b2"])
                    P.dve(SCAN(B2[:, 0:NLAT], B0[:, 0:NLAT], B1[:, 0:NLAT], B2[:, T - 1:T]), reads=b0t + ["b1", "b2"],
                          writes=["b2"])
                    P.pool(CP(HS[:, 0:nout], B2[:, 0:nout]), reads=["b2"], writes=["hs"])
                else:
                    P.dve(SCAN(B2[:, NLAT:T][:, ::-1], B0[:, NLAT:T][:, ::-1], B1[:, NLAT:T][:, ::-1], 0.0),
                          reads=b0t + ["b1"], writes=["b2"])
                    P.dve(SCAN(B2[:, 0:NLAT][:, ::-1], B0[:, 0:NLAT][:, ::-1], B1[:, 0:NLAT][:, ::-1], B2[:, NLAT:NLAT + 1]),
                          reads=b0t + ["b1", "b2"], writes=["b2"])
                    P.pool(TT(HS[:, 0:nout], HS[:, 0:nout], B2[:, 0:nout], ALU.add), reads=["b2", "hs"], writes=["hs"])
            def ev_g(t, ps, btok):
                t0, n = TILES[t]
                P.act(ACTF(B0[:, t0:t0 + n], ps, AF.Identity), reads=[btok], writes=b0t)
            self.linear_fm(W, OFF_RGG + cc * 128, [0, 1, 2, 3] + ([4] if need_ctx else []), ev_g)
            G, Q = B0[:, 0:nout], B1[:, 0:nout]
            P.dve(TT(Q, G, G, ALU.mult), reads=b0t, writes=["b1"])
            P.dve(TS(Q, Q, 0.044715, ALU.mult, 1.0, ALU.add), reads=["b1"], writes=["b1"])
            P.dve(TT(Q, Q, G, ALU.mult), reads=["b1"] + b0t, writes=["b1"])
            P.act(ACTF(Q, Q, AF.Tanh, scale=0.7978845608028654), reads=["b1"], writes=["b1"])
            P.dve(STT(Q, Q, 1.0, G, ALU.add, ALU.mult), reads=["b1"] + b0t, writes=["b1"])
            P.dve(STT(self.MX[:, cc, 0:nout], Q, 0.5, HS[:, 0:nout], ALU.mult, ALU.mult), reads=["b1", "hs"],
                  writes=["mx.%d" % t for t in range(5 if need_ctx else 4)])
        self.mods_step(1000)

    def ssd_mixer(self, b, l):
        P = self.P
        need_ctx = (l == 0)
        W = self.w_in[l]
        o = 0
        RAW = self.scr_f(o, 2312); o += 2312
        TMP = self.scr_f(o, T); o += T
        XS = self.scr_b(o, NBLK * 256).rearrange("p (i d) -> p i d", i=NBLK); o += NBLK * 128
        BF = self.scr_b(o, T); o += T // 2
        CF = self.scr_b(o, T); o += T // 2
        BT = self.scr_b(o, NBLK * 128).rearrange("p (i d) -> p i d", i=NBLK); o += NBLK * 64
        SZ = self.scr_b(o, NBLK * 256).rearrange("p (i d) -> p i d", i=NBLK); o += NBLK * 128
        DTR = self.scr_f(o, 144); o += 144
        DTv = self.scr_f(o, 144); o += 144
        ADT = self.scr_f(o, 144); o += 144
        SPS = self.scr_f(o, 576); o += 576
        assert o <= SCRW, o
        DT3 = DTv.rearrange("p (i e) -> p i e", i=NBLK)
        ADT3 = ADT.rearrange("p (i e) -> p i e", i=NBLK)
        self.rot = [0, 1, 2, 3]
        for ch in range(4):
            for (a0, a1) in ((0, 2), (2050, 2053), (2309, 2312)):
                P.pool(MSET(RAW[:, a0:a1], 0.0), writes=["rxpad"])

            def ev_x(t, ps, btok):
                t0, n = TILES[t]
                off = t0 + 2 if t < 4 else t0 + 5
                P.act(ACTF(RAW[:, off:off + n], ps, AF.Identity), reads=[btok], writes=["rx.%d" % t])
            self.linear_fm(W, OFF_XBC + ch * 128, [0, 1, 2, 3, 4], ev_x)
            for (x0, r0, n, rt, wt) in ((0, 0, NLAT, ["rx.0", "rx.1", "rx.2", "rx.3"], ["xc.0", "xc.1", "xc.2", "xc.3"]),
                                        (NLAT, 2051, NCTX, ["rx.4"], ["xc.4"])):
                wo = PRM_OFF["scw"] + (l * 4 + ch) * 4
                P.act(ACTF(TMP[:, x0:x0 + n], RAW[:, r0:r0 + n], AF.Identity, bias=self.prmcol("scb", l * 4 + ch),
                           scale=self.prm[:, wo:wo + 1]), reads=rt + ["rxpad", "prm"], writes=wt)
                for k in range(1, 4):
                    P.dve(STT(TMP[:, x0:x0 + n], RAW[:, r0 + k:r0 + k + n], self.prm[:, wo + k:wo + k + 1],
                              TMP[:, x0:x0 + n], ALU.mult, ALU.add), reads=rt + wt + ["rxpad", "prm"], writes=wt)
                if ch == 3:
                    P.act(ACTF(CF[:, x0:x0 + n], TMP[:, x0:x0 + n], AF.Silu), reads=wt, writes=["cf"])
                else:
                    P.act(ACTF(TMP[:, x0:x0 + n], TMP[:, x0:x0 + n], AF.Silu), reads=wt, writes=wt)
                    if ch == 2:
                        P.pool(CP(BF[:, x0:x0 + n], TMP[:, x0:x0 + n]), reads=wt, writes=["bf"])
            if ch < 3:
                for i0 in range(0, NBLK, 4):
                    blks = list(range(i0, min(i0 + 4, NBLK)))
                    bank, btok = self.next_bank()
                    P.pe(TRS([(bank[:, gi * 128:(gi + 1) * 128], TMP[:, i * 128:(i + 1) * 128]) for gi, i in
                              enumerate(blks)], self.ident[:]),
                         reads=["xc.%d" % min(i // 4, 4) for i in blks] + ["ident"], writes=[btok])
                    src = bank[:, 0:len(blks) * 128].rearrange("p (i d) -> p i d", i=len(blks))
                    if ch < 2:
                        P.dve(CP(XS[:, i0:i0 + len(blks), ch * 128:(ch + 1) * 128], src), reads=[btok], writes=["xs"])
                    else:
                        P.dve(CP(BT[:, i0:i0 + len(blks), :], src), reads=[btok], writes=["bt"])
        for zc in range(2):
            def ev_z(blks, ps, btok, zc=zc):
                i0 = blks[0]
                P.act(ACTF(SZ[:, i0:i0 + len(blks), zc * 128:(zc + 1) * 128],
                           ps.rearrange("p (i d) -> p i d", i=len(blks)), AF.Silu), reads=[btok], writes=["sz"])
            self.linear_tm(W, OFF_SZ + zc * 128, 128, list(range(NBLK)), ev_z, 4)

        def ev_dt(blks, ps, btok):
            do = ROW_OFF["dtb"] + l * 8
            P.dve(TT(DTR.rearrange("p (i e) -> p i e", i=NBLK), ps.rearrange("p (i e) -> p i e", i=NBLK),
                     self.rowp[:, do:do + 8].unsqueeze(1).to_broadcast([128, NBLK, 8]), ALU.add),
                  reads=[btok, "rowp"], writes=["dtr"])
        self.linear_tm(W, OFF_DT, 8, list(range(NBLK)), ev_dt, NBLK)
        self.softplus(DTv, DTR, 144, SPS, (["dtr"], ["dt"]))
        P.dve(TT(ADT3, DT3, self.aneg[:, l * 8:l * 8 + 8].unsqueeze(1).to_broadcast([128, NBLK, 8]), ALU.mult),
              reads=["dt", "aneg"], writes=["adt"])
        P.barrier()
        if self.o.get("ssd_prep_only"):
            P.pool(MSET(self.MX[:, 2:4, :], 0.0), writes=["mx.%d" % t for t in range(5)])
            return
        xf = self.xn[:].bitcast(F32)
        o = 0
        Y = xf[:, o:o + NBLK * 256].rearrange("p (i d) -> p i d", i=NBLK); o += NBLK * 256
        LT = xf[:, o:o + 512]; o += 512
        D1 = xf[:, o:o + 512]; o += 512
        GM = xf[:, o:o + 256]; o += 256
        T1 = xf[:, o:o + 256]; o += 256
        T2 = xf[:, o:o + 256]; o += 256
        VV = xf[:, o:o + 256]; o += 256
        OT = xf[:, o:o + 256]; o += 256
        JK = xf[:, o:o + 256]; o += 256
        H = xf[:, o:o + 128]; o += 128
        SM = xf[:, o:o + 64]; o += 64
        Mb = self.xn[:, 2 * o:2 * o + 512].rearrange("p (h t) -> p h t", h=4); o += 256
        XD = self.xn[:, 2 * o:2 * o + 256].rearrange("p (h d) -> p h d", h=4); o += 128
        XDW = self.xn[:, 2 * o:2 * o + 256].rearrange("p (h d) -> p h d", h=4); o += 128
        HB = self.xn[:, 2 * o:2 * o + 128]; o += 64
        assert o <= 9216
        cst, ecs, wdec, dtw, dch, ss = (SM[:, i * 4:(i + 1) * 4] for i in range(6))
        rsd = SM[:, 24:26]
        ssq = SM[:, 26:28]
        PB = self.pb
        dsk = self.rowp[:, ROW_OFF["dsk"] + l * 4:ROW_OFF["dsk"] + l * 4 + 4]
        sng = self.rowp[:, ROW_OFF["sng"] + l * 256:ROW_OFF["sng"] + (l + 1) * 256]
        for d in range(2):
            U = self.uf if d == 0 else self.ub
            utok = "uf" if d == 0 else "ub"
            order = [16, 17] + list(range(16)) if d == 0 else [17, 16] + list(range(15, -1, -1))
            edge = 127 if d == 0 else 0
            P.dve(MSET(H, 0.0), writes=["H"])
            P.dve(MSET(HB, 0.0), writes=["HB"])
            if d >= self.o.get("ssd_dirs", 2):
                break
            for k in order[:self.o.get("ssd_nchunks", 18)]:
                want_y = ((k < 16) or need_ctx) and not self.o.get("ssd_noy")
                blk = slice(k * 128, (k + 1) * 128)
                adt = ADT3[:, k, d * 4:d * 4 + 4]
                mm = [(PB[3][:, 0:4], U[:], adt, True, True)]
                for h in range(4):
                    mm.append((PB[2][:, h * 128:(h + 1) * 128], ADT3[:, k, d * 4 + h:d * 4 + h + 1].to_broadcast([128, 128]),
                               U[:], True, True))
                P.pe(MMS(mm), reads=["adt", utok], writes=["pb3", "pb2"])
                P.dve(CP(cst, PB[3][:, 0:4]), reads=["pb3"], writes=["cst"])
                csb = PB[2][:, 0:512].rearrange("p (h t) -> p h t", h=4)
                if want_y:
                    P.dve(TT(D1.rearrange("p (h t) -> p h t", h=4), csb, cst.unsqueeze(2).to_broadcast([128, 4, 128]),
                             ALU.subtract), reads=["pb2", "cst"], writes=["d1"])
                    P.dve(STT(D1, D1, -1.0, D1, ALU.mult, ALU.min), reads=["d1"], writes=["d1"])
                    P.act(ACTF(LT, D1, AF.Exp), reads=["d1"], writes=["lt"])
                ys = self.o.get("ssd_ystage", 9)
                if want_y and ys >= 2:
                    P.pe(MMS([(PB[1 - g][:, 0:128], BF[64 * g:64 * g + 64, blk], CF[64 * g:64 * g + 64, blk],
                               True, True) for g in range(2)]), reads=["bf", "cf"], writes=["pb1", "pb0"])
                    for g in range(2):
                        P.dve(TT(GM[:, g * 128:(g + 1) * 128], PB[1 - g][:, 0:128], U[:], ALU.mult),
                              reads=["pb%d" % (1 - g), utok], writes=["gm"])
                    P.dve(TT(Mb.rearrange("p (g a) t -> p g a t", g=2), LT.rearrange("p (g a t) -> p g a t", g=2, a=2),
                             GM.rearrange("p (g t) -> p g t", g=2).unsqueeze(2).to_broadcast([128, 2, 2, 128]), ALU.mult),
                          reads=["lt", "gm"], writes=["mb"])
                dts = DT3[:, k, d * 4:d * 4 + 4]
                XS4 = XS[:, k, :].rearrange("p (h d) -> p h d", h=4)
                if want_y and ys >= 3:
                    P.dve(TT(XD, XS4, dts.unsqueeze(2).to_broadcast([128, 4, 64]), ALU.mult), reads=["xs", "dt"],
                          writes=["xd"])
                if want_y and ys >= 4:
                    P.pe(MMS([(PB[4][:, h * 64:(h + 1) * 64], Mb[:, h, :], XD[:, h, :], True, True) for h in range(4)]
                             + [(PB[5 + 2 * g][:, 0:128], CF[64 * g:64 * g + 64, blk], HB[64 * g:64 * g + 64, :],
                                 True, True) for g in range(2)]),
                         reads=["mb", "xd", "cf", "HB"], writes=["pb4", "pb5", "pb7"])
                if want_y and ys >= 5:
                    P.act(ACTF(ecs, cst, AF.Exp), reads=["cst"], writes=["ecs"])
                    for g in range(2):
                        P.dve(TT(T1[:, g * 128:(g + 1) * 128].rearrange("p (h d) -> p h d", h=2),
                                 PB[5 + 2 * g][:, 0:128].rearrange("p (h d) -> p h d", h=2),
                                 ecs[:, 2 * g:2 * g + 2].unsqueeze(2).to_broadcast([128, 2, 64]), ALU.mult),
                              reads=["pb%d" % (5 + 2 * g), "ecs"], writes=["t1"])
                    P.dve(TT(T1, T1, PB[4][:, 0:256], ALU.add), reads=["t1", "pb4"], writes=["t1"])
                    if d == 0:
                        P.dve(TT(T2.rearrange("p (h d) -> p h d", h=4), XS4,
                                 dsk.unsqueeze(2).to_broadcast([128, 4, 64]), ALU.mult), reads=["xs", "rowp"],
                              writes=["t2"])
                        P.dve(TT(Y[:, k, :], T1, T2, ALU.add), reads=["t1", "t2"], writes=["y.%d" % k])
                    else:
                        P.dve(TT(Y[:, k, :], Y[:, k, :], T1, ALU.add), reads=["t1", "y.%d" % k], writes=["y.%d" % k])
                P.dve(TT(wdec, csb[:, :, edge], cst, ALU.subtract), reads=["pb2", "cst"], writes=["wdec"])
                P.act(ACTF(wdec, wdec, AF.Exp), reads=["wdec"], writes=["wdec"])
                P.dve(TT(dtw, wdec, dts, ALU.mult), reads=["wdec", "dt"], writes=["dtw"])
                P.dve(TT(XDW, XS4, dtw.unsqueeze(2).to_broadcast([128, 4, 64]), ALU.mult), reads=["xs", "dtw"],
                      writes=["xdw"])
                P.pe(MMS([(PB[6][64 * g:64 * g + 64, 0:128], BT[:, k, 64 * g:64 * g + 64],
                           XDW[:, 2 * g:2 * g + 2, :].rearrange("p a d -> p (a d)"), True, True) for g in range(2)]),
                     reads=["bt", "xdw"], writes=["pb6"])
                for g in range(2):
                    P.act(ACTF(dch[64 * g:64 * g + 64, 0:2], csb[64 * g:64 * g + 64, 2 * g:2 * g + 2, edge], AF.Exp),
                          reads=["pb2"], writes=["dch"])
                P.dve(TT(H.rearrange("p (a d) -> p a d", a=2), H.rearrange("p (a d) -> p a d", a=2),
                         dch[:, 0:2].unsqueeze(2).to_broadcast([128, 2, 64]), ALU.mult), reads=["H", "dch"], writes=["H"])
                P.dve(TT(H, H, PB[6][:, 0:128], ALU.add), reads=["H", "pb6"], writes=["H"])
                P.dve(CP(HB, H), reads=["H"], writes=["HB"])
                if d == 1 and want_y and ys >= 6:
                    P.dve(TT(VV, Y[:, k, :], SZ[:, k, :], ALU.mult), reads=["y.%d" % k, "sz"], writes=["vv"])
                    for g in range(2):
                        P.act(ACTF(JK[:, g * 128:(g + 1) * 128], VV[:, g * 128:(g + 1) * 128], AF.Square,
                                   accum_out=ssq[:, g:g + 1]), reads=["vv"], writes=["jk", "ssq"])
                    P.act(ACTF(rsd, ssq, AF.Ln, bias=EPS, scale=1.0 / 128.0), reads=["ssq"], writes=["rsd"])
                    P.act(ACTF(rsd, rsd, AF.Exp, scale=-0.5), reads=["rsd"], writes=["rsd"])
                    for g in range(2):
                        P.dve(STT(OT[:, g * 128:(g + 1) * 128], VV[:, g * 128:(g + 1) * 128], rsd[:, g:g + 1],
                                  sng[:, g * 128:(g + 1) * 128], ALU.mult, ALU.mult), reads=["vv", "rsd", "rowp"],
                              writes=["ot"])
                    P.pe(TRS([(PB[7][:, g * 128:(g + 1) * 128], OT[:, g * 128:(g + 1) * 128]) for g in range(2)],
                             self.ident[:]), reads=["ot", "ident"], writes=["pb7"])
                    P.act(ACTF(self.MX[:, 2:4, k * 128:(k + 1) * 128],
                               PB[7][:, 0:256].rearrange("p (g t) -> p g t", g=2), AF.Identity), reads=["pb7"],
                          writes=["mx.%d" % min(k // 4, 4)])

    def layer(self, b, l):
        P = self.P
        o = self.o
        need_ctx = (l == 0)
        allt = [0, 1, 2, 3, 4]
        outt = allt if need_ctx else [0, 1, 2, 3]
        if o.get("mix", True):
            self.norm(b, l, 0, allt)
            P.barrier()
            if o.get("da", True):
                self.da_mixer(b, l)
                self.wout_partial(b, l, 512, outt)
                P.barrier()
            if o.get("rg", True) or o.get("ssd", True):
                if o.get("rg", True):
                    self.rg_mixer(b, l)
                else:
                    P.pool(MSET(self.MX[:, 0:2, :], 0.0), writes=["mx.%d" % t for t in range(5)])
                P.barrier()
                if o.get("ssd", True):
                    self.ssd_mixer(b, l)
                else:
                    P.pool(MSET(self.MX[:, 2:4, :], 0.0), writes=["mx.%d" % t for t in range(5)])
                self.wout_partial(b, l, 0, outt)
                P.barrier()
        self.mods_step(1000)
        if o.get("ffn", True):
            self.norm(b, l, 1, outt)
            P.barrier()
            self.ffn(b, l, outt)
            P.barrier()

    def run(self):
        self.setup()
        for b in range(self.NB):
            self.load_x(b)
            self.P.barrier()
            for l in range(self.o.get("layers", 2)):
                self.layer(b, l)
            self.final(b)
            self.P.barrier()


def build_nc(NB=2, opts=None):
    nc = bass.Bass("TRN2", target_bir_lowering=False)
    with contextlib.ExitStack() as st:
        P = Prog(nc, st)
        K = Kern(nc, P, NB, opts or {})
        K.run()
        P.emit(final_reads=["out"])
    return nc


_NC_CACHE = {}


def kernel(x, c, ctx, c_ctx, w_mod, b_mod, norm1_g, w_in, rg_conv_w, rg_conv_b, rg_w_a, rg_b_a,
           rg_w_x, rg_b_x, rg_lambda, ssd_conv_w, ssd_conv_b, ssd_dt_bias, ssd_a_log, ssd_d,
           ssd_norm_g, da_lambda, da_subln_g, w_out, norm2_g, w_gate, w_up, w_down, final_norm_g,
           _opts=None, _ncores=8):
    inp = dict(w_mod=w_mod, b_mod=b_mod, norm1_g=norm1_g, w_in=w_in, rg_conv_w=rg_conv_w, rg_conv_b=rg_conv_b,
               rg_w_a=rg_w_a, rg_b_a=rg_b_a, rg_w_x=rg_w_x, rg_b_x=rg_b_x, rg_lambda=rg_lambda,
               ssd_conv_w=ssd_conv_w, ssd_conv_b=ssd_conv_b, ssd_dt_bias=ssd_dt_bias, ssd_a_log=ssd_a_log,
               ssd_d=ssd_d, ssd_norm_g=ssd_norm_g, da_lambda=da_lambda, da_subln_g=da_subln_g, w_out=w_out,
               norm2_g=norm2_g, w_gate=w_gate, w_up=w_up, w_down=w_down, final_norm_g=final_norm_g)
    shared = pack_shared(inp)
    x = np.asarray(x, np.float32)
    ctx = np.asarray(ctx, np.float32)
    c = np.asarray(c, np.float32)
    c_ctx = np.asarray(c_ctx, np.float32)
    NB = 2
    in_maps = []
    for i in range(_ncores):
        cc = np.stack([c[2 * i], c[2 * i + 1], c_ctx], axis=0)
        ct = np.ascontiguousarray(cc.reshape(3, 8, 128).transpose(2, 1, 0))
        m = dict(shared)
        m["x"] = np.ascontiguousarray(x[2 * i:2 * i + 2])
        m["ctx"] = np.ascontiguousarray(ctx[2 * i:2 * i + 2])
        m["ct"] = ct
        in_maps.append(m)
    key = repr(sorted((_opts or {}).items()))
    if key not in _NC_CACHE:
        _NC_CACHE[key] = build_nc(NB, _opts)
    nc = _NC_CACHE[key]
    res = run_bass_kernel_spmd(nc, in_maps, core_ids=list(range(_ncores)))
    out = np.concatenate([np.asarray(r["out"], np.float32) for r in res.results], axis=0)
    return out
```

```python
import contextlib
import math
import numpy as np
import concourse.bass as bass
import concourse.mybir as mybir
from concourse.bass_utils import run_bass_kernel_spmd

F32 = mybir.dt.float32
BF16 = mybir.dt.bfloat16
ALU = mybir.AluOpType
AF = mybir.ActivationFunctionType

D = 1024
NLAT = 2048
NCTX = 256
T = NLAT + NCTX
NBLK = T // 128
TILES = [(0, 512), (512, 512), (1024, 512), (1536, 512), (2048, 256)]
DFF = 2816
DIN = 2824
DINX = DIN + 1024
EPS = 1e-6
OFF_RGX, OFF_RGG, OFF_SZ, OFF_XBC, OFF_DT, OFF_Q, OFF_K, OFF_V = 0, 256, 512, 768, 1280, 1288, 1800, 2312
OFF_QS, OFF_KS = DIN, DIN + 512
SCRW = 14848


class _Op:
    __slots__ = ("eng", "fn", "waits", "tok", "clock", "dma", "idx")


class Prog:
    ENGS = ("pe", "act", "dve", "pool", "sp")

    def __init__(self, nc, stack, n_dma_sems=12):
        self.nc = nc
        self.stack = stack
        self.ops = {e: [] for e in self.ENGS}
        self.sem = {e: stack.enter_context(nc.semaphore("S_" + e)) for e in self.ENGS}
        self.cnt = {e: 0 for e in self.ENGS}
        self.know = {e: {} for e in self.ENGS}
        self.last_op = {e: None for e in self.ENGS}
        self.dma_sems = {}
        for q in ("sp", "pool"):
            self.dma_sems[q] = [[stack.enter_context(nc.semaphore("D_%s%d" % (q, i))), 0, None]
                                for i in range(n_dma_sems)]
        self.dma_rr = {q: 0 for q in self.dma_sems}
        self.last_w = {}
        self.readers = {}
        self.nops = 0
        self.nwaits = 0

    def sb(self, name, shape, dt=F32):
        return self.stack.enter_context(self.nc.sbuf_tensor(name, list(shape), dt))

    def ps(self, name, shape, dt=F32):
        return self.stack.enter_context(self.nc.psum_tensor(name, list(shape), dt))

    def _need(self, e, d, waits):
        if d is None:
            return
        if e == "pe" and d.eng == "pe" and not d.dma:
            return
        sem, val = d.tok
        k = self.know[e]
        if k.get(sem, 0) >= val:
            return
        waits.append((sem, val))
        for s, v in d.clock.items():
            if k.get(s, 0) < v:
                k[s] = v

    def op(self, eng, fn, reads=(), writes=(), dma=False, extra_deps=()):
        o = _Op()
        o.eng = eng
        o.fn = fn
        o.dma = dma
        o.idx = self.nops
        self.nops += 1
        deps = {}
        for r in reads:
            d = self.last_w.get(r)
            if d is not None:
                deps[d.idx] = d
        for w in writes:
            d = self.last_w.get(w)
            if d is not None:
                deps[d.idx] = d
            for d in self.readers.get(w, ()):
                deps[d.idx] = d
        for d in extra_deps:
            if d is not None:
                deps[d.idx] = d
        waits = []
        for i in sorted(deps, reverse=True):
            self._need(eng, deps[i], waits)
        if dma:
            pool = self.dma_sems[eng]
            slot = pool[self.dma_rr[eng] % len(pool)]
            self.dma_rr[eng] += 1
            if slot[2] is not None:
                self._need(eng, slot[2], waits)
            slot[1] += 16
            o.tok = (slot[0], slot[1])
            slot[2] = o
        elif fn is not None:
            self.cnt[eng] += 1
            o.tok = (self.sem[eng], self.cnt[eng])
        else:
            o.tok = None
        o.waits = waits
        self.nwaits += len(waits)
        o.clock = dict(self.know[eng])
        if o.tok is not None:
            o.clock[o.tok[0]] = o.tok[1]
            if not dma:
                self.last_op[eng] = o
        for w in writes:
            self.last_w[w] = o
            self.readers[w] = []
        for r in reads:
            if r not in writes:
                self.readers.setdefault(r, []).append(o)
        self.ops[eng].append(o)
        return o

    def pe(self, fn, reads=(), writes=()):
        return self.op("pe", fn, reads, writes)

    def act(self, fn, reads=(), writes=()):
        return self.op("act", fn, reads, writes)

    def dve(self, fn, reads=(), writes=()):
        return self.op("dve", fn, reads, writes)

    def pool(self, fn, reads=(), writes=()):
        return self.op("pool", fn, reads, writes)

    def dma(self, fn, reads=(), writes=(), q="sp"):
        return self.op(q, fn, reads, writes, dma=True)

    def barrier(self):
        pend = [self.last_op[e] for e in self.ENGS if self.last_op[e] is not None]
        for q in self.dma_sems:
            for slot in self.dma_sems[q]:
                if slot[2] is not None:
                    pend.append(slot[2])
        for e in self.ENGS:
            self.op(e, None, extra_deps=pend)

    def emit(self, final_reads=()):
        nc = self.nc
        fin_waits = []
        for r in final_reads:
            self._need("sp", self.last_w.get(r), fin_waits)

        def run(e, engine, extra=()):
            for o in self.ops[e]:
                for (s, v) in o.waits:
                    engine.wait_ge(s, v)
                if o.fn is None:
                    continue
                ins = o.fn(engine)
                ins.then_inc(o.tok[0], 16 if o.dma else 1)
            for (s, v) in extra:
                engine.wait_ge(s, v)

        with nc.Block() as block:
            @block.sync
            def _(eng):
                run("sp", eng, fin_waits)

            @block.gpsimd
            def _(eng):
                run("pool", eng)

            @block.scalar
            def _(eng):
                run("act", eng)

            @block.vector
            def _(eng):
                run("dve", eng)

            @block.tensor
            def _(eng):
                run("pe", eng)


def MMS(lst):
    def f(e):
        ins = None
        for (o, l, r, st, sp) in lst:
            ins = e.matmul(o, lhsT=l, rhs=r, start=st, stop=sp)
        return ins
    return f


def TRS(lst, ident):
    def f(e):
        ins = None
        for (o, i) in lst:
            ins = e.transpose(out=o, in_=i, identity=ident)
        return ins
    return f


def ACTF(out, in_, func, bias=0.0, scale=1.0, accum_out=None):
    if accum_out is None:
        return lambda e: e.activation(out=out, in_=in_, func=func, bias=bias, scale=scale)
    return lambda e: e.activation(out=out, in_=in_, func=func, bias=bias, scale=scale, accum_out=accum_out)


def TT(out, a, b, op):
    return lambda e: e.tensor_tensor(out=out, in0=a, in1=b, op=op)


def TS(out, a, s1, op0, s2=None, op1=None):
    if op1 is None:
        return lambda e: e.tensor_scalar(out=out, in0=a, scalar1=s1, scalar2=None, op0=op0)
    return lambda e: e.tensor_scalar(out=out, in0=a, scalar1=s1, scalar2=s2, op0=op0, op1=op1)


def STT(out, in0, scalar, in1, op0, op1):
    return lambda e: e.scalar_tensor_tensor(out=out, in0=in0, scalar=scalar, in1=in1, op0=op0, op1=op1)


def CP(out, in_):
    return lambda e: e.tensor_copy(out=out, in_=in_)


def RECIP(out, in_):
    return lambda e: e.reciprocal(out=out, in_=in_)


def MSET(ap, v):
    return lambda e: e.memset(ap, v)


def DMA(out, in_):
    return lambda e: e.dma_start(out=out, in_=in_)


def SCAN(out, a, b, init):
    return lambda e: e.tensor_tensor_scan(out=out, data0=a, data1=b, initial=init, op0=ALU.mult, op1=ALU.add)


PRM_SPEC = [("n1g", 16), ("n2g", 16), ("fng", 8), ("bmod", 96), ("rgcw", 16), ("rgcb", 4), ("rgba", 8),
            ("rgbx", 8), ("rglam", 8), ("scw", 32), ("scb", 8), ("subg", 2)]
PRM_OFF = {}
_o = 0
for _n, _w in PRM_SPEC:
    PRM_OFF[_n] = _o
    _o += _w
NPRM = _o
ROW_SPEC = [("dtb", 16), ("alog", 16), ("dsk", 8), ("sng", 512), ("dal", 512)]
ROW_OFF = {}
_o = 0
for _n, _w in ROW_SPEC:
    ROW_OFF[_n] = _o
    _o += _w
NROW = _o


def _fm(v):
    v = np.asarray(v, np.float32)
    lead = v.shape[:-1]
    n = v.shape[-1] // 128
    v = v.reshape(lead + (n, 128))
    return np.moveaxis(v, -1, 0).reshape(128, -1)


def pack_shared(inp):
    f = lambda k: np.asarray(inp[k], np.float32)
    prm = np.zeros((128, NPRM), np.float32)

    def put(name, arr):
        o = PRM_OFF[name]
        prm[:, o:o + arr.shape[1]] = arr

    put("n1g", _fm(f("norm1_g")))
    put("n2g", _fm(f("norm2_g")))
    put("fng", _fm(f("final_norm_g")))
    put("bmod", _fm(f("b_mod")))
    w = f("rg_conv_w").reshape(2, 4, 2, 128).transpose(3, 0, 2, 1).reshape(128, 16)
    put("rgcw", w)
    put("rgcb", _fm(f("rg_conv_b")))
    put("rgba", _fm(f("rg_b_a")))
    put("rgbx", _fm(f("rg_b_x")))
    put("rglam", _fm(f("rg_lambda")))
    w = f("ssd_conv_w").reshape(2, 4, 4, 128).transpose(3, 0, 2, 1).reshape(128, 32)
    put("scw", w)
    put("scb", _fm(f("ssd_conv_b")))
    put("subg", _fm(f("da_subln_g")))
    row = np.zeros((NROW,), np.float32)
    row[ROW_OFF["dtb"]:ROW_OFF["dtb"] + 16] = f("ssd_dt_bias").reshape(-1)
    row[ROW_OFF["alog"]:ROW_OFF["alog"] + 16] = f("ssd_a_log").reshape(-1)
    row[ROW_OFF["dsk"]:ROW_OFF["dsk"] + 8] = f("ssd_d").reshape(-1)
    row[ROW_OFF["sng"]:ROW_OFF["sng"] + 512] = f("ssd_norm_g").reshape(-1)
    row[ROW_OFF["dal"]:ROW_OFF["dal"] + 512] = f("da_lambda").reshape(-1)
    rowp = np.ascontiguousarray(np.broadcast_to(row[None, :], (128, NROW)))
    bd = np.zeros((128, 16, 128), np.float32)
    wa, wx = f("rg_w_a"), f("rg_w_x")
    for l in range(2):
        for d in range(2):
            for cc in range(2):
                for ax, wsrc in enumerate((wa, wx)):
                    idx = ((l * 2 + d) * 2 + cc) * 2 + ax
                    for blk in range(2):
                        bd[blk * 64:(blk + 1) * 64, idx, blk * 64:(blk + 1) * 64] = wsrc[l, d, 2 * cc + blk]
    j = np.arange(64)
    sw = np.where((j % 32) < 16, j + 16, j - 16)
    win = f("w_in")
    qidx = np.concatenate([OFF_Q + h * 128 + c * 64 + sw for h in range(4) for c in range(2)])
    kidx = np.concatenate([OFF_K + h * 128 + c * 64 + sw for h in range(4) for c in range(2)])
    winx = np.ascontiguousarray(np.concatenate([win, win[:, :, qidx], win[:, :, kidx]], axis=2))
    ident = np.eye(128, dtype=np.float32)
    tt = np.arange(128)
    uf = (tt[:, None] <= tt[None, :]).astype(np.float32)
    ub = (tt[:, None] >= tt[None, :]).astype(np.float32)
    t = np.arange(NLAT)
    row_i = (t // 64).astype(np.float32)
    col_i = (t % 64).astype(np.float32)
    half = 32
    inv_freq = np.power(np.float32(10000.0), -np.arange(0, half, 2, dtype=np.float32) / np.float32(half)).astype(np.float32)
    ang_r = row_i[:, None] * inv_freq[None, :]
    ang_c = col_i[:, None] * inv_freq[None, :]
    rope = np.zeros((128, 2, NLAT), np.float32)
    for p in range(128):
        jj = p % 64
        ang = ang_r if jj < 32 else ang_c
        i = jj % 16
        rope[p, 0] = np.cos(ang[:, i])
        sgn = -1.0 if (jj % 32) < 16 else 1.0
        rope[p, 1] = sgn * np.sin(ang[:, i])
    return dict(prm=prm, rowp=rowp, bd=bd, w_in_x=winx, ident=ident, uf=uf, ub=ub, rope=rope,
                w_mod=f("w_mod"), w_out=f("w_out"), w_gate=f("w_gate"), w_up=f("w_up"), w_down=f("w_down"))


class Kern:
    def __init__(self, nc, P, NB, opts):
        self.nc = nc
        self.P = P
        self.NB = NB
        self.o = opts
        dt_in = lambda n, s: nc.dram_tensor(n, list(s), F32, kind="ExternalInput").ap()
        self.x = dt_in("x", [NB, NLAT, D])
        self.ctx = dt_in("ctx", [NB, NCTX, D])
        self.ct = dt_in("ct", [128, 8, 3])
        self.prm_d = dt_in("prm", [128, NPRM])
        self.rowp_d = dt_in("rowp", [128, NROW])
        self.bd_d = dt_in("bd", [128, 16, 128])
        self.w_in = dt_in("w_in_x", [2, D, DINX])
        self.ident_d = dt_in("ident", [128, 128])
        self.uf_d = dt_in("uf", [128, 128])
        self.ub_d = dt_in("ub", [128, 128])
        self.rope_d = dt_in("rope", [128, 2, NLAT])
        self.w_mod = dt_in("w_mod", [2, D, 6 * D])
        self.w_out = dt_in("w_out", [2, D, D])
        self.w_gate = dt_in("w_gate", [2, D, DFF])
        self.w_up = dt_in("w_up", [2, D, DFF])
        self.w_down = dt_in("w_down", [2, DFF, D])
        self.out = nc.dram_tensor("out", [NB, NLAT, D], F32, kind="ExternalOutput").ap()
        sb = P.sb
        self.resid = sb("resid", [128, 8 * T])
        self.R3 = self.resid[:].rearrange("p (c t) -> p c t", c=8)
        self.xn = sb("xn", [128, 8 * T], BF16)
        self.XN = self.xn[:].rearrange("p (c t) -> p c t", c=8)
        self.mixb = sb("mixb", [128, 4 * T], BF16)
        self.MX = self.mixb[:].rearrange("p (c t) -> p c t", c=4)
        self.scr = sb("scr", [128, SCRW])
        self.WB = [sb("wb%d" % i, [128, 1024], BF16) for i in range(4)]
        self.wrr = 0
        self.ident = sb("ident_s", [128, 128])
        self.uf = sb("uf_s", [128, 128])
        self.ub = sb("ub_s", [128, 128])
        self.ones_f = sb("ones_f", [128, 128])
        self.ones_b = sb("ones_b", [128, 128], BF16)
        self.prm = sb("prm_s", [128, NPRM])
        self.rowp = sb("rowp_s", [128, NROW])
        self.bd = sb("bd_s", [128, 16 * 128], BF16)
        self.BD = self.bd[:].rearrange("p (i j) -> p i j", i=16)
        self.modt = sb("modt", [128, 2 * 48 * 3])
        self.MT = self.modt[:].rearrange("p (l j n) -> p l j n", l=2, j=48)
        self.amod = sb("amod", [128, 2 * 2 * 3 * 8])
        self.AM = self.amod[:].rearrange("p (l w n c) -> p l w n c", l=2, w=2, n=3)
        self.small = sb("small", [128, 256])
        self.sml_o = 0
        self.pb = [P.ps("pb%d" % i, [128, 512]) for i in range(8)]
        self.rot = [0, 1, 2, 3]
        self.rr = 0

    def prmcol(self, name, i):
        o = PRM_OFF[name] + i
        return self.prm[:, o:o + 1]

    def salloc(self, n):
        o = self.sml_o
        self.sml_o += n
        assert self.sml_o <= 256
        return self.small[:, o:o + n]

    def scr_f(self, off, n):
        return self.scr[:, off:off + n]

    def scr_b(self, off, n):
        assert n % 2 == 0
        return self.scr[:, off:off + n // 2].bitcast(BF16)

    def next_bank(self):
        i = self.rot[self.rr % len(self.rot)]
        self.rr += 1
        return self.pb[i], "pb%d" % i

    def load_w(self, wsrc, k0, kch, col0, ncols=128):
        i = self.wrr % len(self.WB)
        self.wrr += 1
        wb = self.WB[i][:, 0:kch * ncols].rearrange("p (c n) -> p c n", c=kch)
        src = wsrc.rearrange("(c p) n -> p c n", p=128)[:, k0:k0 + kch, col0:col0 + ncols]
        tok = "wb%d" % i
        self.P.dma(DMA(wb, src), writes=[tok], q="pool")
        return wb, tok

    def linear_fm(self, wsrc, col0, tiles, evac, kch=8, k0=0, rhs=None, rtok=None, ncols=128):
        wb, wtok = self.load_w(wsrc, k0, kch, col0, ncols)
        if rhs is None:
            rhs = lambda kc, t0, n: self.XN[:, kc, t0:t0 + n]
            rtok = lambda t: ["xn.%d" % t]
        for t in tiles:
            t0, n = TILES[t]
            bank, btok = self.next_bank()
            out = bank[0:ncols, 0:n]
            mms = [(out, wb[:, kc, :], rhs(kc, t0, n), kc == 0, kc == kch - 1) for kc in range(kch)]
            self.P.pe(MMS(mms), reads=[wtok] + rtok(t), writes=[btok])
            evac(t, out, btok)

    def linear_fm2(self, wsrc, colA, colB, tiles, evac2, extraA=(), evacA=None):
        wa, atok = self.load_w(wsrc, 0, 8, colA)
        wb2, btok2 = self.load_w(wsrc, 0, 8, colB)
        for t in tiles:
            t0, n = TILES[t]
            ba, bat = self.next_bank()
            bb, bbt = self.next_bank()
            self.P.pe(MMS([(ba[:, 0:n], wa[:, kc, :], self.XN[:, kc, t0:t0 + n], kc == 0, kc == 7) for kc in range(8)]
                          + [(bb[:, 0:n], wb2[:, kc, :], self.XN[:, kc, t0:t0 + n], kc == 0, kc == 7) for kc in range(8)]),
                      reads=[atok, btok2, "xn.%d" % t], writes=[bat, bbt])
            evac2(t, ba[:, 0:n], bat, bb[:, 0:n], bbt)
        for t in extraA:
            t0, n = TILES[t]
            ba, bat = self.next_bank()
            self.P.pe(MMS([(ba[:, 0:n], wa[:, kc, :], self.XN[:, kc, t0:t0 + n], kc == 0, kc == 7) for kc in range(8)]),
                      reads=[atok, "xn.%d" % t], writes=[bat])
            evacA(t, ba[:, 0:n], bat)

    def linear_tm(self, wsrc, col0, ncols, blocks, evac, grp):
        wb, wtok = self.load_w(wsrc, 0, 8, col0, ncols)
        for g0 in range(0, len(blocks), grp):
            blks = blocks[g0:g0 + grp]
            bank, btok = self.next_bank()
            mms = []
            rt = set()
            for gi, i in enumerate(blks):
                out = bank[:, gi * ncols:(gi + 1) * ncols]
                rt.add("xn.%d" % min(i // 4, 4))
                for kc in range(8):
                    mms.append((out, self.XN[:, kc, i * 128:(i + 1) * 128], wb[:, kc, :], kc == 0, kc == 7))
            self.P.pe(MMS(mms), reads=[wtok] + sorted(rt), writes=[btok])
            evac(blks, bank[:, 0:len(blks) * ncols], btok)

    def softplus(self, out, x, n, s0, toks):
        P = self.P
        ax, e, z, p = (s0[:, i * n:(i + 1) * n] for i in range(4))
        rd, wr = toks
        tk = ["spl.tmp"]
        P.dve(STT(ax, x, -1.0, x, ALU.mult, ALU.max), reads=rd, writes=tk)
        P.act(ACTF(e, ax, AF.Exp, scale=-1.0), reads=tk, writes=tk)
        P.dve(TS(z, e, 2.0, ALU.add), reads=tk, writes=tk)
        P.dve(RECIP(z, z), reads=tk, writes=tk)
        P.dve(TT(z, z, e, ALU.mult), reads=tk, writes=tk)
        P.dve(TT(e, z, z, ALU.mult), reads=tk, writes=tk)
        P.dve(TS(p, e, 1.0 / 13.0, ALU.mult, 1.0 / 11.0, ALU.add), reads=tk, writes=tk)
        for cst in (1.0 / 9.0, 1.0 / 7.0, 1.0 / 5.0, 1.0 / 3.0, 1.0):
            P.dve(TT(p, p, e, ALU.mult), reads=tk, writes=tk)
            P.dve(TS(p, p, cst, ALU.add), reads=tk, writes=tk)
        P.dve(TT(p, p, z, ALU.mult), reads=tk, writes=tk)
        P.dve(TS(ax, x, 0.0, ALU.max), reads=rd + tk, writes=tk)
        P.dve(STT(out, p, 2.0, ax, ALU.mult, ALU.add), reads=tk, writes=wr)

    def mods_emit(self, l, jj):
        P = self.P
        wb, wtok = self.load_w(self.w_mod[l], 0, 8, jj * 128)
        bi = 5 + (self.mod_i % 2)
        self.mod_i += 1
        out = self.pb[bi][:, 0:3]
        P.pe(MMS([(out, wb[:, kc, :], self.ctb[:, kc, :], kc == 0, kc == 7) for kc in range(8)]),
             reads=[wtok, "ctb"], writes=["pb%d" % bi])
        P.dve(TS(self.MT[:, l, jj, :], out, self.prmcol("bmod", l * 48 + jj), ALU.add), reads=["pb%d" % bi, "prm"],
              writes=["modt"])

    def mods_finish(self, l, w):
        gname = "n1g" if w == 0 else "n2g"
        go = PRM_OFF[gname] + l * 8
        jj0 = 8 + 24 * w
        for n in range(3):
            self.P.dve(STT(self.AM[:, l, w, n, :], self.MT[:, l, jj0:jj0 + 8, n], 1.0, self.prm[:, go:go + 8],
                           ALU.add, ALU.mult), reads=["modt", "prm"], writes=["amod"])

    def mods_step(self, n):
        while n > 0 and self.mod_q:
            l, jj = self.mod_q.pop(0)
            self.mods_emit(l, jj)
            n -= 1
            if not self.mod_q:
                self.mods_finish(0, 1)
                self.mods_finish(1, 0)
                self.mods_finish(1, 1)

    def setup(self):
        P = self.P
        P.dma(DMA(self.ident[:], self.ident_d), writes=["ident"])
        P.dma(DMA(self.uf[:], self.uf_d), writes=["uf"])
        P.dma(DMA(self.ub[:], self.ub_d), writes=["ub"])
        P.dma(DMA(self.prm[:], self.prm_d), writes=["prm"])
        P.dma(DMA(self.rowp[:], self.rowp_d), writes=["rowp"])
        P.dma(DMA(self.BD, self.bd_d), writes=["bd"], q="pool")
        P.pool(MSET(self.ones_f[:], 1.0), writes=["ones_f"])
        P.pool(MSET(self.ones_b[:], 1.0), writes=["ones_b"])
        ctile = self.scr_f(0, 24).rearrange("p (k n) -> p k n", k=8)
        sig = self.scr_f(24, 24).rearrange("p (k n) -> p k n", k=8)
        P.dma(DMA(ctile, self.ct), writes=["ct"])
        P.act(ACTF(sig, ctile, AF.Sigmoid), reads=["ct"], writes=["sig"])
        P.dve(TT(ctile, ctile, sig, ALU.mult), reads=["ct", "sig"], writes=["ct"])
        self.ctb = self.salloc(12).bitcast(BF16)[:, 0:24].rearrange("p (k n) -> p k n", k=8)
        P.dve(CP(self.ctb, ctile), reads=["ct"], writes=["ctb"])
        self.mod_q = [(0, jj) for jj in range(24, 48)] + [(1, jj) for jj in range(48)]
        self.mod_i = 0
        for jj in range(24):
            self.mods_emit(0, jj)
        self.mods_finish(0, 0)
        self.m8sp = self.salloc(8)
        nl = self.salloc(8)
        lo = PRM_OFF["rglam"]
        P.dve(TS(nl, self.prm[:, lo:lo + 8], -1.0, ALU.mult), reads=["prm"], writes=["nl"])
        self.softplus(self.m8sp, nl, 8, self.scr_f(8300, 32), (["nl"], ["m8sp"]))
        P.dve(TS(self.m8sp, self.m8sp, -8.0, ALU.mult), reads=["m8sp"], writes=["m8sp"])
        self.hm = self.salloc(8)
        self.hba = self.salloc(8)
        self.hbx = self.salloc(8)
        P.dve(TS(self.hm, self.m8sp, 0.5, ALU.mult), reads=["m8sp"], writes=["hm"])
        oa, ox = PRM_OFF["rgba"], PRM_OFF["rgbx"]
        P.dve(TS(self.hba, self.prm[:, oa:oa + 8], 0.5, ALU.mult), reads=["prm"], writes=["hba"])
        P.dve(TS(self.hbx, self.prm[:, ox:ox + 8], 0.5, ALU.mult), reads=["prm"], writes=["hbx"])
        self.aneg = self.salloc(16)
        ao = ROW_OFF["alog"]
        P.act(ACTF(self.aneg, self.rowp[:, ao:ao + 16], AF.Exp), reads=["rowp"], writes=["aneg"])
        P.dve(TS(self.aneg, self.aneg, -1.0, ALU.mult), reads=["aneg"], writes=["aneg"])
        self.nlam = self.salloc(2)
        self.gsc = self.salloc(2)
        pr = self.scr_f(8400, 128)
        sm = self.salloc(4)
        for l in range(2):
            do = ROW_OFF["dal"] + l * 256
            lam_init = 0.8 - 0.6 * math.exp(-0.3 * l)
            for i in range(2):
                P.dve(TT(pr[:, 0:64], self.rowp[:, do + 128 * i:do + 128 * i + 64],
                         self.rowp[:, do + 128 * i + 64:do + 128 * i + 128], ALU.mult), reads=["rowp"], writes=["pr"])
                P.dve(lambda e, o_=sm[:, 2 * l + i:2 * l + i + 1], i_=pr[:, 0:64]: e.reduce_sum(
                    out=o_, in_=i_, axis=mybir.AxisListType.X), reads=["pr"], writes=["sm"])
            P.act(ACTF(sm[:, 2 * l:2 * l + 2], sm[:, 2 * l:2 * l + 2], AF.Exp), reads=["sm"], writes=["sm"])
            P.dve(TT(self.nlam[:, l:l + 1], sm[:, 2 * l + 1:2 * l + 2], sm[:, 2 * l:2 * l + 1], ALU.subtract),
                  reads=["sm"], writes=["nlam"])
            P.dve(TS(self.nlam[:, l:l + 1], self.nlam[:, l:l + 1], -lam_init, ALU.add), reads=["nlam"], writes=["nlam"])
            P.dve(TS(self.gsc[:, l:l + 1], self.prmcol("subg", l), 1.0 - lam_init, ALU.mult), reads=["prm"],
                  writes=["gsc"])
        P.barrier()

    def load_x(self, b):
        P = self.P
        STG = [self.scr_f(i * 1024, 1024) for i in range(2)]
        self.rot = [0, 1, 2, 3]
        for i in range(NBLK):
            src = self.x[b, i * 128:(i + 1) * 128, :] if i < 16 else self.ctx[b, (i - 16) * 128:(i - 15) * 128, :]
            stg = STG[i % 2]
            stok = "stg%d" % (i % 2)
            P.dma(DMA(stg, src), writes=[stok])
            t = min(i // 4, 4)
            for half in range(2):
                bank, btok = self.next_bank()
                P.pe(TRS([(bank[:, j * 128:(j + 1) * 128], stg[:, (half * 4 + j) * 128:(half * 4 + j + 1) * 128])
                          for j in range(4)], self.ident[:]), reads=[stok, "ident"], writes=[btok])
                dst = self.R3[:, half * 4:half * 4 + 4, i * 128:(i + 1) * 128]
                srcp = bank[:, 0:512].rearrange("p (c t) -> p c t", c=4)
                wr = ["r.%d.%d" % (c, t) for c in range(half * 4, half * 4 + 4)]
                if half == 0:
                    P.act(ACTF(dst, srcp, AF.Identity), reads=[btok], writes=wr)
                else:
                    P.dve(CP(dst, srcp), reads=[btok], writes=wr)

    def rstd_tile(self, t, RS, SQ):
        P = self.P
        t0, n = TILES[t]
        bank = self.pb[7]
        for c in range(8):
            sq = SQ[c % 2]
            P.dve(TT(sq[:, :n], self.R3[:, c, t0:t0 + n], self.R3[:, c, t0:t0 + n], ALU.mult),
                  reads=["r.%d.%d" % (c, t)], writes=["sq%d" % (c % 2)])
            P.pe(MMS([(bank[:, :n], self.ones_b[:], sq[:, :n], c == 0, c == 7)]), reads=["sq%d" % (c % 2), "ones_b"],
                 writes=["pb7"])
        P.act(ACTF(RS[:, :n], bank[:, :n], AF.Sqrt, bias=EPS, scale=1.0 / D), reads=["pb7"], writes=["rs"])
        P.dve(RECIP(RS[:, :n], RS[:, :n]), reads=["rs"], writes=["rs"])

    def norm(self, b, l, w, tiles):
        P = self.P
        SQ = [self.scr_b(i * 256, 512) for i in range(2)]
        RS = self.scr_f(512, 512)
        TM = [self.scr_f(1024 + i * 512, 512) for i in range(2)]
        for t in tiles:
            t0, n = TILES[t]
            nn = b if t < 4 else 2
            self.rstd_tile(t, RS, SQ)
            for c in range(8):
                tm = TM[c % 2]
                P.dve(TT(tm[:, :n], self.R3[:, c, t0:t0 + n], RS[:, :n], ALU.mult), reads=["r.%d.%d" % (c, t), "rs"],
                      writes=["tm%d" % (c % 2)])
                P.act(ACTF(self.XN[:, c, t0:t0 + n], tm[:, :n], AF.Identity,
                           bias=self.MT[:, l, 24 * w + c, nn:nn + 1], scale=self.AM[:, l, w, nn, c:c + 1]),
                      reads=["tm%d" % (c % 2), "modt", "amod"], writes=["xn.%d" % t])

    def wout_partial(self, b, l, rows0, tiles):
        P = self.P
        self.rot = [0, 1, 2, 3]
        for c in range(8):
            def evac(t, ps, btok, c=c):
                t0, n = TILES[t]
                nn = b if t < 4 else 2
                dst = self.R3[:, c, t0:t0 + n]
                P.dve(STT(dst, ps, self.MT[:, l, 16 + c, nn:nn + 1], dst, ALU.mult, ALU.add),
                      reads=[btok, "modt", "r.%d.%d" % (c, t)], writes=["r.%d.%d" % (c, t)])
            self.linear_fm(self.w_out[l], c * 128, tiles, evac, kch=4, k0=rows0 // 128,
                           rhs=lambda kc, t0, n: self.MX[:, kc, t0:t0 + n], rtok=lambda t: ["mx.%d" % t])

    def ffn(self, b, l, tiles):
        P = self.P
        HH = self.scr_b(0, 6 * T).rearrange("p (j t) -> p j t", j=6)
        SG = [self.scr_f(7000 + i * 512, 512) for i in range(2)]
        groups = [(0, 6), (6, 6), (12, 5), (17, 5)]
        it = 0
        for (j0, J) in groups:
            for jl in range(J):
                j = j0 + jl
                wg, gtok = self.load_w(self.w_gate[l], 0, 8, j * 128)
                wu, utok = self.load_w(self.w_up[l], 0, 8, j * 128)
                for t in tiles:
                    t0, n = TILES[t]
                    bg, bgt = self.pb[it % 2], "pb%d" % (it % 2)
                    bu, but = self.pb[2 + it % 2], "pb%d" % (2 + it % 2)
                    sg = SG[it % 2]
                    sgt = "sg%d" % (it % 2)
                    it += 1
                    P.pe(MMS([(bg[:, :n], wg[:, kc, :], self.XN[:, kc, t0:t0 + n], kc == 0, kc == 7) for kc in range(8)]),
                         reads=[gtok, "xn.%d" % t], writes=[bgt])
                    P.pe(MMS([(bu[:, :n], wu[:, kc, :], self.XN[:, kc, t0:t0 + n], kc == 0, kc == 7) for kc in range(8)]),
                         reads=[utok, "xn.%d" % t], writes=[but])
                    P.act(ACTF(sg[:, :n], bg[:, :n], AF.Silu), reads=[bgt], writes=[sgt])
                    P.dve(TT(HH[:, jl, t0:t0 + n], sg[:, :n], bu[:, :n], ALU.mult), reads=[sgt, but],
                          writes=["hh.%d.%d" % (jl, t)])
            for c in range(8):
                wd, dtok = self.load_w(self.w_down[l], j0, J, c * 128)
                for t in tiles:
                    t0, n = TILES[t]
                    nn = b if t < 4 else 2
                    bi = 4 + (it % 3)
                    it += 1
                    bank, btok = self.pb[bi], "pb%d" % bi
                    P.pe(MMS([(bank[:, :n], wd[:, jl, :], HH[:, jl, t0:t0 + n], jl == 0, jl == J - 1) for jl in range(J)]),
                         reads=[dtok] + ["hh.%d.%d" % (jl, t) for jl in range(J)], writes=[btok])
                    dst = self.R3[:, c, t0:t0 + n]
                    P.dve(STT(dst, bank[:, :n], self.MT[:, l, 40 + c, nn:nn + 1], dst, ALU.mult, ALU.add),
                          reads=[btok, "modt", "r.%d.%d" % (c, t)], writes=["r.%d.%d" % (c, t)])

    def final(self, b):
        P = self.P
        SQ = [self.scr_b(i * 256, 512) for i in range(2)]
        RS = self.scr_f(512, 512)
        TM = [self.scr_f(1024 + i * 512, 512) for i in range(2)]
        YN = self.scr_f(2048, 4096).rearrange("p (c t) -> p c t", c=8)
        OST = [self.scr_f(6144 + i * 1024, 1024) for i in range(2)]
        self.rot = [0, 1, 2, 3]
        oi = 0
        for t in range(4):
            t0, n = TILES[t]
            self.rstd_tile(t, RS, SQ)
            for c in range(8):
                tm = TM[c % 2]
                P.dve(TT(tm[:, :n], self.R3[:, c, t0:t0 + n], RS[:, :n], ALU.mult), reads=["r.%d.%d" % (c, t), "rs"],
                      writes=["tm%d" % (c % 2)])
                P.act(ACTF(YN[:, c, :], tm[:, :n], AF.Identity, scale=self.prmcol("fng", c)),
                      reads=["tm%d" % (c % 2), "prm"], writes=["yn.%d" % c])
            for j in range(4):
                ost = OST[oi % 2]
                otok = "ost%d" % (oi % 2)
                oi += 1
                for half in range(2):
                    bank, btok = self.next_bank()
                    P.pe(TRS([(bank[:, cc * 128:(cc + 1) * 128], YN[:, half * 4 + cc, j * 128:(j + 1) * 128])
                              for cc in range(4)], self.ident[:]),
                         reads=["yn.%d" % (half * 4 + cc) for cc in range(4)] + ["ident"], writes=[btok])
                    if half == 0:
                        P.act(ACTF(ost[:, 0:512], bank[:, 0:512], AF.Identity), reads=[btok], writes=[otok])
                    else:
                        P.dve(CP(ost[:, 512:1024], bank[:, 0:512]), reads=[btok], writes=[otok])
                r0 = t0 + j * 128
                P.dma(DMA(self.out[b, r0:r0 + 128, :], ost), reads=[otok], writes=["out"])

    def da_mixer(self, b, l):
        P = self.P
        need_ctx = (l == 0)
        o = 0
        ROPE = self.scr_f(o, 4096).rearrange("p (a t) -> p a t", a=2); o += 4096
        QR = self.scr_b(o, T); o += T // 2
        KR = self.scr_b(o, T); o += T // 2
        VT = self.scr_b(o, T).rearrange("p (i d) -> p i d", i=NBLK); o += T // 2
        T1 = [self.scr_f(o + i * 512, 512) for i in range(2)]; o += 1024
        T2 = [self.scr_f(o + i * 512, 512) for i in range(2)]; o += 1024
        PT = [self.scr_b(o + i * 256, 512) for i in range(4)]; o += 1024
        RC = self.scr_f(o, 512); o += 512
        TC = [self.scr_f(o + i * 512, 512) for i in range(2)]; o += 1024
        O2 = self.scr_f(o, 512); o += 512
        OV = [self.scr_f(o + i * 512, 512) for i in range(2)]; o += 1024
        assert o <= SCRW, o
        P.dma(DMA(ROPE, self.rope_d), writes=["rope"])
        lat = [0, 1, 2, 3]
        W = self.w_in[l]
        ti = [0]
        pend = {"A": None, "B": None}
        fcount = [0]

        def emitA():
            f = pend["A"]
            pend["A"] = None
            if f is not None:
                f()

        def emitB():
            f = pend["B"]
            pend["B"] = None
            if f is not None:
                f()

        def make_final(h, qt, q0, qn):
            fi = fcount[0] % 2
            fcount[0] += 1
            ov = OV[fi][:, :qn]
            ovt = "ov%d" % fi

            def fa():
                for c in range(2):
                    P.dve(RECIP(RC[:, :qn], T1[c][:, :qn]), reads=["t1.%d" % c], writes=["rc"])
                    P.dve(TT(TC[c][:, :qn], TC[c][:, :qn], RC[:, :qn], ALU.mult), reads=["tc%d" % c, "rc"],
                          writes=["tc%d" % c])
                P.dve(STT(ov, TC[1][:, :qn], self.nlam[:, l:l + 1], TC[0][:, :qn], ALU.mult, ALU.add),
                      reads=["tc0", "tc1", "nlam"], writes=[ovt])
                P.dve(TT(O2[:, :qn], ov, ov, ALU.mult), reads=[ovt], writes=["o2"])

            def fb():
                P.pe(MMS([(self.pb[3][:, :qn], self.ones_f[:], O2[:, :qn], True, True)]), reads=["o2", "ones_f"],
                     writes=["pb3"])
                P.act(ACTF(O2[:, :qn], self.pb[3][:, :qn], AF.Ln, bias=EPS, scale=1.0 / 128.0), reads=["pb3"],
                      writes=["o2"])
                P.act(ACTF(O2[:, :qn], O2[:, :qn], AF.Exp, scale=-0.5), reads=["o2"], writes=["o2"])
                P.dve(TT(ov, ov, O2[:, :qn], ALU.mult), reads=[ovt, "o2"], writes=[ovt])
                P.act(ACTF(self.MX[:, h, q0:q0 + qn], ov, AF.Identity, scale=self.gsc[:, l:l + 1]),
                      reads=[ovt, "gsc"], writes=["mx.%d" % qt])
            return fa, fb

        for h in range(4):
            self.rot = [0, 1, 2, 3]

            def ev_rope(dst, dtok):
                def ev(t, pa, ta, pb_, tb):
                    t0, n = TILES[t]
                    i = ti[0] % 2
                    ti[0] += 1
                    P.dve(TT(T1[i][:, :n], pa, ROPE[:, 0, t0:t0 + n], ALU.mult), reads=[ta, "rope"], writes=["t1.%d" % i])
                    P.dve(TT(T2[i][:, :n], pb_, ROPE[:, 1, t0:t0 + n], ALU.mult), reads=[tb, "rope"], writes=["t2.%d" % i])
                    P.pool(TT(dst[:, t0:t0 + n], T1[i][:, :n], T2[i][:, :n], ALU.add), reads=["t1.%d" % i, "t2.%d" % i],
                           writes=["%s.%d" % (dtok, t)])
                return ev

            def ev_plain(dst, dtok):
                def ev(t, ps, btok):
                    t0, n = TILES[t]
                    P.act(ACTF(dst[:, t0:t0 + n], ps, AF.Identity), reads=[btok], writes=["%s.%d" % (dtok, t)])
                return ev
            emitA()
            self.linear_fm2(W, OFF_Q + h * 128, OFF_QS + h * 128, lat, ev_rope(QR, "qr"), [4] if need_ctx else [],
                            ev_plain(QR, "qr"))
            emitB()
            self.linear_fm2(W, OFF_K + h * 128, OFF_KS + h * 128, lat, ev_rope(KR, "kr"), [4], ev_plain(KR, "kr"))

            def ev_v(blks, ps, btok):
                i0 = blks[0]
                P.act(ACTF(VT[:, i0:i0 + len(blks), :], ps.rearrange("p (i d) -> p i d", i=len(blks)), AF.Identity),
                      reads=[btok], writes=["vt.%d" % i for i in blks])
            self.linear_tm(W, OFF_V + h * 128, 128, list(range(NBLK)), ev_v, 4)

            SB = [(0, 1), (2, 3)]
            for qt in (lat + ([4] if need_ctx else [])):
                q0, qn = TILES[qt]
                kts = list(range(NBLK)) if qt < 4 else [16, 17]
                nk = len(kts)

                def emit_s(ii):
                    kt = kts[ii]
                    pa, pb_ = SB[ii % 2]
                    P.pe(MMS([(self.pb[pa][:, :qn], KR[0:64, kt * 128:(kt + 1) * 128], QR[0:64, q0:q0 + qn], True, True),
                              (self.pb[pb_][:, :qn], KR[64:128, kt * 128:(kt + 1) * 128], QR[64:128, q0:q0 + qn], True,
                               True)]),
                         reads=["kr.%d" % min(kt // 4, 4), "qr.%d" % qt], writes=["pb%d" % pa, "pb%d" % pb_])
                emit_s(0)
                for ii, kt in enumerate(kts):
                    if ii == 0:
                        emitA()
                    if ii == min(4, nk - 1):
                        emitB()
                    if ii + 1 < nk:
                        emit_s(ii + 1)
                    first, last = (ii == 0), (ii == nk - 1)
                    for c in range(2):
                        sb = SB[ii % 2][c]
                        pi_ = (2 * ii + c) % 4
                        P.act(ACTF(PT[pi_][:, :qn], self.pb[sb][:, :qn], AF.Exp, scale=0.125), reads=["pb%d" % sb],
                              writes=["pt%d" % pi_])
                        P.pe(MMS([(self.pb[4 + c][:, :qn], VT[:, kt, :], PT[pi_][:, :qn], first, last),
                                  (self.pb[6 + c][:, :qn], self.ones_b[:], PT[pi_][:, :qn], first, last)]),
                             reads=["pt%d" % pi_, "vt.%d" % kt, "ones_b"], writes=["pb%d" % (4 + c), "pb%d" % (6 + c)])
                emitA()
                emitB()
                for c in range(2):
                    P.dve(CP(TC[c][:, :qn], self.pb[4 + c][:, :qn]), reads=["pb%d" % (4 + c)], writes=["tc%d" % c])
                    P.dve(CP(T1[c][:, :qn], self.pb[6 + c][:, :qn]), reads=["pb%d" % (6 + c)], writes=["t1.%d" % c])
                pend["A"], pend["B"] = make_final(h, qt, q0, qn)
        emitA()
        emitB()

    def rg_mixer(self, b, l):
        P = self.P
        need_ctx = (l == 0)
        o = 0
        RX = self.scr_f(o, 2312); o += 2312
        XC = self.scr_f(o, T); o += T
        XCB = self.scr_b(o, T); o += T // 2
        HS = self.scr_f(o, T); o += T
        B1 = self.scr_f(o, T); o += T
        B2 = self.scr_f(o, T); o += T
        TL = [self.scr_f(o + i * 512, 512) for i in range(4)]; o += 2048
        assert o <= SCRW, o
        B0 = RX[:, 0:T]
        b0t = ["rx.%d" % t for t in range(5)] + ["rxpad"]
        W = self.w_in[l]
        self.rot = [0, 1, 2, 3]
        it = 0
        nout = T if need_ctx else NLAT
        for cc in range(2):
            for (a0, a1) in ((0, 2), (2050, 2053), (2309, 2312)):
                P.pool(MSET(RX[:, a0:a1], 0.0), writes=["rxpad"])

            def ev_x(t, ps, btok):
                t0, n = TILES[t]
                off = t0 + 2 if t < 4 else t0 + 5
                P.act(ACTF(RX[:, off:off + n], ps, AF.Identity), reads=[btok], writes=["rx.%d" % t])
            self.linear_fm(W, OFF_RGX + cc * 128, [0, 1, 2, 3, 4], ev_x)
            for (x0, r0, n, rt, wt) in ((0, 0, NLAT, ["rx.0", "rx.1", "rx.2", "rx.3"], ["xc.0", "xc.1", "xc.2", "xc.3"]),
                                        (NLAT, 2051, NCTX, ["rx.4"], ["xc.4"])):
                wo = PRM_OFF["rgcw"] + (l * 2 + cc) * 4
                P.act(ACTF(XC[:, x0:x0 + n], RX[:, r0:r0 + n], AF.Identity, bias=self.prmcol("rgcb", l * 2 + cc),
                           scale=self.prm[:, wo:wo + 1]), reads=rt + ["rxpad", "prm"], writes=wt)
                for k in range(1, 4):
                    P.dve(STT(XC[:, x0:x0 + n], RX[:, r0 + k:r0 + k + n], self.prm[:, wo + k:wo + k + 1], XC[:, x0:x0 + n],
                              ALU.mult, ALU.add), reads=rt + wt + ["rxpad", "prm"], writes=wt)
                P.pool(CP(XCB[:, x0:x0 + n], XC[:, x0:x0 + n]), reads=wt, writes=["xcb.%d" % t for t in
                                                                                  ([0, 1, 2, 3] if x0 == 0 else [4])])
            xct = ["xc.%d" % t for t in range(5)]
            for d in range(2):
                pi = (l * 2 + d) * 2 + cc
                for t in range(5):
                    t0, n = TILES[t]
                    TA, TX = TL[(it % 2) * 2], TL[(it % 2) * 2 + 1]
                    sfx = "%d" % (it % 2)
                    it += 1
                    ba, bat = self.next_bank()
                    bx, bxt = self.next_bank()
                    P.pe(MMS([(ba[:, :n], self.BD[:, pi * 2, :], XCB[:, t0:t0 + n], True, True),
                              (bx[:, :n], self.BD[:, pi * 2 + 1, :], XCB[:, t0:t0 + n], True, True)]),
                         reads=["bd", "xcb.%d" % t], writes=[bat, bxt])
                    P.act(ACTF(TA[:, :n], ba[:, :n], AF.Tanh, bias=self.hba[:, pi:pi + 1], scale=0.5),
                          reads=[bat, "hba"], writes=["ta" + sfx])
                    P.act(ACTF(B0[:, t0:t0 + n], TA[:, :n], AF.Exp, bias=self.hm[:, pi:pi + 1],
                               scale=self.hm[:, pi:pi + 1]), reads=["ta" + sfx, "hm"], writes=b0t)
                    P.act(ACTF(B1[:, t0:t0 + n], B0[:, t0:t0 + n], AF.Square), reads=b0t, writes=["b1"])
                    P.act(ACTF(TX[:, :n], bx[:, :n], AF.Tanh, bias=self.hbx[:, pi:pi + 1], scale=0.5),
                          reads=[bxt, "hbx"], writes=["tx" + sfx])
                    P.dve(STT(B2[:, t0:t0 + n], TX[:, :n], 1.0, XC[:, t0:t0 + n], ALU.add, ALU.mult),
                          reads=["tx" + sfx, "xc.%d" % t], writes=["b2"])
                    self.mods_step(4)
                P.act(ACTF(B1, B1, AF.Sqrt, bias=1.0, scale=-(1.0 - 2.0 ** -23)), reads=["b1"], writes=["b1"])
                P.dve(STT(B1, B2, 0.5, B1, ALU.mult, ALU.mult), reads=["b1", "b2"], writes=["b1"])
                if d == 0:
                    P.dve(SCAN(B2[:, NLAT:T], B0[:, NLAT:T], B1[:, NLAT:T], 0.0), reads=b0t + ["b1"], writes=["b2"])
                    P.dve(SCAN(B2[:, 0:NLAT], B0[:, 0:NLAT], B1[:, 0:NLAT], B2[:, T - 1:T]), reads=b0t + ["b1", "b2"],
                          writes=["b2"])
                    P.pool(CP(HS[:, 0:nout], B2[:, 0:nout]), reads=["b2"], writes=["hs"])
                else:
                    P.dve(SCAN(B2[:, NLAT:T][:, ::-1], B0[:, NLAT:T][:, ::-1], B1[:, NLAT:T][:, ::-1], 0.0),
                          reads=b0t + ["b1"], writes=["b2"])
                    P.dve(SCAN(B2[:, 0:NLAT][:, ::-1], B0[:, 0:NLAT][:, ::-1], B1[:, 0:NLAT][:, ::-1], B2[:, NLAT:NLAT + 1]),
                          reads=b0t + ["b1", "b2"], writes=["b2"])
                    P.pool(TT(HS[:, 0:nout], HS[:, 0:nout], B2[:, 0:nout], ALU.add), reads=["b2", "hs"], writes=["hs"])
            def ev_g(t, ps, btok):
                t0, n = TILES[t]
                P.act(ACTF(B0[:, t0:t0 + n], ps, AF.Identity), reads=[btok], writes=b0t)
            self.linear_fm(W, OFF_RGG + cc * 128, [0, 1, 2, 3] + ([4] if need_ctx else []), ev_g)
            G, Q = B0[:, 0:nout], B1[:, 0:nout]
            P.dve(TT(Q, G, G, ALU.mult), reads=b0t, writes=["b1"])
            P.dve(TS(Q, Q, 0.044715, ALU.mult, 1.0, ALU.add), reads=["b1"], writes=["b1"])
            P.dve(TT(Q, Q, G, ALU.mult), reads=["b1"] + b0t, writes=["b1"])
            P.act(ACTF(Q, Q, AF.Tanh, scale=0.7978845608028654), reads=["b1"], writes=["b1"])
            P.dve(STT(Q, Q, 1.0, G, ALU.add, ALU.mult), reads=["b1"] + b0t, writes=["b1"])
            P.dve(STT(self.MX[:, cc, 0:nout], Q, 0.5, HS[:, 0:nout], ALU.mult, ALU.mult), reads=["b1", "hs"],
                  writes=["mx.%d" % t for t in range(5 if need_ctx else 4)])
        self.mods_step(1000)

    def ssd_mixer(self, b, l):
        P = self.P
        need_ctx = (l == 0)
        W = self.w_in[l]
        o = 0
        RAW = self.scr_f(o, 2312); o += 2312
        TMP = self.scr_f(o, T); o += T
        XS = self.scr_b(o, NBLK * 256).rearrange("p (i d) -> p i d", i=NBLK); o += NBLK * 128
        BF = self.scr_b(o, T); o += T // 2
        CF = self.scr_b(o, T); o += T // 2
        BT = self.scr_b(o, NBLK * 128).rearrange("p (i d) -> p i d", i=NBLK); o += NBLK * 64
        SZ = self.scr_b(o, NBLK * 256).rearrange("p (i d) -> p i d", i=NBLK); o += NBLK * 128
        DTR = self.scr_f(o, 144); o += 144
        DTv = self.scr_f(o, 144); o += 144
        ADT = self.scr_f(o, 144); o += 144
        SPS = self.scr_f(o, 576); o += 576
        assert o <= SCRW, o
        DT3 = DTv.rearrange("p (i e) -> p i e", i=NBLK)
        ADT3 = ADT.rearrange("p (i e) -> p i e", i=NBLK)
        self.rot = [0, 1, 2, 3]
        for ch in range(4):
            for (a0, a1) in ((0, 2), (2050, 2053), (2309, 2312)):
                P.pool(MSET(RAW[:, a0:a1], 0.0), writes=["rxpad"])

            def ev_x(t, ps, btok):
                t0, n = TILES[t]
                off = t0 + 2 if t < 4 else t0 + 5
                P.act(ACTF(RAW[:, off:off + n], ps, AF.Identity), reads=[btok], writes=["rx.%d" % t])
            self.linear_fm(W, OFF_XBC + ch * 128, [0, 1, 2, 3, 4], ev_x)
            for (x0, r0, n, rt, wt) in ((0, 0, NLAT, ["rx.0", "rx.1", "rx.2", "rx.3"], ["xc.0", "xc.1", "xc.2", "xc.3"]),
                                        (NLAT, 2051, NCTX, ["rx.4"], ["xc.4"])):
                wo = PRM_OFF["scw"] + (l * 4 + ch) * 4
                P.act(ACTF(TMP[:, x0:x0 + n], RAW[:, r0:r0 + n], AF.Identity, bias=self.prmcol("scb", l * 4 + ch),
                           scale=self.prm[:, wo:wo + 1]), reads=rt + ["rxpad", "prm"], writes=wt)
                for k in range(1, 4):
                    P.dve(STT(TMP[:, x0:x0 + n], RAW[:, r0 + k:r0 + k + n], self.prm[:, wo + k:wo + k + 1],
                              TMP[:, x0:x0 + n], ALU.mult, ALU.add), reads=rt + wt + ["rxpad", "prm"], writes=wt)
                if ch == 3:
                    P.act(ACTF(CF[:, x0:x0 + n], TMP[:, x0:x0 + n], AF.Silu), reads=wt, writes=["cf"])
                else:
                    P.act(ACTF(TMP[:, x0:x0 + n], TMP[:, x0:x0 + n], AF.Silu), reads=wt, writes=wt)
                    if ch == 2:
                        P.pool(CP(BF[:, x0:x0 + n], TMP[:, x0:x0 + n]), reads=wt, writes=["bf"])
            if ch < 3:
                for i0 in range(0, NBLK, 4):
                    blks = list(range(i0, min(i0 + 4, NBLK)))
                    bank, btok = self.next_bank()
                    P.pe(TRS([(bank[:, gi * 128:(gi + 1) * 128], TMP[:, i * 128:(i + 1) * 128]) for gi, i in
                              enumerate(blks)], self.ident[:]),
                         reads=["xc.%d" % min(i // 4, 4) for i in blks] + ["ident"], writes=[btok])
                    src = bank[:, 0:len(blks) * 128].rearrange("p (i d) -> p i d", i=len(blks))
                    if ch < 2:
                        P.dve(CP(XS[:, i0:i0 + len(blks), ch * 128:(ch + 1) * 128], src), reads=[btok], writes=["xs"])
                    else:
                        P.dve(CP(BT[:, i0:i0 + len(blks), :], src), reads=[btok], writes=["bt"])
        for zc in range(2):
            def ev_z(blks, ps, btok, zc=zc):
                i0 = blks[0]
                P.act(ACTF(SZ[:, i0:i0 + len(blks), zc * 128:(zc + 1) * 128],
                           ps.rearrange("p (i d) -> p i d", i=len(blks)), AF.Silu), reads=[btok], writes=["sz"])
            self.linear_tm(W, OFF_SZ + zc * 128, 128, list(range(NBLK)), ev_z, 4)

        def ev_dt(blks, ps, btok):
            do = ROW_OFF["dtb"] + l * 8
            P.dve(TT(DTR.rearrange("p (i e) -> p i e", i=NBLK), ps.rearrange("p (i e) -> p i e", i=NBLK),
                     self.rowp[:, do:do + 8].unsqueeze(1).to_broadcast([128, NBLK, 8]), ALU.add),
                  reads=[btok, "rowp"], writes=["dtr"])
        self.linear_tm(W, OFF_DT, 8, list(range(NBLK)), ev_dt, NBLK)
        self.softplus(DTv, DTR, 144, SPS, (["dtr"], ["dt"]))
        P.dve(TT(ADT3, DT3, self.aneg[:, l * 8:l * 8 + 8].unsqueeze(1).to_broadcast([128, NBLK, 8]), ALU.mult),
              reads=["dt", "aneg"], writes=["adt"])
        P.barrier()
        if self.o.get("ssd_prep_only"):
            P.pool(MSET(self.MX[:, 2:4, :], 0.0), writes=["mx.%d" % t for t in range(5)])
            return
        xf = self.xn[:].bitcast(F32)
        o = 0
        Y = xf[:, o:o + NBLK * 256].rearrange("p (i d) -> p i d", i=NBLK); o += NBLK * 256
        LT = xf[:, o:o + 512]; o += 512
        D1 = xf[:, o:o + 512]; o += 512
        GM = xf[:, o:o + 256]; o += 256
        T1 = xf[:, o:o + 256]; o += 256
        T2 = xf[:, o:o + 256]; o += 256
        VV = xf[:, o:o + 256]; o += 256
        OT = xf[:, o:o + 256]; o += 256
        JK = xf[:, o:o + 256]; o += 256
        H = xf[:, o:o + 128]; o += 128
        SM = xf[:, o:o + 64]; o += 64
        Mb = self.xn[:, 2 * o:2 * o + 512].rearrange("p (h t) -> p h t", h=4); o += 256
        XD = self.xn[:, 2 * o:2 * o + 256].rearrange("p (h d) -> p h d", h=4); o += 128
        XDW = self.xn[:, 2 * o:2 * o + 256].rearrange("p (h d) -> p h d", h=4); o += 128
        HB = self.xn[:, 2 * o:2 * o + 128]; o += 64
        assert o <= 9216
        cst, ecs, wdec, dtw, dch, ss = (SM[:, i * 4:(i + 1) * 4] for i in range(6))
        rsd = SM[:, 24:26]
        ssq = SM[:, 26:28]
        PB = self.pb
        dsk = self.rowp[:, ROW_OFF["dsk"] + l * 4:ROW_OFF["dsk"] + l * 4 + 4]
        sng = self.rowp[:, ROW_OFF["sng"] + l * 256:ROW_OFF["sng"] + (l + 1) * 256]
        for d in range(2):
            U = self.uf if d == 0 else self.ub
            utok = "uf" if d == 0 else "ub"
            order = [16, 17] + list(range(16)) if d == 0 else [17, 16] + list(range(15, -1, -1))
            edge = 127 if d == 0 else 0
            P.dve(MSET(H, 0.0), writes=["H"])
            P.dve(MSET(HB, 0.0), writes=["HB"])
            if d >= self.o.get("ssd_dirs", 2):
                break
            for k in order[:self.o.get("ssd_nchunks", 18)]:
                want_y = ((k < 16) or need_ctx) and not self.o.get("ssd_noy")
                blk = slice(k * 128, (k + 1) * 128)
                adt = ADT3[:, k, d * 4:d * 4 + 4]
                mm = [(PB[3][:, 0:4], U[:], adt, True, True)]
                for h in range(4):
                    mm.append((PB[2][:, h * 128:(h + 1) * 128], ADT3[:, k, d * 4 + h:d * 4 + h + 1].to_broadcast([128, 128]),
                               U[:], True, True))
                P.pe(MMS(mm), reads=["adt", utok], writes=["pb3", "pb2"])
                P.dve(CP(cst, PB[3][:, 0:4]), reads=["pb3"], writes=["cst"])
                csb = PB[2][:, 0:512].rearrange("p (h t) -> p h t", h=4)
                if want_y:
                    P.dve(TT(D1.rearrange("p (h t) -> p h t", h=4), csb, cst.unsqueeze(2).to_broadcast([128, 4, 128]),
                             ALU.subtract), reads=["pb2", "cst"], writes=["d1"])
                    P.dve(STT(D1, D1, -1.0, D1, ALU.mult, ALU.min), reads=["d1"], writes=["d1"])
                    P.act(ACTF(LT, D1, AF.Exp), reads=["d1"], writes=["lt"])
                ys = self.o.get("ssd_ystage", 9)
                if want_y and ys >= 2:
                    P.pe(MMS([(PB[1 - g][:, 0:128], BF[64 * g:64 * g + 64, blk], CF[64 * g:64 * g + 64, blk],
                               True, True) for g in range(2)]), reads=["bf", "cf"], writes=["pb1", "pb0"])
                    for g in range(2):
                        P.dve(TT(GM[:, g * 128:(g + 1) * 128], PB[1 - g][:, 0:128], U[:], ALU.mult),
                              reads=["pb%d" % (1 - g), utok], writes=["gm"])
                    P.dve(TT(Mb.rearrange("p (g a) t -> p g a t", g=2), LT.rearrange("p (g a t) -> p g a t", g=2, a=2),
                             GM.rearrange("p (g t) -> p g t", g=2).unsqueeze(2).to_broadcast([128, 2, 2, 128]), ALU.mult),
                          reads=["lt", "gm"], writes=["mb"])
                dts = DT3[:, k, d * 4:d * 4 + 4]
                XS4 = XS[:, k, :].rearrange("p (h d) -> p h d", h=4)
                if want_y and ys >= 3:
                    P.dve(TT(XD, XS4, dts.unsqueeze(2).to_broadcast([128, 4, 64]), ALU.mult), reads=["xs", "dt"],
                          writes=["xd"])
                if want_y and ys >= 4:
                    P.pe(MMS([(PB[4][:, h * 64:(h + 1) * 64], Mb[:, h, :], XD[:, h, :], True, True) for h in range(4)]
                             + [(PB[5 + 2 * g][:, 0:128], CF[64 * g:64 * g + 64, blk], HB[64 * g:64 * g + 64, :],
                                 True, True) for g in range(2)]),
                         reads=["mb", "xd", "cf", "HB"], writes=["pb4", "pb5", "pb7"])
                if want_y and ys >= 5:
                    P.act(ACTF(ecs, cst, AF.Exp), reads=["cst"], writes=["ecs"])
                    for g in range(2):
                        P.dve(TT(T1[:, g * 128:(g + 1) * 128].rearrange("p (h d) -> p h d", h=2),
                                 PB[5 + 2 * g][:, 0:128].rearrange("p (h d) -> p h d", h=2),
                                 ecs[:, 2 * g:2 * g + 2].unsqueeze(2).to_broadcast([128, 2, 64]), ALU.mult),
                              reads=["pb%d" % (5 + 2 * g), "ecs"], writes=["t1"])
                    P.dve(TT(T1, T1, PB[4][:, 0:256], ALU.add), reads=["t1", "pb4"], writes=["t1"])
                    if d == 0:
                        P.dve(TT(T2.rearrange("p (h d) -> p h d", h=4), XS4,
                                 dsk.unsqueeze(2).to_broadcast([128, 4, 64]), ALU.mult), reads=["xs", "rowp"],
                              writes=["t2"])
                        P.dve(TT(Y[:, k, :], T1, T2, ALU.add), reads=["t1", "t2"], writes=["y.%d" % k])
                    else:
                        P.dve(TT(Y[:, k, :], Y[:, k, :], T1, ALU.add), reads=["t1", "y.%d" % k], writes=["y.%d" % k])
                P.dve(TT(wdec, csb[:, :, edge], cst, ALU.subtract), reads=["pb2", "cst"], writes=["wdec"])
                P.act(ACTF(wdec, wdec, AF.Exp), reads=["wdec"], writes=["wdec"])
                P.dve(TT(dtw, wdec, dts, ALU.mult), reads=["wdec", "dt"], writes=["dtw"])
                P.dve(TT(XDW, XS4, dtw.unsqueeze(2).to_broadcast([128, 4, 64]), ALU.mult), reads=["xs", "dtw"],
                      writes=["xdw"])
                P.pe(MMS([(PB[6][64 * g:64 * g + 64, 0:128], BT[:, k, 64 * g:64 * g + 64],
                           XDW[:, 2 * g:2 * g + 2, :].rearrange("p a d -> p (a d)"), True, True) for g in range(2)]),
                     reads=["bt", "xdw"], writes=["pb6"])
                for g in range(2):
                    P.act(ACTF(dch[64 * g:64 * g + 64, 0:2], csb[64 * g:64 * g + 64, 2 * g:2 * g + 2, edge], AF.Exp),
                          reads=["pb2"], writes=["dch"])
                P.dve(TT(H.rearrange("p (a d) -> p a d", a=2), H.rearrange("p (a d) -> p a d", a=2),
                         dch[:, 0:2].unsqueeze(2).to_broadcast([128, 2, 64]), ALU.mult), reads=["H", "dch"], writes=["H"])
                P.dve(TT(H, H, PB[6][:, 0:128], ALU.add), reads=["H", "pb6"], writes=["H"])
                P.dve(CP(HB, H), reads=["H"], writes=["HB"])
                if d == 1 and want_y and ys >= 6:
                    P.dve(TT(VV, Y[:, k, :], SZ[:, k, :], ALU.mult), reads=["y.%d" % k, "sz"], writes=["vv"])
                    for g in range(2):
                        P.act(ACTF(JK[:, g * 128:(g + 1) * 128], VV[:, g * 128:(g + 1) * 128], AF.Square,
                                   accum_out=ssq[:, g:g + 1]), reads=["vv"], writes=["jk", "ssq"])
                    P.act(ACTF(rsd, ssq, AF.Ln, bias=EPS, scale=1.0 / 128.0), reads=["ssq"], writes=["rsd"])
                    P.act(ACTF(rsd, rsd, AF.Exp, scale=-0.5), reads=["rsd"], writes=["rsd"])
                    for g in range(2):
                        P.dve(STT(OT[:, g * 128:(g + 1) * 128], VV[:, g * 128:(g + 1) * 128], rsd[:, g:g + 1],
                                  sng[:, g * 128:(g + 1) * 128], ALU.mult, ALU.mult), reads=["vv", "rsd", "rowp"],
                              writes=["ot"])
                    P.pe(TRS([(PB[7][:, g * 128:(g + 1) * 128], OT[:, g * 128:(g + 1) * 128]) for g in range(2)],
                             self.ident[:]), reads=["ot", "ident"], writes=["pb7"])
                    P.act(ACTF(self.MX[:, 2:4, k * 128:(k + 1) * 128],
                               PB[7][:, 0:256].rearrange("p (g t) -> p g t", g=2), AF.Identity), reads=["pb7"],
                          writes=["mx.%d" % min(k // 4, 4)])

    def layer(self, b, l):
        P = self.P
        o = self.o
        need_ctx = (l == 0)
        allt = [0, 1, 2, 3, 4]
        outt = allt if need_ctx else [0, 1, 2, 3]
        if o.get("mix", True):
            self.norm(b, l, 0, allt)
            P.barrier()
            if o.get("da", True):
                self.da_mixer(b, l)
                self.wout_partial(b, l, 512, outt)
                P.barrier()
            if o.get("rg", True) or o.get("ssd", True):
                if o.get("rg", True):
                    self.rg_mixer(b, l)
                else:
                    P.pool(MSET(self.MX[:, 0:2, :], 0.0), writes=["mx.%d" % t for t in range(5)])
                P.barrier()
                if o.get("ssd", True):
                    self.ssd_mixer(b, l)
                else:
                    P.pool(MSET(self.MX[:, 2:4, :], 0.0), writes=["mx.%d" % t for t in range(5)])
                self.wout_partial(b, l, 0, outt)
                P.barrier()
        self.mods_step(1000)
        if o.get("ffn", True):
            self.norm(b, l, 1, outt)
            P.barrier()
            self.ffn(b, l, outt)
            P.barrier()

    def run(self):
        self.setup()
        for b in range(self.NB):
            self.load_x(b)
            self.P.barrier()
            for l in range(self.o.get("layers", 2)):
                self.layer(b, l)
            self.final(b)
            self.P.barrier()


def build_nc(NB=2, opts=None):
    nc = bass.Bass("TRN2", target_bir_lowering=False)
    with contextlib.ExitStack() as st:
        P = Prog(nc, st)
        K = Kern(nc, P, NB, opts or {})
        K.run()
        P.emit(final_reads=["out"])
    return nc


_NC_CACHE = {}


def kernel(x, c, ctx, c_ctx, w_mod, b_mod, norm1_g, w_in, rg_conv_w, rg_conv_b, rg_w_a, rg_b_a,
           rg_w_x, rg_b_x, rg_lambda, ssd_conv_w, ssd_conv_b, ssd_dt_bias, ssd_a_log, ssd_d,
           ssd_norm_g, da_lambda, da_subln_g, w_out, norm2_g, w_gate, w_up, w_down, final_norm_g,
           _opts=None, _ncores=8):
    inp = dict(w_mod=w_mod, b_mod=b_mod, norm1_g=norm1_g, w_in=w_in, rg_conv_w=rg_conv_w, rg_conv_b=rg_conv_b,
               rg_w_a=rg_w_a, rg_b_a=rg_b_a, rg_w_x=rg_w_x, rg_b_x=rg_b_x, rg_lambda=rg_lambda,
               ssd_conv_w=ssd_conv_w, ssd_conv_b=ssd_conv_b, ssd_dt_bias=ssd_dt_bias, ssd_a_log=ssd_a_log,
               ssd_d=ssd_d, ssd_norm_g=ssd_norm_g, da_lambda=da_lambda, da_subln_g=da_subln_g, w_out=w_out,
               norm2_g=norm2_g, w_gate=w_gate, w_up=w_up, w_down=w_down, final_norm_g=final_norm_g)
    shared = pack_shared(inp)
    x = np.asarray(x, np.float32)
    ctx = np.asarray(ctx, np.float32)
    c = np.asarray(c, np.float32)
    c_ctx = np.asarray(c_ctx, np.float32)
    NB = 2
    in_maps = []
    for i in range(_ncores):
        cc = np.stack([c[2 * i], c[2 * i + 1], c_ctx], axis=0)
        ct = np.ascontiguousarray(cc.reshape(3, 8, 128).transpose(2, 1, 0))
        m = dict(shared)
        m["x"] = np.ascontiguousarray(x[2 * i:2 * i + 2])
        m["ctx"] = np.ascontiguousarray(ctx[2 * i:2 * i + 2])
        m["ct"] = ct
        in_maps.append(m)
    key = repr(sorted((_opts or {}).items()))
    if key not in _NC_CACHE:
        _NC_CACHE[key] = build_nc(NB, _opts)
    nc = _NC_CACHE[key]
    res = run_bass_kernel_spmd(nc, in_maps, core_ids=list(range(_ncores)))
    out = np.concatenate([np.asarray(r["out"], np.float32) for r in res.results], axis=0)
    return out
```

```python
import contextlib
import math
import numpy as np
import concourse.bass as bass
import concourse.mybir as mybir
from concourse.bass_utils import run_bass_kernel_spmd

F32 = mybir.dt.float32
BF16 = mybir.dt.bfloat16
ALU = mybir.AluOpType
AF = mybir.ActivationFunctionType

D = 1024
NLAT = 2048
NCTX = 256
T = NLAT + NCTX
NBLK = T // 128
TILES = [(0, 512), (512, 512), (1024, 512), (1536, 512), (2048, 256)]
DFF = 2816
DIN = 2824
DINX = DIN + 1024
EPS = 1e-6
OFF_RGX, OFF_RGG, OFF_SZ, OFF_XBC, OFF_DT, OFF_Q, OFF_K, OFF_V = 0, 256, 512, 768, 1280, 1288, 1800, 2312
OFF_QS, OFF_KS = DIN, DIN + 512
SCRW = 14848


class _Op:
    __slots__ = ("eng", "fn", "waits", "tok", "clock", "dma", "idx")


class Prog:
    ENGS = ("pe", "act", "dve", "pool", "sp")

    def __init__(self, nc, stack, n_dma_sems=12):
        self.nc = nc
        self.stack = stack
        self.ops = {e: [] for e in self.ENGS}
        self.sem = {e: stack.enter_context(nc.semaphore("S_" + e)) for e in self.ENGS}
        self.cnt = {e: 0 for e in self.ENGS}
        self.know = {e: {} for e in self.ENGS}
        self.last_op = {e: None for e in self.ENGS}
        self.dma_sems = {}
        for q in ("sp", "pool"):
            self.dma_sems[q] = [[stack.enter_context(nc.semaphore("D_%s%d" % (q, i))), 0, None]
                                for i in range(n_dma_sems)]
        self.dma_rr = {q: 0 for q in self.dma_sems}
        self.last_w = {}
        self.readers = {}
        self.nops = 0
        self.nwaits = 0

    def sb(self, name, shape, dt=F32):
        return self.stack.enter_context(self.nc.sbuf_tensor(name, list(shape), dt))

    def ps(self, name, shape, dt=F32):
        return self.stack.enter_context(self.nc.psum_tensor(name, list(shape), dt))

    def _need(self, e, d, waits):
        if d is None:
            return
        if e == "pe" and d.eng == "pe" and not d.dma:
            return
        sem, val = d.tok
        k = self.know[e]
        if k.get(sem, 0) >= val:
            return
        waits.append((sem, val))
        for s, v in d.clock.items():
            if k.get(s, 0) < v:
                k[s] = v

    def op(self, eng, fn, reads=(), writes=(), dma=False, extra_deps=()):
        o = _Op()
        o.eng = eng
        o.fn = fn
        o.dma = dma
        o.idx = self.nops
        self.nops += 1
        deps = {}
        psum_r = [r for r in reads if r.startswith("pb") and r not in writes]
        if psum_r:
            writes = list(writes) + psum_r
            reads = [r for r in reads if r not in psum_r]
        for r in reads:
            d = self.last_w.get(r)
            if d is not None:
                deps[d.idx] = d
        for w in writes:
            d = self.last_w.get(w)
            if d is not None:
                deps[d.idx] = d
            for d in self.readers.get(w, ()):
                deps[d.idx] = d
        for d in extra_deps:
            if d is not None:
                deps[d.idx] = d
        waits = []
        for i in sorted(deps, reverse=True):
            self._need(eng, deps[i], waits)
        if dma:
            pool = self.dma_sems[eng]
            slot = pool[self.dma_rr[eng] % len(pool)]
            self.dma_rr[eng] += 1
            if slot[2] is not None:
                self._need(eng, slot[2], waits)
            slot[1] += 16
            o.tok = (slot[0], slot[1])
            slot[2] = o
        elif fn is not None:
            self.cnt[eng] += 1
            o.tok = (self.sem[eng], self.cnt[eng])
        else:
            o.tok = None
        o.waits = waits
        self.nwaits += len(waits)
        o.clock = dict(self.know[eng])
        if o.tok is not None:
            o.clock[o.tok[0]] = o.tok[1]
            if not dma:
                self.last_op[eng] = o
        for w in writes:
            self.last_w[w] = o
            self.readers[w] = []
        for r in reads:
            if r not in writes:
                self.readers.setdefault(r, []).append(o)
        self.ops[eng].append(o)
        return o

    def pe(self, fn, reads=(), writes=()):
        return self.op("pe", fn, reads, writes)

    def act(self, fn, reads=(), writes=()):
        return self.op("act", fn, reads, writes)

    def dve(self, fn, reads=(), writes=()):
        return self.op("dve", fn, reads, writes)

    def pool(self, fn, reads=(), writes=()):
        return self.op("pool", fn, reads, writes)

    def dma(self, fn, reads=(), writes=(), q="sp"):
        return self.op(q, fn, reads, writes, dma=True)

    def barrier(self):
        pend = [self.last_op[e] for e in self.ENGS if self.last_op[e] is not None]
        for q in self.dma_sems:
            for slot in self.dma_sems[q]:
                if slot[2] is not None:
                    pend.append(slot[2])
        for e in self.ENGS:
            self.op(e, None, extra_deps=pend)

    def emit(self, final_reads=()):
        nc = self.nc
        fin_waits = []
        for r in final_reads:
            self._need("sp", self.last_w.get(r), fin_waits)

        def run(e, engine, extra=()):
            for o in self.ops[e]:
                for (s, v) in o.waits:
                    engine.wait_ge(s, v)
                if o.fn is None:
                    continue
                ins = o.fn(engine)
                ins.then_inc(o.tok[0], 16 if o.dma else 1)
            for (s, v) in extra:
                engine.wait_ge(s, v)

        with nc.Block() as block:
            @block.sync
            def _(eng):
                run("sp", eng, fin_waits)

            @block.gpsimd
            def _(eng):
                run("pool", eng)

            @block.scalar
            def _(eng):
                run("act", eng)

            @block.vector
            def _(eng):
                run("dve", eng)

            @block.tensor
            def _(eng):
                run("pe", eng)


def MMS(lst):
    def f(e):
        ins = None
        for (o, l, r, st, sp) in lst:
            ins = e.matmul(o, lhsT=l, rhs=r, start=st, stop=sp)
        return ins
    return f


def TRS(lst, ident):
    def f(e):
        ins = None
        for (o, i) in lst:
            ins = e.transpose(out=o, in_=i, identity=ident)
        return ins
    return f


def ACTF(out, in_, func, bias=0.0, scale=1.0, accum_out=None):
    if accum_out is None:
        return lambda e: e.activation(out=out, in_=in_, func=func, bias=bias, scale=scale)
    return lambda e: e.activation(out=out, in_=in_, func=func, bias=bias, scale=scale, accum_out=accum_out)


def TT(out, a, b, op):
    return lambda e: e.tensor_tensor(out=out, in0=a, in1=b, op=op)


def TS(out, a, s1, op0, s2=None, op1=None):
    if op1 is None:
        return lambda e: e.tensor_scalar(out=out, in0=a, scalar1=s1, scalar2=None, op0=op0)
    return lambda e: e.tensor_scalar(out=out, in0=a, scalar1=s1, scalar2=s2, op0=op0, op1=op1)


def STT(out, in0, scalar, in1, op0, op1):
    return lambda e: e.scalar_tensor_tensor(out=out, in0=in0, scalar=scalar, in1=in1, op0=op0, op1=op1)


def CP(out, in_):
    return lambda e: e.tensor_copy(out=out, in_=in_)


def RECIP(out, in_):
    return lambda e: e.reciprocal(out=out, in_=in_)


def MSET(ap, v):
    return lambda e: e.memset(ap, v)


def DMA(out, in_):
    return lambda e: e.dma_start(out=out, in_=in_)


def SCAN(out, a, b, init):
    return lambda e: e.tensor_tensor_scan(out=out, data0=a, data1=b, initial=init, op0=ALU.mult, op1=ALU.add)


PRM_SPEC = [("n1g", 16), ("n2g", 16), ("fng", 8), ("bmod", 96), ("rgcw", 16), ("rgcb", 4), ("rgba", 8),
            ("rgbx", 8), ("rglam", 8), ("scw", 32), ("scb", 8), ("subg", 2)]
PRM_OFF = {}
_o = 0
for _n, _w in PRM_SPEC:
    PRM_OFF[_n] = _o
    _o += _w
NPRM = _o
ROW_SPEC = [("dtb", 16), ("alog", 16), ("dsk", 8), ("sng", 512), ("dal", 512)]
ROW_OFF = {}
_o = 0
for _n, _w in ROW_SPEC:
    ROW_OFF[_n] = _o
    _o += _w
NROW = _o


def _fm(v):
    v = np.asarray(v, np.float32)
    lead = v.shape[:-1]
    n = v.shape[-1] // 128
    v = v.reshape(lead + (n, 128))
    return np.moveaxis(v, -1, 0).reshape(128, -1)


def pack_shared(inp):
    f = lambda k: np.asarray(inp[k], np.float32)
    prm = np.zeros((128, NPRM), np.float32)

    def put(name, arr):
        o = PRM_OFF[name]
        prm[:, o:o + arr.shape[1]] = arr

    put("n1g", _fm(f("norm1_g")))
    put("n2g", _fm(f("norm2_g")))
    put("fng", _fm(f("final_norm_g")))
    put("bmod", _fm(f("b_mod")))
    w = f("rg_conv_w").reshape(2, 4, 2, 128).transpose(3, 0, 2, 1).reshape(128, 16)
    put("rgcw", w)
    put("rgcb", _fm(f("rg_conv_b")))
    put("rgba", _fm(f("rg_b_a")))
    put("rgbx", _fm(f("rg_b_x")))
    put("rglam", _fm(f("rg_lambda")))
    w = f("ssd_conv_w").reshape(2, 4, 4, 128).transpose(3, 0, 2, 1).reshape(128, 32)
    put("scw", w)
    put("scb", _fm(f("ssd_conv_b")))
    put("subg", _fm(f("da_subln_g")))
    row = np.zeros((NROW,), np.float32)
    row[ROW_OFF["dtb"]:ROW_OFF["dtb"] + 16] = f("ssd_dt_bias").reshape(-1)
    row[ROW_OFF["alog"]:ROW_OFF["alog"] + 16] = f("ssd_a_log").reshape(-1)
    row[ROW_OFF["dsk"]:ROW_OFF["dsk"] + 8] = f("ssd_d").reshape(-1)
    row[ROW_OFF["sng"]:ROW_OFF["sng"] + 512] = f("ssd_norm_g").reshape(-1)
    row[ROW_OFF["dal"]:ROW_OFF["dal"] + 512] = f("da_lambda").reshape(-1)
    rowp = np.ascontiguousarray(np.broadcast_to(row[None, :], (128, NROW)))
    bd = np.zeros((128, 16, 128), np.float32)
    wa, wx = f("rg_w_a"), f("rg_w_x")
    for l in range(2):
        for d in range(2):
            for cc in range(2):
                for ax, wsrc in enumerate((wa, wx)):
                    idx = ((l * 2 + d) * 2 + cc) * 2 + ax
                    for blk in range(2):
                        bd[blk * 64:(blk + 1) * 64, idx, blk * 64:(blk + 1) * 64] = wsrc[l, d, 2 * cc + blk]
    j = np.arange(64)
    sw = np.where((j % 32) < 16, j + 16, j - 16)
    win = f("w_in")
    qidx = np.concatenate([OFF_Q + h * 128 + c * 64 + sw for h in range(4) for c in range(2)])
    kidx = np.concatenate([OFF_K + h * 128 + c * 64 + sw for h in range(4) for c in range(2)])
    winx = np.ascontiguousarray(np.concatenate([win, win[:, :, qidx], win[:, :, kidx]], axis=2))
    ident = np.eye(128, dtype=np.float32)
    tt = np.arange(128)
    uf = (tt[:, None] <= tt[None, :]).astype(np.float32)
    ub = (tt[:, None] >= tt[None, :]).astype(np.float32)
    t = np.arange(NLAT)
    row_i = (t // 64).astype(np.float32)
    col_i = (t % 64).astype(np.float32)
    half = 32
    inv_freq = np.power(np.float32(10000.0), -np.arange(0, half, 2, dtype=np.float32) / np.float32(half)).astype(np.float32)
    ang_r = row_i[:, None] * inv_freq[None, :]
    ang_c = col_i[:, None] * inv_freq[None, :]
    rope = np.zeros((128, 2, NLAT), np.float32)
    for p in range(128):
        jj = p % 64
        ang = ang_r if jj < 32 else ang_c
        i = jj % 16
        rope[p, 0] = np.cos(ang[:, i])
        sgn = -1.0 if (jj % 32) < 16 else 1.0
        rope[p, 1] = sgn * np.sin(ang[:, i])
    return dict(prm=prm, rowp=rowp, bd=bd, w_in_x=winx, ident=ident, uf=uf, ub=ub, rope=rope,
                w_mod=f("w_mod"), w_out=f("w_out"), w_gate=f("w_gate"), w_up=f("w_up"), w_down=f("w_down"))


class Kern:
    def __init__(self, nc, P, NB, opts):
        self.nc = nc
        self.P = P
        self.NB = NB
        self.o = opts
        dt_in = lambda n, s: nc.dram_tensor(n, list(s), F32, kind="ExternalInput").ap()
        self.x = dt_in("x", [NB, NLAT, D])
        self.ctx = dt_in("ctx", [NB, NCTX, D])
        self.ct = dt_in("ct", [128, 8, 3])
        self.prm_d = dt_in("prm", [128, NPRM])
        self.rowp_d = dt_in("rowp", [128, NROW])
        self.bd_d = dt_in("bd", [128, 16, 128])
        self.w_in = dt_in("w_in_x", [2, D, DINX])
        self.ident_d = dt_in("ident", [128, 128])
        self.uf_d = dt_in("uf", [128, 128])
        self.ub_d = dt_in("ub", [128, 128])
        self.rope_d = dt_in("rope", [128, 2, NLAT])
        self.w_mod = dt_in("w_mod", [2, D, 6 * D])
        self.w_out = dt_in("w_out", [2, D, D])
        self.w_gate = dt_in("w_gate", [2, D, DFF])
        self.w_up = dt_in("w_up", [2, D, DFF])
        self.w_down = dt_in("w_down", [2, DFF, D])
        self.out = nc.dram_tensor("out", [NB, NLAT, D], F32, kind="ExternalOutput").ap()
        sb = P.sb
        self.resid = sb("resid", [128, 8 * T])
        self.R3 = self.resid[:].rearrange("p (c t) -> p c t", c=8)
        self.xn = sb("xn", [128, 8 * T], BF16)
        self.XN = self.xn[:].rearrange("p (c t) -> p c t", c=8)
        self.mixb = sb("mixb", [128, 4 * T], BF16)
        self.MX = self.mixb[:].rearrange("p (c t) -> p c t", c=4)
        self.scr = sb("scr", [128, SCRW])
        self.WB = [sb("wb%d" % i, [128, 1024], BF16) for i in range(4)]
        self.wrr = 0
        self.ident = sb("ident_s", [128, 128])
        self.uf = sb("uf_s", [128, 128])
        self.ub = sb("ub_s", [128, 128])
        self.ones_f = sb("ones_f", [128, 128])
        self.ones_b = sb("ones_b", [128, 128], BF16)
        self.prm = sb("prm_s", [128, NPRM])
        self.rowp = sb("rowp_s", [128, NROW])
        self.bd = sb("bd_s", [128, 16 * 128], BF16)
        self.BD = self.bd[:].rearrange("p (i j) -> p i j", i=16)
        self.modt = sb("modt", [128, 2 * 48 * 3])
        self.MT = self.modt[:].rearrange("p (l j n) -> p l j n", l=2, j=48)
        self.amod = sb("amod", [128, 2 * 2 * 3 * 8])
        self.AM = self.amod[:].rearrange("p (l w n c) -> p l w n c", l=2, w=2, n=3)
        self.small = sb("small", [128, 256])
        self.sml_o = 0
        self.pb = [P.ps("pb%d" % i, [128, 512]) for i in range(8)]
        self.rot = [0, 1, 2, 3]
        self.rr = 0

    def prmcol(self, name, i):
        o = PRM_OFF[name] + i
        return self.prm[:, o:o + 1]

    def salloc(self, n):
        o = self.sml_o
        self.sml_o += n
        assert self.sml_o <= 256
        return self.small[:, o:o + n]

    def scr_f(self, off, n):
        return self.scr[:, off:off + n]

    def scr_b(self, off, n):
        assert n % 2 == 0
        return self.scr[:, off:off + n // 2].bitcast(BF16)

    def next_bank(self):
        i = self.rot[self.rr % len(self.rot)]
        self.rr += 1
        return self.pb[i], "pb%d" % i

    def load_w(self, wsrc, k0, kch, col0, ncols=128):
        i = self.wrr % len(self.WB)
        self.wrr += 1
        wb = self.WB[i][:, 0:kch * ncols].rearrange("p (c n) -> p c n", c=kch)
        src = wsrc.rearrange("(c p) n -> p c n", p=128)[:, k0:k0 + kch, col0:col0 + ncols]
        tok = "wb%d" % i
        self.P.dma(DMA(wb, src), writes=[tok], q="pool")
        return wb, tok

    def linear_fm(self, wsrc, col0, tiles, evac, kch=8, k0=0, rhs=None, rtok=None, ncols=128):
        wb, wtok = self.load_w(wsrc, k0, kch, col0, ncols)
        if rhs is None:
            rhs = lambda kc, t0, n: self.XN[:, kc, t0:t0 + n]
            rtok = lambda t: ["xn.%d" % t]
        for t in tiles:
            t0, n = TILES[t]
            bank, btok = self.next_bank()
            out = bank[0:ncols, 0:n]
            mms = [(out, wb[:, kc, :], rhs(kc, t0, n), kc == 0, kc == kch - 1) for kc in range(kch)]
            self.P.pe(MMS(mms), reads=[wtok] + rtok(t), writes=[btok])
            evac(t, out, btok)

    def linear_fm2(self, wsrc, colA, colB, tiles, evac2, extraA=(), evacA=None):
        wa, atok = self.load_w(wsrc, 0, 8, colA)
        wb2, btok2 = self.load_w(wsrc, 0, 8, colB)
        for t in tiles:
            t0, n = TILES[t]
            ba, bat = self.next_bank()
            bb, bbt = self.next_bank()
            self.P.pe(MMS([(ba[:, 0:n], wa[:, kc, :], self.XN[:, kc, t0:t0 + n], kc == 0, kc == 7) for kc in range(8)]
                          + [(bb[:, 0:n], wb2[:, kc, :], self.XN[:, kc, t0:t0 + n], kc == 0, kc == 7) for kc in range(8)]),
                      reads=[atok, btok2, "xn.%d" % t], writes=[bat, bbt])
            evac2(t, ba[:, 0:n], bat, bb[:, 0:n], bbt)
        for t in extraA:
            t0, n = TILES[t]
            ba, bat = self.next_bank()
            self.P.pe(MMS([(ba[:, 0:n], wa[:, kc, :], self.XN[:, kc, t0:t0 + n], kc == 0, kc == 7) for kc in range(8)]),
                      reads=[atok, "xn.%d" % t], writes=[bat])
            evacA(t, ba[:, 0:n], bat)

    def linear_tm(self, wsrc, col0, ncols, blocks, evac, grp):
        wb, wtok = self.load_w(wsrc, 0, 8, col0, ncols)
        for g0 in range(0, len(blocks), grp):
            blks = blocks[g0:g0 + grp]
            bank, btok = self.next_bank()
            mms = []
            rt = set()
            for gi, i in enumerate(blks):
                out = bank[:, gi * ncols:(gi + 1) * ncols]
                rt.add("xn.%d" % min(i // 4, 4))
                for kc in range(8):
                    mms.append((out, self.XN[:, kc, i * 128:(i + 1) * 128], wb[:, kc, :], kc == 0, kc == 7))
            self.P.pe(MMS(mms), reads=[wtok] + sorted(rt), writes=[btok])
            evac(blks, bank[:, 0:len(blks) * ncols], btok)

    def softplus(self, out, x, n, s0, toks):
        P = self.P
        ax, e, z, p = (s0[:, i * n:(i + 1) * n] for i in range(4))
        rd, wr = toks
        tk = ["spl.tmp"]
        P.dve(STT(ax, x, -1.0, x, ALU.mult, ALU.max), reads=rd, writes=tk)
        P.act(ACTF(e, ax, AF.Exp, scale=-1.0), reads=tk, writes=tk)
        P.dve(TS(z, e, 2.0, ALU.add), reads=tk, writes=tk)
        P.dve(RECIP(z, z), reads=tk, writes=tk)
        P.dve(TT(z, z, e, ALU.mult), reads=tk, writes=tk)
        P.dve(TT(e, z, z, ALU.mult), reads=tk, writes=tk)
        P.dve(TS(p, e, 1.0 / 13.0, ALU.mult, 1.0 / 11.0, ALU.add), reads=tk, writes=tk)
        for cst in (1.0 / 9.0, 1.0 / 7.0, 1.0 / 5.0, 1.0 / 3.0, 1.0):
            P.dve(TT(p, p, e, ALU.mult), reads=tk, writes=tk)
            P.dve(TS(p, p, cst, ALU.add), reads=tk, writes=tk)
        P.dve(TT(p, p, z, ALU.mult), reads=tk, writes=tk)
        P.dve(TS(ax, x, 0.0, ALU.max), reads=rd + tk, writes=tk)
        P.dve(STT(out, p, 2.0, ax, ALU.mult, ALU.add), reads=tk, writes=wr)

    def mods_emit(self, l, jj):
        P = self.P
        wb, wtok = self.load_w(self.w_mod[l], 0, 8, jj * 128)
        bi = 5 + (self.mod_i % 2)
        self.mod_i += 1
        out = self.pb[bi][:, 0:3]
        P.pe(MMS([(out, wb[:, kc, :], self.ctb[:, kc, :], kc == 0, kc == 7) for kc in range(8)]),
             reads=[wtok, "ctb"], writes=["pb%d" % bi])
        P.dve(TS(self.MT[:, l, jj, :], out, self.prmcol("bmod", l * 48 + jj), ALU.add), reads=["pb%d" % bi, "prm"],
              writes=["modt"])

    def mods_finish(self, l, w):
        gname = "n1g" if w == 0 else "n2g"
        go = PRM_OFF[gname] + l * 8
        jj0 = 8 + 24 * w
        for n in range(3):
            self.P.dve(STT(self.AM[:, l, w, n, :], self.MT[:, l, jj0:jj0 + 8, n], 1.0, self.prm[:, go:go + 8],
                           ALU.add, ALU.mult), reads=["modt", "prm"], writes=["amod"])

    def mods_step(self, n):
        while n > 0 and self.mod_q:
            l, jj = self.mod_q.pop(0)
            self.mods_emit(l, jj)
            n -= 1
            if not self.mod_q:
                self.mods_finish(0, 1)
                self.mods_finish(1, 0)
                self.mods_finish(1, 1)

    def setup(self):
        P = self.P
        P.dma(DMA(self.ident[:], self.ident_d), writes=["ident"])
        P.dma(DMA(self.uf[:], self.uf_d), writes=["uf"])
        P.dma(DMA(self.ub[:], self.ub_d), writes=["ub"])
        P.dma(DMA(self.prm[:], self.prm_d), writes=["prm"])
        P.dma(DMA(self.rowp[:], self.rowp_d), writes=["rowp"])
        P.dma(DMA(self.BD, self.bd_d), writes=["bd"], q="pool")
        P.pool(MSET(self.ones_f[:], 1.0), writes=["ones_f"])
        P.pool(MSET(self.ones_b[:], 1.0), writes=["ones_b"])
        ctile = self.scr_f(0, 24).rearrange("p (k n) -> p k n", k=8)
        sig = self.scr_f(24, 24).rearrange("p (k n) -> p k n", k=8)
        P.dma(DMA(ctile, self.ct), writes=["ct"])
        P.act(ACTF(sig, ctile, AF.Sigmoid), reads=["ct"], writes=["sig"])
        P.dve(TT(ctile, ctile, sig, ALU.mult), reads=["ct", "sig"], writes=["ct"])
        self.ctb = self.salloc(12).bitcast(BF16)[:, 0:24].rearrange("p (k n) -> p k n", k=8)
        P.dve(CP(self.ctb, ctile), reads=["ct"], writes=["ctb"])
        self.mod_q = [(0, jj) for jj in range(24, 48)] + [(1, jj) for jj in range(48)]
        self.mod_i = 0
        for jj in range(24):
            self.mods_emit(0, jj)
        self.mods_finish(0, 0)
        self.m8sp = self.salloc(8)
        nl = self.salloc(8)
        lo = PRM_OFF["rglam"]
        P.dve(TS(nl, self.prm[:, lo:lo + 8], -1.0, ALU.mult), reads=["prm"], writes=["nl"])
        self.softplus(self.m8sp, nl, 8, self.scr_f(8300, 32), (["nl"], ["m8sp"]))
        P.dve(TS(self.m8sp, self.m8sp, -8.0, ALU.mult), reads=["m8sp"], writes=["m8sp"])
        self.hm = self.salloc(8)
        self.hba = self.salloc(8)
        self.hbx = self.salloc(8)
        P.dve(TS(self.hm, self.m8sp, 0.5, ALU.mult), reads=["m8sp"], writes=["hm"])
        oa, ox = PRM_OFF["rgba"], PRM_OFF["rgbx"]
        P.dve(TS(self.hba, self.prm[:, oa:oa + 8], 0.5, ALU.mult), reads=["prm"], writes=["hba"])
        P.dve(TS(self.hbx, self.prm[:, ox:ox + 8], 0.5, ALU.mult), reads=["prm"], writes=["hbx"])
        self.aneg = self.salloc(16)
        ao = ROW_OFF["alog"]
        P.act(ACTF(self.aneg, self.rowp[:, ao:ao + 16], AF.Exp), reads=["rowp"], writes=["aneg"])
        P.dve(TS(self.aneg, self.aneg, -1.0, ALU.mult), reads=["aneg"], writes=["aneg"])
        self.nlam = self.salloc(2)
        self.gsc = self.salloc(2)
        pr = self.scr_f(8400, 128)
        sm = self.salloc(4)
        for l in range(2):
            do = ROW_OFF["dal"] + l * 256
            lam_init = 0.8 - 0.6 * math.exp(-0.3 * l)
            for i in range(2):
                P.dve(TT(pr[:, 0:64], self.rowp[:, do + 128 * i:do + 128 * i + 64],
                         self.rowp[:, do + 128 * i + 64:do + 128 * i + 128], ALU.mult), reads=["rowp"], writes=["pr"])
                P.dve(lambda e, o_=sm[:, 2 * l + i:2 * l + i + 1], i_=pr[:, 0:64]: e.reduce_sum(
                    out=o_, in_=i_, axis=mybir.AxisListType.X), reads=["pr"], writes=["sm"])
            P.act(ACTF(sm[:, 2 * l:2 * l + 2], sm[:, 2 * l:2 * l + 2], AF.Exp), reads=["sm"], writes=["sm"])
            P.dve(TT(self.nlam[:, l:l + 1], sm[:, 2 * l + 1:2 * l + 2], sm[:, 2 * l:2 * l + 1], ALU.subtract),
                  reads=["sm"], writes=["nlam"])
            P.dve(TS(self.nlam[:, l:l + 1], self.nlam[:, l:l + 1], -lam_init, ALU.add), reads=["nlam"], writes=["nlam"])
            P.dve(TS(self.gsc[:, l:l + 1], self.prmcol("subg", l), 1.0 - lam_init, ALU.mult), reads=["prm"],
                  writes=["gsc"])
        P.barrier()

    def load_x(self, b):
        P = self.P
        STG = [self.scr_f(i * 1024, 1024) for i in range(2)]
        self.rot = [0, 1, 2, 3]
        for i in range(NBLK):
            src = self.x[b, i * 128:(i + 1) * 128, :] if i < 16 else self.ctx[b, (i - 16) * 128:(i - 15) * 128, :]
            stg = STG[i % 2]
            stok = "stg%d" % (i % 2)
            P.dma(DMA(stg, src), writes=[stok])
            t = min(i // 4, 4)
            for half in range(2):
                bank, btok = self.next_bank()
                P.pe(TRS([(bank[:, j * 128:(j + 1) * 128], stg[:, (half * 4 + j) * 128:(half * 4 + j + 1) * 128])
                          for j in range(4)], self.ident[:]), reads=[stok, "ident"], writes=[btok])
                dst = self.R3[:, half * 4:half * 4 + 4, i * 128:(i + 1) * 128]
                srcp = bank[:, 0:512].rearrange("p (c t) -> p c t", c=4)
                wr = ["r.%d.%d" % (c, t) for c in range(half * 4, half * 4 + 4)]
                if half == 0:
                    P.act(ACTF(dst, srcp, AF.Identity), reads=[btok], writes=wr)
                else:
                    P.dve(CP(dst, srcp), reads=[btok], writes=wr)

    def rstd_tile(self, t, RS, SQ):
        P = self.P
        t0, n = TILES[t]
        bank = self.pb[7]
        for c in range(8):
            sq = SQ[c % 2]
            P.dve(TT(sq[:, :n], self.R3[:, c, t0:t0 + n], self.R3[:, c, t0:t0 + n], ALU.mult),
                  reads=["r.%d.%d" % (c, t)], writes=["sq%d" % (c % 2)])
            P.pe(MMS([(bank[:, :n], self.ones_b[:], sq[:, :n], c == 0, c == 7)]), reads=["sq%d" % (c % 2), "ones_b"],
                 writes=["pb7"])
        P.act(ACTF(RS[:, :n], bank[:, :n], AF.Sqrt, bias=EPS, scale=1.0 / D), reads=["pb7"], writes=["rs"])
        P.dve(RECIP(RS[:, :n], RS[:, :n]), reads=["rs"], writes=["rs"])

    def norm(self, b, l, w, tiles):
        P = self.P
        SQ = [self.scr_b(i * 256, 512) for i in range(2)]
        RS = self.scr_f(512, 512)
        TM = [self.scr_f(1024 + i * 512, 512) for i in range(2)]
        for t in tiles:
            t0, n = TILES[t]
            nn = b if t < 4 else 2
            self.rstd_tile(t, RS, SQ)
            for c in range(8):
                tm = TM[c % 2]
                P.dve(TT(tm[:, :n], self.R3[:, c, t0:t0 + n], RS[:, :n], ALU.mult), reads=["r.%d.%d" % (c, t), "rs"],
                      writes=["tm%d" % (c % 2)])
                P.act(ACTF(self.XN[:, c, t0:t0 + n], tm[:, :n], AF.Identity,
                           bias=self.MT[:, l, 24 * w + c, nn:nn + 1], scale=self.AM[:, l, w, nn, c:c + 1]),
                      reads=["tm%d" % (c % 2), "modt", "amod"], writes=["xn.%d" % t])

    def wout_partial(self, b, l, rows0, tiles):
        P = self.P
        self.rot = [0, 1, 2, 3]
        for c in range(8):
            def evac(t, ps, btok, c=c):
                t0, n = TILES[t]
                nn = b if t < 4 else 2
                dst = self.R3[:, c, t0:t0 + n]
                P.dve(STT(dst, ps, self.MT[:, l, 16 + c, nn:nn + 1], dst, ALU.mult, ALU.add),
                      reads=[btok, "modt", "r.%d.%d" % (c, t)], writes=["r.%d.%d" % (c, t)])
            self.linear_fm(self.w_out[l], c * 128, tiles, evac, kch=4, k0=rows0 // 128,
                           rhs=lambda kc, t0, n: self.MX[:, kc, t0:t0 + n], rtok=lambda t: ["mx.%d" % t])

    def ffn(self, b, l, tiles):
        P = self.P
        HH = self.scr_b(0, 6 * T).rearrange("p (j t) -> p j t", j=6)
        SG = [self.scr_f(7000 + i * 512, 512) for i in range(2)]
        groups = [(0, 6), (6, 6), (12, 5), (17, 5)]
        it = 0
        for (j0, J) in groups:
            for jl in range(J):
                j = j0 + jl
                wg, gtok = self.load_w(self.w_gate[l], 0, 8, j * 128)
                wu, utok = self.load_w(self.w_up[l], 0, 8, j * 128)
                for t in tiles:
                    t0, n = TILES[t]
                    bg, bgt = self.pb[it % 2], "pb%d" % (it % 2)
                    bu, but = self.pb[2 + it % 2], "pb%d" % (2 + it % 2)
                    sg = SG[it % 2]
                    sgt = "sg%d" % (it % 2)
                    it += 1
                    P.pe(MMS([(bg[:, :n], wg[:, kc, :], self.XN[:, kc, t0:t0 + n], kc == 0, kc == 7) for kc in range(8)]),
                         reads=[gtok, "xn.%d" % t], writes=[bgt])
                    P.pe(MMS([(bu[:, :n], wu[:, kc, :], self.XN[:, kc, t0:t0 + n], kc == 0, kc == 7) for kc in range(8)]),
                         reads=[utok, "xn.%d" % t], writes=[but])
                    P.act(ACTF(sg[:, :n], bg[:, :n], AF.Silu), reads=[bgt], writes=[sgt])
                    P.dve(TT(HH[:, jl, t0:t0 + n], sg[:, :n], bu[:, :n], ALU.mult), reads=[sgt, but],
                          writes=["hh.%d.%d" % (jl, t)])
            for c in range(8):
                wd, dtok = self.load_w(self.w_down[l], j0, J, c * 128)
                for t in tiles:
                    t0, n = TILES[t]
                    nn = b if t < 4 else 2
                    bi = 4 + (it % 3)
                    it += 1
                    bank, btok = self.pb[bi], "pb%d" % bi
                    P.pe(MMS([(bank[:, :n], wd[:, jl, :], HH[:, jl, t0:t0 + n], jl == 0, jl == J - 1) for jl in range(J)]),
                         reads=[dtok] + ["hh.%d.%d" % (jl, t) for jl in range(J)], writes=[btok])
                    dst = self.R3[:, c, t0:t0 + n]
                    P.dve(STT(dst, bank[:, :n], self.MT[:, l, 40 + c, nn:nn + 1], dst, ALU.mult, ALU.add),
                          reads=[btok, "modt", "r.%d.%d" % (c, t)], writes=["r.%d.%d" % (c, t)])

    def final(self, b):
        P = self.P
        SQ = [self.scr_b(i * 256, 512) for i in range(2)]
        RS = self.scr_f(512, 512)
        TM = [self.scr_f(1024 + i * 512, 512) for i in range(2)]
        YN = self.scr_f(2048, 4096).rearrange("p (c t) -> p c t", c=8)
        OST = [self.scr_f(6144 + i * 1024, 1024) for i in range(2)]
        self.rot = [0, 1, 2, 3]
        oi = 0
        for t in range(4):
            t0, n = TILES[t]
            self.rstd_tile(t, RS, SQ)
            for c in range(8):
                tm = TM[c % 2]
                P.dve(TT(tm[:, :n], self.R3[:, c, t0:t0 + n], RS[:, :n], ALU.mult), reads=["r.%d.%d" % (c, t), "rs"],
                      writes=["tm%d" % (c % 2)])
                P.act(ACTF(YN[:, c, :], tm[:, :n], AF.Identity, scale=self.prmcol("fng", c)),
                      reads=["tm%d" % (c % 2), "prm"], writes=["yn.%d" % c])
            for j in range(4):
                ost = OST[oi % 2]
                otok = "ost%d" % (oi % 2)
                oi += 1
                for half in range(2):
                    bank, btok = self.next_bank()
                    P.pe(TRS([(bank[:, cc * 128:(cc + 1) * 128], YN[:, half * 4 + cc, j * 128:(j + 1) * 128])
                              for cc in range(4)], self.ident[:]),
                         reads=["yn.%d" % (half * 4 + cc) for cc in range(4)] + ["ident"], writes=[btok])
                    if half == 0:
                        P.act(ACTF(ost[:, 0:512], bank[:, 0:512], AF.Identity), reads=[btok], writes=[otok])
                    else:
                        P.dve(CP(ost[:, 512:1024], bank[:, 0:512]), reads=[btok], writes=[otok])
                r0 = t0 + j * 128
                P.dma(DMA(self.out[b, r0:r0 + 128, :], ost), reads=[otok], writes=["out"])

    def da_mixer(self, b, l):
        P = self.P
        need_ctx = (l == 0)
        o = 0
        ROPE = self.scr_f(o, 4096).rearrange("p (a t) -> p a t", a=2); o += 4096
        QR = self.scr_b(o, T); o += T // 2
        KR = self.scr_b(o, T); o += T // 2
        VT = self.scr_b(o, T).rearrange("p (i d) -> p i d", i=NBLK); o += T // 2
        T1 = [self.scr_f(o + i * 512, 512) for i in range(2)]; o += 1024
        T2 = [self.scr_f(o + i * 512, 512) for i in range(2)]; o += 1024
        PT = [self.scr_b(o + i * 256, 512) for i in range(4)]; o += 1024
        RC = self.scr_f(o, 512); o += 512
        TC = [self.scr_f(o + i * 512, 512) for i in range(2)]; o += 1024
        O2 = self.scr_f(o, 512); o += 512
        OV = [self.scr_f(o + i * 512, 512) for i in range(2)]; o += 1024
        assert o <= SCRW, o
        P.dma(DMA(ROPE, self.rope_d), writes=["rope"])
        lat = [0, 1, 2, 3]
        W = self.w_in[l]
        ti = [0]
        pend = {"A": None, "B": None}
        fcount = [0]

        def emitA():
            f = pend["A"]
            pend["A"] = None
            if f is not None:
                f()

        def emitB():
            f = pend["B"]
            pend["B"] = None
            if f is not None:
                f()

        def make_final(h, qt, q0, qn):
            fi = fcount[0] % 2
            fcount[0] += 1
            ov = OV[fi][:, :qn]
            ovt = "ov%d" % fi

            def fa():
                for c in range(2):
                    P.dve(RECIP(RC[:, :qn], T1[c][:, :qn]), reads=["t1.%d" % c], writes=["rc"])
                    P.dve(TT(TC[c][:, :qn], TC[c][:, :qn], RC[:, :qn], ALU.mult), reads=["tc%d" % c, "rc"],
                          writes=["tc%d" % c])
                P.dve(STT(ov, TC[1][:, :qn], self.nlam[:, l:l + 1], TC[0][:, :qn], ALU.mult, ALU.add),
                      reads=["tc0", "tc1", "nlam"], writes=[ovt])
                P.dve(TT(O2[:, :qn], ov, ov, ALU.mult), reads=[ovt], writes=["o2"])

            def fb():
                P.pe(MMS([(self.pb[3][:, :qn], self.ones_f[:], O2[:, :qn], True, True)]), reads=["o2", "ones_f"],
                     writes=["pb3"])
                P.act(ACTF(O2[:, :qn], self.pb[3][:, :qn], AF.Ln, bias=EPS, scale=1.0 / 128.0), reads=["pb3"],
                      writes=["o2"])
                P.act(ACTF(O2[:, :qn], O2[:, :qn], AF.Exp, scale=-0.5), reads=["o2"], writes=["o2"])
                P.dve(TT(ov, ov, O2[:, :qn], ALU.mult), reads=[ovt, "o2"], writes=[ovt])
                P.act(ACTF(self.MX[:, h, q0:q0 + qn], ov, AF.Identity, scale=self.gsc[:, l:l + 1]),
                      reads=[ovt, "gsc"], writes=["mx.%d" % qt])
            return fa, fb

        for h in range(4):
            self.rot = [0, 1, 2, 3]

            def ev_rope(dst, dtok):
                def ev(t, pa, ta, pb_, tb):
                    t0, n = TILES[t]
                    i = ti[0] % 2
                    ti[0] += 1
                    P.dve(TT(T1[i][:, :n], pa, ROPE[:, 0, t0:t0 + n], ALU.mult), reads=[ta, "rope"], writes=["t1.%d" % i])
                    P.dve(TT(T2[i][:, :n], pb_, ROPE[:, 1, t0:t0 + n], ALU.mult), reads=[tb, "rope"], writes=["t2.%d" % i])
                    P.pool(TT(dst[:, t0:t0 + n], T1[i][:, :n], T2[i][:, :n], ALU.add), reads=["t1.%d" % i, "t2.%d" % i],
                           writes=["%s.%d" % (dtok, t)])
                return ev

            def ev_plain(dst, dtok):
                def ev(t, ps, btok):
                    t0, n = TILES[t]
                    P.act(ACTF(dst[:, t0:t0 + n], ps, AF.Identity), reads=[btok], writes=["%s.%d" % (dtok, t)])
                return ev
            emitA()
            self.linear_fm2(W, OFF_Q + h * 128, OFF_QS + h * 128, lat, ev_rope(QR, "qr"), [4] if need_ctx else [],
                            ev_plain(QR, "qr"))
            emitB()
            self.linear_fm2(W, OFF_K + h * 128, OFF_KS + h * 128, lat, ev_rope(KR, "kr"), [4], ev_plain(KR, "kr"))

            def ev_v(blks, ps, btok):
                i0 = blks[0]
                P.act(ACTF(VT[:, i0:i0 + len(blks), :], ps.rearrange("p (i d) -> p i d", i=len(blks)), AF.Identity),
                      reads=[btok], writes=["vt.%d" % i for i in blks])
            self.linear_tm(W, OFF_V + h * 128, 128, list(range(NBLK)), ev_v, 4)

            SB = [(0, 1), (2, 3)]
            for qt in (lat + ([4] if need_ctx else [])):
                q0, qn = TILES[qt]
                kts = list(range(NBLK)) if qt < 4 else [16, 17]
                nk = len(kts)

                def emit_s(ii):
                    kt = kts[ii]
                    pa, pb_ = SB[ii % 2]
                    P.pe(MMS([(self.pb[pa][:, :qn], KR[0:64, kt * 128:(kt + 1) * 128], QR[0:64, q0:q0 + qn], True, True),
                              (self.pb[pb_][:, :qn], KR[64:128, kt * 128:(kt + 1) * 128], QR[64:128, q0:q0 + qn], True,
                               True)]),
                         reads=["kr.%d" % min(kt // 4, 4), "qr.%d" % qt], writes=["pb%d" % pa, "pb%d" % pb_])
                emit_s(0)
                for ii, kt in enumerate(kts):
                    if ii == 0:
                        emitA()
                    if ii == min(4, nk - 1):
                        emitB()
                    if ii + 1 < nk:
                        emit_s(ii + 1)
                    first, last = (ii == 0), (ii == nk - 1)
                    for c in range(2):
                        sb = SB[ii % 2][c]
                        pi_ = (2 * ii + c) % 4
                        P.act(ACTF(PT[pi_][:, :qn], self.pb[sb][:, :qn], AF.Exp, scale=0.125), reads=["pb%d" % sb],
                              writes=["pt%d" % pi_])
                        P.pe(MMS([(self.pb[4 + c][:, :qn], VT[:, kt, :], PT[pi_][:, :qn], first, last),
                                  (self.pb[6 + c][:, :qn], self.ones_b[:], PT[pi_][:, :qn], first, last)]),
                             reads=["pt%d" % pi_, "vt.%d" % kt, "ones_b"], writes=["pb%d" % (4 + c), "pb%d" % (6 + c)])
                emitA()
                emitB()
                for c in range(2):
                    P.dve(CP(TC[c][:, :qn], self.pb[4 + c][:, :qn]), reads=["pb%d" % (4 + c)], writes=["tc%d" % c])
                    P.act(ACTF(T1[c][:, :qn], self.pb[6 + c][:, :qn], AF.Identity), reads=["pb%d" % (6 + c)],
                          writes=["t1.%d" % c])
                pend["A"], pend["B"] = make_final(h, qt, q0, qn)
        emitA()
        emitB()

    def rg_mixer(self, b, l):
        P = self.P
        need_ctx = (l == 0)
        o = 0
        RX = self.scr_f(o, 2312); o += 2312
        XC = self.scr_f(o, T); o += T
        XCB = self.scr_b(o, T); o += T // 2
        HS = self.scr_f(o, T); o += T
        B1 = self.scr_f(o, T); o += T
        B2 = self.scr_f(o, T); o += T
        TL = [self.scr_f(o + i * 512, 512) for i in range(4)]; o += 2048
        assert o <= SCRW, o
        B0 = RX[:, 0:T]
        b0t = ["rx.%d" % t for t in range(5)] + ["rxpad"]
        W = self.w_in[l]
        self.rot = [0, 1, 2, 3]
        it = 0
        nout = T if need_ctx else NLAT
        for cc in range(2):
            for (a0, a1) in ((0, 2), (2050, 2053), (2309, 2312)):
                P.pool(MSET(RX[:, a0:a1], 0.0), writes=["rxpad"])

            def ev_x(t, ps, btok):
                t0, n = TILES[t]
                off = t0 + 2 if t < 4 else t0 + 5
                P.act(ACTF(RX[:, off:off + n], ps, AF.Identity), reads=[btok], writes=["rx.%d" % t])
            self.linear_fm(W, OFF_RGX + cc * 128, [0, 1, 2, 3, 4], ev_x)
            for (x0, r0, n, rt, wt) in ((0, 0, NLAT, ["rx.0", "rx.1", "rx.2", "rx.3"], ["xc.0", "xc.1", "xc.2", "xc.3"]),
                                        (NLAT, 2051, NCTX, ["rx.4"], ["xc.4"])):
                wo = PRM_OFF["rgcw"] + (l * 2 + cc) * 4
                P.act(ACTF(XC[:, x0:x0 + n], RX[:, r0:r0 + n], AF.Identity, bias=self.prmcol("rgcb", l * 2 + cc),
                           scale=self.prm[:, wo:wo + 1]), reads=rt + ["rxpad", "prm"], writes=wt)
                for k in range(1, 4):
                    P.dve(STT(XC[:, x0:x0 + n], RX[:, r0 + k:r0 + k + n], self.prm[:, wo + k:wo + k + 1], XC[:, x0:x0 + n],
                              ALU.mult, ALU.add), reads=rt + wt + ["rxpad", "prm"], writes=wt)
                P.pool(CP(XCB[:, x0:x0 + n], XC[:, x0:x0 + n]), reads=wt, writes=["xcb.%d" % t for t in
                                                                                  ([0, 1, 2, 3] if x0 == 0 else [4])])
            xct = ["xc.%d" % t for t in range(5)]
            for d in range(2):
                pi = (l * 2 + d) * 2 + cc
                for t in range(5):
                    t0, n = TILES[t]
                    TA, TX = TL[(it % 2) * 2], TL[(it % 2) * 2 + 1]
                    sfx = "%d" % (it % 2)
                    it += 1
                    ba, bat = self.next_bank()
                    bx, bxt = self.next_bank()
                    P.pe(MMS([(ba[:, :n], self.BD[:, pi * 2, :], XCB[:, t0:t0 + n], True, True),
                              (bx[:, :n], self.BD[:, pi * 2 + 1, :], XCB[:, t0:t0 + n], True, True)]),
                         reads=["bd", "xcb.%d" % t], writes=[bat, bxt])
                    P.act(ACTF(TA[:, :n], ba[:, :n], AF.Tanh, bias=self.hba[:, pi:pi + 1], scale=0.5),
                          reads=[bat, "hba"], writes=["ta" + sfx])
                    P.act(ACTF(B0[:, t0:t0 + n], TA[:, :n], AF.Exp, bias=self.hm[:, pi:pi + 1],
                               scale=self.hm[:, pi:pi + 1]), reads=["ta" + sfx, "hm"], writes=b0t)
                    P.act(ACTF(B1[:, t0:t0 + n], B0[:, t0:t0 + n], AF.Square), reads=b0t, writes=["b1"])
                    P.act(ACTF(TX[:, :n], bx[:, :n], AF.Tanh, bias=self.hbx[:, pi:pi + 1], scale=0.5),
                          reads=[bxt, "hbx"], writes=["tx" + sfx])
                    P.dve(STT(B2[:, t0:t0 + n], TX[:, :n], 1.0, XC[:, t0:t0 + n], ALU.add, ALU.mult),
                          reads=["tx" + sfx, "xc.%d" % t], writes=["b2"])
                    self.mods_step(4)
                P.act(ACTF(B1, B1, AF.Sqrt, bias=1.0, scale=-(1.0 - 2.0 ** -23)), reads=["b1"], writes=["b1"])
                P.dve(STT(B1, B2, 0.5, B1, ALU.mult, ALU.mult), reads=["b1", "b2"], writes=["b1"])
                if d == 0:
                    P.dve(SCAN(B2[:, NLAT:T], B0[:, NLAT:T], B1[:, NLAT:T], 0.0), reads=b0t + ["b1"], writes=["b2"])
                    P.dve(SCAN(B2[:, 0:NLAT], B0[:, 0:NLAT], B1[:, 0:NLAT], B2[:, T - 1:T]), reads=b0t + ["b1", "b2"],
                          writes=["b2"])
                    P.pool(CP(HS[:, 0:nout], B2[:, 0:nout]), reads=["b2"], writes=["hs"])
                else:
                    P.dve(SCAN(B2[:, NLAT:T][:, ::-1], B0[:, NLAT:T][:, ::-1], B1[:, NLAT:T][:, ::-1], 0.0),
                          reads=b0t + ["b1"], writes=["b2"])
                    P.dve(SCAN(B2[:, 0:NLAT][:, ::-1], B0[:, 0:NLAT][:, ::-1], B1[:, 0:NLAT][:, ::-1], B2[:, NLAT:NLAT + 1]),
                          reads=b0t + ["b1", "b2"], writes=["b2"])
                    P.pool(TT(HS[:, 0:nout], HS[:, 0:nout], B2[:, 0:nout], ALU.add), reads=["b2", "hs"], writes=["hs"])
            def ev_g(t, ps, btok):
                t0, n = TILES[t]
                P.act(ACTF(B0[:, t0:t0 + n], ps, AF.Identity), reads=[btok], writes=b0t)
            self.linear_fm(W, OFF_RGG + cc * 128, [0, 1, 2, 3] + ([4] if need_ctx else []), ev_g)
            G, Q = B0[:, 0:nout], B1[:, 0:nout]
            P.dve(TT(Q, G, G, ALU.mult), reads=b0t, writes=["b1"])
            P.dve(TS(Q, Q, 0.044715, ALU.mult, 1.0, ALU.add), reads=["b1"], writes=["b1"])
            P.dve(TT(Q, Q, G, ALU.mult), reads=["b1"] + b0t, writes=["b1"])
            P.act(ACTF(Q, Q, AF.Tanh, scale=0.7978845608028654), reads=["b1"], writes=["b1"])
            P.dve(STT(Q, Q, 1.0, G, ALU.add, ALU.mult), reads=["b1"] + b0t, writes=["b1"])
            P.dve(STT(self.MX[:, cc, 0:nout], Q, 0.5, HS[:, 0:nout], ALU.mult, ALU.mult), reads=["b1", "hs"],
                  writes=["mx.%d" % t for t in range(5 if need_ctx else 4)])
        self.mods_step(1000)

    def ssd_mixer(self, b, l):
        P = self.P
        need_ctx = (l == 0)
        W = self.w_in[l]
        o = 0
        RAW = self.scr_f(o, 2312); o += 2312
        TMP = self.scr_f(o, T); o += T
        XS = self.scr_b(o, NBLK * 256).rearrange("p (i d) -> p i d", i=NBLK); o += NBLK * 128
        BF = self.scr_b(o, T); o += T // 2
        CF = self.scr_b(o, T); o += T // 2
        BT = self.scr_b(o, NBLK * 128).rearrange("p (i d) -> p i d", i=NBLK); o += NBLK * 64
        SZ = self.scr_b(o, NBLK * 256).rearrange("p (i d) -> p i d", i=NBLK); o += NBLK * 128
        DTR = self.scr_f(o, 144); o += 144
        DTv = self.scr_f(o, 144); o += 144
        ADT = self.scr_f(o, 144); o += 144
        SPS = self.scr_f(o, 576); o += 576
        assert o <= SCRW, o
        DT3 = DTv.rearrange("p (i e) -> p i e", i=NBLK)
        ADT3 = ADT.rearrange("p (i e) -> p i e", i=NBLK)
        self.rot = [0, 1, 2, 3]
        for ch in range(4):
            for (a0, a1) in ((0, 2), (2050, 2053), (2309, 2312)):
                P.pool(MSET(RAW[:, a0:a1], 0.0), writes=["rxpad"])

            def ev_x(t, ps, btok):
                t0, n = TILES[t]
                off = t0 + 2 if t < 4 else t0 + 5
                P.act(ACTF(RAW[:, off:off + n], ps, AF.Identity), reads=[btok], writes=["rx.%d" % t])
            self.linear_fm(W, OFF_XBC + ch * 128, [0, 1, 2, 3, 4], ev_x)
            for (x0, r0, n, rt, wt) in ((0, 0, NLAT, ["rx.0", "rx.1", "rx.2", "rx.3"], ["xc.0", "xc.1", "xc.2", "xc.3"]),
                                        (NLAT, 2051, NCTX, ["rx.4"], ["xc.4"])):
                wo = PRM_OFF["scw"] + (l * 4 + ch) * 4
                P.act(ACTF(TMP[:, x0:x0 + n], RAW[:, r0:r0 + n], AF.Identity, bias=self.prmcol("scb", l * 4 + ch),
                           scale=self.prm[:, wo:wo + 1]), reads=rt + ["rxpad", "prm"], writes=wt)
                for k in range(1, 4):
                    P.dve(STT(TMP[:, x0:x0 + n], RAW[:, r0 + k:r0 + k + n], self.prm[:, wo + k:wo + k + 1],
                              TMP[:, x0:x0 + n], ALU.mult, ALU.add), reads=rt + wt + ["rxpad", "prm"], writes=wt)
                if ch == 3:
                    P.act(ACTF(CF[:, x0:x0 + n], TMP[:, x0:x0 + n], AF.Silu), reads=wt, writes=["cf"])
                else:
                    P.act(ACTF(TMP[:, x0:x0 + n], TMP[:, x0:x0 + n], AF.Silu), reads=wt, writes=wt)
                    if ch == 2:
                        P.pool(CP(BF[:, x0:x0 + n], TMP[:, x0:x0 + n]), reads=wt, writes=["bf"])
            if ch < 3:
                for i0 in range(0, NBLK, 4):
                    blks = list(range(i0, min(i0 + 4, NBLK)))
                    bank, btok = self.next_bank()
                    P.pe(TRS([(bank[:, gi * 128:(gi + 1) * 128], TMP[:, i * 128:(i + 1) * 128]) for gi, i in
                              enumerate(blks)], self.ident[:]),
                         reads=["xc.%d" % min(i // 4, 4) for i in blks] + ["ident"], writes=[btok])
                    src = bank[:, 0:len(blks) * 128].rearrange("p (i d) -> p i d", i=len(blks))
                    if ch < 2:
                        P.dve(CP(XS[:, i0:i0 + len(blks), ch * 128:(ch + 1) * 128], src), reads=[btok], writes=["xs"])
                    else:
                        P.dve(CP(BT[:, i0:i0 + len(blks), :], src), reads=[btok], writes=["bt"])
        for zc in range(2):
            def ev_z(blks, ps, btok, zc=zc):
                i0 = blks[0]
                P.act(ACTF(SZ[:, i0:i0 + len(blks), zc * 128:(zc + 1) * 128],
                           ps.rearrange("p (i d) -> p i d", i=len(blks)), AF.Silu), reads=[btok], writes=["sz"])
            self.linear_tm(W, OFF_SZ + zc * 128, 128, list(range(NBLK)), ev_z, 4)

        def ev_dt(blks, ps, btok):
            do = ROW_OFF["dtb"] + l * 8
            P.dve(TT(DTR.rearrange("p (i e) -> p i e", i=NBLK), ps.rearrange("p (i e) -> p i e", i=NBLK),
                     self.rowp[:, do:do + 8].unsqueeze(1).to_broadcast([128, NBLK, 8]), ALU.add),
                  reads=[btok, "rowp"], writes=["dtr"])
        self.linear_tm(W, OFF_DT, 8, list(range(NBLK)), ev_dt, NBLK)
        self.softplus(DTv, DTR, 144, SPS, (["dtr"], ["dt"]))
        P.dve(TT(ADT3, DT3, self.aneg[:, l * 8:l * 8 + 8].unsqueeze(1).to_broadcast([128, NBLK, 8]), ALU.mult),
              reads=["dt", "aneg"], writes=["adt"])
        P.barrier()
        if self.o.get("ssd_prep_only"):
            P.pool(MSET(self.MX[:, 2:4, :], 0.0), writes=["mx.%d" % t for t in range(5)])
            return
        xf = self.xn[:].bitcast(F32)
        o = 0
        Y = xf[:, o:o + NBLK * 256].rearrange("p (i d) -> p i d", i=NBLK); o += NBLK * 256
        S = []
        for d in range(2):
            sd = {}
            sd["LT"] = xf[:, o:o + 512]; o += 512
            sd["D1"] = xf[:, o:o + 512]; o += 512
            sd["GM"] = xf[:, o:o + 256]; o += 256
            sd["T1"] = xf[:, o:o + 256]; o += 256
            sd["H"] = xf[:, o:o + 128]; o += 128
            sd["SM"] = xf[:, o:o + 64]; o += 64
            sd["Mb"] = self.xn[:, 2 * o:2 * o + 512].rearrange("p (h t) -> p h t", h=4); o += 256
            sd["XD"] = self.xn[:, 2 * o:2 * o + 256].rearrange("p (h d) -> p h d", h=4); o += 128
            sd["XDW"] = self.xn[:, 2 * o:2 * o + 256].rearrange("p (h d) -> p h d", h=4); o += 128
            sd["HB"] = self.xn[:, 2 * o:2 * o + 128]; o += 64
            S.append(sd)
        assert o <= 9216, o
        BFm = [self.scr_b(g * 1152, T) for g in range(2)]
        CFm = [self.scr_b(2304 + g * 1152, T) for g in range(2)]
        P.pool(MSET(self.scr_f(0, 4608), 0.0), writes=["bfm", "cfm"])
        for g in range(2):
            P.pool(CP(BFm[g][64 * g:64 * g + 64, :], BF[64 * g:64 * g + 64, :]), reads=["bf", "bfm"], writes=["bfm"])
            P.pool(CP(CFm[g][64 * g:64 * g + 64, :], CF[64 * g:64 * g + 64, :]), reads=["cf", "cfm"], writes=["cfm"])
        fo = 13700
        VV = self.scr_f(fo, 256)
        OT = self.scr_f(fo + 256, 256)
        JK = self.scr_f(fo + 512, 256)
        FS = self.scr_f(fo + 768, 8)
        assert fo + 776 <= SCRW
        rsd, ssq = FS[:, 0:2], FS[:, 2:4]
        PB = self.pb
        dsk = self.rowp[:, ROW_OFF["dsk"] + l * 4:ROW_OFF["dsk"] + l * 4 + 4]
        sng = self.rowp[:, ROW_OFF["sng"] + l * 256:ROW_OFF["sng"] + (l + 1) * 256]
        ytk = ["y.%d" % k for k in range(NBLK)]
        P.dve(TT(Y.rearrange("p i (h d) -> p i h d", h=4), XS.rearrange("p i (h d) -> p i h d", h=4),
                 dsk.unsqueeze(1).unsqueeze(3).to_broadcast([128, NBLK, 4, 64]), ALU.mult), reads=["xs", "rowp"], writes=ytk)
        for d in range(2):
            P.dve(MSET(S[d]["H"], 0.0), writes=["H%d" % d])
            P.dve(MSET(S[d]["HB"], 0.0), writes=["HB%d" % d])
        orders = [[16, 17] + list(range(16)), [17, 16] + list(range(15, -1, -1))]

        def chunk_step(d, k):
            sd = S[d]
            sx = "%d" % d
            U = self.uf if d == 0 else self.ub
            utok = "uf" if d == 0 else "ub"
            edge = 127 if d == 0 else 0
            A, B, C, Dk = (PB[4 * d + i] for i in range(4))
            At, Bt, Ct, Dt = ("pb%d" % (4 * d + i) for i in range(4))
            cst, ecs, wdec, dtw, dch = (sd["SM"][:, i * 4:(i + 1) * 4] for i in range(5))
            LT, D1, GM, T1, H, HB, Mb, XD, XDW = (sd[n] for n in ("LT", "D1", "GM", "T1", "H", "HB", "Mb", "XD", "XDW"))
            want_y = (k < 16) or need_ctx
            blk = slice(k * 128, (k + 1) * 128)
            adt = ADT3[:, k, d * 4:d * 4 + 4]
            mm = [(Dk[:, 0:4], U[:], adt, True, True)]
            for h in range(4):
                mm.append((A[:, h * 128:(h + 1) * 128], ADT3[:, k, d * 4 + h:d * 4 + h + 1].to_broadcast([128, 128]),
                           U[:], True, True))
            P.pe(MMS(mm), reads=["adt", utok], writes=[Dt, At])
            P.dve(CP(cst, Dk[:, 0:4]), reads=[Dt], writes=["cst" + sx])
            csb = A[:, 0:512].rearrange("p (h t) -> p h t", h=4)
            dts = DT3[:, k, d * 4:d * 4 + 4]
            XS4 = XS[:, k, :].rearrange("p (h d) -> p h d", h=4)
            if want_y:
                P.dve(TT(D1.rearrange("p (h t) -> p h t", h=4), csb, cst.unsqueeze(2).to_broadcast([128, 4, 128]),
                         ALU.subtract), reads=[At, "cst" + sx], writes=["d1" + sx])
                P.dve(STT(D1, D1, -1.0, D1, ALU.mult, ALU.min), reads=["d1" + sx], writes=["d1" + sx])
                P.act(ACTF(LT, D1, AF.Exp), reads=["d1" + sx], writes=["lt" + sx])
                P.pe(MMS([(B[:, g * 128:(g + 1) * 128], BFm[g][:, blk], CF[:, blk], True, True) for g in range(2)]),
                     reads=["bfm", "cf"], writes=[Bt])
                P.dve(TT(GM.rearrange("p (g t) -> p g t", g=2), B[:, 0:256].rearrange("p (g t) -> p g t", g=2),
                         U[:].unsqueeze(1).to_broadcast([128, 2, 128]), ALU.mult), reads=[Bt, utok], writes=["gm" + sx])
                P.dve(TT(Mb.rearrange("p (g a) t -> p g a t", g=2), LT.rearrange("p (g a t) -> p g a t", g=2, a=2),
                         GM.rearrange("p (g t) -> p g t", g=2).unsqueeze(2).to_broadcast([128, 2, 2, 128]), ALU.mult),
                      reads=["lt" + sx, "gm" + sx], writes=["mb" + sx])
                P.dve(TT(XD, XS4, dts.unsqueeze(2).to_broadcast([128, 4, 64]), ALU.mult), reads=["xs", "dt"],
                      writes=["xd" + sx])
                P.pe(MMS([(C[:, h * 64:(h + 1) * 64], Mb[:, h, :], XD[:, h, :], True, True) for h in range(4)]
                         + [(C[:, 256 + g * 128:256 + (g + 1) * 128], CFm[g][:, blk], HB[:, :], True, True)
                            for g in range(2)]),
                     reads=["mb" + sx, "xd" + sx, "cfm", "HB" + sx], writes=[Ct])
                P.act(ACTF(ecs, cst, AF.Exp), reads=["cst" + sx], writes=["ecs" + sx])
                P.dve(TT(T1.rearrange("p (h d) -> p h d", h=4), C[:, 256:512].rearrange("p (h d) -> p h d", h=4),
                         ecs.unsqueeze(2).to_broadcast([128, 4, 64]), ALU.mult), reads=[Ct, "ecs" + sx], writes=["t1" + sx])
                P.dve(TT(T1, T1, C[:, 0:256], ALU.add), reads=["t1" + sx, Ct], writes=["t1" + sx])
                P.dve(TT(Y[:, k, :], Y[:, k, :], T1, ALU.add), reads=["t1" + sx, "y.%d" % k], writes=["y.%d" % k])
            P.dve(TT(wdec, csb[:, :, edge], cst, ALU.subtract), reads=[At, "cst" + sx], writes=["wdec" + sx])
            P.act(ACTF(wdec, wdec, AF.Exp), reads=["wdec" + sx], writes=["wdec" + sx])
            P.dve(TT(dtw, wdec, dts, ALU.mult), reads=["wdec" + sx, "dt"], writes=["dtw" + sx])
            P.dve(TT(XDW, XS4, dtw.unsqueeze(2).to_broadcast([128, 4, 64]), ALU.mult), reads=["xs", "dtw" + sx],
                  writes=["xdw" + sx])
            P.pe(MMS([(Dk[64 * g:64 * g + 64, 128:256], BT[:, k, 64 * g:64 * g + 64],
                       XDW[:, 2 * g:2 * g + 2, :].rearrange("p a d -> p (a d)"), True, True) for g in range(2)]),
                 reads=["bt", "xdw" + sx], writes=[Dt])
            for g in range(2):
                P.act(ACTF(dch[64 * g:64 * g + 64, 0:2], csb[64 * g:64 * g + 64, 2 * g:2 * g + 2, edge], AF.Exp),
                      reads=[At], writes=["dch" + sx])
            P.dve(TT(H.rearrange("p (a d) -> p a d", a=2), H.rearrange("p (a d) -> p a d", a=2),
                     dch[:, 0:2].unsqueeze(2).to_broadcast([128, 2, 64]), ALU.mult), reads=["H" + sx, "dch" + sx],
                  writes=["H" + sx])
            P.dve(TT(H, H, Dk[:, 128:256], ALU.add), reads=["H" + sx, Dt], writes=["H" + sx])
            P.dve(CP(HB, H), reads=["H" + sx], writes=["HB" + sx])

        def finalize(k):
            P.dve(TT(VV, Y[:, k, :], SZ[:, k, :], ALU.mult), reads=["y.%d" % k, "sz"], writes=["vv"])
            for g in range(2):
                P.act(ACTF(JK[:, g * 128:(g + 1) * 128], VV[:, g * 128:(g + 1) * 128], AF.Square,
                           accum_out=ssq[:, g:g + 1]), reads=["vv"], writes=["jk", "ssq"])
            P.act(ACTF(rsd, ssq, AF.Ln, bias=EPS, scale=1.0 / 128.0), reads=["ssq"], writes=["rsd"])
            P.act(ACTF(rsd, rsd, AF.Exp, scale=-0.5), reads=["rsd"], writes=["rsd"])
            for g in range(2):
                P.dve(STT(OT[:, g * 128:(g + 1) * 128], VV[:, g * 128:(g + 1) * 128], rsd[:, g:g + 1],
                          sng[:, g * 128:(g + 1) * 128], ALU.mult, ALU.mult), reads=["vv", "rsd", "rowp"], writes=["ot"])
            P.pe(TRS([(PB[5][:, g * 128:(g + 1) * 128], OT[:, g * 128:(g + 1) * 128]) for g in range(2)], self.ident[:]),
                 reads=["ot", "ident"], writes=["pb5"])
            P.act(ACTF(self.MX[:, 2:4, k * 128:(k + 1) * 128], PB[5][:, 0:256].rearrange("p (g t) -> p g t", g=2),
                       AF.Identity), reads=["pb5"], writes=["mx.%d" % min(k // 4, 4)])

        done = {}
        for j in range(NBLK):
            for d in range(2):
                k = orders[d][j]
                chunk_step(d, k)
                done[k] = done.get(k, 0) + 1
            for d in range(2):
                k = orders[d][j]
                if done.get(k) == 2 and ((k < 16) or need_ctx):
                    finalize(k)
                    done[k] = 3

    def layer(self, b, l):
        P = self.P
        o = self.o
        need_ctx = (l == 0)
        allt = [0, 1, 2, 3, 4]
        outt = allt if need_ctx else [0, 1, 2, 3]
        if o.get("mix", True):
            self.norm(b, l, 0, allt)
            P.barrier()
            if o.get("da", True):
                self.da_mixer(b, l)
                self.wout_partial(b, l, 512, outt)
                P.barrier()
            if o.get("rg", True) or o.get("ssd", True):
                if o.get("rg", True):
                    self.rg_mixer(b, l)
                else:
                    P.pool(MSET(self.MX[:, 0:2, :], 0.0), writes=["mx.%d" % t for t in range(5)])
                P.barrier()
                if o.get("ssd", True):
                    self.ssd_mixer(b, l)
                else:
                    P.pool(MSET(self.MX[:, 2:4, :], 0.0), writes=["mx.%d" % t for t in range(5)])
                self.wout_partial(b, l, 0, outt)
                P.barrier()
        self.mods_step(1000)
        if o.get("ffn", True):
            self.norm(b, l, 1, outt)
            P.barrier()
            self.ffn(b, l, outt)
            P.barrier()

    def run(self):
        self.setup()
        for b in range(self.NB):
            self.load_x(b)
            self.P.barrier()
            for l in range(self.o.get("layers", 2)):
                self.layer(b, l)
            self.final(b)
            self.P.barrier()


def build_nc(NB=2, opts=None):
    nc = bass.Bass("TRN2", target_bir_lowering=False)
    with contextlib.ExitStack() as st:
        P = Prog(nc, st)
        K = Kern(nc, P, NB, opts or {})
        K.run()
        P.emit(final_reads=["out"])
    return nc


_NC_CACHE = {}


def kernel(x, c, ctx, c_ctx, w_mod, b_mod, norm1_g, w_in, rg_conv_w, rg_conv_b, rg_w_a, rg_b_a,
           rg_w_x, rg_b_x, rg_lambda, ssd_conv_w, ssd_conv_b, ssd_dt_bias, ssd_a_log, ssd_d,
           ssd_norm_g, da_lambda, da_subln_g, w_out, norm2_g, w_gate, w_up, w_down, final_norm_g,
           _opts=None, _ncores=8):
    inp = dict(w_mod=w_mod, b_mod=b_mod, norm1_g=norm1_g, w_in=w_in, rg_conv_w=rg_conv_w, rg_conv_b=rg_conv_b,
               rg_w_a=rg_w_a, rg_b_a=rg_b_a, rg_w_x=rg_w_x, rg_b_x=rg_b_x, rg_lambda=rg_lambda,
               ssd_conv_w=ssd_conv_w, ssd_conv_b=ssd_conv_b, ssd_dt_bias=ssd_dt_bias, ssd_a_log=ssd_a_log,
               ssd_d=ssd_d, ssd_norm_g=ssd_norm_g, da_lambda=da_lambda, da_subln_g=da_subln_g, w_out=w_out,
               norm2_g=norm2_g, w_gate=w_gate, w_up=w_up, w_down=w_down, final_norm_g=final_norm_g)
    shared = pack_shared(inp)
    x = np.asarray(x, np.float32)
    ctx = np.asarray(ctx, np.float32)
    c = np.asarray(c, np.float32)
    c_ctx = np.asarray(c_ctx, np.float32)
    NB = 2
    in_maps = []
    for i in range(_ncores):
        cc = np.stack([c[2 * i], c[2 * i + 1], c_ctx], axis=0)
        ct = np.ascontiguousarray(cc.reshape(3, 8, 128).transpose(2, 1, 0))
        m = dict(shared)
        m["x"] = np.ascontiguousarray(x[2 * i:2 * i + 2])
        m["ctx"] = np.ascontiguousarray(ctx[2 * i:2 * i + 2])
        m["ct"] = ct
        in_maps.append(m)
    key = repr(sorted((_opts or {}).items()))
    if key not in _NC_CACHE:
        _NC_CACHE[key] = build_nc(NB, _opts)
    nc = _NC_CACHE[key]
    res = run_bass_kernel_spmd(nc, in_maps, core_ids=list(range(_ncores)))
    out = np.concatenate([np.asarray(r["out"], np.float32) for r in res.results], axis=0)
    return out
```

```python
import contextlib
import math
import numpy as np
import concourse.bass as bass
import concourse.mybir as mybir
from concourse.bass_utils import run_bass_kernel_spmd

F32 = mybir.dt.float32
BF16 = mybir.dt.bfloat16
ALU = mybir.AluOpType
AF = mybir.ActivationFunctionType

D = 1024
NLAT = 2048
NCTX = 256
T = NLAT + NCTX
NBLK = T // 128
TILES = [(0, 512), (512, 512), (1024, 512), (1536, 512), (2048, 256)]
DFF = 2816
DIN = 2824
DINX = DIN + 1024
EPS = 1e-6
OFF_RGX, OFF_RGG, OFF_SZ, OFF_XBC, OFF_DT, OFF_Q, OFF_K, OFF_V = 0, 256, 512, 768, 1280, 1288, 1800, 2312
OFF_QS, OFF_KS = DIN, DIN + 512
SCRW = 14848


class _Op:
    __slots__ = ("eng", "fn", "waits", "tok", "clock", "dma", "idx")


class Prog:
    ENGS = ("pe", "act", "dve", "pool", "sp")

    def __init__(self, nc, stack, n_dma_sems=12):
        self.nc = nc
        self.stack = stack
        self.ops = {e: [] for e in self.ENGS}
        self.sem = {e: stack.enter_context(nc.semaphore("S_" + e)) for e in self.ENGS}
        self.cnt = {e: 0 for e in self.ENGS}
        self.know = {e: {} for e in self.ENGS}
        self.last_op = {e: None for e in self.ENGS}
        self.dma_sems = {}
        for q in ("sp", "pool"):
            self.dma_sems[q] = [[stack.enter_context(nc.semaphore("D_%s%d" % (q, i))), 0, None]
                                for i in range(n_dma_sems)]
        self.dma_rr = {q: 0 for q in self.dma_sems}
        self.last_w = {}
        self.readers = {}
        self.nops = 0
        self.nwaits = 0

    def sb(self, name, shape, dt=F32):
        return self.stack.enter_context(self.nc.sbuf_tensor(name, list(shape), dt))

    def ps(self, name, shape, dt=F32):
        return self.stack.enter_context(self.nc.psum_tensor(name, list(shape), dt))

    def _need(self, e, d, waits):
        if d is None:
            return
        if e == "pe" and d.eng == "pe" and not d.dma:
            return
        sem, val = d.tok
        k = self.know[e]
        if k.get(sem, 0) >= val:
            return
        waits.append((sem, val))
        for s, v in d.clock.items():
            if k.get(s, 0) < v:
                k[s] = v

    def op(self, eng, fn, reads=(), writes=(), dma=False, extra_deps=()):
        o = _Op()
        o.eng = eng
        o.fn = fn
        o.dma = dma
        o.idx = self.nops
        self.nops += 1
        deps = {}
        psum_r = [r for r in reads if r.startswith("pb") and r not in writes]
        if psum_r:
            writes = list(writes) + psum_r
            reads = [r for r in reads if r not in psum_r]
        for r in reads:
            d = self.last_w.get(r)
            if d is not None:
                deps[d.idx] = d
        for w in writes:
            d = self.last_w.get(w)
            if d is not None:
                deps[d.idx] = d
            for d in self.readers.get(w, ()):
                deps[d.idx] = d
        for d in extra_deps:
            if d is not None:
                deps[d.idx] = d
        waits = []
        for i in sorted(deps, reverse=True):
            self._need(eng, deps[i], waits)
        if dma:
            pool = self.dma_sems[eng]
            slot = pool[self.dma_rr[eng] % len(pool)]
            self.dma_rr[eng] += 1
            if slot[2] is not None:
                self._need(eng, slot[2], waits)
            slot[1] += 16
            o.tok = (slot[0], slot[1])
            slot[2] = o
        elif fn is not None:
            self.cnt[eng] += 1
            o.tok = (self.sem[eng], self.cnt[eng])
        else:
            o.tok = None
        o.waits = waits
        self.nwaits += len(waits)
        o.clock = dict(self.know[eng])
        if o.tok is not None:
            o.clock[o.tok[0]] = o.tok[1]
            if not dma:
                self.last_op[eng] = o
        for w in writes:
            self.last_w[w] = o
            self.readers[w] = []
        for r in reads:
            if r not in writes:
                self.readers.setdefault(r, []).append(o)
        self.ops[eng].append(o)
        return o

    def pe(self, fn, reads=(), writes=()):
        return self.op("pe", fn, reads, writes)

    def act(self, fn, reads=(), writes=()):
        return self.op("act", fn, reads, writes)

    def dve(self, fn, reads=(), writes=()):
        return self.op("dve", fn, reads, writes)

    def pool(self, fn, reads=(), writes=()):
        return self.op("pool", fn, reads, writes)

    def dma(self, fn, reads=(), writes=(), q="sp"):
        return self.op(q, fn, reads, writes, dma=True)

    def barrier(self):
        pend = [self.last_op[e] for e in self.ENGS if self.last_op[e] is not None]
        for q in self.dma_sems:
            for slot in self.dma_sems[q]:
                if slot[2] is not None:
                    pend.append(slot[2])
        for e in self.ENGS:
            self.op(e, None, extra_deps=pend)

    def emit(self, final_reads=()):
        nc = self.nc
        fin_waits = []
        for r in final_reads:
            self._need("sp", self.last_w.get(r), fin_waits)

        def run(e, engine, extra=()):
            for o in self.ops[e]:
                for (s, v) in o.waits:
                    engine.wait_ge(s, v)
                if o.fn is None:
                    continue
                ins = o.fn(engine)
                ins.then_inc(o.tok[0], 16 if o.dma else 1)
            for (s, v) in extra:
                engine.wait_ge(s, v)

        with nc.Block() as block:
            @block.sync
            def _(eng):
                run("sp", eng, fin_waits)

            @block.gpsimd
            def _(eng):
                run("pool", eng)

            @block.scalar
            def _(eng):
                run("act", eng)

            @block.vector
            def _(eng):
                run("dve", eng)

            @block.tensor
            def _(eng):
                run("pe", eng)


def MMS(lst):
    def f(e):
        ins = None
        for (o, l, r, st, sp) in lst:
            ins = e.matmul(o, lhsT=l, rhs=r, start=st, stop=sp)
        return ins
    return f


def TRS(lst, ident):
    def f(e):
        ins = None
        for (o, i) in lst:
            ins = e.transpose(out=o, in_=i, identity=ident)
        return ins
    return f


def ACTF(out, in_, func, bias=0.0, scale=1.0, accum_out=None):
    if accum_out is None:
        return lambda e: e.activation(out=out, in_=in_, func=func, bias=bias, scale=scale)
    return lambda e: e.activation(out=out, in_=in_, func=func, bias=bias, scale=scale, accum_out=accum_out)


def TT(out, a, b, op):
    return lambda e: e.tensor_tensor(out=out, in0=a, in1=b, op=op)


def TS(out, a, s1, op0, s2=None, op1=None):
    if op1 is None:
        return lambda e: e.tensor_scalar(out=out, in0=a, scalar1=s1, scalar2=None, op0=op0)
    return lambda e: e.tensor_scalar(out=out, in0=a, scalar1=s1, scalar2=s2, op0=op0, op1=op1)


def STT(out, in0, scalar, in1, op0, op1):
    return lambda e: e.scalar_tensor_tensor(out=out, in0=in0, scalar=scalar, in1=in1, op0=op0, op1=op1)


def CP(out, in_):
    return lambda e: e.tensor_copy(out=out, in_=in_)


def RECIP(out, in_):
    return lambda e: e.reciprocal(out=out, in_=in_)


def MSET(ap, v):
    return lambda e: e.memset(ap, v)


def DMA(out, in_):
    return lambda e: e.dma_start(out=out, in_=in_)


def SCAN(out, a, b, init):
    return lambda e: e.tensor_tensor_scan(out=out, data0=a, data1=b, initial=init, op0=ALU.mult, op1=ALU.add)


PRM_SPEC = [("n1g", 16), ("n2g", 16), ("fng", 8), ("bmod", 96), ("rgcw", 16), ("rgcb", 4), ("rgba", 8),
            ("rgbx", 8), ("rglam", 8), ("scw", 32), ("scb", 8), ("subg", 2)]
PRM_OFF = {}
_o = 0
for _n, _w in PRM_SPEC:
    PRM_OFF[_n] = _o
    _o += _w
NPRM = _o
ROW_SPEC = [("dtb", 16), ("alog", 16), ("dsk", 8), ("sng", 512), ("dal", 512)]
ROW_OFF = {}
_o = 0
for _n, _w in ROW_SPEC:
    ROW_OFF[_n] = _o
    _o += _w
NROW = _o


def _fm(v):
    v = np.asarray(v, np.float32)
    lead = v.shape[:-1]
    n = v.shape[-1] // 128
    v = v.reshape(lead + (n, 128))
    return np.moveaxis(v, -1, 0).reshape(128, -1)


def pack_shared(inp):
    f = lambda k: np.asarray(inp[k], np.float32)
    prm = np.zeros((128, NPRM), np.float32)

    def put(name, arr):
        o = PRM_OFF[name]
        prm[:, o:o + arr.shape[1]] = arr

    put("n1g", _fm(f("norm1_g")))
    put("n2g", _fm(f("norm2_g")))
    put("fng", _fm(f("final_norm_g")))
    put("bmod", _fm(f("b_mod")))
    w = f("rg_conv_w").reshape(2, 4, 2, 128).transpose(3, 0, 2, 1).reshape(128, 16)
    put("rgcw", w)
    put("rgcb", _fm(f("rg_conv_b")))
    put("rgba", _fm(f("rg_b_a")))
    put("rgbx", _fm(f("rg_b_x")))
    put("rglam", _fm(f("rg_lambda")))
    w = f("ssd_conv_w").reshape(2, 4, 4, 128).transpose(3, 0, 2, 1).reshape(128, 32)
    put("scw", w)
    put("scb", _fm(f("ssd_conv_b")))
    put("subg", _fm(f("da_subln_g")))
    row = np.zeros((NROW,), np.float32)
    row[ROW_OFF["dtb"]:ROW_OFF["dtb"] + 16] = f("ssd_dt_bias").reshape(-1)
    row[ROW_OFF["alog"]:ROW_OFF["alog"] + 16] = f("ssd_a_log").reshape(-1)
    row[ROW_OFF["dsk"]:ROW_OFF["dsk"] + 8] = f("ssd_d").reshape(-1)
    row[ROW_OFF["sng"]:ROW_OFF["sng"] + 512] = f("ssd_norm_g").reshape(-1)
    row[ROW_OFF["dal"]:ROW_OFF["dal"] + 512] = f("da_lambda").reshape(-1)
    rowp = np.ascontiguousarray(np.broadcast_to(row[None, :], (128, NROW)))
    bd = np.zeros((128, 16, 128), np.float32)
    wa, wx = f("rg_w_a"), f("rg_w_x")
    for l in range(2):
        for d in range(2):
            for cc in range(2):
                for ax, wsrc in enumerate((wa, wx)):
                    idx = ((l * 2 + d) * 2 + cc) * 2 + ax
                    for blk in range(2):
                        bd[blk * 64:(blk + 1) * 64, idx, blk * 64:(blk + 1) * 64] = wsrc[l, d, 2 * cc + blk]
    j = np.arange(64)
    sw = np.where((j % 32) < 16, j + 16, j - 16)
    win = f("w_in")
    qidx = np.concatenate([OFF_Q + h * 128 + c * 64 + sw for h in range(4) for c in range(2)])
    kidx = np.concatenate([OFF_K + h * 128 + c * 64 + sw for h in range(4) for c in range(2)])
    winx = np.ascontiguousarray(np.concatenate([win, win[:, :, qidx], win[:, :, kidx]], axis=2))
    ident = np.eye(128, dtype=np.float32)
    tt = np.arange(128)
    uf = (tt[:, None] <= tt[None, :]).astype(np.float32)
    ub = (tt[:, None] >= tt[None, :]).astype(np.float32)
    t = np.arange(NLAT)
    row_i = (t // 64).astype(np.float32)
    col_i = (t % 64).astype(np.float32)
    half = 32
    inv_freq = np.power(np.float32(10000.0), -np.arange(0, half, 2, dtype=np.float32) / np.float32(half)).astype(np.float32)
    ang_r = row_i[:, None] * inv_freq[None, :]
    ang_c = col_i[:, None] * inv_freq[None, :]
    rope = np.zeros((128, 2, NLAT), np.float32)
    for p in range(128):
        jj = p % 64
        ang = ang_r if jj < 32 else ang_c
        i = jj % 16
        rope[p, 0] = np.cos(ang[:, i])
        sgn = -1.0 if (jj % 32) < 16 else 1.0
        rope[p, 1] = sgn * np.sin(ang[:, i])
    return dict(prm=prm, rowp=rowp, bd=bd, w_in_x=winx, ident=ident, uf=uf, ub=ub, rope=rope,
                w_mod=f("w_mod"), w_out=f("w_out"), w_gate=f("w_gate"), w_up=f("w_up"), w_down=f("w_down"))


class Kern:
    def __init__(self, nc, P, NB, opts):
        self.nc = nc
        self.P = P
        self.NB = NB
        self.o = opts
        dt_in = lambda n, s: nc.dram_tensor(n, list(s), F32, kind="ExternalInput").ap()
        self.x = dt_in("x", [NB, NLAT, D])
        self.ctx = dt_in("ctx", [NB, NCTX, D])
        self.ct = dt_in("ct", [128, 8, 3])
        self.prm_d = dt_in("prm", [128, NPRM])
        self.rowp_d = dt_in("rowp", [128, NROW])
        self.bd_d = dt_in("bd", [128, 16, 128])
        self.w_in = dt_in("w_in_x", [2, D, DINX])
        self.ident_d = dt_in("ident", [128, 128])
        self.uf_d = dt_in("uf", [128, 128])
        self.ub_d = dt_in("ub", [128, 128])
        self.rope_d = dt_in("rope", [128, 2, NLAT])
        self.w_mod = dt_in("w_mod", [2, D, 6 * D])
        self.w_out = dt_in("w_out", [2, D, D])
        self.w_gate = dt_in("w_gate", [2, D, DFF])
        self.w_up = dt_in("w_up", [2, D, DFF])
        self.w_down = dt_in("w_down", [2, DFF, D])
        self.out = nc.dram_tensor("out", [NB, NLAT, D], F32, kind="ExternalOutput").ap()
        sb = P.sb
        self.resid = sb("resid", [128, 8 * T])
        self.R3 = self.resid[:].rearrange("p (c t) -> p c t", c=8)
        self.xn = sb("xn", [128, 8 * T], BF16)
        self.XN = self.xn[:].rearrange("p (c t) -> p c t", c=8)
        self.mixb = sb("mixb", [128, 4 * T], BF16)
        self.MX = self.mixb[:].rearrange("p (c t) -> p c t", c=4)
        self.scr = sb("scr", [128, SCRW])
        self.WB = [sb("wb%d" % i, [128, 1024], BF16) for i in range(4)]
        self.wrr = 0
        self.ident = sb("ident_s", [128, 128])
        self.uf = sb("uf_s", [128, 128])
        self.ub = sb("ub_s", [128, 128])
        self.ones_f = sb("ones_f", [128, 128])
        self.ones_b = sb("ones_b", [128, 128], BF16)
        self.prm = sb("prm_s", [128, NPRM])
        self.rowp = sb("rowp_s", [128, NROW])
        self.bd = sb("bd_s", [128, 16 * 128], BF16)
        self.BD = self.bd[:].rearrange("p (i j) -> p i j", i=16)
        self.modt = sb("modt", [128, 2 * 48 * 3])
        self.MT = self.modt[:].rearrange("p (l j n) -> p l j n", l=2, j=48)
        self.amod = sb("amod", [128, 2 * 2 * 3 * 8])
        self.AM = self.amod[:].rearrange("p (l w n c) -> p l w n c", l=2, w=2, n=3)
        self.small = sb("small", [128, 256])
        self.sml_o = 0
        self.pb = [P.ps("pb%d" % i, [128, 512]) for i in range(8)]
        self.rot = [0, 1, 2, 3]
        self.rr = 0

    def prmcol(self, name, i):
        o = PRM_OFF[name] + i
        return self.prm[:, o:o + 1]

    def salloc(self, n):
        o = self.sml_o
        self.sml_o += n
        assert self.sml_o <= 256
        return self.small[:, o:o + n]

    def scr_f(self, off, n):
        return self.scr[:, off:off + n]

    def scr_b(self, off, n):
        assert n % 2 == 0
        return self.scr[:, off:off + n // 2].bitcast(BF16)

    def next_bank(self):
        i = self.rot[self.rr % len(self.rot)]
        self.rr += 1
        return self.pb[i], "pb%d" % i

    def load_w(self, wsrc, k0, kch, col0, ncols=128):
        i = self.wrr % len(self.WB)
        self.wrr += 1
        wb = self.WB[i][:, 0:kch * ncols].rearrange("p (c n) -> p c n", c=kch)
        src = wsrc.rearrange("(c p) n -> p c n", p=128)[:, k0:k0 + kch, col0:col0 + ncols]
        tok = "wb%d" % i
        self.P.dma(DMA(wb, src), writes=[tok], q="pool")
        return wb, tok

    def linear_fm(self, wsrc, col0, tiles, evac, kch=8, k0=0, rhs=None, rtok=None, ncols=128):
        wb, wtok = self.load_w(wsrc, k0, kch, col0, ncols)
        if rhs is None:
            rhs = lambda kc, t0, n: self.XN[:, kc, t0:t0 + n]
            rtok = lambda t: ["xn.%d" % t]
        for t in tiles:
            t0, n = TILES[t]
            bank, btok = self.next_bank()
            out = bank[0:ncols, 0:n]
            mms = [(out, wb[:, kc, :], rhs(kc, t0, n), kc == 0, kc == kch - 1) for kc in range(kch)]
            self.P.pe(MMS(mms), reads=[wtok] + rtok(t), writes=[btok])
            evac(t, out, btok)

    def linear_fm2(self, wsrc, colA, colB, tiles, evac2, extraA=(), evacA=None):
        wa, atok = self.load_w(wsrc, 0, 8, colA)
        wb2, btok2 = self.load_w(wsrc, 0, 8, colB)
        for t in tiles:
            t0, n = TILES[t]
            ba, bat = self.next_bank()
            bb, bbt = self.next_bank()
            self.P.pe(MMS([(ba[:, 0:n], wa[:, kc, :], self.XN[:, kc, t0:t0 + n], kc == 0, kc == 7) for kc in range(8)]
                          + [(bb[:, 0:n], wb2[:, kc, :], self.XN[:, kc, t0:t0 + n], kc == 0, kc == 7) for kc in range(8)]),
                      reads=[atok, btok2, "xn.%d" % t], writes=[bat, bbt])
            evac2(t, ba[:, 0:n], bat, bb[:, 0:n], bbt)
        for t in extraA:
            t0, n = TILES[t]
            ba, bat = self.next_bank()
            self.P.pe(MMS([(ba[:, 0:n], wa[:, kc, :], self.XN[:, kc, t0:t0 + n], kc == 0, kc == 7) for kc in range(8)]),
                      reads=[atok, "xn.%d" % t], writes=[bat])
            evacA(t, ba[:, 0:n], bat)

    def linear_tm(self, wsrc, col0, ncols, blocks, evac, grp):
        wb, wtok = self.load_w(wsrc, 0, 8, col0, ncols)
        for g0 in range(0, len(blocks), grp):
            blks = blocks[g0:g0 + grp]
            bank, btok = self.next_bank()
            mms = []
            rt = set()
            for gi, i in enumerate(blks):
                out = bank[:, gi * ncols:(gi + 1) * ncols]
                rt.add("xn.%d" % min(i // 4, 4))
                for kc in range(8):
                    mms.append((out, self.XN[:, kc, i * 128:(i + 1) * 128], wb[:, kc, :], kc == 0, kc == 7))
            self.P.pe(MMS(mms), reads=[wtok] + sorted(rt), writes=[btok])
            evac(blks, bank[:, 0:len(blks) * ncols], btok)

    def softplus(self, out, x, n, s0, toks):
        P = self.P
        ax, e, z, p = (s0[:, i * n:(i + 1) * n] for i in range(4))
        rd, wr = toks
        tk = ["spl.tmp"]
        P.dve(STT(ax, x, -1.0, x, ALU.mult, ALU.max), reads=rd, writes=tk)
        P.act(ACTF(e, ax, AF.Exp, scale=-1.0), reads=tk, writes=tk)
        P.dve(TS(z, e, 2.0, ALU.add), reads=tk, writes=tk)
        P.dve(RECIP(z, z), reads=tk, writes=tk)
        P.dve(TT(z, z, e, ALU.mult), reads=tk, writes=tk)
        P.dve(TT(e, z, z, ALU.mult), reads=tk, writes=tk)
        P.dve(TS(p, e, 1.0 / 13.0, ALU.mult, 1.0 / 11.0, ALU.add), reads=tk, writes=tk)
        for cst in (1.0 / 9.0, 1.0 / 7.0, 1.0 / 5.0, 1.0 / 3.0, 1.0):
            P.dve(TT(p, p, e, ALU.mult), reads=tk, writes=tk)
            P.dve(TS(p, p, cst, ALU.add), reads=tk, writes=tk)
        P.dve(TT(p, p, z, ALU.mult), reads=tk, writes=tk)
        P.dve(TS(ax, x, 0.0, ALU.max), reads=rd + tk, writes=tk)
        P.dve(STT(out, p, 2.0, ax, ALU.mult, ALU.add), reads=tk, writes=wr)

    def mods_emit(self, l, jj):
        P = self.P
        wb, wtok = self.load_w(self.w_mod[l], 0, 8, jj * 128)
        bi = 5 + (self.mod_i % 2)
        self.mod_i += 1
        out = self.pb[bi][:, 0:3]
        P.pe(MMS([(out, wb[:, kc, :], self.ctb[:, kc, :], kc == 0, kc == 7) for kc in range(8)]),
             reads=[wtok, "ctb"], writes=["pb%d" % bi])
        P.dve(TS(self.MT[:, l, jj, :], out, self.prmcol("bmod", l * 48 + jj), ALU.add), reads=["pb%d" % bi, "prm"],
              writes=["modt"])

    def mods_finish(self, l, w):
        gname = "n1g" if w == 0 else "n2g"
        go = PRM_OFF[gname] + l * 8
        jj0 = 8 + 24 * w
        for n in range(3):
            self.P.dve(STT(self.AM[:, l, w, n, :], self.MT[:, l, jj0:jj0 + 8, n], 1.0, self.prm[:, go:go + 8],
                           ALU.add, ALU.mult), reads=["modt", "prm"], writes=["amod"])

    def mods_step(self, n):
        while n > 0 and self.mod_q:
            l, jj = self.mod_q.pop(0)
            self.mods_emit(l, jj)
            n -= 1
            if not self.mod_q:
                self.mods_finish(0, 1)
                self.mods_finish(1, 0)
                self.mods_finish(1, 1)

    def setup(self):
        P = self.P
        P.dma(DMA(self.ident[:], self.ident_d), writes=["ident"])
        P.dma(DMA(self.uf[:], self.uf_d), writes=["uf"])
        P.dma(DMA(self.ub[:], self.ub_d), writes=["ub"])
        P.dma(DMA(self.prm[:], self.prm_d), writes=["prm"])
        P.dma(DMA(self.rowp[:], self.rowp_d), writes=["rowp"])
        P.dma(DMA(self.BD, self.bd_d), writes=["bd"], q="pool")
        P.pool(MSET(self.ones_f[:], 1.0), writes=["ones_f"])
        P.pool(MSET(self.ones_b[:], 1.0), writes=["ones_b"])
        ctile = self.scr_f(0, 24).rearrange("p (k n) -> p k n", k=8)
        sig = self.scr_f(24, 24).rearrange("p (k n) -> p k n", k=8)
        P.dma(DMA(ctile, self.ct), writes=["ct"])
        P.act(ACTF(sig, ctile, AF.Sigmoid), reads=["ct"], writes=["sig"])
        P.dve(TT(ctile, ctile, sig, ALU.mult), reads=["ct", "sig"], writes=["ct"])
        self.ctb = self.salloc(12).bitcast(BF16)[:, 0:24].rearrange("p (k n) -> p k n", k=8)
        P.dve(CP(self.ctb, ctile), reads=["ct"], writes=["ctb"])
        self.mod_q = [(0, jj) for jj in range(24, 48)] + [(1, jj) for jj in range(48)]
        self.mod_i = 0
        for jj in range(24):
            self.mods_emit(0, jj)
        self.mods_finish(0, 0)
        self.m8sp = self.salloc(8)
        nl = self.salloc(8)
        lo = PRM_OFF["rglam"]
        P.dve(TS(nl, self.prm[:, lo:lo + 8], -1.0, ALU.mult), reads=["prm"], writes=["nl"])
        self.softplus(self.m8sp, nl, 8, self.scr_f(8300, 32), (["nl"], ["m8sp"]))
        P.dve(TS(self.m8sp, self.m8sp, -8.0, ALU.mult), reads=["m8sp"], writes=["m8sp"])
        self.hm = self.salloc(8)
        self.hba = self.salloc(8)
        self.hbx = self.salloc(8)
        P.dve(TS(self.hm, self.m8sp, 0.5, ALU.mult), reads=["m8sp"], writes=["hm"])
        oa, ox = PRM_OFF["rgba"], PRM_OFF["rgbx"]
        P.dve(TS(self.hba, self.prm[:, oa:oa + 8], 0.5, ALU.mult), reads=["prm"], writes=["hba"])
        P.dve(TS(self.hbx, self.prm[:, ox:ox + 8], 0.5, ALU.mult), reads=["prm"], writes=["hbx"])
        self.aneg = self.salloc(16)
        ao = ROW_OFF["alog"]
        P.act(ACTF(self.aneg, self.rowp[:, ao:ao + 16], AF.Exp), reads=["rowp"], writes=["aneg"])
        P.dve(TS(self.aneg, self.aneg, -1.0, ALU.mult), reads=["aneg"], writes=["aneg"])
        self.nlam = self.salloc(2)
        self.gsc = self.salloc(2)
        pr = self.scr_f(8400, 128)
        sm = self.salloc(4)
        for l in range(2):
            do = ROW_OFF["dal"] + l * 256
            lam_init = 0.8 - 0.6 * math.exp(-0.3 * l)
            for i in range(2):
                P.dve(TT(pr[:, 0:64], self.rowp[:, do + 128 * i:do + 128 * i + 64],
                         self.rowp[:, do + 128 * i + 64:do + 128 * i + 128], ALU.mult), reads=["rowp"], writes=["pr"])
                P.dve(lambda e, o_=sm[:, 2 * l + i:2 * l + i + 1], i_=pr[:, 0:64]: e.reduce_sum(
                    out=o_, in_=i_, axis=mybir.AxisListType.X), reads=["pr"], writes=["sm"])
            P.act(ACTF(sm[:, 2 * l:2 * l + 2], sm[:, 2 * l:2 * l + 2], AF.Exp), reads=["sm"], writes=["sm"])
            P.dve(TT(self.nlam[:, l:l + 1], sm[:, 2 * l + 1:2 * l + 2], sm[:, 2 * l:2 * l + 1], ALU.subtract),
                  reads=["sm"], writes=["nlam"])
            P.dve(TS(self.nlam[:, l:l + 1], self.nlam[:, l:l + 1], -lam_init, ALU.add), reads=["nlam"], writes=["nlam"])
            P.dve(TS(self.gsc[:, l:l + 1], self.prmcol("subg", l), 1.0 - lam_init, ALU.mult), reads=["prm"],
                  writes=["gsc"])
        P.barrier()

    def load_x(self, b):
        P = self.P
        STG = [self.scr_f(i * 1024, 1024) for i in range(2)]
        self.rot = [0, 1, 2, 3]
        for i in range(NBLK):
            src = self.x[b, i * 128:(i + 1) * 128, :] if i < 16 else self.ctx[b, (i - 16) * 128:(i - 15) * 128, :]
            stg = STG[i % 2]
            stok = "stg%d" % (i % 2)
            P.dma(DMA(stg, src), writes=[stok])
            t = min(i // 4, 4)
            for half in range(2):
                bank, btok = self.next_bank()
                P.pe(TRS([(bank[:, j * 128:(j + 1) * 128], stg[:, (half * 4 + j) * 128:(half * 4 + j + 1) * 128])
                          for j in range(4)], self.ident[:]), reads=[stok, "ident"], writes=[btok])
                dst = self.R3[:, half * 4:half * 4 + 4, i * 128:(i + 1) * 128]
                srcp = bank[:, 0:512].rearrange("p (c t) -> p c t", c=4)
                wr = ["r.%d.%d" % (c, t) for c in range(half * 4, half * 4 + 4)]
                if half == 0:
                    P.act(ACTF(dst, srcp, AF.Identity), reads=[btok], writes=wr)
                else:
                    P.dve(CP(dst, srcp), reads=[btok], writes=wr)

    def rstd_tile(self, t, RS, SQ):
        P = self.P
        t0, n = TILES[t]
        bank = self.pb[7]
        for c in range(8):
            sq = SQ[c % 2]
            P.dve(TT(sq[:, :n], self.R3[:, c, t0:t0 + n], self.R3[:, c, t0:t0 + n], ALU.mult),
                  reads=["r.%d.%d" % (c, t)], writes=["sq%d" % (c % 2)])
            P.pe(MMS([(bank[:, :n], self.ones_b[:], sq[:, :n], c == 0, c == 7)]), reads=["sq%d" % (c % 2), "ones_b"],
                 writes=["pb7"])
        P.act(ACTF(RS[:, :n], bank[:, :n], AF.Sqrt, bias=EPS, scale=1.0 / D), reads=["pb7"], writes=["rs"])
        P.dve(RECIP(RS[:, :n], RS[:, :n]), reads=["rs"], writes=["rs"])

    def norm(self, b, l, w, tiles):
        P = self.P
        SQ = [self.scr_b(i * 256, 512) for i in range(2)]
        RS = self.scr_f(512, 512)
        TM = [self.scr_f(1024 + i * 512, 512) for i in range(2)]
        for t in tiles:
            t0, n = TILES[t]
            nn = b if t < 4 else 2
            self.rstd_tile(t, RS, SQ)
            for c in range(8):
                tm = TM[c % 2]
                P.dve(TT(tm[:, :n], self.R3[:, c, t0:t0 + n], RS[:, :n], ALU.mult), reads=["r.%d.%d" % (c, t), "rs"],
                      writes=["tm%d" % (c % 2)])
                P.act(ACTF(self.XN[:, c, t0:t0 + n], tm[:, :n], AF.Identity,
                           bias=self.MT[:, l, 24 * w + c, nn:nn + 1], scale=self.AM[:, l, w, nn, c:c + 1]),
                      reads=["tm%d" % (c % 2), "modt", "amod"], writes=["xn.%d" % t])

    def wout_partial(self, b, l, rows0, tiles):
        P = self.P
        self.rot = [0, 1, 2, 3]
        for c in range(8):
            def evac(t, ps, btok, c=c):
                t0, n = TILES[t]
                nn = b if t < 4 else 2
                dst = self.R3[:, c, t0:t0 + n]
                P.dve(STT(dst, ps, self.MT[:, l, 16 + c, nn:nn + 1], dst, ALU.mult, ALU.add),
                      reads=[btok, "modt", "r.%d.%d" % (c, t)], writes=["r.%d.%d" % (c, t)])
            self.linear_fm(self.w_out[l], c * 128, tiles, evac, kch=4, k0=rows0 // 128,
                           rhs=lambda kc, t0, n: self.MX[:, kc, t0:t0 + n], rtok=lambda t: ["mx.%d" % t])

    def ffn(self, b, l, tiles):
        P = self.P
        HH = self.scr_b(0, 6 * T).rearrange("p (j t) -> p j t", j=6)
        SG = [self.scr_f(7000 + i * 512, 512) for i in range(2)]
        groups = [(0, 6), (6, 6), (12, 5), (17, 5)]
        it = 0
        for (j0, J) in groups:
            for jl in range(J):
                j = j0 + jl
                wg, gtok = self.load_w(self.w_gate[l], 0, 8, j * 128)
                wu, utok = self.load_w(self.w_up[l], 0, 8, j * 128)
                for t in tiles:
                    t0, n = TILES[t]
                    bg, bgt = self.pb[it % 2], "pb%d" % (it % 2)
                    bu, but = self.pb[2 + it % 2], "pb%d" % (2 + it % 2)
                    sg = SG[it % 2]
                    sgt = "sg%d" % (it % 2)
                    it += 1
                    P.pe(MMS([(bg[:, :n], wg[:, kc, :], self.XN[:, kc, t0:t0 + n], kc == 0, kc == 7) for kc in range(8)]),
                         reads=[gtok, "xn.%d" % t], writes=[bgt])
                    P.pe(MMS([(bu[:, :n], wu[:, kc, :], self.XN[:, kc, t0:t0 + n], kc == 0, kc == 7) for kc in range(8)]),
                         reads=[utok, "xn.%d" % t], writes=[but])
                    P.act(ACTF(sg[:, :n], bg[:, :n], AF.Silu), reads=[bgt], writes=[sgt])
                    P.dve(TT(HH[:, jl, t0:t0 + n], sg[:, :n], bu[:, :n], ALU.mult), reads=[sgt, but],
                          writes=["hh.%d.%d" % (jl, t)])
            for c in range(8):
                wd, dtok = self.load_w(self.w_down[l], j0, J, c * 128)
                for t in tiles:
                    t0, n = TILES[t]
                    nn = b if t < 4 else 2
                    bi = 4 + (it % 3)
                    it += 1
                    bank, btok = self.pb[bi], "pb%d" % bi
                    P.pe(MMS([(bank[:, :n], wd[:, jl, :], HH[:, jl, t0:t0 + n], jl == 0, jl == J - 1) for jl in range(J)]),
                         reads=[dtok] + ["hh.%d.%d" % (jl, t) for jl in range(J)], writes=[btok])
                    dst = self.R3[:, c, t0:t0 + n]
                    P.dve(STT(dst, bank[:, :n], self.MT[:, l, 40 + c, nn:nn + 1], dst, ALU.mult, ALU.add),
                          reads=[btok, "modt", "r.%d.%d" % (c, t)], writes=["r.%d.%d" % (c, t)])

    def final(self, b):
        P = self.P
        SQ = [self.scr_b(i * 256, 512) for i in range(2)]
        RS = self.scr_f(512, 512)
        TM = [self.scr_f(1024 + i * 512, 512) for i in range(2)]
        YN = self.scr_f(2048, 4096).rearrange("p (c t) -> p c t", c=8)
        OST = [self.scr_f(6144 + i * 1024, 1024) for i in range(2)]
        self.rot = [0, 1, 2, 3]
        oi = 0
        for t in range(4):
            t0, n = TILES[t]
            self.rstd_tile(t, RS, SQ)
            for c in range(8):
                tm = TM[c % 2]
                P.dve(TT(tm[:, :n], self.R3[:, c, t0:t0 + n], RS[:, :n], ALU.mult), reads=["r.%d.%d" % (c, t), "rs"],
                      writes=["tm%d" % (c % 2)])
                P.act(ACTF(YN[:, c, :], tm[:, :n], AF.Identity, scale=self.prmcol("fng", c)),
                      reads=["tm%d" % (c % 2), "prm"], writes=["yn.%d" % c])
            for j in range(4):
                ost = OST[oi % 2]
                otok = "ost%d" % (oi % 2)
                oi += 1
                for half in range(2):
                    bank, btok = self.next_bank()
                    P.pe(TRS([(bank[:, cc * 128:(cc + 1) * 128], YN[:, half * 4 + cc, j * 128:(j + 1) * 128])
                              for cc in range(4)], self.ident[:]),
                         reads=["yn.%d" % (half * 4 + cc) for cc in range(4)] + ["ident"], writes=[btok])
                    if half == 0:
                        P.act(ACTF(ost[:, 0:512], bank[:, 0:512], AF.Identity), reads=[btok], writes=[otok])
                    else:
                        P.dve(CP(ost[:, 512:1024], bank[:, 0:512]), reads=[btok], writes=[otok])
                r0 = t0 + j * 128
                P.dma(DMA(self.out[b, r0:r0 + 128, :], ost), reads=[otok], writes=["out"])

    def da_mixer(self, b, l):
        P = self.P
        need_ctx = (l == 0)
        o = 0
        ROPE = self.scr_f(o, 4096).rearrange("p (a t) -> p a t", a=2); o += 4096
        QR = self.scr_b(o, T); o += T // 2
        KR = self.scr_b(o, T); o += T // 2
        VT = self.scr_b(o, T).rearrange("p (i d) -> p i d", i=NBLK); o += T // 2
        T1 = [self.scr_f(o + i * 512, 512) for i in range(2)]; o += 1024
        T2 = [self.scr_f(o + i * 512, 512) for i in range(2)]; o += 1024
        PT = [self.scr_b(o + i * 256, 512) for i in range(4)]; o += 1024
        RC = self.scr_f(o, 512); o += 512
        TC = [self.scr_f(o + i * 512, 512) for i in range(2)]; o += 1024
        O2 = self.scr_f(o, 512); o += 512
        OV = [self.scr_f(o + i * 512, 512) for i in range(2)]; o += 1024
        assert o <= SCRW, o
        P.dma(DMA(ROPE, self.rope_d), writes=["rope"])
        lat = [0, 1, 2, 3]
        W = self.w_in[l]
        ti = [0]
        pend = {"A": None, "B": None}
        fcount = [0]

        def emitA():
            f = pend["A"]
            pend["A"] = None
            if f is not None:
                f()

        def emitB():
            f = pend["B"]
            pend["B"] = None
            if f is not None:
                f()

        def make_final(h, qt, q0, qn):
            fi = fcount[0] % 2
            fcount[0] += 1
            ov = OV[fi][:, :qn]
            ovt = "ov%d" % fi

            def fa():
                for c in range(2):
                    P.dve(RECIP(RC[:, :qn], T1[c][:, :qn]), reads=["t1.%d" % c], writes=["rc"])
                    P.dve(TT(TC[c][:, :qn], TC[c][:, :qn], RC[:, :qn], ALU.mult), reads=["tc%d" % c, "rc"],
                          writes=["tc%d" % c])
                P.dve(STT(ov, TC[1][:, :qn], self.nlam[:, l:l + 1], TC[0][:, :qn], ALU.mult, ALU.add),
                      reads=["tc0", "tc1", "nlam"], writes=[ovt])
                P.dve(TT(O2[:, :qn], ov, ov, ALU.mult), reads=[ovt], writes=["o2"])

            def fb():
                P.pe(MMS([(self.pb[3][:, :qn], self.ones_f[:], O2[:, :qn], True, True)]), reads=["o2", "ones_f"],
                     writes=["pb3"])
                P.act(ACTF(O2[:, :qn], self.pb[3][:, :qn], AF.Ln, bias=EPS, scale=1.0 / 128.0), reads=["pb3"],
                      writes=["o2"])
                P.act(ACTF(O2[:, :qn], O2[:, :qn], AF.Exp, scale=-0.5), reads=["o2"], writes=["o2"])
                P.dve(TT(ov, ov, O2[:, :qn], ALU.mult), reads=[ovt, "o2"], writes=[ovt])
                P.act(ACTF(self.MX[:, h, q0:q0 + qn], ov, AF.Identity, scale=self.gsc[:, l:l + 1]),
                      reads=[ovt, "gsc"], writes=["mx.%d" % qt])
            return fa, fb

        for h in range(4):
            self.rot = [0, 1, 2, 3]

            def ev_rope(dst, dtok):
                def ev(t, pa, ta, pb_, tb):
                    t0, n = TILES[t]
                    i = ti[0] % 2
                    ti[0] += 1
                    P.dve(TT(T1[i][:, :n], pa, ROPE[:, 0, t0:t0 + n], ALU.mult), reads=[ta, "rope"], writes=["t1.%d" % i])
                    P.dve(TT(T2[i][:, :n], pb_, ROPE[:, 1, t0:t0 + n], ALU.mult), reads=[tb, "rope"], writes=["t2.%d" % i])
                    P.pool(TT(dst[:, t0:t0 + n], T1[i][:, :n], T2[i][:, :n], ALU.add), reads=["t1.%d" % i, "t2.%d" % i],
                           writes=["%s.%d" % (dtok, t)])
                return ev

            def ev_plain(dst, dtok):
                def ev(t, ps, btok):
                    t0, n = TILES[t]
                    P.act(ACTF(dst[:, t0:t0 + n], ps, AF.Identity), reads=[btok], writes=["%s.%d" % (dtok, t)])
                return ev
            emitA()
            self.linear_fm2(W, OFF_Q + h * 128, OFF_QS + h * 128, lat, ev_rope(QR, "qr"), [4] if need_ctx else [],
                            ev_plain(QR, "qr"))
            emitB()
            self.linear_fm2(W, OFF_K + h * 128, OFF_KS + h * 128, lat, ev_rope(KR, "kr"), [4], ev_plain(KR, "kr"))

            def ev_v(blks, ps, btok):
                i0 = blks[0]
                P.act(ACTF(VT[:, i0:i0 + len(blks), :], ps.rearrange("p (i d) -> p i d", i=len(blks)), AF.Identity),
                      reads=[btok], writes=["vt.%d" % i for i in blks])
            self.linear_tm(W, OFF_V + h * 128, 128, list(range(NBLK)), ev_v, 4)

            SB = [(0, 1), (2, 3)]
            for qt in (lat + ([4] if need_ctx else [])):
                q0, qn = TILES[qt]
                kts = list(range(NBLK)) if qt < 4 else [16, 17]
                nk = len(kts)

                def emit_s(ii):
                    kt = kts[ii]
                    pa, pb_ = SB[ii % 2]
                    P.pe(MMS([(self.pb[pa][:, :qn], KR[0:64, kt * 128:(kt + 1) * 128], QR[0:64, q0:q0 + qn], True, True),
                              (self.pb[pb_][:, :qn], KR[64:128, kt * 128:(kt + 1) * 128], QR[64:128, q0:q0 + qn], True,
                               True)]),
                         reads=["kr.%d" % min(kt // 4, 4), "qr.%d" % qt], writes=["pb%d" % pa, "pb%d" % pb_])
                emit_s(0)
                for ii, kt in enumerate(kts):
                    if ii == 0:
                        emitA()
                    if ii == min(4, nk - 1):
                        emitB()
                    if ii + 1 < nk:
                        emit_s(ii + 1)
                    first, last = (ii == 0), (ii == nk - 1)
                    for c in range(2):
                        sb = SB[ii % 2][c]
                        pi_ = (2 * ii + c) % 4
                        P.act(ACTF(PT[pi_][:, :qn], self.pb[sb][:, :qn], AF.Exp, scale=0.125), reads=["pb%d" % sb],
                              writes=["pt%d" % pi_])
                        P.pe(MMS([(self.pb[4 + c][:, :qn], VT[:, kt, :], PT[pi_][:, :qn], first, last),
                                  (self.pb[6 + c][:, :qn], self.ones_b[:], PT[pi_][:, :qn], first, last)]),
                             reads=["pt%d" % pi_, "vt.%d" % kt, "ones_b"], writes=["pb%d" % (4 + c), "pb%d" % (6 + c)])
                emitA()
                emitB()
                for c in range(2):
                    P.dve(CP(TC[c][:, :qn], self.pb[4 + c][:, :qn]), reads=["pb%d" % (4 + c)], writes=["tc%d" % c])
                    P.act(ACTF(T1[c][:, :qn], self.pb[6 + c][:, :qn], AF.Identity), reads=["pb%d" % (6 + c)],
                          writes=["t1.%d" % c])
                pend["A"], pend["B"] = make_final(h, qt, q0, qn)
        emitA()
        emitB()

    def rg_mixer(self, b, l):
        P = self.P
        need_ctx = (l == 0)
        o = 0
        RX = self.scr_f(o, 2312); o += 2312
        XC = self.scr_f(o, T); o += T
        XCB = self.scr_b(o, T); o += T // 2
        HS = self.scr_f(o, T); o += T
        B1 = self.scr_f(o, T); o += T
        B2 = self.scr_f(o, T); o += T
        TL = [self.scr_f(o + i * 512, 512) for i in range(4)]; o += 2048
        assert o <= SCRW, o
        B0 = RX[:, 0:T]
        b0t = ["rx.%d" % t for t in range(5)] + ["rxpad"]
        W = self.w_in[l]
        self.rot = [0, 1, 2, 3]
        it = 0
        nout = T if need_ctx else NLAT
        for cc in range(2):
            for (a0, a1) in ((0, 2), (2050, 2053), (2309, 2312)):
                P.pool(MSET(RX[:, a0:a1], 0.0), writes=["rxpad"])

            def ev_x(t, ps, btok):
                t0, n = TILES[t]
                off = t0 + 2 if t < 4 else t0 + 5
                P.act(ACTF(RX[:, off:off + n], ps, AF.Identity), reads=[btok], writes=["rx.%d" % t])
            self.linear_fm(W, OFF_RGX + cc * 128, [0, 1, 2, 3, 4], ev_x)
            for (x0, r0, n, rt, wt) in ((0, 0, NLAT, ["rx.0", "rx.1", "rx.2", "rx.3"], ["xc.0", "xc.1", "xc.2", "xc.3"]),
                                        (NLAT, 2051, NCTX, ["rx.4"], ["xc.4"])):
                wo = PRM_OFF["rgcw"] + (l * 2 + cc) * 4
                P.act(ACTF(XC[:, x0:x0 + n], RX[:, r0:r0 + n], AF.Identity, bias=self.prmcol("rgcb", l * 2 + cc),
                           scale=self.prm[:, wo:wo + 1]), reads=rt + ["rxpad", "prm"], writes=wt)
                for k in range(1, 4):
                    P.dve(STT(XC[:, x0:x0 + n], RX[:, r0 + k:r0 + k + n], self.prm[:, wo + k:wo + k + 1], XC[:, x0:x0 + n],
                              ALU.mult, ALU.add), reads=rt + wt + ["rxpad", "prm"], writes=wt)
                P.pool(CP(XCB[:, x0:x0 + n], XC[:, x0:x0 + n]), reads=wt, writes=["xcb.%d" % t for t in
                                                                                  ([0, 1, 2, 3] if x0 == 0 else [4])])
            xct = ["xc.%d" % t for t in range(5)]
            for d in range(2):
                pi = (l * 2 + d) * 2 + cc
                for t in range(5):
                    t0, n = TILES[t]
                    TA, TX = TL[(it % 2) * 2], TL[(it % 2) * 2 + 1]
                    sfx = "%d" % (it % 2)
                    it += 1
                    ba, bat = self.next_bank()
                    bx, bxt = self.next_bank()
                    P.pe(MMS([(ba[:, :n], self.BD[:, pi * 2, :], XCB[:, t0:t0 + n], True, True),
                              (bx[:, :n], self.BD[:, pi * 2 + 1, :], XCB[:, t0:t0 + n], True, True)]),
                         reads=["bd", "xcb.%d" % t], writes=[bat, bxt])
                    P.act(ACTF(TA[:, :n], ba[:, :n], AF.Tanh, bias=self.hba[:, pi:pi + 1], scale=0.5),
                          reads=[bat, "hba"], writes=["ta" + sfx])
                    P.act(ACTF(B0[:, t0:t0 + n], TA[:, :n], AF.Exp, bias=self.hm[:, pi:pi + 1],
                               scale=self.hm[:, pi:pi + 1]), reads=["ta" + sfx, "hm"], writes=b0t)
                    P.act(ACTF(B1[:, t0:t0 + n], B0[:, t0:t0 + n], AF.Square), reads=b0t, writes=["b1"])
                    P.act(ACTF(TX[:, :n], bx[:, :n], AF.Tanh, bias=self.hbx[:, pi:pi + 1], scale=0.5),
                          reads=[bxt, "hbx"], writes=["tx" + sfx])
                    P.dve(STT(B2[:, t0:t0 + n], TX[:, :n], 1.0, XC[:, t0:t0 + n], ALU.add, ALU.mult),
                          reads=["tx" + sfx, "xc.%d" % t], writes=["b2"])
                    self.mods_step(4)
                P.act(ACTF(B1, B1, AF.Sqrt, bias=1.0, scale=-(1.0 - 2.0 ** -23)), reads=["b1"], writes=["b1"])
                P.dve(STT(B1, B2, 0.5, B1, ALU.mult, ALU.mult), reads=["b1", "b2"], writes=["b1"])
                if d == 0:
                    P.dve(SCAN(B2[:, NLAT:T], B0[:, NLAT:T], B1[:, NLAT:T], 0.0), reads=b0t + ["b1"], writes=["b2"])
                    P.dve(SCAN(B2[:, 0:NLAT], B0[:, 0:NLAT], B1[:, 0:NLAT], B2[:, T - 1:T]), reads=b0t + ["b1", "b2"],
                          writes=["b2"])
                    P.pool(CP(HS[:, 0:nout], B2[:, 0:nout]), reads=["b2"], writes=["hs"])
                else:
                    P.dve(SCAN(B2[:, NLAT:T][:, ::-1], B0[:, NLAT:T][:, ::-1], B1[:, NLAT:T][:, ::-1], 0.0),
                          reads=b0t + ["b1"], writes=["b2"])
                    P.dve(SCAN(B2[:, 0:NLAT][:, ::-1], B0[:, 0:NLAT][:, ::-1], B1[:, 0:NLAT][:, ::-1], B2[:, NLAT:NLAT + 1]),
                          reads=b0t + ["b1", "b2"], writes=["b2"])
                    P.pool(TT(HS[:, 0:nout], HS[:, 0:nout], B2[:, 0:nout], ALU.add), reads=["b2", "hs"], writes=["hs"])
            def ev_g(t, ps, btok):
                t0, n = TILES[t]
                P.act(ACTF(B0[:, t0:t0 + n], ps, AF.Identity), reads=[btok], writes=b0t)
            self.linear_fm(W, OFF_RGG + cc * 128, [0, 1, 2, 3] + ([4] if need_ctx else []), ev_g)
            G, Q = B0[:, 0:nout], B1[:, 0:nout]
            P.dve(TT(Q, G, G, ALU.mult), reads=b0t, writes=["b1"])
            P.dve(TS(Q, Q, 0.044715, ALU.mult, 1.0, ALU.add), reads=["b1"], writes=["b1"])
            P.dve(TT(Q, Q, G, ALU.mult), reads=["b1"] + b0t, writes=["b1"])
            P.act(ACTF(Q, Q, AF.Tanh, scale=0.7978845608028654), reads=["b1"], writes=["b1"])
            P.dve(STT(Q, Q, 1.0, G, ALU.add, ALU.mult), reads=["b1"] + b0t, writes=["b1"])
            P.dve(STT(self.MX[:, cc, 0:nout], Q, 0.5, HS[:, 0:nout], ALU.mult, ALU.mult), reads=["b1", "hs"],
                  writes=["mx.%d" % t for t in range(5 if need_ctx else 4)])
        self.mods_step(1000)

    def ssd_mixer(self, b, l):
        P = self.P
        need_ctx = (l == 0)
        W = self.w_in[l]
        o = 0
        RAW = self.scr_f(o, 2312); o += 2312
        TMP = self.scr_f(o, T); o += T
        XS = self.scr_b(o, NBLK * 256).rearrange("p (i d) -> p i d", i=NBLK); o += NBLK * 128
        BF = self.scr_b(o, T); o += T // 2
        CF = self.scr_b(o, T); o += T // 2
        BT = self.scr_b(o, NBLK * 128).rearrange("p (i d) -> p i d", i=NBLK); o += NBLK * 64
        SZ = self.scr_b(o, NBLK * 256).rearrange("p (i d) -> p i d", i=NBLK); o += NBLK * 128
        DTR = self.scr_f(o, 144); o += 144
        DTv = self.scr_f(o, 144); o += 144
        ADT = self.scr_f(o, 144); o += 144
        SPS = self.scr_f(o, 576); o += 576
        assert o <= SCRW, o
        DT3 = DTv.rearrange("p (i e) -> p i e", i=NBLK)
        ADT3 = ADT.rearrange("p (i e) -> p i e", i=NBLK)
        self.rot = [0, 1, 2, 3]
        for ch in range(4):
            for (a0, a1) in ((0, 2), (2050, 2053), (2309, 2312)):
                P.pool(MSET(RAW[:, a0:a1], 0.0), writes=["rxpad"])

            def ev_x(t, ps, btok):
                t0, n = TILES[t]
                off = t0 + 2 if t < 4 else t0 + 5
                P.act(ACTF(RAW[:, off:off + n], ps, AF.Identity), reads=[btok], writes=["rx.%d" % t])
            self.linear_fm(W, OFF_XBC + ch * 128, [0, 1, 2, 3, 4], ev_x)
            for (x0, r0, n, rt, wt) in ((0, 0, NLAT, ["rx.0", "rx.1", "rx.2", "rx.3"], ["xc.0", "xc.1", "xc.2", "xc.3"]),
                                        (NLAT, 2051, NCTX, ["rx.4"], ["xc.4"])):
                wo = PRM_OFF["scw"] + (l * 4 + ch) * 4
                P.act(ACTF(TMP[:, x0:x0 + n], RAW[:, r0:r0 + n], AF.Identity, bias=self.prmcol("scb", l * 4 + ch),
                           scale=self.prm[:, wo:wo + 1]), reads=rt + ["rxpad", "prm"], writes=wt)
                for k in range(1, 4):
                    P.dve(STT(TMP[:, x0:x0 + n], RAW[:, r0 + k:r0 + k + n], self.prm[:, wo + k:wo + k + 1],
                              TMP[:, x0:x0 + n], ALU.mult, ALU.add), reads=rt + wt + ["rxpad", "prm"], writes=wt)
                if ch == 3:
                    P.act(ACTF(CF[:, x0:x0 + n], TMP[:, x0:x0 + n], AF.Silu), reads=wt, writes=["cf"])
                else:
                    P.act(ACTF(TMP[:, x0:x0 + n], TMP[:, x0:x0 + n], AF.Silu), reads=wt, writes=wt)
                    if ch == 2:
                        P.pool(CP(BF[:, x0:x0 + n], TMP[:, x0:x0 + n]), reads=wt, writes=["bf"])
            if ch < 3:
                for i0 in range(0, NBLK, 4):
                    blks = list(range(i0, min(i0 + 4, NBLK)))
                    bank, btok = self.next_bank()
                    P.pe(TRS([(bank[:, gi * 128:(gi + 1) * 128], TMP[:, i * 128:(i + 1) * 128]) for gi, i in
                              enumerate(blks)], self.ident[:]),
                         reads=["xc.%d" % min(i // 4, 4) for i in blks] + ["ident"], writes=[btok])
                    src = bank[:, 0:len(blks) * 128].rearrange("p (i d) -> p i d", i=len(blks))
                    if ch < 2:
                        P.dve(CP(XS[:, i0:i0 + len(blks), ch * 128:(ch + 1) * 128], src), reads=[btok], writes=["xs"])
                    else:
                        P.dve(CP(BT[:, i0:i0 + len(blks), :], src), reads=[btok], writes=["bt"])
        for zc in range(2):
            def ev_z(blks, ps, btok, zc=zc):
                i0 = blks[0]
                P.act(ACTF(SZ[:, i0:i0 + len(blks), zc * 128:(zc + 1) * 128],
                           ps.rearrange("p (i d) -> p i d", i=len(blks)), AF.Silu), reads=[btok], writes=["sz"])
            self.linear_tm(W, OFF_SZ + zc * 128, 128, list(range(NBLK)), ev_z, 4)

        def ev_dt(blks, ps, btok):
            do = ROW_OFF["dtb"] + l * 8
            P.dve(TT(DTR.rearrange("p (i e) -> p i e", i=NBLK), ps.rearrange("p (i e) -> p i e", i=NBLK),
                     self.rowp[:, do:do + 8].unsqueeze(1).to_broadcast([128, NBLK, 8]), ALU.add),
                  reads=[btok, "rowp"], writes=["dtr"])
        self.linear_tm(W, OFF_DT, 8, list(range(NBLK)), ev_dt, NBLK)
        self.softplus(DTv, DTR, 144, SPS, (["dtr"], ["dt"]))
        P.dve(TT(ADT3, DT3, self.aneg[:, l * 8:l * 8 + 8].unsqueeze(1).to_broadcast([128, NBLK, 8]), ALU.mult),
              reads=["dt", "aneg"], writes=["adt"])
        P.barrier()
        if self.o.get("ssd_prep_only"):
            P.pool(MSET(self.MX[:, 2:4, :], 0.0), writes=["mx.%d" % t for t in range(5)])
            return
        xf = self.xn[:].bitcast(F32)
        o = 0
        Y = xf[:, o:o + NBLK * 256].rearrange("p (i d) -> p i d", i=NBLK); o += NBLK * 256
        S = []
        for d in range(2):
            sd = {}
            sd["LT"] = xf[:, o:o + 512]; o += 512
            sd["D1"] = xf[:, o:o + 512]; o += 512
            sd["GM"] = xf[:, o:o + 256]; o += 256
            sd["T1"] = xf[:, o:o + 256]; o += 256
            sd["H"] = xf[:, o:o + 128]; o += 128
            sd["SM"] = xf[:, o:o + 64]; o += 64
            sd["Mb"] = self.xn[:, 2 * o:2 * o + 512].rearrange("p (h t) -> p h t", h=4); o += 256
            sd["XD"] = self.xn[:, 2 * o:2 * o + 256].rearrange("p (h d) -> p h d", h=4); o += 128
            sd["XDW"] = self.xn[:, 2 * o:2 * o + 256].rearrange("p (h d) -> p h d", h=4); o += 128
            sd["HB"] = self.xn[:, 2 * o:2 * o + 128]; o += 64
            S.append(sd)
        assert o <= 9216, o
        BFm = [self.scr_b(g * 1152, T) for g in range(2)]
        CFm = [self.scr_b(2304 + g * 1152, T) for g in range(2)]
        P.pool(MSET(self.scr_f(0, 4608), 0.0), writes=["bfm", "cfm"])
        for g in range(2):
            P.pool(CP(BFm[g][64 * g:64 * g + 64, :], BF[64 * g:64 * g + 64, :]), reads=["bf", "bfm"], writes=["bfm"])
            P.pool(CP(CFm[g][64 * g:64 * g + 64, :], CF[64 * g:64 * g + 64, :]), reads=["cf", "cfm"], writes=["cfm"])
        fo = 13700
        VV = self.scr_f(fo, 256)
        OT = self.scr_f(fo + 256, 256)
        JK = self.scr_f(fo + 512, 256)
        FS = self.scr_f(fo + 768, 8)
        assert fo + 776 <= SCRW
        rsd, ssq = FS[:, 0:2], FS[:, 2:4]
        PB = self.pb
        dsk = self.rowp[:, ROW_OFF["dsk"] + l * 4:ROW_OFF["dsk"] + l * 4 + 4]
        sng = self.rowp[:, ROW_OFF["sng"] + l * 256:ROW_OFF["sng"] + (l + 1) * 256]
        ytk = ["y.%d" % k for k in range(NBLK)]
        P.dve(TT(Y.rearrange("p i (h d) -> p i h d", h=4), XS.rearrange("p i (h d) -> p i h d", h=4),
                 dsk.unsqueeze(1).unsqueeze(3).to_broadcast([128, NBLK, 4, 64]), ALU.mult), reads=["xs", "rowp"], writes=ytk)
        for d in range(2):
            P.dve(MSET(S[d]["H"], 0.0), writes=["H%d" % d])
            P.dve(MSET(S[d]["HB"], 0.0), writes=["HB%d" % d])
        orders = [[16, 17] + list(range(16)), [17, 16] + list(range(15, -1, -1))]

        class _Rec:
            def __init__(self):
                self.calls = []

            def __getattr__(self, name):
                def f(*a_, **k_):
                    self.calls.append((name, a_, k_))
                return f

        def chunk_step(d, k, P):
            sd = S[d]
            sx = "%d" % d
            U = self.uf if d == 0 else self.ub
            utok = "uf" if d == 0 else "ub"
            edge = 127 if d == 0 else 0
            A, B, C, Dk = (PB[4 * d + i] for i in range(4))
            At, Bt, Ct, Dt = ("pb%d" % (4 * d + i) for i in range(4))
            cst, ecs, wdec, dtw, dch = (sd["SM"][:, i * 4:(i + 1) * 4] for i in range(5))
            LT, D1, GM, T1, H, HB, Mb, XD, XDW = (sd[n] for n in ("LT", "D1", "GM", "T1", "H", "HB", "Mb", "XD", "XDW"))
            want_y = (k < 16) or need_ctx
            blk = slice(k * 128, (k + 1) * 128)
            adt = ADT3[:, k, d * 4:d * 4 + 4]
            mm = [(Dk[:, 0:4], U[:], adt, True, True)]
            for h in range(4):
                mm.append((A[:, h * 128:(h + 1) * 128], ADT3[:, k, d * 4 + h:d * 4 + h + 1].to_broadcast([128, 128]),
                           U[:], True, True))
            P.pe(MMS(mm), reads=["adt", utok], writes=[Dt, At])
            P.dve(CP(cst, Dk[:, 0:4]), reads=[Dt], writes=["cst" + sx])
            csb = A[:, 0:512].rearrange("p (h t) -> p h t", h=4)
            dts = DT3[:, k, d * 4:d * 4 + 4]
            XS4 = XS[:, k, :].rearrange("p (h d) -> p h d", h=4)
            if want_y:
                P.dve(TT(D1.rearrange("p (h t) -> p h t", h=4), csb, cst.unsqueeze(2).to_broadcast([128, 4, 128]),
                         ALU.subtract), reads=[At, "cst" + sx], writes=["d1" + sx])
                P.dve(STT(D1, D1, -1.0, D1, ALU.mult, ALU.min), reads=["d1" + sx], writes=["d1" + sx])
                P.act(ACTF(LT, D1, AF.Exp), reads=["d1" + sx], writes=["lt" + sx])
                P.pe(MMS([(B[:, g * 128:(g + 1) * 128], BFm[g][:, blk], CF[:, blk], True, True) for g in range(2)]),
                     reads=["bfm", "cf"], writes=[Bt])
                P.dve(TT(GM.rearrange("p (g t) -> p g t", g=2), B[:, 0:256].rearrange("p (g t) -> p g t", g=2),
                         U[:].unsqueeze(1).to_broadcast([128, 2, 128]), ALU.mult), reads=[Bt, utok], writes=["gm" + sx])
                P.dve(TT(Mb.rearrange("p (g a) t -> p g a t", g=2), LT.rearrange("p (g a t) -> p g a t", g=2, a=2),
                         GM.rearrange("p (g t) -> p g t", g=2).unsqueeze(2).to_broadcast([128, 2, 2, 128]), ALU.mult),
                      reads=["lt" + sx, "gm" + sx], writes=["mb" + sx])
                P.dve(TT(XD, XS4, dts.unsqueeze(2).to_broadcast([128, 4, 64]), ALU.mult), reads=["xs", "dt"],
                      writes=["xd" + sx])
                P.pe(MMS([(C[:, h * 64:(h + 1) * 64], Mb[:, h, :], XD[:, h, :], True, True) for h in range(4)]
                         + [(C[:, 256 + g * 128:256 + (g + 1) * 128], CFm[g][:, blk], HB[:, :], True, True)
                            for g in range(2)]),
                     reads=["mb" + sx, "xd" + sx, "cfm", "HB" + sx], writes=[Ct])
                P.act(ACTF(ecs, cst, AF.Exp), reads=["cst" + sx], writes=["ecs" + sx])
                P.dve(TT(T1.rearrange("p (h d) -> p h d", h=4), C[:, 256:512].rearrange("p (h d) -> p h d", h=4),
                         ecs.unsqueeze(2).to_broadcast([128, 4, 64]), ALU.mult), reads=[Ct, "ecs" + sx], writes=["t1" + sx])
                P.dve(TT(T1, T1, C[:, 0:256], ALU.add), reads=["t1" + sx, Ct], writes=["t1" + sx])
                P.dve(TT(Y[:, k, :], Y[:, k, :], T1, ALU.add), reads=["t1" + sx, "y.%d" % k], writes=["y.%d" % k])
            P.dve(TT(wdec, csb[:, :, edge], cst, ALU.subtract), reads=[At, "cst" + sx], writes=["wdec" + sx])
            P.act(ACTF(wdec, wdec, AF.Exp), reads=["wdec" + sx], writes=["wdec" + sx])
            P.dve(TT(dtw, wdec, dts, ALU.mult), reads=["wdec" + sx, "dt"], writes=["dtw" + sx])
            P.dve(TT(XDW, XS4, dtw.unsqueeze(2).to_broadcast([128, 4, 64]), ALU.mult), reads=["xs", "dtw" + sx],
                  writes=["xdw" + sx])
            P.pe(MMS([(Dk[64 * g:64 * g + 64, 128:256], BT[:, k, 64 * g:64 * g + 64],
                       XDW[:, 2 * g:2 * g + 2, :].rearrange("p a d -> p (a d)"), True, True) for g in range(2)]),
                 reads=["bt", "xdw" + sx], writes=[Dt])
            for g in range(2):
                P.act(ACTF(dch[64 * g:64 * g + 64, 0:2], csb[64 * g:64 * g + 64, 2 * g:2 * g + 2, edge], AF.Exp),
                      reads=[At], writes=["dch" + sx])
            P.dve(TT(H.rearrange("p (a d) -> p a d", a=2), H.rearrange("p (a d) -> p a d", a=2),
                     dch[:, 0:2].unsqueeze(2).to_broadcast([128, 2, 64]), ALU.mult), reads=["H" + sx, "dch" + sx],
                  writes=["H" + sx])
            P.dve(TT(H, H, Dk[:, 128:256], ALU.add), reads=["H" + sx, Dt], writes=["H" + sx])
            P.dve(CP(HB, H), reads=["H" + sx], writes=["HB" + sx])

        def finalize(k, P):
            P.dve(TT(VV, Y[:, k, :], SZ[:, k, :], ALU.mult), reads=["y.%d" % k, "sz"], writes=["vv"])
            for g in range(2):
                P.act(ACTF(JK[:, g * 128:(g + 1) * 128], VV[:, g * 128:(g + 1) * 128], AF.Square,
                           accum_out=ssq[:, g:g + 1]), reads=["vv"], writes=["jk", "ssq"])
            P.act(ACTF(rsd, ssq, AF.Ln, bias=EPS, scale=1.0 / 128.0), reads=["ssq"], writes=["rsd"])
            P.act(ACTF(rsd, rsd, AF.Exp, scale=-0.5), reads=["rsd"], writes=["rsd"])
            for g in range(2):
                P.dve(STT(OT[:, g * 128:(g + 1) * 128], VV[:, g * 128:(g + 1) * 128], rsd[:, g:g + 1],
                          sng[:, g * 128:(g + 1) * 128], ALU.mult, ALU.mult), reads=["vv", "rsd", "rowp"], writes=["ot"])
            P.pe(TRS([(PB[5][:, g * 128:(g + 1) * 128], OT[:, g * 128:(g + 1) * 128]) for g in range(2)], self.ident[:]),
                 reads=["ot", "ident"], writes=["pb5"])
            P.act(ACTF(self.MX[:, 2:4, k * 128:(k + 1) * 128], PB[5][:, 0:256].rearrange("p (g t) -> p g t", g=2),
                       AF.Identity), reads=["pb5"], writes=["mx.%d" % min(k // 4, 4)])

        def replay(streams):
            n = max([len(c) for c in streams] + [0])
            for i in range(n):
                for c in streams:
                    if i < len(c):
                        name, a_, k_ = c[i]
                        getattr(P, name)(*a_, **k_)

        done = {}
        fin_prev = []
        for j in range(NBLK):
            streams = []
            for d in range(2):
                r = _Rec()
                chunk_step(d, orders[d][j], r)
                streams.append(r.calls)
            streams.append(fin_prev)
            replay(streams)
            fr = _Rec()
            for d in range(2):
                k = orders[d][j]
                done[k] = done.get(k, 0) + 1
            for d in range(2):
                k = orders[d][j]
                if done.get(k) == 2 and ((k < 16) or need_ctx):
                    finalize(k, fr)
                    done[k] = 3
            fin_prev = fr.calls
        replay([fin_prev])

    def layer(self, b, l):
        P = self.P
        o = self.o
        need_ctx = (l == 0)
        allt = [0, 1, 2, 3, 4]
        outt = allt if need_ctx else [0, 1, 2, 3]
        if o.get("mix", True):
            self.norm(b, l, 0, allt)
            P.barrier()
            if o.get("da", True):
                self.da_mixer(b, l)
                self.wout_partial(b, l, 512, outt)
                P.barrier()
            if o.get("rg", True) or o.get("ssd", True):
                if o.get("rg", True):
                    self.rg_mixer(b, l)
                else:
                    P.pool(MSET(self.MX[:, 0:2, :], 0.0), writes=["mx.%d" % t for t in range(5)])
                P.barrier()
                if o.get("ssd", True):
                    self.ssd_mixer(b, l)
                else:
                    P.pool(MSET(self.MX[:, 2:4, :], 0.0), writes=["mx.%d" % t for t in range(5)])
                self.wout_partial(b, l, 0, outt)
                P.barrier()
        self.mods_step(1000)
        if o.get("ffn", True):
            self.norm(b, l, 1, outt)
            P.barrier()
            self.ffn(b, l, outt)
            P.barrier()

    def run(self):
        self.setup()
        for b in range(self.NB):
            self.load_x(b)
            self.P.barrier()
            for l in range(self.o.get("layers", 2)):
                self.layer(b, l)
            self.final(b)
            self.P.barrier()


def build_nc(NB=2, opts=None):
    nc = bass.Bass("TRN2", target_bir_lowering=False)
    with contextlib.ExitStack() as st:
        P = Prog(nc, st)
        K = Kern(nc, P, NB, opts or {})
        K.run()
        P.emit(final_reads=["out"])
    return nc


_NC_CACHE = {}


def kernel(x, c, ctx, c_ctx, w_mod, b_mod, norm1_g, w_in, rg_conv_w, rg_conv_b, rg_w_a, rg_b_a,
           rg_w_x, rg_b_x, rg_lambda, ssd_conv_w, ssd_conv_b, ssd_dt_bias, ssd_a_log, ssd_d,
           ssd_norm_g, da_lambda, da_subln_g, w_out, norm2_g, w_gate, w_up, w_down, final_norm_g,
           _opts=None, _ncores=8):
    inp = dict(w_mod=w_mod, b_mod=b_mod, norm1_g=norm1_g, w_in=w_in, rg_conv_w=rg_conv_w, rg_conv_b=rg_conv_b,
               rg_w_a=rg_w_a, rg_b_a=rg_b_a, rg_w_x=rg_w_x, rg_b_x=rg_b_x, rg_lambda=rg_lambda,
               ssd_conv_w=ssd_conv_w, ssd_conv_b=ssd_conv_b, ssd_dt_bias=ssd_dt_bias, ssd_a_log=ssd_a_log,
               ssd_d=ssd_d, ssd_norm_g=ssd_norm_g, da_lambda=da_lambda, da_subln_g=da_subln_g, w_out=w_out,
               norm2_g=norm2_g, w_gate=w_gate, w_up=w_up, w_down=w_down, final_norm_g=final_norm_g)
    shared = pack_shared(inp)
    x = np.asarray(x, np.float32)
    ctx = np.asarray(ctx, np.float32)
    c = np.asarray(c, np.float32)
    c_ctx = np.asarray(c_ctx, np.float32)
    NB = 2
    in_maps = []
    for i in range(_ncores):
        cc = np.stack([c[2 * i], c[2 * i + 1], c_ctx], axis=0)
        ct = np.ascontiguousarray(cc.reshape(3, 8, 128).transpose(2, 1, 0))
        m = dict(shared)
        m["x"] = np.ascontiguousarray(x[2 * i:2 * i + 2])
        m["ctx"] = np.ascontiguousarray(ctx[2 * i:2 * i + 2])
        m["ct"] = ct
        in_maps.append(m)
    key = repr(sorted((_opts or {}).items()))
    if key not in _NC_CACHE:
        _NC_CACHE[key] = build_nc(NB, _opts)
    nc = _NC_CACHE[key]
    res = run_bass_kernel_spmd(nc, in_maps, core_ids=list(range(_ncores)))
    out = np.concatenate([np.asarray(r["out"], np.float32) for r in res.results], axis=0)
    return out
```

```python
import contextlib
import math
import numpy as np
import concourse.bass as bass
import concourse.mybir as mybir
from concourse.bass_utils import run_bass_kernel_spmd

F32 = mybir.dt.float32
BF16 = mybir.dt.bfloat16
ALU = mybir.AluOpType
AF = mybir.ActivationFunctionType

D = 1024
NLAT = 2048
NCTX = 256
T = NLAT + NCTX
NBLK = T // 128
TILES = [(0, 512), (512, 512), (1024, 512), (1536, 512), (2048, 256)]
DFF = 2816
DIN = 2824
DINX = DIN + 1024
EPS = 1e-6
OFF_RGX, OFF_RGG, OFF_SZ, OFF_XBC, OFF_DT, OFF_Q, OFF_K, OFF_V = 0, 256, 512, 768, 1280, 1288, 1800, 2312
OFF_QS, OFF_KS = DIN, DIN + 512
SCRW = 14848


class _Op:
    __slots__ = ("eng", "fn", "waits", "tok", "clock", "dma", "idx")


class Prog:
    ENGS = ("pe", "act", "dve", "pool", "sp")

    def __init__(self, nc, stack, n_dma_sems=12):
        self.nc = nc
        self.stack = stack
        self.ops = {e: [] for e in self.ENGS}
        self.sem = {e: stack.enter_context(nc.semaphore("S_" + e)) for e in self.ENGS}
        self.cnt = {e: 0 for e in self.ENGS}
        self.know = {e: {} for e in self.ENGS}
        self.last_op = {e: None for e in self.ENGS}
        self.dma_sems = {}
        for q in ("sp", "pool"):
            self.dma_sems[q] = [[stack.enter_context(nc.semaphore("D_%s%d" % (q, i))), 0, None]
                                for i in range(n_dma_sems)]
        self.dma_rr = {q: 0 for q in self.dma_sems}
        self.last_w = {}
        self.readers = {}
        self.nops = 0
        self.nwaits = 0

    def sb(self, name, shape, dt=F32):
        return self.stack.enter_context(self.nc.sbuf_tensor(name, list(shape), dt))

    def ps(self, name, shape, dt=F32):
        return self.stack.enter_context(self.nc.psum_tensor(name, list(shape), dt))

    def _need(self, e, d, waits):
        if d is None:
            return
        if e == "pe" and d.eng == "pe" and not d.dma:
            return
        sem, val = d.tok
        k = self.know[e]
        if k.get(sem, 0) >= val:
            return
        waits.append((sem, val))
        for s, v in d.clock.items():
            if k.get(s, 0) < v:
                k[s] = v

    def op(self, eng, fn, reads=(), writes=(), dma=False, extra_deps=()):
        o = _Op()
        o.eng = eng
        o.fn = fn
        o.dma = dma
        o.idx = self.nops
        self.nops += 1
        deps = {}
        psum_r = [r for r in reads if r.startswith("pb") and r not in writes]
        if psum_r:
            writes = list(writes) + psum_r
            reads = [r for r in reads if r not in psum_r]
        for r in reads:
            d = self.last_w.get(r)
            if d is not None:
                deps[d.idx] = d
        for w in writes:
            d = self.last_w.get(w)
            if d is not None:
                deps[d.idx] = d
            for d in self.readers.get(w, ()):
                deps[d.idx] = d
        for d in extra_deps:
            if d is not None:
                deps[d.idx] = d
        waits = []
        for i in sorted(deps, reverse=True):
            self._need(eng, deps[i], waits)
        if dma:
            pool = self.dma_sems[eng]
            slot = pool[self.dma_rr[eng] % len(pool)]
            self.dma_rr[eng] += 1
            if slot[2] is not None:
                self._need(eng, slot[2], waits)
            slot[1] += 16
            o.tok = (slot[0], slot[1])
            slot[2] = o
        elif fn is not None:
            self.cnt[eng] += 1
            o.tok = (self.sem[eng], self.cnt[eng])
        else:
            o.tok = None
        o.waits = waits
        self.nwaits += len(waits)
        o.clock = dict(self.know[eng])
        if o.tok is not None:
            o.clock[o.tok[0]] = o.tok[1]
            if not dma:
                self.last_op[eng] = o
        for w in writes:
            self.last_w[w] = o
            self.readers[w] = []
        for r in reads:
            if r not in writes:
                self.readers.setdefault(r, []).append(o)
        self.ops[eng].append(o)
        return o

    def pe(self, fn, reads=(), writes=()):
        return self.op("pe", fn, reads, writes)

    def act(self, fn, reads=(), writes=()):
        return self.op("act", fn, reads, writes)

    def dve(self, fn, reads=(), writes=()):
        return self.op("dve", fn, reads, writes)

    def pool(self, fn, reads=(), writes=()):
        return self.op("pool", fn, reads, writes)

    def dma(self, fn, reads=(), writes=(), q="sp"):
        return self.op(q, fn, reads, writes, dma=True)

    def barrier(self):
        pend = [self.last_op[e] for e in self.ENGS if self.last_op[e] is not None]
        for q in self.dma_sems:
            for slot in self.dma_sems[q]:
                if slot[2] is not None:
                    pend.append(slot[2])
        for e in self.ENGS:
            self.op(e, None, extra_deps=pend)

    def emit(self, final_reads=()):
        nc = self.nc
        fin_waits = []
        for r in final_reads:
            self._need("sp", self.last_w.get(r), fin_waits)

        def run(e, engine, extra=()):
            for o in self.ops[e]:
                for (s, v) in o.waits:
                    engine.wait_ge(s, v)
                if o.fn is None:
                    continue
                ins = o.fn(engine)
                ins.then_inc(o.tok[0], 16 if o.dma else 1)
            for (s, v) in extra:
                engine.wait_ge(s, v)

        with nc.Block() as block:
            @block.sync
            def _(eng):
                run("sp", eng, fin_waits)

            @block.gpsimd
            def _(eng):
                run("pool", eng)

            @block.scalar
            def _(eng):
                run("act", eng)

            @block.vector
            def _(eng):
                run("dve", eng)

            @block.tensor
            def _(eng):
                run("pe", eng)


def MMS(lst):
    def f(e):
        ins = None
        for (o, l, r, st, sp) in lst:
            ins = e.matmul(o, lhsT=l, rhs=r, start=st, stop=sp)
        return ins
    return f


def TRS(lst, ident):
    def f(e):
        ins = None
        for (o, i) in lst:
            ins = e.transpose(out=o, in_=i, identity=ident)
        return ins
    return f


def ACTF(out, in_, func, bias=0.0, scale=1.0, accum_out=None):
    if accum_out is None:
        return lambda e: e.activation(out=out, in_=in_, func=func, bias=bias, scale=scale)
    return lambda e: e.activation(out=out, in_=in_, func=func, bias=bias, scale=scale, accum_out=accum_out)


def TT(out, a, b, op):
    return lambda e: e.tensor_tensor(out=out, in0=a, in1=b, op=op)


def TS(out, a, s1, op0, s2=None, op1=None):
    if op1 is None:
        return lambda e: e.tensor_scalar(out=out, in0=a, scalar1=s1, scalar2=None, op0=op0)
    return lambda e: e.tensor_scalar(out=out, in0=a, scalar1=s1, scalar2=s2, op0=op0, op1=op1)


def STT(out, in0, scalar, in1, op0, op1):
    return lambda e: e.scalar_tensor_tensor(out=out, in0=in0, scalar=scalar, in1=in1, op0=op0, op1=op1)


def CP(out, in_):
    return lambda e: e.tensor_copy(out=out, in_=in_)


def RECIP(out, in_):
    return lambda e: e.reciprocal(out=out, in_=in_)


def MSET(ap, v):
    return lambda e: e.memset(ap, v)


def DMA(out, in_):
    return lambda e: e.dma_start(out=out, in_=in_)


def SCAN(out, a, b, init):
    return lambda e: e.tensor_tensor_scan(out=out, data0=a, data1=b, initial=init, op0=ALU.mult, op1=ALU.add)


PRM_SPEC = [("n1g", 16), ("n2g", 16), ("fng", 8), ("bmod", 96), ("rgcw", 16), ("rgcb", 4), ("rgba", 8),
            ("rgbx", 8), ("rglam", 8), ("scw", 32), ("scb", 8), ("subg", 2)]
PRM_OFF = {}
_o = 0
for _n, _w in PRM_SPEC:
    PRM_OFF[_n] = _o
    _o += _w
NPRM = _o
ROW_SPEC = [("dtb", 16), ("alog", 16), ("dsk", 8), ("sng", 512), ("dal", 512)]
ROW_OFF = {}
_o = 0
for _n, _w in ROW_SPEC:
    ROW_OFF[_n] = _o
    _o += _w
NROW = _o


def _fm(v):
    v = np.asarray(v, np.float32)
    lead = v.shape[:-1]
    n = v.shape[-1] // 128
    v = v.reshape(lead + (n, 128))
    return np.moveaxis(v, -1, 0).reshape(128, -1)


def pack_shared(inp):
    f = lambda k: np.asarray(inp[k], np.float32)
    prm = np.zeros((128, NPRM), np.float32)

    def put(name, arr):
        o = PRM_OFF[name]
        prm[:, o:o + arr.shape[1]] = arr

    put("n1g", _fm(f("norm1_g")))
    put("n2g", _fm(f("norm2_g")))
    put("fng", _fm(f("final_norm_g")))
    put("bmod", _fm(f("b_mod")))
    w = f("rg_conv_w").reshape(2, 4, 2, 128).transpose(3, 0, 2, 1).reshape(128, 16)
    put("rgcw", w)
    put("rgcb", _fm(f("rg_conv_b")))
    put("rgba", _fm(f("rg_b_a")))
    put("rgbx", _fm(f("rg_b_x")))
    put("rglam", _fm(f("rg_lambda")))
    w = f("ssd_conv_w").reshape(2, 4, 4, 128).transpose(3, 0, 2, 1).reshape(128, 32)
    put("scw", w)
    put("scb", _fm(f("ssd_conv_b")))
    put("subg", _fm(f("da_subln_g")))
    row = np.zeros((NROW,), np.float32)
    row[ROW_OFF["dtb"]:ROW_OFF["dtb"] + 16] = f("ssd_dt_bias").reshape(-1)
    row[ROW_OFF["alog"]:ROW_OFF["alog"] + 16] = f("ssd_a_log").reshape(-1)
    row[ROW_OFF["dsk"]:ROW_OFF["dsk"] + 8] = f("ssd_d").reshape(-1)
    row[ROW_OFF["sng"]:ROW_OFF["sng"] + 512] = f("ssd_norm_g").reshape(-1)
    row[ROW_OFF["dal"]:ROW_OFF["dal"] + 512] = f("da_lambda").reshape(-1)
    rowp = np.ascontiguousarray(np.broadcast_to(row[None, :], (128, NROW)))
    bd = np.zeros((128, 16, 128), np.float32)
    wa, wx = f("rg_w_a"), f("rg_w_x")
    for l in range(2):
        for d in range(2):
            for cc in range(2):
                for ax, wsrc in enumerate((wa, wx)):
                    idx = ((l * 2 + d) * 2 + cc) * 2 + ax
                    for blk in range(2):
                        bd[blk * 64:(blk + 1) * 64, idx, blk * 64:(blk + 1) * 64] = wsrc[l, d, 2 * cc + blk]
    j = np.arange(64)
    sw = np.where((j % 32) < 16, j + 16, j - 16)
    win = f("w_in")
    qidx = np.concatenate([OFF_Q + h * 128 + c * 64 + sw for h in range(4) for c in range(2)])
    kidx = np.concatenate([OFF_K + h * 128 + c * 64 + sw for h in range(4) for c in range(2)])
    winx = np.ascontiguousarray(np.concatenate([win, win[:, :, qidx], win[:, :, kidx]], axis=2))
    ident = np.eye(128, dtype=np.float32)
    tt = np.arange(128)
    uf = (tt[:, None] <= tt[None, :]).astype(np.float32)
    ub = (tt[:, None] >= tt[None, :]).astype(np.float32)
    t = np.arange(NLAT)
    row_i = (t // 64).astype(np.float32)
    col_i = (t % 64).astype(np.float32)
    half = 32
    inv_freq = np.power(np.float32(10000.0), -np.arange(0, half, 2, dtype=np.float32) / np.float32(half)).astype(np.float32)
    ang_r = row_i[:, None] * inv_freq[None, :]
    ang_c = col_i[:, None] * inv_freq[None, :]
    rope = np.zeros((128, 2, NLAT), np.float32)
    for p in range(128):
        jj = p % 64
        ang = ang_r if jj < 32 else ang_c
        i = jj % 16
        rope[p, 0] = np.cos(ang[:, i])
        sgn = -1.0 if (jj % 32) < 16 else 1.0
        rope[p, 1] = sgn * np.sin(ang[:, i])
    return dict(prm=prm, rowp=rowp, bd=bd, w_in_x=winx, ident=ident, uf=uf, ub=ub, rope=rope,
                w_mod=f("w_mod"), w_out=f("w_out"), w_gate=f("w_gate"), w_up=f("w_up"), w_down=f("w_down"))


class Kern:
    def __init__(self, nc, P, NB, opts):
        self.nc = nc
        self.P = P
        self.NB = NB
        self.o = opts
        dt_in = lambda n, s: nc.dram_tensor(n, list(s), F32, kind="ExternalInput").ap()
        self.x = dt_in("x", [NB, NLAT, D])
        self.ctx = dt_in("ctx", [NB, NCTX, D])
        self.ct = dt_in("ct", [128, 8, 3])
        self.prm_d = dt_in("prm", [128, NPRM])
        self.rowp_d = dt_in("rowp", [128, NROW])
        self.bd_d = dt_in("bd", [128, 16, 128])
        self.w_in = dt_in("w_in_x", [2, D, DINX])
        self.ident_d = dt_in("ident", [128, 128])
        self.uf_d = dt_in("uf", [128, 128])
        self.ub_d = dt_in("ub", [128, 128])
        self.rope_d = dt_in("rope", [128, 2, NLAT])
        self.w_mod = dt_in("w_mod", [2, D, 6 * D])
        self.w_out = dt_in("w_out", [2, D, D])
        self.w_gate = dt_in("w_gate", [2, D, DFF])
        self.w_up = dt_in("w_up", [2, D, DFF])
        self.w_down = dt_in("w_down", [2, DFF, D])
        self.out = nc.dram_tensor("out", [NB, NLAT, D], F32, kind="ExternalOutput").ap()
        sb = P.sb
        self.resid = sb("resid", [128, 8 * T])
        self.R3 = self.resid[:].rearrange("p (c t) -> p c t", c=8)
        self.xn = sb("xn", [128, 8 * T], BF16)
        self.XN = self.xn[:].rearrange("p (c t) -> p c t", c=8)
        self.mixb = sb("mixb", [128, 4 * T], BF16)
        self.MX = self.mixb[:].rearrange("p (c t) -> p c t", c=4)
        self.scr = sb("scr", [128, SCRW])
        self.WB = [sb("wb%d" % i, [128, 1024], BF16) for i in range(4)]
        self.wrr = 0
        self.ident = sb("ident_s", [128, 128])
        self.uf = sb("uf_s", [128, 128])
        self.ub = sb("ub_s", [128, 128])
        self.ones_f = sb("ones_f", [128, 128])
        self.ones_b = sb("ones_b", [128, 128], BF16)
        self.prm = sb("prm_s", [128, NPRM])
        self.rowp = sb("rowp_s", [128, NROW])
        self.bd = sb("bd_s", [128, 16 * 128], BF16)
        self.BD = self.bd[:].rearrange("p (i j) -> p i j", i=16)
        self.modt = sb("modt", [128, 2 * 48 * 3])
        self.MT = self.modt[:].rearrange("p (l j n) -> p l j n", l=2, j=48)
        self.amod = sb("amod", [128, 2 * 2 * 3 * 8])
        self.AM = self.amod[:].rearrange("p (l w n c) -> p l w n c", l=2, w=2, n=3)
        self.small = sb("small", [128, 256])
        self.sml_o = 0
        self.pb = [P.ps("pb%d" % i, [128, 512]) for i in range(8)]
        self.rot = [0, 1, 2, 3]
        self.rr = 0

    def prmcol(self, name, i):
        o = PRM_OFF[name] + i
        return self.prm[:, o:o + 1]

    def salloc(self, n):
        o = self.sml_o
        self.sml_o += n
        assert self.sml_o <= 256
        return self.small[:, o:o + n]

    def scr_f(self, off, n):
        return self.scr[:, off:off + n]

    def scr_b(self, off, n):
        assert n % 2 == 0
        return self.scr[:, off:off + n // 2].bitcast(BF16)

    def next_bank(self):
        i = self.rot[self.rr % len(self.rot)]
        self.rr += 1
        return self.pb[i], "pb%d" % i

    def load_w(self, wsrc, k0, kch, col0, ncols=128):
        i = self.wrr % len(self.WB)
        self.wrr += 1
        wb = self.WB[i][:, 0:kch * ncols].rearrange("p (c n) -> p c n", c=kch)
        src = wsrc.rearrange("(c p) n -> p c n", p=128)[:, k0:k0 + kch, col0:col0 + ncols]
        tok = "wb%d" % i
        self.P.dma(DMA(wb, src), writes=[tok], q="pool")
        return wb, tok

    def linear_fm(self, wsrc, col0, tiles, evac, kch=8, k0=0, rhs=None, rtok=None, ncols=128):
        wb, wtok = self.load_w(wsrc, k0, kch, col0, ncols)
        if rhs is None:
            rhs = lambda kc, t0, n: self.XN[:, kc, t0:t0 + n]
            rtok = lambda t: ["xn.%d" % t]
        for t in tiles:
            t0, n = TILES[t]
            bank, btok = self.next_bank()
            out = bank[0:ncols, 0:n]
            mms = [(out, wb[:, kc, :], rhs(kc, t0, n), kc == 0, kc == kch - 1) for kc in range(kch)]
            self.P.pe(MMS(mms), reads=[wtok] + rtok(t), writes=[btok])
            evac(t, out, btok)

    def linear_fm2(self, wsrc, colA, colB, tiles, evac2, extraA=(), evacA=None):
        wa, atok = self.load_w(wsrc, 0, 8, colA)
        wb2, btok2 = self.load_w(wsrc, 0, 8, colB)
        for t in tiles:
            t0, n = TILES[t]
            ba, bat = self.next_bank()
            bb, bbt = self.next_bank()
            self.P.pe(MMS([(ba[:, 0:n], wa[:, kc, :], self.XN[:, kc, t0:t0 + n], kc == 0, kc == 7) for kc in range(8)]
                          + [(bb[:, 0:n], wb2[:, kc, :], self.XN[:, kc, t0:t0 + n], kc == 0, kc == 7) for kc in range(8)]),
                      reads=[atok, btok2, "xn.%d" % t], writes=[bat, bbt])
            evac2(t, ba[:, 0:n], bat, bb[:, 0:n], bbt)
        for t in extraA:
            t0, n = TILES[t]
            ba, bat = self.next_bank()
            self.P.pe(MMS([(ba[:, 0:n], wa[:, kc, :], self.XN[:, kc, t0:t0 + n], kc == 0, kc == 7) for kc in range(8)]),
                      reads=[atok, "xn.%d" % t], writes=[bat])
            evacA(t, ba[:, 0:n], bat)

    def linear_tm(self, wsrc, col0, ncols, blocks, evac, grp):
        wb, wtok = self.load_w(wsrc, 0, 8, col0, ncols)
        for g0 in range(0, len(blocks), grp):
            blks = blocks[g0:g0 + grp]
            bank, btok = self.next_bank()
            mms = []
            rt = set()
            for gi, i in enumerate(blks):
                out = bank[:, gi * ncols:(gi + 1) * ncols]
                rt.add("xn.%d" % min(i // 4, 4))
                for kc in range(8):
                    mms.append((out, self.XN[:, kc, i * 128:(i + 1) * 128], wb[:, kc, :], kc == 0, kc == 7))
            self.P.pe(MMS(mms), reads=[wtok] + sorted(rt), writes=[btok])
            evac(blks, bank[:, 0:len(blks) * ncols], btok)

    def softplus(self, out, x, n, s0, toks):
        P = self.P
        ax, e, z, p = (s0[:, i * n:(i + 1) * n] for i in range(4))
        rd, wr = toks
        tk = ["spl.tmp"]
        P.dve(STT(ax, x, -1.0, x, ALU.mult, ALU.max), reads=rd, writes=tk)
        P.act(ACTF(e, ax, AF.Exp, scale=-1.0), reads=tk, writes=tk)
        P.dve(TS(z, e, 2.0, ALU.add), reads=tk, writes=tk)
        P.dve(RECIP(z, z), reads=tk, writes=tk)
        P.dve(TT(z, z, e, ALU.mult), reads=tk, writes=tk)
        P.dve(TT(e, z, z, ALU.mult), reads=tk, writes=tk)
        P.dve(TS(p, e, 1.0 / 13.0, ALU.mult, 1.0 / 11.0, ALU.add), reads=tk, writes=tk)
        for cst in (1.0 / 9.0, 1.0 / 7.0, 1.0 / 5.0, 1.0 / 3.0, 1.0):
            P.dve(TT(p, p, e, ALU.mult), reads=tk, writes=tk)
            P.dve(TS(p, p, cst, ALU.add), reads=tk, writes=tk)
        P.dve(TT(p, p, z, ALU.mult), reads=tk, writes=tk)
        P.dve(TS(ax, x, 0.0, ALU.max), reads=rd + tk, writes=tk)
        P.dve(STT(out, p, 2.0, ax, ALU.mult, ALU.add), reads=tk, writes=wr)

    def mods_emit(self, l, jj):
        P = self.P
        wb, wtok = self.load_w(self.w_mod[l], 0, 8, jj * 128)
        bi = 5 + (self.mod_i % 2)
        self.mod_i += 1
        out = self.pb[bi][:, 0:3]
        P.pe(MMS([(out, wb[:, kc, :], self.ctb[:, kc, :], kc == 0, kc == 7) for kc in range(8)]),
             reads=[wtok, "ctb"], writes=["pb%d" % bi])
        P.dve(TS(self.MT[:, l, jj, :], out, self.prmcol("bmod", l * 48 + jj), ALU.add), reads=["pb%d" % bi, "prm"],
              writes=["modt"])

    def mods_finish(self, l, w):
        gname = "n1g" if w == 0 else "n2g"
        go = PRM_OFF[gname] + l * 8
        jj0 = 8 + 24 * w
        for n in range(3):
            self.P.dve(STT(self.AM[:, l, w, n, :], self.MT[:, l, jj0:jj0 + 8, n], 1.0, self.prm[:, go:go + 8],
                           ALU.add, ALU.mult), reads=["modt", "prm"], writes=["amod"])

    def mods_step(self, n):
        while n > 0 and self.mod_q:
            l, jj = self.mod_q.pop(0)
            self.mods_emit(l, jj)
            n -= 1
            if not self.mod_q:
                self.mods_finish(0, 1)
                self.mods_finish(1, 0)
                self.mods_finish(1, 1)

    def setup(self):
        P = self.P
        P.dma(DMA(self.ident[:], self.ident_d), writes=["ident"])
        P.dma(DMA(self.uf[:], self.uf_d), writes=["uf"])
        P.dma(DMA(self.ub[:], self.ub_d), writes=["ub"])
        P.dma(DMA(self.prm[:], self.prm_d), writes=["prm"])
        P.dma(DMA(self.rowp[:], self.rowp_d), writes=["rowp"])
        P.dma(DMA(self.BD, self.bd_d), writes=["bd"], q="pool")
        P.pool(MSET(self.ones_f[:], 1.0), writes=["ones_f"])
        P.pool(MSET(self.ones_b[:], 1.0), writes=["ones_b"])
        ctile = self.scr_f(0, 24).rearrange("p (k n) -> p k n", k=8)
        sig = self.scr_f(24, 24).rearrange("p (k n) -> p k n", k=8)
        P.dma(DMA(ctile, self.ct), writes=["ct"])
        P.act(ACTF(sig, ctile, AF.Sigmoid), reads=["ct"], writes=["sig"])
        P.dve(TT(ctile, ctile, sig, ALU.mult), reads=["ct", "sig"], writes=["ct"])
        self.ctb = self.salloc(12).bitcast(BF16)[:, 0:24].rearrange("p (k n) -> p k n", k=8)
        P.dve(CP(self.ctb, ctile), reads=["ct"], writes=["ctb"])
        self.mod_q = [(0, jj) for jj in range(24, 48)] + [(1, jj) for jj in range(48)]
        self.mod_i = 0
        for jj in range(24):
            self.mods_emit(0, jj)
        self.mods_finish(0, 0)
        self.m8sp = self.salloc(8)
        nl = self.salloc(8)
        lo = PRM_OFF["rglam"]
        P.dve(TS(nl, self.prm[:, lo:lo + 8], -1.0, ALU.mult), reads=["prm"], writes=["nl"])
        self.softplus(self.m8sp, nl, 8, self.scr_f(8300, 32), (["nl"], ["m8sp"]))
        P.dve(TS(self.m8sp, self.m8sp, -8.0, ALU.mult), reads=["m8sp"], writes=["m8sp"])
        self.hm = self.salloc(8)
        self.hba = self.salloc(8)
        self.hbx = self.salloc(8)
        P.dve(TS(self.hm, self.m8sp, 0.5, ALU.mult), reads=["m8sp"], writes=["hm"])
        oa, ox = PRM_OFF["rgba"], PRM_OFF["rgbx"]
        P.dve(TS(self.hba, self.prm[:, oa:oa + 8], 0.5, ALU.mult), reads=["prm"], writes=["hba"])
        P.dve(TS(self.hbx, self.prm[:, ox:ox + 8], 0.5, ALU.mult), reads=["prm"], writes=["hbx"])
        self.aneg = self.salloc(16)
        ao = ROW_OFF["alog"]
        P.act(ACTF(self.aneg, self.rowp[:, ao:ao + 16], AF.Exp), reads=["rowp"], writes=["aneg"])
        P.dve(TS(self.aneg, self.aneg, -1.0, ALU.mult), reads=["aneg"], writes=["aneg"])
        self.nlam = self.salloc(2)
        self.gsc = self.salloc(2)
        pr = self.scr_f(8400, 128)
        sm = self.salloc(4)
        for l in range(2):
            do = ROW_OFF["dal"] + l * 256
            lam_init = 0.8 - 0.6 * math.exp(-0.3 * l)
            for i in range(2):
                P.dve(TT(pr[:, 0:64], self.rowp[:, do + 128 * i:do + 128 * i + 64],
                         self.rowp[:, do + 128 * i + 64:do + 128 * i + 128], ALU.mult), reads=["rowp"], writes=["pr"])
                P.dve(lambda e, o_=sm[:, 2 * l + i:2 * l + i + 1], i_=pr[:, 0:64]: e.reduce_sum(
                    out=o_, in_=i_, axis=mybir.AxisListType.X), reads=["pr"], writes=["sm"])
            P.act(ACTF(sm[:, 2 * l:2 * l + 2], sm[:, 2 * l:2 * l + 2], AF.Exp), reads=["sm"], writes=["sm"])
            P.dve(TT(self.nlam[:, l:l + 1], sm[:, 2 * l + 1:2 * l + 2], sm[:, 2 * l:2 * l + 1], ALU.subtract),
                  reads=["sm"], writes=["nlam"])
            P.dve(TS(self.nlam[:, l:l + 1], self.nlam[:, l:l + 1], -lam_init, ALU.add), reads=["nlam"], writes=["nlam"])
            P.dve(TS(self.gsc[:, l:l + 1], self.prmcol("subg", l), 1.0 - lam_init, ALU.mult), reads=["prm"],
                  writes=["gsc"])
        P.barrier()

    def load_x(self, b):
        P = self.P
        STG = [self.scr_f(i * 1024, 1024) for i in range(2)]
        self.rot = [0, 1, 2, 3]
        for i in range(NBLK):
            src = self.x[b, i * 128:(i + 1) * 128, :] if i < 16 else self.ctx[b, (i - 16) * 128:(i - 15) * 128, :]
            stg = STG[i % 2]
            stok = "stg%d" % (i % 2)
            P.dma(DMA(stg, src), writes=[stok])
            t = min(i // 4, 4)
            for half in range(2):
                bank, btok = self.next_bank()
                P.pe(TRS([(bank[:, j * 128:(j + 1) * 128], stg[:, (half * 4 + j) * 128:(half * 4 + j + 1) * 128])
                          for j in range(4)], self.ident[:]), reads=[stok, "ident"], writes=[btok])
                dst = self.R3[:, half * 4:half * 4 + 4, i * 128:(i + 1) * 128]
                srcp = bank[:, 0:512].rearrange("p (c t) -> p c t", c=4)
                wr = ["r.%d.%d" % (c, t) for c in range(half * 4, half * 4 + 4)]
                if half == 0:
                    P.act(ACTF(dst, srcp, AF.Identity), reads=[btok], writes=wr)
                else:
                    P.dve(CP(dst, srcp), reads=[btok], writes=wr)

    def rstd_tile(self, t, RS, SQ, P, sx="", bank=7):
        t0, n = TILES[t]
        pbk = self.pb[bank]
        btok = "pb%d" % bank
        for c in range(8):
            sq = SQ[c % 2]
            P.dve(TT(sq[:, :n], self.R3[:, c, t0:t0 + n], self.R3[:, c, t0:t0 + n], ALU.mult),
                  reads=["r.%d.%d" % (c, t)], writes=["sq%d%s" % (c % 2, sx)])
            P.pe(MMS([(pbk[:, :n], self.ones_b[:], sq[:, :n], c == 0, c == 7)]), reads=["sq%d%s" % (c % 2, sx), "ones_b"],
                 writes=[btok])
        P.act(ACTF(RS[:, :n], pbk[:, :n], AF.Ln, bias=EPS, scale=1.0 / D), reads=[btok], writes=["rs" + sx])
        P.act(ACTF(RS[:, :n], RS[:, :n], AF.Exp, scale=-0.5), reads=["rs" + sx], writes=["rs" + sx])

    class _Rec:
        def __init__(self):
            self.calls = []

        def __getattr__(self, name):
            def f(*a_, **k_):
                self.calls.append((name, a_, k_))
            return f

    def replay(self, streams):
        n = max([len(c) for c in streams] + [0])
        for i in range(n):
            for c in streams:
                if i < len(c):
                    name, a_, k_ = c[i]
                    getattr(self.P, name)(*a_, **k_)

    def norm_scratch(self, si):
        o = si * 2048
        SQ = [self.scr_b(o + i * 256, 512) for i in range(2)]
        RS = self.scr_f(o + 512, 512)
        TM = [self.scr_f(o + 1024 + i * 512, 512) for i in range(2)]
        return SQ, RS, TM

    def norm(self, b, l, w, tiles):
        streams = [Kern._Rec(), Kern._Rec()]
        for i, t in enumerate(tiles):
            si = i % 2
            P = streams[si]
            sx = ".%d" % si
            SQ, RS, TM = self.norm_scratch(si)
            t0, n = TILES[t]
            nn = b if t < 4 else 2
            self.rstd_tile(t, RS, SQ, P, sx, bank=6 + si)
            for c in range(8):
                tm = TM[c % 2]
                P.dve(TT(tm[:, :n], self.R3[:, c, t0:t0 + n], RS[:, :n], ALU.mult), reads=["r.%d.%d" % (c, t), "rs" + sx],
                      writes=["tm%d%s" % (c % 2, sx)])
                P.act(ACTF(self.XN[:, c, t0:t0 + n], tm[:, :n], AF.Identity,
                           bias=self.MT[:, l, 24 * w + c, nn:nn + 1], scale=self.AM[:, l, w, nn, c:c + 1]),
                      reads=["tm%d%s" % (c % 2, sx), "modt", "amod"], writes=["xn.%d" % t])
        self.replay([st.calls for st in streams])

    def wout_partial(self, b, l, rows0, tiles):
        P = self.P
        self.rot = [0, 1, 2, 3]
        for c in range(8):
            def evac(t, ps, btok, c=c):
                t0, n = TILES[t]
                nn = b if t < 4 else 2
                dst = self.R3[:, c, t0:t0 + n]
                P.dve(STT(dst, ps, self.MT[:, l, 16 + c, nn:nn + 1], dst, ALU.mult, ALU.add),
                      reads=[btok, "modt", "r.%d.%d" % (c, t)], writes=["r.%d.%d" % (c, t)])
            self.linear_fm(self.w_out[l], c * 128, tiles, evac, kch=4, k0=rows0 // 128,
                           rhs=lambda kc, t0, n: self.MX[:, kc, t0:t0 + n], rtok=lambda t: ["mx.%d" % t])

    def ffn(self, b, l, tiles):
        P = self.P
        HH = self.scr_b(0, 6 * T).rearrange("p (j t) -> p j t", j=6)
        SG = [self.scr_f(7000 + i * 512, 512) for i in range(2)]
        groups = [(0, 6), (6, 6), (12, 5), (17, 5)]
        it = 0
        for (j0, J) in groups:
            for jl in range(J):
                j = j0 + jl
                wg, gtok = self.load_w(self.w_gate[l], 0, 8, j * 128)
                wu, utok = self.load_w(self.w_up[l], 0, 8, j * 128)
                for t in tiles:
                    t0, n = TILES[t]
                    bg, bgt = self.pb[it % 2], "pb%d" % (it % 2)
                    bu, but = self.pb[2 + it % 2], "pb%d" % (2 + it % 2)
                    sg = SG[it % 2]
                    sgt = "sg%d" % (it % 2)
                    it += 1
                    P.pe(MMS([(bg[:, :n], wg[:, kc, :], self.XN[:, kc, t0:t0 + n], kc == 0, kc == 7) for kc in range(8)]),
                         reads=[gtok, "xn.%d" % t], writes=[bgt])
                    P.pe(MMS([(bu[:, :n], wu[:, kc, :], self.XN[:, kc, t0:t0 + n], kc == 0, kc == 7) for kc in range(8)]),
                         reads=[utok, "xn.%d" % t], writes=[but])
                    P.act(ACTF(sg[:, :n], bg[:, :n], AF.Silu), reads=[bgt], writes=[sgt])
                    P.dve(TT(HH[:, jl, t0:t0 + n], sg[:, :n], bu[:, :n], ALU.mult), reads=[sgt, but],
                          writes=["hh.%d.%d" % (jl, t)])
            for c in range(8):
                wd, dtok = self.load_w(self.w_down[l], j0, J, c * 128)
                for t in tiles:
                    t0, n = TILES[t]
                    nn = b if t < 4 else 2
                    bi = 4 + (it % 3)
                    it += 1
                    bank, btok = self.pb[bi], "pb%d" % bi
                    P.pe(MMS([(bank[:, :n], wd[:, jl, :], HH[:, jl, t0:t0 + n], jl == 0, jl == J - 1) for jl in range(J)]),
                         reads=[dtok] + ["hh.%d.%d" % (jl, t) for jl in range(J)], writes=[btok])
                    dst = self.R3[:, c, t0:t0 + n]
                    P.dve(STT(dst, bank[:, :n], self.MT[:, l, 40 + c, nn:nn + 1], dst, ALU.mult, ALU.add),
                          reads=[btok, "modt", "r.%d.%d" % (c, t)], writes=["r.%d.%d" % (c, t)])

    def final(self, b):
        P = self.P
        SQ, RS, TM = self.norm_scratch(0)
        YN = self.scr_f(2048, 4096).rearrange("p (c t) -> p c t", c=8)
        OST = [self.scr_f(6144 + i * 1024, 1024) for i in range(2)]
        self.rot = [0, 1, 2, 3]
        oi = 0
        for t in range(4):
            t0, n = TILES[t]
            self.rstd_tile(t, RS, SQ, P)
            for c in range(8):
                tm = TM[c % 2]
                P.dve(TT(tm[:, :n], self.R3[:, c, t0:t0 + n], RS[:, :n], ALU.mult), reads=["r.%d.%d" % (c, t), "rs"],
                      writes=["tm%d" % (c % 2)])
                P.act(ACTF(YN[:, c, :], tm[:, :n], AF.Identity, scale=self.prmcol("fng", c)),
                      reads=["tm%d" % (c % 2), "prm"], writes=["yn.%d" % c])
            for j in range(4):
                ost = OST[oi % 2]
                otok = "ost%d" % (oi % 2)
                oi += 1
                for half in range(2):
                    bank, btok = self.next_bank()
                    P.pe(TRS([(bank[:, cc * 128:(cc + 1) * 128], YN[:, half * 4 + cc, j * 128:(j + 1) * 128])
                              for cc in range(4)], self.ident[:]),
                         reads=["yn.%d" % (half * 4 + cc) for cc in range(4)] + ["ident"], writes=[btok])
                    if half == 0:
                        P.act(ACTF(ost[:, 0:512], bank[:, 0:512], AF.Identity), reads=[btok], writes=[otok])
                    else:
                        P.dve(CP(ost[:, 512:1024], bank[:, 0:512]), reads=[btok], writes=[otok])
                r0 = t0 + j * 128
                P.dma(DMA(self.out[b, r0:r0 + 128, :], ost), reads=[otok], writes=["out"])

    def da_mixer(self, b, l):
        P = self.P
        need_ctx = (l == 0)
        o = 0
        ROPE = self.scr_f(o, 4096).rearrange("p (a t) -> p a t", a=2); o += 4096
        QR = self.scr_b(o, T); o += T // 2
        KR = self.scr_b(o, T); o += T // 2
        VT = self.scr_b(o, T).rearrange("p (i d) -> p i d", i=NBLK); o += T // 2
        T1 = [self.scr_f(o + i * 512, 512) for i in range(2)]; o += 1024
        T2 = [self.scr_f(o + i * 512, 512) for i in range(2)]; o += 1024
        PT = [self.scr_b(o + i * 256, 512) for i in range(4)]; o += 1024
        RC = self.scr_f(o, 512); o += 512
        TC = [self.scr_f(o + i * 512, 512) for i in range(2)]; o += 1024
        O2 = self.scr_f(o, 512); o += 512
        OV = [self.scr_f(o + i * 512, 512) for i in range(2)]; o += 1024
        OSQ = self.scr_b(o, 512); o += 256
        assert o <= SCRW, o
        P.dma(DMA(ROPE, self.rope_d), writes=["rope"])
        lat = [0, 1, 2, 3]
        W = self.w_in[l]
        ti = [0]
        pend = {"A": None, "B": None}
        fcount = [0]

        def emitA():
            f = pend["A"]
            pend["A"] = None
            if f is not None:
                f()

        def emitB():
            f = pend["B"]
            pend["B"] = None
            if f is not None:
                f()

        def make_final(h, qt, q0, qn):
            fi = fcount[0] % 2
            fcount[0] += 1
            ov = OV[fi][:, :qn]
            ovt = "ov%d" % fi

            def fa():
                for c in range(2):
                    P.dve(RECIP(RC[:, :qn], T1[c][:, :qn]), reads=["t1.%d" % c], writes=["rc"])
                    P.dve(TT(TC[c][:, :qn], TC[c][:, :qn], RC[:, :qn], ALU.mult), reads=["tc%d" % c, "rc"],
                          writes=["tc%d" % c])
                P.dve(STT(ov, TC[1][:, :qn], self.nlam[:, l:l + 1], TC[0][:, :qn], ALU.mult, ALU.add),
                      reads=["tc0", "tc1", "nlam"], writes=[ovt])
                P.dve(TT(OSQ[:, :qn], ov, ov, ALU.mult), reads=[ovt], writes=["osq"])

            def fb():
                P.pe(MMS([(self.pb[3][:, :qn], self.ones_b[:], OSQ[:, :qn], True, True)]), reads=["osq", "ones_b"],
                     writes=["pb3"])
                P.act(ACTF(O2[:, :qn], self.pb[3][:, :qn], AF.Ln, bias=EPS, scale=1.0 / 128.0), reads=["pb3"],
                      writes=["o2"])
                P.act(ACTF(O2[:, :qn], O2[:, :qn], AF.Exp, scale=-0.5), reads=["o2"], writes=["o2"])
                P.dve(TT(ov, ov, O2[:, :qn], ALU.mult), reads=[ovt, "o2"], writes=[ovt])
                P.act(ACTF(self.MX[:, h, q0:q0 + qn], ov, AF.Identity, scale=self.gsc[:, l:l + 1]),
                      reads=[ovt, "gsc"], writes=["mx.%d" % qt])
            return fa, fb

        for h in range(4):
            self.rot = [0, 1, 2, 3]

            def ev_rope(dst, dtok):
                def ev(t, pa, ta, pb_, tb):
                    t0, n = TILES[t]
                    i = ti[0] % 2
                    ti[0] += 1
                    P.dve(TT(T1[i][:, :n], pa, ROPE[:, 0, t0:t0 + n], ALU.mult), reads=[ta, "rope"], writes=["t1.%d" % i])
                    P.dve(TT(T2[i][:, :n], pb_, ROPE[:, 1, t0:t0 + n], ALU.mult), reads=[tb, "rope"], writes=["t2.%d" % i])
                    P.pool(TT(dst[:, t0:t0 + n], T1[i][:, :n], T2[i][:, :n], ALU.add), reads=["t1.%d" % i, "t2.%d" % i],
                           writes=["%s.%d" % (dtok, t)])
                return ev

            def ev_plain(dst, dtok):
                def ev(t, ps, btok):
                    t0, n = TILES[t]
                    P.act(ACTF(dst[:, t0:t0 + n], ps, AF.Identity), reads=[btok], writes=["%s.%d" % (dtok, t)])
                return ev
            emitA()
            self.linear_fm2(W, OFF_Q + h * 128, OFF_QS + h * 128, lat, ev_rope(QR, "qr"), [4] if need_ctx else [],
                            ev_plain(QR, "qr"))
            emitB()
            self.linear_fm2(W, OFF_K + h * 128, OFF_KS + h * 128, lat, ev_rope(KR, "kr"), [4], ev_plain(KR, "kr"))

            def ev_v(blks, ps, btok):
                i0 = blks[0]
                P.act(ACTF(VT[:, i0:i0 + len(blks), :], ps.rearrange("p (i d) -> p i d", i=len(blks)), AF.Identity),
                      reads=[btok], writes=["vt.%d" % i for i in blks])
            self.linear_tm(W, OFF_V + h * 128, 128, list(range(NBLK)), ev_v, 4)

            SB = [(0, 1), (2, 3)]
            for qt in (lat + ([4] if need_ctx else [])):
                q0, qn = TILES[qt]
                kts = list(range(NBLK)) if qt < 4 else [16, 17]
                nk = len(kts)

                def emit_s(ii):
                    kt = kts[ii]
                    pa, pb_ = SB[ii % 2]
                    P.pe(MMS([(self.pb[pa][:, :qn], KR[0:64, kt * 128:(kt + 1) * 128], QR[0:64, q0:q0 + qn], True, True),
                              (self.pb[pb_][:, :qn], KR[64:128, kt * 128:(kt + 1) * 128], QR[64:128, q0:q0 + qn], True,
                               True)]),
                         reads=["kr.%d" % min(kt // 4, 4), "qr.%d" % qt], writes=["pb%d" % pa, "pb%d" % pb_])
                emit_s(0)
                for ii, kt in enumerate(kts):
                    if ii == 0:
                        emitA()
                    if ii == min(12, nk - 1):
                        emitB()
                    if ii + 1 < nk:
                        emit_s(ii + 1)
                    first, last = (ii == 0), (ii == nk - 1)
                    for c in range(2):
                        sb = SB[ii % 2][c]
                        pi_ = (2 * ii + c) % 4
                        P.act(ACTF(PT[pi_][:, :qn], self.pb[sb][:, :qn], AF.Exp, scale=0.125), reads=["pb%d" % sb],
                              writes=["pt%d" % pi_])
                        P.pe(MMS([(self.pb[4 + c][:, :qn], VT[:, kt, :], PT[pi_][:, :qn], first, last),
                                  (self.pb[6 + c][:, :qn], self.ones_b[:], PT[pi_][:, :qn], first, last)]),
                             reads=["pt%d" % pi_, "vt.%d" % kt, "ones_b"], writes=["pb%d" % (4 + c), "pb%d" % (6 + c)])
                emitA()
                emitB()
                for c in range(2):
                    P.dve(CP(TC[c][:, :qn], self.pb[4 + c][:, :qn]), reads=["pb%d" % (4 + c)], writes=["tc%d" % c])
                    P.act(ACTF(T1[c][:, :qn], self.pb[6 + c][:, :qn], AF.Identity), reads=["pb%d" % (6 + c)],
                          writes=["t1.%d" % c])
                pend["A"], pend["B"] = make_final(h, qt, q0, qn)
        emitA()
        emitB()

    def rg_mixer(self, b, l):
        P = self.P
        need_ctx = (l == 0)
        o = 0
        RX = self.scr_f(o, 2312); o += 2312
        XC = self.scr_f(o, T); o += T
        XCB = self.scr_b(o, T); o += T // 2
        HS = self.scr_f(o, T); o += T
        B1 = self.scr_f(o, T); o += T
        B2 = self.scr_f(o, T); o += T
        TL = [self.scr_f(o + i * 512, 512) for i in range(4)]; o += 2048
        assert o <= SCRW, o
        B0 = RX[:, 0:T]
        b0t = ["rx.%d" % t for t in range(5)] + ["rxpad"]
        W = self.w_in[l]
        self.rot = [0, 1, 2, 3]
        it = 0
        nout = T if need_ctx else NLAT
        for cc in range(2):
            for (a0, a1) in ((0, 2), (2050, 2053), (2309, 2312)):
                P.pool(MSET(RX[:, a0:a1], 0.0), writes=["rxpad"])

            def ev_x(t, ps, btok):
                t0, n = TILES[t]
                off = t0 + 2 if t < 4 else t0 + 5
                P.act(ACTF(RX[:, off:off + n], ps, AF.Identity), reads=[btok], writes=["rx.%d" % t])
            self.linear_fm(W, OFF_RGX + cc * 128, [0, 1, 2, 3, 4], ev_x)
            for (x0, r0, n, rt, wt) in ((0, 0, NLAT, ["rx.0", "rx.1", "rx.2", "rx.3"], ["xc.0", "xc.1", "xc.2", "xc.3"]),
                                        (NLAT, 2051, NCTX, ["rx.4"], ["xc.4"])):
                wo = PRM_OFF["rgcw"] + (l * 2 + cc) * 4
                P.act(ACTF(XC[:, x0:x0 + n], RX[:, r0:r0 + n], AF.Identity, bias=self.prmcol("rgcb", l * 2 + cc),
                           scale=self.prm[:, wo:wo + 1]), reads=rt + ["rxpad", "prm"], writes=wt)
                for k in range(1, 4):
                    P.dve(STT(XC[:, x0:x0 + n], RX[:, r0 + k:r0 + k + n], self.prm[:, wo + k:wo + k + 1], XC[:, x0:x0 + n],
                              ALU.mult, ALU.add), reads=rt + wt + ["rxpad", "prm"], writes=wt)
                P.pool(CP(XCB[:, x0:x0 + n], XC[:, x0:x0 + n]), reads=wt, writes=["xcb.%d" % t for t in
                                                                                  ([0, 1, 2, 3] if x0 == 0 else [4])])
            xct = ["xc.%d" % t for t in range(5)]
            for d in range(2):
                pi = (l * 2 + d) * 2 + cc
                for t in range(5):
                    t0, n = TILES[t]
                    TA, TX = TL[(it % 2) * 2], TL[(it % 2) * 2 + 1]
                    sfx = "%d" % (it % 2)
                    it += 1
                    ba, bat = self.next_bank()
                    bx, bxt = self.next_bank()
                    P.pe(MMS([(ba[:, :n], self.BD[:, pi * 2, :], XCB[:, t0:t0 + n], True, True),
                              (bx[:, :n], self.BD[:, pi * 2 + 1, :], XCB[:, t0:t0 + n], True, True)]),
                         reads=["bd", "xcb.%d" % t], writes=[bat, bxt])
                    P.act(ACTF(TA[:, :n], ba[:, :n], AF.Tanh, bias=self.hba[:, pi:pi + 1], scale=0.5),
                          reads=[bat, "hba"], writes=["ta" + sfx])
                    P.act(ACTF(B0[:, t0:t0 + n], TA[:, :n], AF.Exp, bias=self.hm[:, pi:pi + 1],
                               scale=self.hm[:, pi:pi + 1]), reads=["ta" + sfx, "hm"], writes=b0t)
                    P.act(ACTF(B1[:, t0:t0 + n], B0[:, t0:t0 + n], AF.Square), reads=b0t, writes=["b1"])
                    P.act(ACTF(TX[:, :n], bx[:, :n], AF.Tanh, bias=self.hbx[:, pi:pi + 1], scale=0.5),
                          reads=[bxt, "hbx"], writes=["tx" + sfx])
                    P.dve(STT(B2[:, t0:t0 + n], TX[:, :n], 1.0, XC[:, t0:t0 + n], ALU.add, ALU.mult),
                          reads=["tx" + sfx, "xc.%d" % t], writes=["b2"])
                    self.mods_step(4)
                P.act(ACTF(B1, B1, AF.Sqrt, bias=1.0, scale=-(1.0 - 2.0 ** -23)), reads=["b1"], writes=["b1"])
                P.dve(STT(B1, B2, 0.5, B1, ALU.mult, ALU.mult), reads=["b1", "b2"], writes=["b1"])
                if d == 0:
                    P.dve(SCAN(B2[:, NLAT:T], B0[:, NLAT:T], B1[:, NLAT:T], 0.0), reads=b0t + ["b1"], writes=["b2"])
                    P.dve(SCAN(B2[:, 0:NLAT], B0[:, 0:NLAT], B1[:, 0:NLAT], B2[:, T - 1:T]), reads=b0t + ["b1", "b2"],
                          writes=["b2"])
                    P.pool(CP(HS[:, 0:nout], B2[:, 0:nout]), reads=["b2"], writes=["hs"])
                else:
                    P.dve(SCAN(B2[:, NLAT:T][:, ::-1], B0[:, NLAT:T][:, ::-1], B1[:, NLAT:T][:, ::-1], 0.0),
                          reads=b0t + ["b1"], writes=["b2"])
                    P.dve(SCAN(B2[:, 0:NLAT][:, ::-1], B0[:, 0:NLAT][:, ::-1], B1[:, 0:NLAT][:, ::-1], B2[:, NLAT:NLAT + 1]),
                          reads=b0t + ["b1", "b2"], writes=["b2"])
                    P.pool(TT(HS[:, 0:nout], HS[:, 0:nout], B2[:, 0:nout], ALU.add), reads=["b2", "hs"], writes=["hs"])
            def ev_g(t, ps, btok):
                t0, n = TILES[t]
                P.act(ACTF(B0[:, t0:t0 + n], ps, AF.Identity), reads=[btok], writes=b0t)
            self.linear_fm(W, OFF_RGG + cc * 128, [0, 1, 2, 3] + ([4] if need_ctx else []), ev_g)
            G, Q = B0[:, 0:nout], B1[:, 0:nout]
            P.dve(TT(Q, G, G, ALU.mult), reads=b0t, writes=["b1"])
            P.dve(TS(Q, Q, 0.044715, ALU.mult, 1.0, ALU.add), reads=["b1"], writes=["b1"])
            P.dve(TT(Q, Q, G, ALU.mult), reads=["b1"] + b0t, writes=["b1"])
            P.act(ACTF(Q, Q, AF.Tanh, scale=0.7978845608028654), reads=["b1"], writes=["b1"])
            P.dve(STT(Q, Q, 1.0, G, ALU.add, ALU.mult), reads=["b1"] + b0t, writes=["b1"])
            P.dve(STT(self.MX[:, cc, 0:nout], Q, 0.5, HS[:, 0:nout], ALU.mult, ALU.mult), reads=["b1", "hs"],
                  writes=["mx.%d" % t for t in range(5 if need_ctx else 4)])
        self.mods_step(1000)

    def ssd_mixer(self, b, l):
        P = self.P
        need_ctx = (l == 0)
        W = self.w_in[l]
        o = 0
        RAW = self.scr_f(o, 2312); o += 2312
        TMP = self.scr_f(o, T); o += T
        XS = self.scr_b(o, NBLK * 256).rearrange("p (i d) -> p i d", i=NBLK); o += NBLK * 128
        BF = self.scr_b(o, T); o += T // 2
        CF = self.scr_b(o, T); o += T // 2
        BT = self.scr_b(o, NBLK * 128).rearrange("p (i d) -> p i d", i=NBLK); o += NBLK * 64
        SZ = self.scr_b(o, NBLK * 256).rearrange("p (i d) -> p i d", i=NBLK); o += NBLK * 128
        DTR = self.scr_f(o, 144); o += 144
        DTv = self.scr_f(o, 144); o += 144
        ADT = self.scr_f(o, 144); o += 144
        SPS = self.scr_f(o, 576); o += 576
        assert o <= SCRW, o
        DT3 = DTv.rearrange("p (i e) -> p i e", i=NBLK)
        ADT3 = ADT.rearrange("p (i e) -> p i e", i=NBLK)
        self.rot = [0, 1, 2, 3]
        for ch in range(4):
            for (a0, a1) in ((0, 2), (2050, 2053), (2309, 2312)):
                P.pool(MSET(RAW[:, a0:a1], 0.0), writes=["rxpad"])

            def ev_x(t, ps, btok):
                t0, n = TILES[t]
                off = t0 + 2 if t < 4 else t0 + 5
                P.act(ACTF(RAW[:, off:off + n], ps, AF.Identity), reads=[btok], writes=["rx.%d" % t])
            self.linear_fm(W, OFF_XBC + ch * 128, [0, 1, 2, 3, 4], ev_x)
            for (x0, r0, n, rt, wt) in ((0, 0, NLAT, ["rx.0", "rx.1", "rx.2", "rx.3"], ["xc.0", "xc.1", "xc.2", "xc.3"]),
                                        (NLAT, 2051, NCTX, ["rx.4"], ["xc.4"])):
                wo = PRM_OFF["scw"] + (l * 4 + ch) * 4
                P.act(ACTF(TMP[:, x0:x0 + n], RAW[:, r0:r0 + n], AF.Identity, bias=self.prmcol("scb", l * 4 + ch),
                           scale=self.prm[:, wo:wo + 1]), reads=rt + ["rxpad", "prm"], writes=wt)
                for k in range(1, 4):
                    P.dve(STT(TMP[:, x0:x0 + n], RAW[:, r0 + k:r0 + k + n], self.prm[:, wo + k:wo + k + 1],
                              TMP[:, x0:x0 + n], ALU.mult, ALU.add), reads=rt + wt + ["rxpad", "prm"], writes=wt)
                if ch == 3:
                    P.act(ACTF(CF[:, x0:x0 + n], TMP[:, x0:x0 + n], AF.Silu), reads=wt, writes=["cf"])
                else:
                    P.act(ACTF(TMP[:, x0:x0 + n], TMP[:, x0:x0 + n], AF.Silu), reads=wt, writes=wt)
                    if ch == 2:
                        P.pool(CP(BF[:, x0:x0 + n], TMP[:, x0:x0 + n]), reads=wt, writes=["bf"])
            if ch < 3:
                for i0 in range(0, NBLK, 4):
                    blks = list(range(i0, min(i0 + 4, NBLK)))
                    bank, btok = self.next_bank()
                    P.pe(TRS([(bank[:, gi * 128:(gi + 1) * 128], TMP[:, i * 128:(i + 1) * 128]) for gi, i in
                              enumerate(blks)], self.ident[:]),
                         reads=["xc.%d" % min(i // 4, 4) for i in blks] + ["ident"], writes=[btok])
                    src = bank[:, 0:len(blks) * 128].rearrange("p (i d) -> p i d", i=len(blks))
                    if ch < 2:
                        P.dve(CP(XS[:, i0:i0 + len(blks), ch * 128:(ch + 1) * 128], src), reads=[btok], writes=["xs"])
                    else:
                        P.dve(CP(BT[:, i0:i0 + len(blks), :], src), reads=[btok], writes=["bt"])
        for zc in range(2):
            def ev_z(blks, ps, btok, zc=zc):
                i0 = blks[0]
                P.act(ACTF(SZ[:, i0:i0 + len(blks), zc * 128:(zc + 1) * 128],
                           ps.rearrange("p (i d) -> p i d", i=len(blks)), AF.Silu), reads=[btok], writes=["sz"])
            self.linear_tm(W, OFF_SZ + zc * 128, 128, list(range(NBLK)), ev_z, 4)

        def ev_dt(blks, ps, btok):
            do = ROW_OFF["dtb"] + l * 8
            P.dve(TT(DTR.rearrange("p (i e) -> p i e", i=NBLK), ps.rearrange("p (i e) -> p i e", i=NBLK),
                     self.rowp[:, do:do + 8].unsqueeze(1).to_broadcast([128, NBLK, 8]), ALU.add),
                  reads=[btok, "rowp"], writes=["dtr"])
        self.linear_tm(W, OFF_DT, 8, list(range(NBLK)), ev_dt, NBLK)
        self.softplus(DTv, DTR, 144, SPS, (["dtr"], ["dt"]))
        P.dve(TT(ADT3, DT3, self.aneg[:, l * 8:l * 8 + 8].unsqueeze(1).to_broadcast([128, NBLK, 8]), ALU.mult),
              reads=["dt", "aneg"], writes=["adt"])
        P.barrier()
        if self.o.get("ssd_prep_only"):
            P.pool(MSET(self.MX[:, 2:4, :], 0.0), writes=["mx.%d" % t for t in range(5)])
            return
        xf = self.xn[:].bitcast(F32)
        o = 0
        Y = xf[:, o:o + NBLK * 256].rearrange("p (i d) -> p i d", i=NBLK); o += NBLK * 256
        S = []
        for d in range(2):
            sd = {}
            sd["LT"] = xf[:, o:o + 512]; o += 512
            sd["D1"] = xf[:, o:o + 512]; o += 512
            sd["GM"] = xf[:, o:o + 256]; o += 256
            sd["T1"] = xf[:, o:o + 256]; o += 256
            sd["H"] = xf[:, o:o + 128]; o += 128
            sd["SM"] = xf[:, o:o + 64]; o += 64
            sd["Mb"] = self.xn[:, 2 * o:2 * o + 512].rearrange("p (h t) -> p h t", h=4); o += 256
            sd["XD"] = self.xn[:, 2 * o:2 * o + 256].rearrange("p (h d) -> p h d", h=4); o += 128
            sd["XDW"] = self.xn[:, 2 * o:2 * o + 256].rearrange("p (h d) -> p h d", h=4); o += 128
            sd["HB"] = self.xn[:, 2 * o:2 * o + 128]; o += 64
            S.append(sd)
        assert o <= 9216, o
        BFm = [self.scr_b(g * 1152, T) for g in range(2)]
        CFm = [self.scr_b(2304 + g * 1152, T) for g in range(2)]
        P.pool(MSET(self.scr_f(0, 4608), 0.0), writes=["bfm", "cfm"])
        for g in range(2):
            P.pool(CP(BFm[g][64 * g:64 * g + 64, :], BF[64 * g:64 * g + 64, :]), reads=["bf", "bfm"], writes=["bfm"])
            P.pool(CP(CFm[g][64 * g:64 * g + 64, :], CF[64 * g:64 * g + 64, :]), reads=["cf", "cfm"], writes=["cfm"])
        fo = 13700
        VV = self.scr_f(fo, 256)
        OT = self.scr_f(fo + 256, 256)
        JK = self.scr_f(fo + 512, 256)
        FS = self.scr_f(fo + 768, 8)
        assert fo + 776 <= SCRW
        rsd, ssq = FS[:, 0:2], FS[:, 2:4]
        PB = self.pb
        dsk = self.rowp[:, ROW_OFF["dsk"] + l * 4:ROW_OFF["dsk"] + l * 4 + 4]
        sng = self.rowp[:, ROW_OFF["sng"] + l * 256:ROW_OFF["sng"] + (l + 1) * 256]
        ytk = ["y.%d" % k for k in range(NBLK)]
        P.dve(TT(Y.rearrange("p i (h d) -> p i h d", h=4), XS.rearrange("p i (h d) -> p i h d", h=4),
                 dsk.unsqueeze(1).unsqueeze(3).to_broadcast([128, NBLK, 4, 64]), ALU.mult), reads=["xs", "rowp"], writes=ytk)
        for d in range(2):
            P.dve(MSET(S[d]["H"], 0.0), writes=["H%d" % d])
            P.dve(MSET(S[d]["HB"], 0.0), writes=["HB%d" % d])
        orders = [[16, 17] + list(range(16)), [17, 16] + list(range(15, -1, -1))]

        class _Rec:
            def __init__(self):
                self.calls = []

            def __getattr__(self, name):
                def f(*a_, **k_):
                    self.calls.append((name, a_, k_))
                return f

        def chunk_step(d, k, P):
            sd = S[d]
            sx = "%d" % d
            U = self.uf if d == 0 else self.ub
            utok = "uf" if d == 0 else "ub"
            edge = 127 if d == 0 else 0
            A, B, C, Dk = (PB[4 * d + i] for i in range(4))
            At, Bt, Ct, Dt = ("pb%d" % (4 * d + i) for i in range(4))
            cst, ecs, wdec, dtw, dch = (sd["SM"][:, i * 4:(i + 1) * 4] for i in range(5))
            LT, D1, GM, T1, H, HB, Mb, XD, XDW = (sd[n] for n in ("LT", "D1", "GM", "T1", "H", "HB", "Mb", "XD", "XDW"))
            want_y = (k < 16) or need_ctx
            blk = slice(k * 128, (k + 1) * 128)
            adt = ADT3[:, k, d * 4:d * 4 + 4]
            mm = [(Dk[:, 0:4], U[:], adt, True, True)]
            for h in range(4):
                mm.append((A[:, h * 128:(h + 1) * 128], ADT3[:, k, d * 4 + h:d * 4 + h + 1].to_broadcast([128, 128]),
                           U[:], True, True))
            P.pe(MMS(mm), reads=["adt", utok], writes=[Dt, At])
            P.dve(CP(cst, Dk[:, 0:4]), reads=[Dt], writes=["cst" + sx])
            csb = A[:, 0:512].rearrange("p (h t) -> p h t", h=4)
            dts = DT3[:, k, d * 4:d * 4 + 4]
            XS4 = XS[:, k, :].rearrange("p (h d) -> p h d", h=4)
            if want_y:
                P.dve(TT(D1.rearrange("p (h t) -> p h t", h=4), csb, cst.unsqueeze(2).to_broadcast([128, 4, 128]),
                         ALU.subtract), reads=[At, "cst" + sx], writes=["d1" + sx])
                P.dve(STT(D1, D1, -1.0, D1, ALU.mult, ALU.min), reads=["d1" + sx], writes=["d1" + sx])
                P.act(ACTF(LT, D1, AF.Exp), reads=["d1" + sx], writes=["lt" + sx])
                P.pe(MMS([(B[:, g * 128:(g + 1) * 128], BFm[g][:, blk], CF[:, blk], True, True) for g in range(2)]),
                     reads=["bfm", "cf"], writes=[Bt])
                P.dve(TT(GM.rearrange("p (g t) -> p g t", g=2), B[:, 0:256].rearrange("p (g t) -> p g t", g=2),
                         U[:].unsqueeze(1).to_broadcast([128, 2, 128]), ALU.mult), reads=[Bt, utok], writes=["gm" + sx])
                P.dve(TT(Mb.rearrange("p (g a) t -> p g a t", g=2), LT.rearrange("p (g a t) -> p g a t", g=2, a=2),
                         GM.rearrange("p (g t) -> p g t", g=2).unsqueeze(2).to_broadcast([128, 2, 2, 128]), ALU.mult),
                      reads=["lt" + sx, "gm" + sx], writes=["mb" + sx])
                P.dve(TT(XD, XS4, dts.unsqueeze(2).to_broadcast([128, 4, 64]), ALU.mult), reads=["xs", "dt"],
                      writes=["xd" + sx])
                P.pe(MMS([(C[:, h * 64:(h + 1) * 64], Mb[:, h, :], XD[:, h, :], True, True) for h in range(4)]
                         + [(C[:, 256 + g * 128:256 + (g + 1) * 128], CFm[g][:, blk], HB[:, :], True, True)
                            for g in range(2)]),
                     reads=["mb" + sx, "xd" + sx, "cfm", "HB" + sx], writes=[Ct])
                P.act(ACTF(ecs, cst, AF.Exp), reads=["cst" + sx], writes=["ecs" + sx])
                P.dve(TT(T1.rearrange("p (h d) -> p h d", h=4), C[:, 256:512].rearrange("p (h d) -> p h d", h=4),
                         ecs.unsqueeze(2).to_broadcast([128, 4, 64]), ALU.mult), reads=[Ct, "ecs" + sx], writes=["t1" + sx])
                P.dve(TT(T1, T1, C[:, 0:256], ALU.add), reads=["t1" + sx, Ct], writes=["t1" + sx])
                P.dve(TT(Y[:, k, :], Y[:, k, :], T1, ALU.add), reads=["t1" + sx, "y.%d" % k], writes=["y.%d" % k])
            P.dve(TT(wdec, csb[:, :, edge], cst, ALU.subtract), reads=[At, "cst" + sx], writes=["wdec" + sx])
            P.act(ACTF(wdec, wdec, AF.Exp), reads=["wdec" + sx], writes=["wdec" + sx])
            P.dve(TT(dtw, wdec, dts, ALU.mult), reads=["wdec" + sx, "dt"], writes=["dtw" + sx])
            P.dve(TT(XDW, XS4, dtw.unsqueeze(2).to_broadcast([128, 4, 64]), ALU.mult), reads=["xs", "dtw" + sx],
                  writes=["xdw" + sx])
            P.pe(MMS([(Dk[64 * g:64 * g + 64, 128:256], BT[:, k, 64 * g:64 * g + 64],
                       XDW[:, 2 * g:2 * g + 2, :].rearrange("p a d -> p (a d)"), True, True) for g in range(2)]),
                 reads=["bt", "xdw" + sx], writes=[Dt])
            for g in range(2):
                P.act(ACTF(dch[64 * g:64 * g + 64, 0:2], csb[64 * g:64 * g + 64, 2 * g:2 * g + 2, edge], AF.Exp),
                      reads=[At], writes=["dch" + sx])
            P.dve(TT(H.rearrange("p (a d) -> p a d", a=2), H.rearrange("p (a d) -> p a d", a=2),
                     dch[:, 0:2].unsqueeze(2).to_broadcast([128, 2, 64]), ALU.mult), reads=["H" + sx, "dch" + sx],
                  writes=["H" + sx])
            P.dve(TT(H, H, Dk[:, 128:256], ALU.add), reads=["H" + sx, Dt], writes=["H" + sx])
            P.dve(CP(HB, H), reads=["H" + sx], writes=["HB" + sx])

        def finalize(k, P):
            P.dve(TT(VV, Y[:, k, :], SZ[:, k, :], ALU.mult), reads=["y.%d" % k, "sz"], writes=["vv"])
            for g in range(2):
                P.act(ACTF(JK[:, g * 128:(g + 1) * 128], VV[:, g * 128:(g + 1) * 128], AF.Square,
                           accum_out=ssq[:, g:g + 1]), reads=["vv"], writes=["jk", "ssq"])
            P.act(ACTF(rsd, ssq, AF.Ln, bias=EPS, scale=1.0 / 128.0), reads=["ssq"], writes=["rsd"])
            P.act(ACTF(rsd, rsd, AF.Exp, scale=-0.5), reads=["rsd"], writes=["rsd"])
            for g in range(2):
                P.dve(STT(OT[:, g * 128:(g + 1) * 128], VV[:, g * 128:(g + 1) * 128], rsd[:, g:g + 1],
                          sng[:, g * 128:(g + 1) * 128], ALU.mult, ALU.mult), reads=["vv", "rsd", "rowp"], writes=["ot"])
            P.pe(TRS([(PB[5][:, g * 128:(g + 1) * 128], OT[:, g * 128:(g + 1) * 128]) for g in range(2)], self.ident[:]),
                 reads=["ot", "ident"], writes=["pb5"])
            P.act(ACTF(self.MX[:, 2:4, k * 128:(k + 1) * 128], PB[5][:, 0:256].rearrange("p (g t) -> p g t", g=2),
                       AF.Identity), reads=["pb5"], writes=["mx.%d" % min(k // 4, 4)])

        def replay(streams):
            n = max([len(c) for c in streams] + [0])
            for i in range(n):
                for c in streams:
                    if i < len(c):
                        name, a_, k_ = c[i]
                        getattr(P, name)(*a_, **k_)

        done = {}
        fin_prev = []
        for j in range(NBLK):
            streams = []
            for d in range(2):
                r = _Rec()
                chunk_step(d, orders[d][j], r)
                streams.append(r.calls)
            streams.append(fin_prev)
            replay(streams)
            fr = _Rec()
            for d in range(2):
                k = orders[d][j]
                done[k] = done.get(k, 0) + 1
            for d in range(2):
                k = orders[d][j]
                if done.get(k) == 2 and ((k < 16) or need_ctx):
                    finalize(k, fr)
                    done[k] = 3
            fin_prev = fr.calls
        replay([fin_prev])

    def layer(self, b, l):
        P = self.P
        o = self.o
        need_ctx = (l == 0)
        allt = [0, 1, 2, 3, 4]
        outt = allt if need_ctx else [0, 1, 2, 3]
        if o.get("mix", True):
            self.norm(b, l, 0, allt)
            P.barrier()
            if o.get("da", True):
                self.da_mixer(b, l)
                self.wout_partial(b, l, 512, outt)
                P.barrier()
            if o.get("rg", True) or o.get("ssd", True):
                if o.get("rg", True):
                    self.rg_mixer(b, l)
                else:
                    P.pool(MSET(self.MX[:, 0:2, :], 0.0), writes=["mx.%d" % t for t in range(5)])
                P.barrier()
                if o.get("ssd", True):
                    self.ssd_mixer(b, l)
                else:
                    P.pool(MSET(self.MX[:, 2:4, :], 0.0), writes=["mx.%d" % t for t in range(5)])
                self.wout_partial(b, l, 0, outt)
                P.barrier()
        self.mods_step(1000)
        if o.get("ffn", True):
            self.norm(b, l, 1, outt)
            P.barrier()
            self.ffn(b, l, outt)
            P.barrier()

    def run(self):
        self.setup()
        for b in range(self.NB):
            self.load_x(b)
            self.P.barrier()
            for l in range(self.o.get("layers", 2)):
                self.layer(b, l)
            self.final(b)
            self.P.barrier()


def build_nc(NB=2, opts=None):
    nc = bass.Bass("TRN2", target_bir_lowering=False)
    with contextlib.ExitStack() as st:
        P = Prog(nc, st)
        K = Kern(nc, P, NB, opts or {})
        K.run()
        P.emit(final_reads=["out"])
    return nc


_NC_CACHE = {}


def kernel(x, c, ctx, c_ctx, w_mod, b_mod, norm1_g, w_in, rg_conv_w, rg_conv_b, rg_w_a, rg_b_a,
           rg_w_x, rg_b_x, rg_lambda, ssd_conv_w, ssd_conv_b, ssd_dt_bias, ssd_a_log, ssd_d,
           ssd_norm_g, da_lambda, da_subln_g, w_out, norm2_g, w_gate, w_up, w_down, final_norm_g,
           _opts=None, _ncores=8):
    inp = dict(w_mod=w_mod, b_mod=b_mod, norm1_g=norm1_g, w_in=w_in, rg_conv_w=rg_conv_w, rg_conv_b=rg_conv_b,
               rg_w_a=rg_w_a, rg_b_a=rg_b_a, rg_w_x=rg_w_x, rg_b_x=rg_b_x, rg_lambda=rg_lambda,
               ssd_conv_w=ssd_conv_w, ssd_conv_b=ssd_conv_b, ssd_dt_bias=ssd_dt_bias, ssd_a_log=ssd_a_log,
               ssd_d=ssd_d, ssd_norm_g=ssd_norm_g, da_lambda=da_lambda, da_subln_g=da_subln_g, w_out=w_out,
               norm2_g=norm2_g, w_gate=w_gate, w_up=w_up, w_down=w_down, final_norm_g=final_norm_g)
    shared = pack_shared(inp)
    x = np.asarray(x, np.float32)
    ctx = np.asarray(ctx, np.float32)
    c = np.asarray(c, np.float32)
    c_ctx = np.asarray(c_ctx, np.float32)
    NB = 2
    in_maps = []
    for i in range(_ncores):
        cc = np.stack([c[2 * i], c[2 * i + 1], c_ctx], axis=0)
        ct = np.ascontiguousarray(cc.reshape(3, 8, 128).transpose(2, 1, 0))
        m = dict(shared)
        m["x"] = np.ascontiguousarray(x[2 * i:2 * i + 2])
        m["ctx"] = np.ascontiguousarray(ctx[2 * i:2 * i + 2])
        m["ct"] = ct
        in_maps.append(m)
    key = repr(sorted((_opts or {}).items()))
    if key not in _NC_CACHE:
        _NC_CACHE[key] = build_nc(NB, _opts)
    nc = _NC_CACHE[key]
    res = run_bass_kernel_spmd(nc, in_maps, core_ids=list(range(_ncores)))
    out = np.concatenate([np.asarray(r["out"], np.float32) for r in res.results], axis=0)
    return out
```

```python
import contextlib
import math
import numpy as np
import concourse.bass as bass
import concourse.mybir as mybir
from concourse.bass_utils import run_bass_kernel_spmd

F32 = mybir.dt.float32
BF16 = mybir.dt.bfloat16
ALU = mybir.AluOpType
AF = mybir.ActivationFunctionType

D = 1024
NLAT = 2048
NCTX = 256
T = NLAT + NCTX
NBLK = T // 128
TILES = [(0, 512), (512, 512), (1024, 512), (1536, 512), (2048, 256)]
DFF = 2816
DIN = 2824
DINX = DIN + 1024
EPS = 1e-6
OFF_RGX, OFF_RGG, OFF_SZ, OFF_XBC, OFF_DT, OFF_Q, OFF_K, OFF_V = 0, 256, 512, 768, 1280, 1288, 1800, 2312
OFF_QS, OFF_KS = DIN, DIN + 512
SCRW = 14848


class _Op:
    __slots__ = ("eng", "fn", "waits", "tok", "clock", "dma", "idx")


class Prog:
    ENGS = ("pe", "act", "dve", "pool", "sp")

    def __init__(self, nc, stack, n_dma_sems=12):
        self.nc = nc
        self.stack = stack
        self.ops = {e: [] for e in self.ENGS}
        self.sem = {e: stack.enter_context(nc.semaphore("S_" + e)) for e in self.ENGS}
        self.cnt = {e: 0 for e in self.ENGS}
        self.know = {e: {} for e in self.ENGS}
        self.last_op = {e: None for e in self.ENGS}
        self.dma_sems = {}
        for q in ("sp", "pool"):
            self.dma_sems[q] = [[stack.enter_context(nc.semaphore("D_%s%d" % (q, i))), 0, None]
                                for i in range(n_dma_sems)]
        self.dma_rr = {q: 0 for q in self.dma_sems}
        self.last_w = {}
        self.readers = {}
        self.nops = 0
        self.nwaits = 0

    def sb(self, name, shape, dt=F32):
        return self.stack.enter_context(self.nc.sbuf_tensor(name, list(shape), dt))

    def ps(self, name, shape, dt=F32):
        return self.stack.enter_context(self.nc.psum_tensor(name, list(shape), dt))

    def _need(self, e, d, waits):
        if d is None:
            return
        if e == "pe" and d.eng == "pe" and not d.dma:
            return
        sem, val = d.tok
        k = self.know[e]
        if k.get(sem, 0) >= val:
            return
        waits.append((sem, val))
        for s, v in d.clock.items():
            if k.get(s, 0) < v:
                k[s] = v

    def op(self, eng, fn, reads=(), writes=(), dma=False, extra_deps=()):
        o = _Op()
        o.eng = eng
        o.fn = fn
        o.dma = dma
        o.idx = self.nops
        self.nops += 1
        deps = {}
        psum_r = [r for r in reads if r.startswith("pb") and r not in writes]
        if psum_r:
            writes = list(writes) + psum_r
            reads = [r for r in reads if r not in psum_r]
        for r in reads:
            d = self.last_w.get(r)
            if d is not None:
                deps[d.idx] = d
        for w in writes:
            d = self.last_w.get(w)
            if d is not None:
                deps[d.idx] = d
            for d in self.readers.get(w, ()):
                deps[d.idx] = d
        for d in extra_deps:
            if d is not None:
                deps[d.idx] = d
        waits = []
        for i in sorted(deps, reverse=True):
            self._need(eng, deps[i], waits)
        if dma:
            pool = self.dma_sems[eng]
            slot = pool[self.dma_rr[eng] % len(pool)]
            self.dma_rr[eng] += 1
            if slot[2] is not None:
                self._need(eng, slot[2], waits)
            slot[1] += 16
            o.tok = (slot[0], slot[1])
            slot[2] = o
        elif fn is not None:
            self.cnt[eng] += 1
            o.tok = (self.sem[eng], self.cnt[eng])
        else:
            o.tok = None
        o.waits = waits
        self.nwaits += len(waits)
        o.clock = dict(self.know[eng])
        if o.tok is not None:
            o.clock[o.tok[0]] = o.tok[1]
            if not dma:
                self.last_op[eng] = o
        for w in writes:
            self.last_w[w] = o
            self.readers[w] = []
        for r in reads:
            if r not in writes:
                self.readers.setdefault(r, []).append(o)
        self.ops[eng].append(o)
        return o

    def pe(self, fn, reads=(), writes=()):
        return self.op("pe", fn, reads, writes)

    def act(self, fn, reads=(), writes=()):
        return self.op("act", fn, reads, writes)

    def dve(self, fn, reads=(), writes=()):
        return self.op("dve", fn, reads, writes)

    def pool(self, fn, reads=(), writes=()):
        return self.op("pool", fn, reads, writes)

    def dma(self, fn, reads=(), writes=(), q="sp"):
        return self.op(q, fn, reads, writes, dma=True)

    def barrier(self):
        pend = [self.last_op[e] for e in self.ENGS if self.last_op[e] is not None]
        for q in self.dma_sems:
            for slot in self.dma_sems[q]:
                if slot[2] is not None:
                    pend.append(slot[2])
        for e in self.ENGS:
            self.op(e, None, extra_deps=pend)

    def emit(self, final_reads=()):
        nc = self.nc
        fin_waits = []
        for r in final_reads:
            self._need("sp", self.last_w.get(r), fin_waits)

        def run(e, engine, extra=()):
            for o in self.ops[e]:
                for (s, v) in o.waits:
                    engine.wait_ge(s, v)
                if o.fn is None:
                    continue
                ins = o.fn(engine)
                ins.then_inc(o.tok[0], 16 if o.dma else 1)
            for (s, v) in extra:
                engine.wait_ge(s, v)

        with nc.Block() as block:
            @block.sync
            def _(eng):
                run("sp", eng, fin_waits)

            @block.gpsimd
            def _(eng):
                run("pool", eng)

            @block.scalar
            def _(eng):
                run("act", eng)

            @block.vector
            def _(eng):
                run("dve", eng)

            @block.tensor
            def _(eng):
                run("pe", eng)


def MMS(lst):
    def f(e):
        ins = None
        for (o, l, r, st, sp) in lst:
            ins = e.matmul(o, lhsT=l, rhs=r, start=st, stop=sp)
        return ins
    return f


def TRS(lst, ident):
    def f(e):
        ins = None
        for (o, i) in lst:
            ins = e.transpose(out=o, in_=i, identity=ident)
        return ins
    return f


def ACTF(out, in_, func, bias=0.0, scale=1.0, accum_out=None):
    if accum_out is None:
        return lambda e: e.activation(out=out, in_=in_, func=func, bias=bias, scale=scale)
    return lambda e: e.activation(out=out, in_=in_, func=func, bias=bias, scale=scale, accum_out=accum_out)


def TT(out, a, b, op):
    return lambda e: e.tensor_tensor(out=out, in0=a, in1=b, op=op)


def TS(out, a, s1, op0, s2=None, op1=None):
    if op1 is None:
        return lambda e: e.tensor_scalar(out=out, in0=a, scalar1=s1, scalar2=None, op0=op0)
    return lambda e: e.tensor_scalar(out=out, in0=a, scalar1=s1, scalar2=s2, op0=op0, op1=op1)


def STT(out, in0, scalar, in1, op0, op1):
    return lambda e: e.scalar_tensor_tensor(out=out, in0=in0, scalar=scalar, in1=in1, op0=op0, op1=op1)


def CP(out, in_):
    return lambda e: e.tensor_copy(out=out, in_=in_)


def RECIP(out, in_):
    return lambda e: e.reciprocal(out=out, in_=in_)


def MSET(ap, v):
    return lambda e: e.memset(ap, v)


def DMA(out, in_):
    return lambda e: e.dma_start(out=out, in_=in_)


def SCAN(out, a, b, init):
    return lambda e: e.tensor_tensor_scan(out=out, data0=a, data1=b, initial=init, op0=ALU.mult, op1=ALU.add)


PRM_SPEC = [("n1g", 16), ("n2g", 16), ("fng", 8), ("bmod", 96), ("rgcw", 16), ("rgcb", 4), ("rgba", 8),
            ("rgbx", 8), ("rglam", 8), ("scw", 32), ("scb", 8), ("subg", 2)]
PRM_OFF = {}
_o = 0
for _n, _w in PRM_SPEC:
    PRM_OFF[_n] = _o
    _o += _w
NPRM = _o
ROW_SPEC = [("dtb", 16), ("alog", 16), ("dsk", 8), ("sng", 512), ("dal", 512)]
ROW_OFF = {}
_o = 0
for _n, _w in ROW_SPEC:
    ROW_OFF[_n] = _o
    _o += _w
NROW = _o


def _fm(v):
    v = np.asarray(v, np.float32)
    lead = v.shape[:-1]
    n = v.shape[-1] // 128
    v = v.reshape(lead + (n, 128))
    return np.moveaxis(v, -1, 0).reshape(128, -1)


def pack_shared(inp):
    f = lambda k: np.asarray(inp[k], np.float32)
    prm = np.zeros((128, NPRM), np.float32)

    def put(name, arr):
        o = PRM_OFF[name]
        prm[:, o:o + arr.shape[1]] = arr

    put("n1g", _fm(f("norm1_g")))
    put("n2g", _fm(f("norm2_g")))
    put("fng", _fm(f("final_norm_g")))
    put("bmod", _fm(f("b_mod")))
    w = f("rg_conv_w").reshape(2, 4, 2, 128).transpose(3, 0, 2, 1).reshape(128, 16)
    put("rgcw", w)
    put("rgcb", _fm(f("rg_conv_b")))
    put("rgba", _fm(f("rg_b_a")))
    put("rgbx", _fm(f("rg_b_x")))
    put("rglam", _fm(f("rg_lambda")))
    w = f("ssd_conv_w").reshape(2, 4, 4, 128).transpose(3, 0, 2, 1).reshape(128, 32)
    put("scw", w)
    put("scb", _fm(f("ssd_conv_b")))
    put("subg", _fm(f("da_subln_g")))
    row = np.zeros((NROW,), np.float32)
    row[ROW_OFF["dtb"]:ROW_OFF["dtb"] + 16] = f("ssd_dt_bias").reshape(-1)
    row[ROW_OFF["alog"]:ROW_OFF["alog"] + 16] = f("ssd_a_log").reshape(-1)
    row[ROW_OFF["dsk"]:ROW_OFF["dsk"] + 8] = f("ssd_d").reshape(-1)
    row[ROW_OFF["sng"]:ROW_OFF["sng"] + 512] = f("ssd_norm_g").reshape(-1)
    row[ROW_OFF["dal"]:ROW_OFF["dal"] + 512] = f("da_lambda").reshape(-1)
    rowp = np.ascontiguousarray(np.broadcast_to(row[None, :], (128, NROW)))
    bd = np.zeros((128, 16, 128), np.float32)
    wa, wx = f("rg_w_a"), f("rg_w_x")
    for l in range(2):
        for d in range(2):
            for cc in range(2):
                for ax, wsrc in enumerate((wa, wx)):
                    idx = ((l * 2 + d) * 2 + cc) * 2 + ax
                    for blk in range(2):
                        bd[blk * 64:(blk + 1) * 64, idx, blk * 64:(blk + 1) * 64] = wsrc[l, d, 2 * cc + blk]
    j = np.arange(64)
    sw = np.where((j % 32) < 16, j + 16, j - 16)
    win = f("w_in")
    qidx = np.concatenate([OFF_Q + h * 128 + c * 64 + sw for h in range(4) for c in range(2)])
    kidx = np.concatenate([OFF_K + h * 128 + c * 64 + sw for h in range(4) for c in range(2)])
    winx = np.ascontiguousarray(np.concatenate([win, win[:, :, qidx], win[:, :, kidx]], axis=2))
    ident = np.eye(128, dtype=np.float32)
    tt = np.arange(128)
    uf = (tt[:, None] <= tt[None, :]).astype(np.float32)
    ub = (tt[:, None] >= tt[None, :]).astype(np.float32)
    t = np.arange(NLAT)
    row_i = (t // 64).astype(np.float32)
    col_i = (t % 64).astype(np.float32)
    half = 32
    inv_freq = np.power(np.float32(10000.0), -np.arange(0, half, 2, dtype=np.float32) / np.float32(half)).astype(np.float32)
    ang_r = row_i[:, None] * inv_freq[None, :]
    ang_c = col_i[:, None] * inv_freq[None, :]
    rope = np.zeros((128, 2, NLAT), np.float32)
    for p in range(128):
        jj = p % 64
        ang = ang_r if jj < 32 else ang_c
        i = jj % 16
        rope[p, 0] = np.cos(ang[:, i])
        sgn = -1.0 if (jj % 32) < 16 else 1.0
        rope[p, 1] = sgn * np.sin(ang[:, i])
    return dict(prm=prm, rowp=rowp, bd=bd, w_in_x=winx, ident=ident, uf=uf, ub=ub, rope=rope,
                w_mod=f("w_mod"), w_out=f("w_out"), w_gate=f("w_gate"), w_up=f("w_up"), w_down=f("w_down"))


class Kern:
    def __init__(self, nc, P, NB, opts):
        self.nc = nc
        self.P = P
        self.NB = NB
        self.o = opts
        dt_in = lambda n, s: nc.dram_tensor(n, list(s), F32, kind="ExternalInput").ap()
        self.x = dt_in("x", [NB, NLAT, D])
        self.ctx = dt_in("ctx", [NB, NCTX, D])
        self.ct = dt_in("ct", [128, 8, 3])
        self.prm_d = dt_in("prm", [128, NPRM])
        self.rowp_d = dt_in("rowp", [128, NROW])
        self.bd_d = dt_in("bd", [128, 16, 128])
        self.w_in = dt_in("w_in_x", [2, D, DINX])
        self.ident_d = dt_in("ident", [128, 128])
        self.uf_d = dt_in("uf", [128, 128])
        self.ub_d = dt_in("ub", [128, 128])
        self.rope_d = dt_in("rope", [128, 2, NLAT])
        self.w_mod = dt_in("w_mod", [2, D, 6 * D])
        self.w_out = dt_in("w_out", [2, D, D])
        self.w_gate = dt_in("w_gate", [2, D, DFF])
        self.w_up = dt_in("w_up", [2, D, DFF])
        self.w_down = dt_in("w_down", [2, DFF, D])
        self.out = nc.dram_tensor("out", [NB, NLAT, D], F32, kind="ExternalOutput").ap()
        sb = P.sb
        self.resid = sb("resid", [128, 8 * T])
        self.R3 = self.resid[:].rearrange("p (c t) -> p c t", c=8)
        self.xn = sb("xn", [128, 8 * T], BF16)
        self.XN = self.xn[:].rearrange("p (c t) -> p c t", c=8)
        self.mixb = sb("mixb", [128, 4 * T], BF16)
        self.MX = self.mixb[:].rearrange("p (c t) -> p c t", c=4)
        self.scr = sb("scr", [128, SCRW])
        self.WB = [sb("wb%d" % i, [128, 1024], BF16) for i in range(5)]
        self.wrr = 0
        self.wpool_rr = {}
        self.ident = sb("ident_s", [128, 128])
        self.uf = sb("uf_s", [128, 128])
        self.ub = sb("ub_s", [128, 128])
        self.ones_f = sb("ones_f", [128, 128])
        self.ones_b = sb("ones_b", [128, 128], BF16)
        self.prm = sb("prm_s", [128, NPRM])
        self.rowp = sb("rowp_s", [128, NROW])
        self.bd = sb("bd_s", [128, 16 * 128], BF16)
        self.BD = self.bd[:].rearrange("p (i j) -> p i j", i=16)
        self.modt = sb("modt", [128, 2 * 48 * 3])
        self.MT = self.modt[:].rearrange("p (l j n) -> p l j n", l=2, j=48)
        self.amod = sb("amod", [128, 2 * 2 * 3 * 8])
        self.AM = self.amod[:].rearrange("p (l w n c) -> p l w n c", l=2, w=2, n=3)
        self.small = sb("small", [128, 256])
        self.sml_o = 0
        self.pb = [P.ps("pb%d" % i, [128, 512]) for i in range(8)]
        self.rot = [0, 1, 2, 3]
        self.rr = 0

    def prmcol(self, name, i):
        o = PRM_OFF[name] + i
        return self.prm[:, o:o + 1]

    def salloc(self, n):
        o = self.sml_o
        self.sml_o += n
        assert self.sml_o <= 256
        return self.small[:, o:o + n]

    def scr_f(self, off, n):
        return self.scr[:, off:off + n]

    def scr_b(self, off, n):
        assert n % 2 == 0
        return self.scr[:, off:off + n // 2].bitcast(BF16)

    def next_bank(self):
        i = self.rot[self.rr % len(self.rot)]
        self.rr += 1
        return self.pb[i], "pb%d" % i

    def load_w(self, wsrc, k0, kch, col0, ncols=128, P=None, wpool=None):
        P = P or self.P
        if wpool is None:
            i = self.wrr % 4
            self.wrr += 1
        else:
            r = self.wpool_rr.get(wpool, 0)
            self.wpool_rr[wpool] = r + 1
            i = wpool[r % len(wpool)]
        wb = self.WB[i][:, 0:kch * ncols].rearrange("p (c n) -> p c n", c=kch)
        src = wsrc.rearrange("(c p) n -> p c n", p=128)[:, k0:k0 + kch, col0:col0 + ncols]
        tok = "wb%d" % i
        P.dma(DMA(wb, src), writes=[tok], q="pool")
        return wb, tok

    def linear_fm(self, wsrc, col0, tiles, evac, kch=8, k0=0, rhs=None, rtok=None, ncols=128, P=None, wpool=None):
        P = P or self.P
        wb, wtok = self.load_w(wsrc, k0, kch, col0, ncols, P=P, wpool=wpool)
        if rhs is None:
            rhs = lambda kc, t0, n: self.XN[:, kc, t0:t0 + n]
            rtok = lambda t: ["xn.%d" % t]
        for t in tiles:
            t0, n = TILES[t]
            bank, btok = self.next_bank()
            out = bank[0:ncols, 0:n]
            mms = [(out, wb[:, kc, :], rhs(kc, t0, n), kc == 0, kc == kch - 1) for kc in range(kch)]
            P.pe(MMS(mms), reads=[wtok] + rtok(t), writes=[btok])
            evac(t, out, btok)

    def linear_fm2(self, wsrc, colA, colB, tiles, evac2, extraA=(), evacA=None, P=None, wpool=None):
        P = P or self.P
        wa, atok = self.load_w(wsrc, 0, 8, colA, P=P, wpool=wpool)
        wb2, btok2 = self.load_w(wsrc, 0, 8, colB, P=P, wpool=wpool)
        for t in tiles:
            t0, n = TILES[t]
            ba, bat = self.next_bank()
            bb, bbt = self.next_bank()
            P.pe(MMS([(ba[:, 0:n], wa[:, kc, :], self.XN[:, kc, t0:t0 + n], kc == 0, kc == 7) for kc in range(8)]
                     + [(bb[:, 0:n], wb2[:, kc, :], self.XN[:, kc, t0:t0 + n], kc == 0, kc == 7) for kc in range(8)]),
                 reads=[atok, btok2, "xn.%d" % t], writes=[bat, bbt])
            evac2(t, ba[:, 0:n], bat, bb[:, 0:n], bbt)
        for t in extraA:
            t0, n = TILES[t]
            ba, bat = self.next_bank()
            P.pe(MMS([(ba[:, 0:n], wa[:, kc, :], self.XN[:, kc, t0:t0 + n], kc == 0, kc == 7) for kc in range(8)]),
                 reads=[atok, "xn.%d" % t], writes=[bat])
            evacA(t, ba[:, 0:n], bat)

    def linear_tm(self, wsrc, col0, ncols, blocks, evac, grp, P=None, wpool=None):
        P = P or self.P
        wb, wtok = self.load_w(wsrc, 0, 8, col0, ncols, P=P, wpool=wpool)
        for g0 in range(0, len(blocks), grp):
            blks = blocks[g0:g0 + grp]
            bank, btok = self.next_bank()
            mms = []
            rt = set()
            for gi, i in enumerate(blks):
                out = bank[:, gi * ncols:(gi + 1) * ncols]
                rt.add("xn.%d" % min(i // 4, 4))
                for kc in range(8):
                    mms.append((out, self.XN[:, kc, i * 128:(i + 1) * 128], wb[:, kc, :], kc == 0, kc == 7))
            P.pe(MMS(mms), reads=[wtok] + sorted(rt), writes=[btok])
            evac(blks, bank[:, 0:len(blks) * ncols], btok)

    def softplus(self, out, x, n, s0, toks, P=None):
        P = P or self.P
        ax, e, z, p = (s0[:, i * n:(i + 1) * n] for i in range(4))
        rd, wr = toks
        tk = ["spl.tmp"]
        P.dve(STT(ax, x, -1.0, x, ALU.mult, ALU.max), reads=rd, writes=tk)
        P.act(ACTF(e, ax, AF.Exp, scale=-1.0), reads=tk, writes=tk)
        P.dve(TS(z, e, 2.0, ALU.add), reads=tk, writes=tk)
        P.dve(RECIP(z, z), reads=tk, writes=tk)
        P.dve(TT(z, z, e, ALU.mult), reads=tk, writes=tk)
        P.dve(TT(e, z, z, ALU.mult), reads=tk, writes=tk)
        P.dve(TS(p, e, 1.0 / 13.0, ALU.mult, 1.0 / 11.0, ALU.add), reads=tk, writes=tk)
        for cst in (1.0 / 9.0, 1.0 / 7.0, 1.0 / 5.0, 1.0 / 3.0, 1.0):
            P.dve(TT(p, p, e, ALU.mult), reads=tk, writes=tk)
            P.dve(TS(p, p, cst, ALU.add), reads=tk, writes=tk)
        P.dve(TT(p, p, z, ALU.mult), reads=tk, writes=tk)
        P.dve(TS(ax, x, 0.0, ALU.max), reads=rd + tk, writes=tk)
        P.dve(STT(out, p, 2.0, ax, ALU.mult, ALU.add), reads=tk, writes=wr)

    def mods_emit(self, l, jj):
        P = self.P
        wb, wtok = self.load_w(self.w_mod[l], 0, 8, jj * 128)
        bi = 5 + (self.mod_i % 2)
        self.mod_i += 1
        out = self.pb[bi][:, 0:3]
        P.pe(MMS([(out, wb[:, kc, :], self.ctb[:, kc, :], kc == 0, kc == 7) for kc in range(8)]),
             reads=[wtok, "ctb"], writes=["pb%d" % bi])
        P.dve(TS(self.MT[:, l, jj, :], out, self.prmcol("bmod", l * 48 + jj), ALU.add), reads=["pb%d" % bi, "prm"],
              writes=["modt"])

    def mods_finish(self, l, w):
        gname = "n1g" if w == 0 else "n2g"
        go = PRM_OFF[gname] + l * 8
        jj0 = 8 + 24 * w
        for n in range(3):
            self.P.dve(STT(self.AM[:, l, w, n, :], self.MT[:, l, jj0:jj0 + 8, n], 1.0, self.prm[:, go:go + 8],
                           ALU.add, ALU.mult), reads=["modt", "prm"], writes=["amod"])

    def mods_step(self, n):
        while n > 0 and self.mod_q:
            l, jj = self.mod_q.pop(0)
            self.mods_emit(l, jj)
            n -= 1
            if not self.mod_q:
                self.mods_finish(0, 1)
                self.mods_finish(1, 0)
                self.mods_finish(1, 1)

    def setup(self):
        P = self.P
        P.dma(DMA(self.ident[:], self.ident_d), writes=["ident"])
        P.dma(DMA(self.uf[:], self.uf_d), writes=["uf"])
        P.dma(DMA(self.ub[:], self.ub_d), writes=["ub"])
        P.dma(DMA(self.prm[:], self.prm_d), writes=["prm"])
        P.dma(DMA(self.rowp[:], self.rowp_d), writes=["rowp"])
        P.dma(DMA(self.BD, self.bd_d), writes=["bd"], q="pool")
        P.pool(MSET(self.ones_f[:], 1.0), writes=["ones_f"])
        P.pool(MSET(self.ones_b[:], 1.0), writes=["ones_b"])
        ctile = self.scr_f(0, 24).rearrange("p (k n) -> p k n", k=8)
        sig = self.scr_f(24, 24).rearrange("p (k n) -> p k n", k=8)
        P.dma(DMA(ctile, self.ct), writes=["ct"])
        P.act(ACTF(sig, ctile, AF.Sigmoid), reads=["ct"], writes=["sig"])
        P.dve(TT(ctile, ctile, sig, ALU.mult), reads=["ct", "sig"], writes=["ct"])
        self.ctb = self.salloc(12).bitcast(BF16)[:, 0:24].rearrange("p (k n) -> p k n", k=8)
        P.dve(CP(self.ctb, ctile), reads=["ct"], writes=["ctb"])
        self.mod_q = [(0, jj) for jj in range(24, 48)] + [(1, jj) for jj in range(48)]
        self.mod_i = 0
        for jj in range(24):
            self.mods_emit(0, jj)
        self.mods_finish(0, 0)
        self.m8sp = self.salloc(8)
        nl = self.salloc(8)
        lo = PRM_OFF["rglam"]
        P.dve(TS(nl, self.prm[:, lo:lo + 8], -1.0, ALU.mult), reads=["prm"], writes=["nl"])
        self.softplus(self.m8sp, nl, 8, self.scr_f(8300, 32), (["nl"], ["m8sp"]))
        P.dve(TS(self.m8sp, self.m8sp, -8.0, ALU.mult), reads=["m8sp"], writes=["m8sp"])
        self.hm = self.salloc(8)
        self.hba = self.salloc(8)
        self.hbx = self.salloc(8)
        P.dve(TS(self.hm, self.m8sp, 0.5, ALU.mult), reads=["m8sp"], writes=["hm"])
        oa, ox = PRM_OFF["rgba"], PRM_OFF["rgbx"]
        P.dve(TS(self.hba, self.prm[:, oa:oa + 8], 0.5, ALU.mult), reads=["prm"], writes=["hba"])
        P.dve(TS(self.hbx, self.prm[:, ox:ox + 8], 0.5, ALU.mult), reads=["prm"], writes=["hbx"])
        self.aneg = self.salloc(16)
        ao = ROW_OFF["alog"]
        P.act(ACTF(self.aneg, self.rowp[:, ao:ao + 16], AF.Exp), reads=["rowp"], writes=["aneg"])
        P.dve(TS(self.aneg, self.aneg, -1.0, ALU.mult), reads=["aneg"], writes=["aneg"])
        self.nlam = self.salloc(2)
        self.gsc = self.salloc(2)
        pr = self.scr_f(8400, 128)
        sm = self.salloc(4)
        for l in range(2):
            do = ROW_OFF["dal"] + l * 256
            lam_init = 0.8 - 0.6 * math.exp(-0.3 * l)
            for i in range(2):
                P.dve(TT(pr[:, 0:64], self.rowp[:, do + 128 * i:do + 128 * i + 64],
                         self.rowp[:, do + 128 * i + 64:do + 128 * i + 128], ALU.mult), reads=["rowp"], writes=["pr"])
                P.dve(lambda e, o_=sm[:, 2 * l + i:2 * l + i + 1], i_=pr[:, 0:64]: e.reduce_sum(
                    out=o_, in_=i_, axis=mybir.AxisListType.X), reads=["pr"], writes=["sm"])
            P.act(ACTF(sm[:, 2 * l:2 * l + 2], sm[:, 2 * l:2 * l + 2], AF.Exp), reads=["sm"], writes=["sm"])
            P.dve(TT(self.nlam[:, l:l + 1], sm[:, 2 * l + 1:2 * l + 2], sm[:, 2 * l:2 * l + 1], ALU.subtract),
                  reads=["sm"], writes=["nlam"])
            P.dve(TS(self.nlam[:, l:l + 1], self.nlam[:, l:l + 1], -lam_init, ALU.add), reads=["nlam"], writes=["nlam"])
            P.dve(TS(self.gsc[:, l:l + 1], self.prmcol("subg", l), 1.0 - lam_init, ALU.mult), reads=["prm"],
                  writes=["gsc"])
        P.barrier()

    def load_x(self, b):
        P = self.P
        STG = [self.scr_f(i * 1024, 1024) for i in range(2)]
        self.rot = [0, 1, 2, 3]
        for i in range(NBLK):
            src = self.x[b, i * 128:(i + 1) * 128, :] if i < 16 else self.ctx[b, (i - 16) * 128:(i - 15) * 128, :]
            stg = STG[i % 2]
            stok = "stg%d" % (i % 2)
            P.dma(DMA(stg, src), writes=[stok])
            t = min(i // 4, 4)
            for half in range(2):
                bank, btok = self.next_bank()
                P.pe(TRS([(bank[:, j * 128:(j + 1) * 128], stg[:, (half * 4 + j) * 128:(half * 4 + j + 1) * 128])
                          for j in range(4)], self.ident[:]), reads=[stok, "ident"], writes=[btok])
                dst = self.R3[:, half * 4:half * 4 + 4, i * 128:(i + 1) * 128]
                srcp = bank[:, 0:512].rearrange("p (c t) -> p c t", c=4)
                wr = ["r.%d.%d" % (c, t) for c in range(half * 4, half * 4 + 4)]
                if half == 0:
                    P.act(ACTF(dst, srcp, AF.Identity), reads=[btok], writes=wr)
                else:
                    P.dve(CP(dst, srcp), reads=[btok], writes=wr)

    def rstd_tile(self, t, RS, SQ, P, sx="", bank=7):
        t0, n = TILES[t]
        pbk = self.pb[bank]
        btok = "pb%d" % bank
        for c in range(8):
            sq = SQ[c % 2]
            P.dve(TT(sq[:, :n], self.R3[:, c, t0:t0 + n], self.R3[:, c, t0:t0 + n], ALU.mult),
                  reads=["r.%d.%d" % (c, t)], writes=["sq%d%s" % (c % 2, sx)])
            P.pe(MMS([(pbk[:, :n], self.ones_b[:], sq[:, :n], c == 0, c == 7)]), reads=["sq%d%s" % (c % 2, sx), "ones_b"],
                 writes=[btok])
        P.act(ACTF(RS[:, :n], pbk[:, :n], AF.Ln, bias=EPS, scale=1.0 / D), reads=[btok], writes=["rs" + sx])
        P.act(ACTF(RS[:, :n], RS[:, :n], AF.Exp, scale=-0.5), reads=["rs" + sx], writes=["rs" + sx])

    class _Rec:
        def __init__(self):
            self.calls = []

        def __getattr__(self, name):
            def f(*a_, **k_):
                self.calls.append((name, a_, k_))
            return f

    def replay(self, streams):
        n = max([len(c) for c in streams] + [0])
        for i in range(n):
            for c in streams:
                if i < len(c):
                    name, a_, k_ = c[i]
                    getattr(self.P, name)(*a_, **k_)

    def norm_scratch(self, si):
        o = si * 2048
        SQ = [self.scr_b(o + i * 256, 512) for i in range(2)]
        RS = self.scr_f(o + 512, 512)
        TM = [self.scr_f(o + 1024 + i * 512, 512) for i in range(2)]
        return SQ, RS, TM

    def norm(self, b, l, w, tiles):
        streams = [Kern._Rec(), Kern._Rec()]
        for i, t in enumerate(tiles):
            si = i % 2
            P = streams[si]
            sx = ".%d" % si
            SQ, RS, TM = self.norm_scratch(si)
            t0, n = TILES[t]
            nn = b if t < 4 else 2
            self.rstd_tile(t, RS, SQ, P, sx, bank=6 + si)
            for c in range(8):
                tm = TM[c % 2]
                P.dve(TT(tm[:, :n], self.R3[:, c, t0:t0 + n], RS[:, :n], ALU.mult), reads=["r.%d.%d" % (c, t), "rs" + sx],
                      writes=["tm%d%s" % (c % 2, sx)])
                P.act(ACTF(self.XN[:, c, t0:t0 + n], tm[:, :n], AF.Identity,
                           bias=self.MT[:, l, 24 * w + c, nn:nn + 1], scale=self.AM[:, l, w, nn, c:c + 1]),
                      reads=["tm%d%s" % (c % 2, sx), "modt", "amod"], writes=["xn.%d" % t])
        self.replay([st.calls for st in streams])

    def wout_partial(self, b, l, rows0, tiles):
        P = self.P
        self.rot = [0, 1, 2, 3]
        for c in range(8):
            def evac(t, ps, btok, c=c):
                t0, n = TILES[t]
                nn = b if t < 4 else 2
                dst = self.R3[:, c, t0:t0 + n]
                P.dve(STT(dst, ps, self.MT[:, l, 16 + c, nn:nn + 1], dst, ALU.mult, ALU.add),
                      reads=[btok, "modt", "r.%d.%d" % (c, t)], writes=["r.%d.%d" % (c, t)])
            self.linear_fm(self.w_out[l], c * 128, tiles, evac, kch=4, k0=rows0 // 128,
                           rhs=lambda kc, t0, n: self.MX[:, kc, t0:t0 + n], rtok=lambda t: ["mx.%d" % t])

    def ffn(self, b, l, tiles):
        P = self.P
        HH = self.scr_b(0, 6 * T).rearrange("p (j t) -> p j t", j=6)
        SG = [self.scr_f(7000 + i * 512, 512) for i in range(2)]
        groups = [(0, 6), (6, 6), (12, 5), (17, 5)]
        it = 0
        for (j0, J) in groups:
            for jl in range(J):
                j = j0 + jl
                wg, gtok = self.load_w(self.w_gate[l], 0, 8, j * 128)
                wu, utok = self.load_w(self.w_up[l], 0, 8, j * 128)
                for t in tiles:
                    t0, n = TILES[t]
                    bg, bgt = self.pb[it % 2], "pb%d" % (it % 2)
                    bu, but = self.pb[2 + it % 2], "pb%d" % (2 + it % 2)
                    sg = SG[it % 2]
                    sgt = "sg%d" % (it % 2)
                    it += 1
                    P.pe(MMS([(bg[:, :n], wg[:, kc, :], self.XN[:, kc, t0:t0 + n], kc == 0, kc == 7) for kc in range(8)]),
                         reads=[gtok, "xn.%d" % t], writes=[bgt])
                    P.pe(MMS([(bu[:, :n], wu[:, kc, :], self.XN[:, kc, t0:t0 + n], kc == 0, kc == 7) for kc in range(8)]),
                         reads=[utok, "xn.%d" % t], writes=[but])
                    P.act(ACTF(sg[:, :n], bg[:, :n], AF.Silu), reads=[bgt], writes=[sgt])
                    P.dve(TT(HH[:, jl, t0:t0 + n], sg[:, :n], bu[:, :n], ALU.mult), reads=[sgt, but],
                          writes=["hh.%d.%d" % (jl, t)])
            for c in range(8):
                wd, dtok = self.load_w(self.w_down[l], j0, J, c * 128)
                for t in tiles:
                    t0, n = TILES[t]
                    nn = b if t < 4 else 2
                    bi = 4 + (it % 3)
                    it += 1
                    bank, btok = self.pb[bi], "pb%d" % bi
                    P.pe(MMS([(bank[:, :n], wd[:, jl, :], HH[:, jl, t0:t0 + n], jl == 0, jl == J - 1) for jl in range(J)]),
                         reads=[dtok] + ["hh.%d.%d" % (jl, t) for jl in range(J)], writes=[btok])
                    dst = self.R3[:, c, t0:t0 + n]
                    P.dve(STT(dst, bank[:, :n], self.MT[:, l, 40 + c, nn:nn + 1], dst, ALU.mult, ALU.add),
                          reads=[btok, "modt", "r.%d.%d" % (c, t)], writes=["r.%d.%d" % (c, t)])

    def final(self, b):
        P = self.P
        SQ, RS, TM = self.norm_scratch(0)
        YN = self.scr_f(2048, 4096).rearrange("p (c t) -> p c t", c=8)
        OST = [self.scr_f(6144 + i * 1024, 1024) for i in range(2)]
        self.rot = [0, 1, 2, 3]
        oi = 0
        for t in range(4):
            t0, n = TILES[t]
            self.rstd_tile(t, RS, SQ, P)
            for c in range(8):
                tm = TM[c % 2]
                P.dve(TT(tm[:, :n], self.R3[:, c, t0:t0 + n], RS[:, :n], ALU.mult), reads=["r.%d.%d" % (c, t), "rs"],
                      writes=["tm%d" % (c % 2)])
                P.act(ACTF(YN[:, c, :], tm[:, :n], AF.Identity, scale=self.prmcol("fng", c)),
                      reads=["tm%d" % (c % 2), "prm"], writes=["yn.%d" % c])
            for j in range(4):
                ost = OST[oi % 2]
                otok = "ost%d" % (oi % 2)
                oi += 1
                for half in range(2):
                    bank, btok = self.next_bank()
                    P.pe(TRS([(bank[:, cc * 128:(cc + 1) * 128], YN[:, half * 4 + cc, j * 128:(j + 1) * 128])
                              for cc in range(4)], self.ident[:]),
                         reads=["yn.%d" % (half * 4 + cc) for cc in range(4)] + ["ident"], writes=[btok])
                    if half == 0:
                        P.act(ACTF(ost[:, 0:512], bank[:, 0:512], AF.Identity), reads=[btok], writes=[otok])
                    else:
                        P.dve(CP(ost[:, 512:1024], bank[:, 0:512]), reads=[btok], writes=[otok])
                r0 = t0 + j * 128
                P.dma(DMA(self.out[b, r0:r0 + 128, :], ost), reads=[otok], writes=["out"])

    def da_mixer(self, b, l):
        P = self.P
        need_ctx = (l == 0)
        o = 0
        ROPE = self.scr_f(o, 4096).rearrange("p (a t) -> p a t", a=2); o += 4096
        QR = self.scr_b(o, T); o += T // 2
        KR = self.scr_b(o, T); o += T // 2
        VT = self.scr_b(o, T).rearrange("p (i d) -> p i d", i=NBLK); o += T // 2
        T1 = [self.scr_f(o + i * 512, 512) for i in range(2)]; o += 1024
        T2 = [self.scr_f(o + i * 512, 512) for i in range(2)]; o += 1024
        PT = [self.scr_b(o + i * 256, 512) for i in range(4)]; o += 1024
        RC = self.scr_f(o, 512); o += 512
        TC = [self.scr_f(o + i * 512, 512) for i in range(2)]; o += 1024
        O2 = self.scr_f(o, 512); o += 512
        OV = [self.scr_f(o + i * 512, 512) for i in range(2)]; o += 1024
        OSQ = self.scr_b(o, 512); o += 256
        assert o <= SCRW, o
        P.dma(DMA(ROPE, self.rope_d), writes=["rope"])
        lat = [0, 1, 2, 3]
        W = self.w_in[l]
        ti = [0]
        pend = {"A": None, "B": None}
        fcount = [0]

        def emitA():
            f = pend["A"]
            pend["A"] = None
            if f is not None:
                f()

        def emitB():
            f = pend["B"]
            pend["B"] = None
            if f is not None:
                f()

        def make_final(h, qt, q0, qn):
            fi = fcount[0] % 2
            fcount[0] += 1
            ov = OV[fi][:, :qn]
            ovt = "ov%d" % fi

            def fa():
                for c in range(2):
                    P.dve(RECIP(RC[:, :qn], T1[c][:, :qn]), reads=["t1.%d" % c], writes=["rc"])
                    P.dve(TT(TC[c][:, :qn], TC[c][:, :qn], RC[:, :qn], ALU.mult), reads=["tc%d" % c, "rc"],
                          writes=["tc%d" % c])
                P.dve(STT(ov, TC[1][:, :qn], self.nlam[:, l:l + 1], TC[0][:, :qn], ALU.mult, ALU.add),
                      reads=["tc0", "tc1", "nlam"], writes=[ovt])
                P.dve(TT(OSQ[:, :qn], ov, ov, ALU.mult), reads=[ovt], writes=["osq"])

            def fb():
                P.pe(MMS([(self.pb[3][:, :qn], self.ones_b[:], OSQ[:, :qn], True, True)]), reads=["osq", "ones_b"],
                     writes=["pb3"])
                P.act(ACTF(O2[:, :qn], self.pb[3][:, :qn], AF.Ln, bias=EPS, scale=1.0 / 128.0), reads=["pb3"],
                      writes=["o2"])
                P.act(ACTF(O2[:, :qn], O2[:, :qn], AF.Exp, scale=-0.5), reads=["o2"], writes=["o2"])
                P.dve(TT(ov, ov, O2[:, :qn], ALU.mult), reads=[ovt, "o2"], writes=[ovt])
                P.act(ACTF(self.MX[:, h, q0:q0 + qn], ov, AF.Identity, scale=self.gsc[:, l:l + 1]),
                      reads=[ovt, "gsc"], writes=["mx.%d" % qt])
            return fa, fb

        for h in range(4):
            self.rot = [0, 1, 2, 3]

            def ev_rope(dst, dtok, P, i):
                def ev(t, pa, ta, pb_, tb):
                    t0, n = TILES[t]
                    P.dve(TT(T1[i][:, :n], pa, ROPE[:, 0, t0:t0 + n], ALU.mult), reads=[ta, "rope"], writes=["t1.%d" % i])
                    P.dve(TT(T2[i][:, :n], pb_, ROPE[:, 1, t0:t0 + n], ALU.mult), reads=[tb, "rope"], writes=["t2.%d" % i])
                    P.pool(TT(dst[:, t0:t0 + n], T1[i][:, :n], T2[i][:, :n], ALU.add), reads=["t1.%d" % i, "t2.%d" % i],
                           writes=["%s.%d" % (dtok, t)])
                return ev

            def ev_plain(dst, dtok, P):
                def ev(t, ps, btok):
                    t0, n = TILES[t]
                    P.act(ACTF(dst[:, t0:t0 + n], ps, AF.Identity), reads=[btok], writes=["%s.%d" % (dtok, t)])
                return ev
            emitA()
            rq, rk = Kern._Rec(), Kern._Rec()
            self.rot = [0, 1, 2, 3]
            self.linear_fm2(W, OFF_Q + h * 128, OFF_QS + h * 128, lat, ev_rope(QR, "qr", rq, 0), [4] if need_ctx else [],
                            ev_plain(QR, "qr", rq), P=rq, wpool=(0, 1))
            self.rot = [4, 5, 6, 7]
            self.linear_fm2(W, OFF_K + h * 128, OFF_KS + h * 128, lat, ev_rope(KR, "kr", rk, 1), [4],
                            ev_plain(KR, "kr", rk), P=rk, wpool=(2, 3))
            self.replay([rq.calls, rk.calls])
            self.rot = [0, 1, 2, 3]
            emitB()

            def ev_v(blks, ps, btok):
                i0 = blks[0]
                P.act(ACTF(VT[:, i0:i0 + len(blks), :], ps.rearrange("p (i d) -> p i d", i=len(blks)), AF.Identity),
                      reads=[btok], writes=["vt.%d" % i for i in blks])
            self.linear_tm(W, OFF_V + h * 128, 128, list(range(NBLK)), ev_v, 4, wpool=(4,))

            SB = [(0, 1), (2, 3)]
            for qt in (lat + ([4] if need_ctx else [])):
                q0, qn = TILES[qt]
                kts = list(range(NBLK)) if qt < 4 else [16, 17]
                nk = len(kts)

                def emit_s(ii):
                    kt = kts[ii]
                    pa, pb_ = SB[ii % 2]
                    P.pe(MMS([(self.pb[pa][:, :qn], KR[0:64, kt * 128:(kt + 1) * 128], QR[0:64, q0:q0 + qn], True, True),
                              (self.pb[pb_][:, :qn], KR[64:128, kt * 128:(kt + 1) * 128], QR[64:128, q0:q0 + qn], True,
                               True)]),
                         reads=["kr.%d" % min(kt // 4, 4), "qr.%d" % qt], writes=["pb%d" % pa, "pb%d" % pb_])
                emit_s(0)
                for ii, kt in enumerate(kts):
                    if ii == 0:
                        emitA()
                    if ii == min(12, nk - 1):
                        emitB()
                    if ii + 1 < nk:
                        emit_s(ii + 1)
                    first, last = (ii == 0), (ii == nk - 1)
                    for c in range(2):
                        sb = SB[ii % 2][c]
                        pi_ = (2 * ii + c) % 4
                        P.act(ACTF(PT[pi_][:, :qn], self.pb[sb][:, :qn], AF.Exp, scale=0.125), reads=["pb%d" % sb],
                              writes=["pt%d" % pi_])
                        P.pe(MMS([(self.pb[4 + c][:, :qn], VT[:, kt, :], PT[pi_][:, :qn], first, last),
                                  (self.pb[6 + c][:, :qn], self.ones_b[:], PT[pi_][:, :qn], first, last)]),
                             reads=["pt%d" % pi_, "vt.%d" % kt, "ones_b"], writes=["pb%d" % (4 + c), "pb%d" % (6 + c)])
                emitA()
                emitB()
                for c in range(2):
                    P.dve(CP(TC[c][:, :qn], self.pb[4 + c][:, :qn]), reads=["pb%d" % (4 + c)], writes=["tc%d" % c])
                    P.act(ACTF(T1[c][:, :qn], self.pb[6 + c][:, :qn], AF.Identity), reads=["pb%d" % (6 + c)],
                          writes=["t1.%d" % c])
                pend["A"], pend["B"] = make_final(h, qt, q0, qn)
        emitA()
        emitB()

    def rg_mixer(self, b, l):
        P = self.P
        need_ctx = (l == 0)
        o = 0
        RX = self.scr_f(o, 2312); o += 2312
        XC = self.scr_f(o, T); o += T
        XCB = self.scr_b(o, T); o += T // 2
        HS = self.scr_f(o, T); o += T
        B1 = self.scr_f(o, T); o += T
        B2 = self.scr_f(o, T); o += T
        TL = [self.scr_f(o + i * 512, 512) for i in range(4)]; o += 2048
        assert o <= SCRW, o
        B0 = RX[:, 0:T]
        b0t = ["rx.%d" % t for t in range(5)] + ["rxpad"]
        W = self.w_in[l]
        self.rot = [0, 1, 2, 3]
        it = 0
        nout = T if need_ctx else NLAT
        for cc in range(2):
            for (a0, a1) in ((0, 2), (2050, 2053), (2309, 2312)):
                P.pool(MSET(RX[:, a0:a1], 0.0), writes=["rxpad"])

            def ev_x(t, ps, btok):
                t0, n = TILES[t]
                off = t0 + 2 if t < 4 else t0 + 5
                P.act(ACTF(RX[:, off:off + n], ps, AF.Identity), reads=[btok], writes=["rx.%d" % t])
            self.linear_fm(W, OFF_RGX + cc * 128, [0, 1, 2, 3, 4], ev_x)
            for (x0, r0, n, rt, wt) in ((0, 0, NLAT, ["rx.0", "rx.1", "rx.2", "rx.3"], ["xc.0", "xc.1", "xc.2", "xc.3"]),
                                        (NLAT, 2051, NCTX, ["rx.4"], ["xc.4"])):
                wo = PRM_OFF["rgcw"] + (l * 2 + cc) * 4
                P.act(ACTF(XC[:, x0:x0 + n], RX[:, r0:r0 + n], AF.Identity, bias=self.prmcol("rgcb", l * 2 + cc),
                           scale=self.prm[:, wo:wo + 1]), reads=rt + ["rxpad", "prm"], writes=wt)
                for k in range(1, 4):
                    P.dve(STT(XC[:, x0:x0 + n], RX[:, r0 + k:r0 + k + n], self.prm[:, wo + k:wo + k + 1], XC[:, x0:x0 + n],
                              ALU.mult, ALU.add), reads=rt + wt + ["rxpad", "prm"], writes=wt)
                P.pool(CP(XCB[:, x0:x0 + n], XC[:, x0:x0 + n]), reads=wt, writes=["xcb.%d" % t for t in
                                                                                  ([0, 1, 2, 3] if x0 == 0 else [4])])
            xct = ["xc.%d" % t for t in range(5)]
            for d in range(2):
                pi = (l * 2 + d) * 2 + cc
                for t in range(5):
                    t0, n = TILES[t]
                    TA, TX = TL[(it % 2) * 2], TL[(it % 2) * 2 + 1]
                    sfx = "%d" % (it % 2)
                    it += 1
                    ba, bat = self.next_bank()
                    bx, bxt = self.next_bank()
                    P.pe(MMS([(ba[:, :n], self.BD[:, pi * 2, :], XCB[:, t0:t0 + n], True, True),
                              (bx[:, :n], self.BD[:, pi * 2 + 1, :], XCB[:, t0:t0 + n], True, True)]),
                         reads=["bd", "xcb.%d" % t], writes=[bat, bxt])
                    P.act(ACTF(TA[:, :n], ba[:, :n], AF.Tanh, bias=self.hba[:, pi:pi + 1], scale=0.5),
                          reads=[bat, "hba"], writes=["ta" + sfx])
                    P.act(ACTF(B0[:, t0:t0 + n], TA[:, :n], AF.Exp, bias=self.hm[:, pi:pi + 1],
                               scale=self.hm[:, pi:pi + 1]), reads=["ta" + sfx, "hm"], writes=b0t)
                    P.act(ACTF(B1[:, t0:t0 + n], B0[:, t0:t0 + n], AF.Square), reads=b0t, writes=["b1"])
                    P.act(ACTF(TX[:, :n], bx[:, :n], AF.Tanh, bias=self.hbx[:, pi:pi + 1], scale=0.5),
                          reads=[bxt, "hbx"], writes=["tx" + sfx])
                    P.dve(STT(B2[:, t0:t0 + n], TX[:, :n], 1.0, XC[:, t0:t0 + n], ALU.add, ALU.mult),
                          reads=["tx" + sfx, "xc.%d" % t], writes=["b2"])
                    self.mods_step(4)
                P.act(ACTF(B1, B1, AF.Sqrt, bias=1.0, scale=-(1.0 - 2.0 ** -23)), reads=["b1"], writes=["b1"])
                P.dve(STT(B1, B2, 0.5, B1, ALU.mult, ALU.mult), reads=["b1", "b2"], writes=["b1"])
                if d == 0:
                    P.dve(SCAN(B2[:, NLAT:T], B0[:, NLAT:T], B1[:, NLAT:T], 0.0), reads=b0t + ["b1"], writes=["b2"])
                    P.dve(SCAN(B2[:, 0:NLAT], B0[:, 0:NLAT], B1[:, 0:NLAT], B2[:, T - 1:T]), reads=b0t + ["b1", "b2"],
                          writes=["b2"])
                    P.pool(CP(HS[:, 0:nout], B2[:, 0:nout]), reads=["b2"], writes=["hs"])
                else:
                    P.dve(SCAN(B2[:, NLAT:T][:, ::-1], B0[:, NLAT:T][:, ::-1], B1[:, NLAT:T][:, ::-1], 0.0),
                          reads=b0t + ["b1"], writes=["b2"])
                    P.dve(SCAN(B2[:, 0:NLAT][:, ::-1], B0[:, 0:NLAT][:, ::-1], B1[:, 0:NLAT][:, ::-1], B2[:, NLAT:NLAT + 1]),
                          reads=b0t + ["b1", "b2"], writes=["b2"])
                    P.pool(TT(HS[:, 0:nout], HS[:, 0:nout], B2[:, 0:nout], ALU.add), reads=["b2", "hs"], writes=["hs"])
            def ev_g(t, ps, btok):
                t0, n = TILES[t]
                P.act(ACTF(B0[:, t0:t0 + n], ps, AF.Identity), reads=[btok], writes=b0t)
            self.linear_fm(W, OFF_RGG + cc * 128, [0, 1, 2, 3] + ([4] if need_ctx else []), ev_g)
            G, Q = B0[:, 0:nout], B1[:, 0:nout]
            P.dve(TT(Q, G, G, ALU.mult), reads=b0t, writes=["b1"])
            P.dve(TS(Q, Q, 0.044715, ALU.mult, 1.0, ALU.add), reads=["b1"], writes=["b1"])
            P.dve(TT(Q, Q, G, ALU.mult), reads=["b1"] + b0t, writes=["b1"])
            P.act(ACTF(Q, Q, AF.Tanh, scale=0.7978845608028654), reads=["b1"], writes=["b1"])
            P.dve(STT(Q, Q, 1.0, G, ALU.add, ALU.mult), reads=["b1"] + b0t, writes=["b1"])
            P.dve(STT(self.MX[:, cc, 0:nout], Q, 0.5, HS[:, 0:nout], ALU.mult, ALU.mult), reads=["b1", "hs"],
                  writes=["mx.%d" % t for t in range(5 if need_ctx else 4)])
        self.mods_step(1000)

    def ssd_mixer(self, b, l):
        P = self.P
        need_ctx = (l == 0)
        W = self.w_in[l]
        o = 0
        RAW = self.scr_f(o, 2312); o += 2312
        TMP = self.scr_f(o, T); o += T
        XS = self.scr_b(o, NBLK * 256).rearrange("p (i d) -> p i d", i=NBLK); o += NBLK * 128
        BF = self.scr_b(o, T); o += T // 2
        CF = self.scr_b(o, T); o += T // 2
        BT = self.scr_b(o, NBLK * 128).rearrange("p (i d) -> p i d", i=NBLK); o += NBLK * 64
        SZ = self.scr_b(o, NBLK * 256).rearrange("p (i d) -> p i d", i=NBLK); o += NBLK * 128
        DTR = self.scr_f(o, 144); o += 144
        DTv = self.scr_f(o, 144); o += 144
        ADT = self.scr_f(o, 144); o += 144
        SPS = self.scr_f(o, 576); o += 576
        assert o <= SCRW, o
        DT3 = DTv.rearrange("p (i e) -> p i e", i=NBLK)
        ADT3 = ADT.rearrange("p (i e) -> p i e", i=NBLK)
        P_real = P
        P = Kern._Rec()
        self.rot = [0, 1, 2, 3]
        for ch in range(4):
            for (a0, a1) in ((0, 2), (2050, 2053), (2309, 2312)):
                P.pool(MSET(RAW[:, a0:a1], 0.0), writes=["rxpad"])

            def ev_x(t, ps, btok):
                t0, n = TILES[t]
                off = t0 + 2 if t < 4 else t0 + 5
                P.act(ACTF(RAW[:, off:off + n], ps, AF.Identity), reads=[btok], writes=["rx.%d" % t])
            self.linear_fm(W, OFF_XBC + ch * 128, [0, 1, 2, 3, 4], ev_x, P=P, wpool=(0, 1))
            for (x0, r0, n, rt, wt) in ((0, 0, NLAT, ["rx.0", "rx.1", "rx.2", "rx.3"], ["xc.0", "xc.1", "xc.2", "xc.3"]),
                                        (NLAT, 2051, NCTX, ["rx.4"], ["xc.4"])):
                wo = PRM_OFF["scw"] + (l * 4 + ch) * 4
                P.act(ACTF(TMP[:, x0:x0 + n], RAW[:, r0:r0 + n], AF.Identity, bias=self.prmcol("scb", l * 4 + ch),
                           scale=self.prm[:, wo:wo + 1]), reads=rt + ["rxpad", "prm"], writes=wt)
                for k in range(1, 4):
                    P.dve(STT(TMP[:, x0:x0 + n], RAW[:, r0 + k:r0 + k + n], self.prm[:, wo + k:wo + k + 1],
                              TMP[:, x0:x0 + n], ALU.mult, ALU.add), reads=rt + wt + ["rxpad", "prm"], writes=wt)
                if ch == 3:
                    P.act(ACTF(CF[:, x0:x0 + n], TMP[:, x0:x0 + n], AF.Silu), reads=wt, writes=["cf"])
                else:
                    P.act(ACTF(TMP[:, x0:x0 + n], TMP[:, x0:x0 + n], AF.Silu), reads=wt, writes=wt)
                    if ch == 2:
                        P.pool(CP(BF[:, x0:x0 + n], TMP[:, x0:x0 + n]), reads=wt, writes=["bf"])
            if ch < 3:
                for i0 in range(0, NBLK, 4):
                    blks = list(range(i0, min(i0 + 4, NBLK)))
                    bank, btok = self.next_bank()
                    P.pe(TRS([(bank[:, gi * 128:(gi + 1) * 128], TMP[:, i * 128:(i + 1) * 128]) for gi, i in
                              enumerate(blks)], self.ident[:]),
                         reads=["xc.%d" % min(i // 4, 4) for i in blks] + ["ident"], writes=[btok])
                    src = bank[:, 0:len(blks) * 128].rearrange("p (i d) -> p i d", i=len(blks))
                    if ch < 2:
                        P.dve(CP(XS[:, i0:i0 + len(blks), ch * 128:(ch + 1) * 128], src), reads=[btok], writes=["xs"])
                    else:
                        P.dve(CP(BT[:, i0:i0 + len(blks), :], src), reads=[btok], writes=["bt"])
        recA = P.calls
        P = Kern._Rec()
        self.rot = [4, 5, 6, 7]
        for zc in range(2):
            def ev_z(blks, ps, btok, zc=zc):
                i0 = blks[0]
                P.act(ACTF(SZ[:, i0:i0 + len(blks), zc * 128:(zc + 1) * 128],
                           ps.rearrange("p (i d) -> p i d", i=len(blks)), AF.Silu), reads=[btok], writes=["sz"])
            self.linear_tm(W, OFF_SZ + zc * 128, 128, list(range(NBLK)), ev_z, 4, P=P, wpool=(2, 3))

        def ev_dt(blks, ps, btok):
            do = ROW_OFF["dtb"] + l * 8
            P.dve(TT(DTR.rearrange("p (i e) -> p i e", i=NBLK), ps.rearrange("p (i e) -> p i e", i=NBLK),
                     self.rowp[:, do:do + 8].unsqueeze(1).to_broadcast([128, NBLK, 8]), ALU.add),
                  reads=[btok, "rowp"], writes=["dtr"])
        self.linear_tm(W, OFF_DT, 8, list(range(NBLK)), ev_dt, NBLK, P=P, wpool=(2, 3))
        self.softplus(DTv, DTR, 144, SPS, (["dtr"], ["dt"]), P=P)
        P.dve(TT(ADT3, DT3, self.aneg[:, l * 8:l * 8 + 8].unsqueeze(1).to_broadcast([128, NBLK, 8]), ALU.mult),
              reads=["dt", "aneg"], writes=["adt"])
        recB = P.calls
        P = P_real
        self.replay([recA, recB])
        self.rot = [0, 1, 2, 3]
        P.barrier()
        if self.o.get("ssd_prep_only"):
            P.pool(MSET(self.MX[:, 2:4, :], 0.0), writes=["mx.%d" % t for t in range(5)])
            return
        xf = self.xn[:].bitcast(F32)
        o = 0
        Y = xf[:, o:o + NBLK * 256].rearrange("p (i d) -> p i d", i=NBLK); o += NBLK * 256
        S = []
        for d in range(2):
            sd = {}
            sd["LT"] = xf[:, o:o + 512]; o += 512
            sd["D1"] = xf[:, o:o + 512]; o += 512
            sd["GM"] = xf[:, o:o + 256]; o += 256
            sd["T1"] = xf[:, o:o + 256]; o += 256
            sd["H"] = xf[:, o:o + 128]; o += 128
            sd["SM"] = xf[:, o:o + 64]; o += 64
            sd["Mb"] = self.xn[:, 2 * o:2 * o + 512].rearrange("p (h t) -> p h t", h=4); o += 256
            sd["XD"] = self.xn[:, 2 * o:2 * o + 256].rearrange("p (h d) -> p h d", h=4); o += 128
            sd["XDW"] = self.xn[:, 2 * o:2 * o + 256].rearrange("p (h d) -> p h d", h=4); o += 128
            sd["HB"] = self.xn[:, 2 * o:2 * o + 128]; o += 64
            S.append(sd)
        assert o <= 9216, o
        BFm = [self.scr_b(g * 1152, T) for g in range(2)]
        CFm = [self.scr_b(2304 + g * 1152, T) for g in range(2)]
        P.pool(MSET(self.scr_f(0, 4608), 0.0), writes=["bfm", "cfm"])
        for g in range(2):
            P.pool(CP(BFm[g][64 * g:64 * g + 64, :], BF[64 * g:64 * g + 64, :]), reads=["bf", "bfm"], writes=["bfm"])
            P.pool(CP(CFm[g][64 * g:64 * g + 64, :], CF[64 * g:64 * g + 64, :]), reads=["cf", "cfm"], writes=["cfm"])
        fo = 13700
        VV = self.scr_f(fo, 256)
        OT = self.scr_f(fo + 256, 256)
        JK = self.scr_f(fo + 512, 256)
        FS = self.scr_f(fo + 768, 8)
        assert fo + 776 <= SCRW
        rsd, ssq = FS[:, 0:2], FS[:, 2:4]
        PB = self.pb
        dsk = self.rowp[:, ROW_OFF["dsk"] + l * 4:ROW_OFF["dsk"] + l * 4 + 4]
        sng = self.rowp[:, ROW_OFF["sng"] + l * 256:ROW_OFF["sng"] + (l + 1) * 256]
        ytk = ["y.%d" % k for k in range(NBLK)]
        P.dve(TT(Y.rearrange("p i (h d) -> p i h d", h=4), XS.rearrange("p i (h d) -> p i h d", h=4),
                 dsk.unsqueeze(1).unsqueeze(3).to_broadcast([128, NBLK, 4, 64]), ALU.mult), reads=["xs", "rowp"], writes=ytk)
        for d in range(2):
            P.dve(MSET(S[d]["H"], 0.0), writes=["H%d" % d])
            P.dve(MSET(S[d]["HB"], 0.0), writes=["HB%d" % d])
        orders = [[16, 17] + list(range(16)), [17, 16] + list(range(15, -1, -1))]

        class _Rec:
            def __init__(self):
                self.calls = []

            def __getattr__(self, name):
                def f(*a_, **k_):
                    self.calls.append((name, a_, k_))
                return f

        def chunk_step(d, k, P):
            sd = S[d]
            sx = "%d" % d
            U = self.uf if d == 0 else self.ub
            utok = "uf" if d == 0 else "ub"
            edge = 127 if d == 0 else 0
            A, B, C, Dk = (PB[4 * d + i] for i in range(4))
            At, Bt, Ct, Dt = ("pb%d" % (4 * d + i) for i in range(4))
            cst, ecs, wdec, dtw, dch = (sd["SM"][:, i * 4:(i + 1) * 4] for i in range(5))
            LT, D1, GM, T1, H, HB, Mb, XD, XDW = (sd[n] for n in ("LT", "D1", "GM", "T1", "H", "HB", "Mb", "XD", "XDW"))
            want_y = (k < 16) or need_ctx
            blk = slice(k * 128, (k + 1) * 128)
            adt = ADT3[:, k, d * 4:d * 4 + 4]
            mm = [(Dk[:, 0:4], U[:], adt, True, True)]
            for h in range(4):
                mm.append((A[:, h * 128:(h + 1) * 128], ADT3[:, k, d * 4 + h:d * 4 + h + 1].to_broadcast([128, 128]),
                           U[:], True, True))
            P.pe(MMS(mm), reads=["adt", utok], writes=[Dt, At])
            P.dve(CP(cst, Dk[:, 0:4]), reads=[Dt], writes=["cst" + sx])
            csb = A[:, 0:512].rearrange("p (h t) -> p h t", h=4)
            dts = DT3[:, k, d * 4:d * 4 + 4]
            XS4 = XS[:, k, :].rearrange("p (h d) -> p h d", h=4)
            if want_y:
                P.dve(TT(D1.rearrange("p (h t) -> p h t", h=4), csb, cst.unsqueeze(2).to_broadcast([128, 4, 128]),
                         ALU.subtract), reads=[At, "cst" + sx], writes=["d1" + sx])
                P.dve(STT(D1, D1, -1.0, D1, ALU.mult, ALU.min), reads=["d1" + sx], writes=["d1" + sx])
                P.act(ACTF(LT, D1, AF.Exp), reads=["d1" + sx], writes=["lt" + sx])
                P.pe(MMS([(B[:, g * 128:(g + 1) * 128], BFm[g][:, blk], CF[:, blk], True, True) for g in range(2)]),
                     reads=["bfm", "cf"], writes=[Bt])
                P.dve(TT(GM.rearrange("p (g t) -> p g t", g=2), B[:, 0:256].rearrange("p (g t) -> p g t", g=2),
                         U[:].unsqueeze(1).to_broadcast([128, 2, 128]), ALU.mult), reads=[Bt, utok], writes=["gm" + sx])
                P.dve(TT(Mb.rearrange("p (g a) t -> p g a t", g=2), LT.rearrange("p (g a t) -> p g a t", g=2, a=2),
                         GM.rearrange("p (g t) -> p g t", g=2).unsqueeze(2).to_broadcast([128, 2, 2, 128]), ALU.mult),
                      reads=["lt" + sx, "gm" + sx], writes=["mb" + sx])
                P.dve(TT(XD, XS4, dts.unsqueeze(2).to_broadcast([128, 4, 64]), ALU.mult), reads=["xs", "dt"],
                      writes=["xd" + sx])
                P.pe(MMS([(C[:, h * 64:(h + 1) * 64], Mb[:, h, :], XD[:, h, :], True, True) for h in range(4)]
                         + [(C[:, 256 + g * 128:256 + (g + 1) * 128], CFm[g][:, blk], HB[:, :], True, True)
                            for g in range(2)]),
                     reads=["mb" + sx, "xd" + sx, "cfm", "HB" + sx], writes=[Ct])
                P.act(ACTF(ecs, cst, AF.Exp), reads=["cst" + sx], writes=["ecs" + sx])
                P.dve(TT(T1.rearrange("p (h d) -> p h d", h=4), C[:, 256:512].rearrange("p (h d) -> p h d", h=4),
                         ecs.unsqueeze(2).to_broadcast([128, 4, 64]), ALU.mult), reads=[Ct, "ecs" + sx], writes=["t1" + sx])
                P.dve(TT(T1, T1, C[:, 0:256], ALU.add), reads=["t1" + sx, Ct], writes=["t1" + sx])
                P.dve(TT(Y[:, k, :], Y[:, k, :], T1, ALU.add), reads=["t1" + sx, "y.%d" % k], writes=["y.%d" % k])
            P.dve(TT(wdec, csb[:, :, edge], cst, ALU.subtract), reads=[At, "cst" + sx], writes=["wdec" + sx])
            P.act(ACTF(wdec, wdec, AF.Exp), reads=["wdec" + sx], writes=["wdec" + sx])
            P.dve(TT(dtw, wdec, dts, ALU.mult), reads=["wdec" + sx, "dt"], writes=["dtw" + sx])
            P.dve(TT(XDW, XS4, dtw.unsqueeze(2).to_broadcast([128, 4, 64]), ALU.mult), reads=["xs", "dtw" + sx],
                  writes=["xdw" + sx])
            P.pe(MMS([(Dk[64 * g:64 * g + 64, 128:256], BT[:, k, 64 * g:64 * g + 64],
                       XDW[:, 2 * g:2 * g + 2, :].rearrange("p a d -> p (a d)"), True, True) for g in range(2)]),
                 reads=["bt", "xdw" + sx], writes=[Dt])
            for g in range(2):
                P.act(ACTF(dch[64 * g:64 * g + 64, 0:2], csb[64 * g:64 * g + 64, 2 * g:2 * g + 2, edge], AF.Exp),
                      reads=[At], writes=["dch" + sx])
            P.dve(TT(H.rearrange("p (a d) -> p a d", a=2), H.rearrange("p (a d) -> p a d", a=2),
                     dch[:, 0:2].unsqueeze(2).to_broadcast([128, 2, 64]), ALU.mult), reads=["H" + sx, "dch" + sx],
                  writes=["H" + sx])
            P.dve(TT(H, H, Dk[:, 128:256], ALU.add), reads=["H" + sx, Dt], writes=["H" + sx])
            P.dve(CP(HB, H), reads=["H" + sx], writes=["HB" + sx])

        def finalize(k, P):
            P.dve(TT(VV, Y[:, k, :], SZ[:, k, :], ALU.mult), reads=["y.%d" % k, "sz"], writes=["vv"])
            for g in range(2):
                P.act(ACTF(JK[:, g * 128:(g + 1) * 128], VV[:, g * 128:(g + 1) * 128], AF.Square,
                           accum_out=ssq[:, g:g + 1]), reads=["vv"], writes=["jk", "ssq"])
            P.act(ACTF(rsd, ssq, AF.Ln, bias=EPS, scale=1.0 / 128.0), reads=["ssq"], writes=["rsd"])
            P.act(ACTF(rsd, rsd, AF.Exp, scale=-0.5), reads=["rsd"], writes=["rsd"])
            for g in range(2):
                P.dve(STT(OT[:, g * 128:(g + 1) * 128], VV[:, g * 128:(g + 1) * 128], rsd[:, g:g + 1],
                          sng[:, g * 128:(g + 1) * 128], ALU.mult, ALU.mult), reads=["vv", "rsd", "rowp"], writes=["ot"])
            P.pe(TRS([(PB[5][:, g * 128:(g + 1) * 128], OT[:, g * 128:(g + 1) * 128]) for g in range(2)], self.ident[:]),
                 reads=["ot", "ident"], writes=["pb5"])
            P.act(ACTF(self.MX[:, 2:4, k * 128:(k + 1) * 128], PB[5][:, 0:256].rearrange("p (g t) -> p g t", g=2),
                       AF.Identity), reads=["pb5"], writes=["mx.%d" % min(k // 4, 4)])

        def replay(streams):
            n = max([len(c) for c in streams] + [0])
            for i in range(n):
                for c in streams:
                    if i < len(c):
                        name, a_, k_ = c[i]
                        getattr(P, name)(*a_, **k_)

        done = {}
        fin_prev = []
        for j in range(NBLK):
            streams = []
            for d in range(2):
                r = _Rec()
                chunk_step(d, orders[d][j], r)
                streams.append(r.calls)
            streams.append(fin_prev)
            replay(streams)
            fr = _Rec()
            for d in range(2):
                k = orders[d][j]
                done[k] = done.get(k, 0) + 1
            for d in range(2):
                k = orders[d][j]
                if done.get(k) == 2 and ((k < 16) or need_ctx):
                    finalize(k, fr)
                    done[k] = 3
            fin_prev = fr.calls
        replay([fin_prev])

    def layer(self, b, l):
        P = self.P
        o = self.o
        need_ctx = (l == 0)
        allt = [0, 1, 2, 3, 4]
        outt = allt if need_ctx else [0, 1, 2, 3]
        if o.get("mix", True):
            self.norm(b, l, 0, allt)
            P.barrier()
            if o.get("da", True):
                self.da_mixer(b, l)
                self.wout_partial(b, l, 512, outt)
                P.barrier()
            if o.get("rg", True) or o.get("ssd", True):
                if o.get("rg", True):
                    self.rg_mixer(b, l)
                else:
                    P.pool(MSET(self.MX[:, 0:2, :], 0.0), writes=["mx.%d" % t for t in range(5)])
                P.barrier()
                if o.get("ssd", True):
                    self.ssd_mixer(b, l)
                else:
                    P.pool(MSET(self.MX[:, 2:4, :], 0.0), writes=["mx.%d" % t for t in range(5)])
                self.wout_partial(b, l, 0, outt)
                P.barrier()
        self.mods_step(1000)
        if o.get("ffn", True):
            self.norm(b, l, 1, outt)
            P.barrier()
            self.ffn(b, l, outt)
            P.barrier()

    def run(self):
        self.setup()
        for b in range(self.NB):
            self.load_x(b)
            self.P.barrier()
            for l in range(self.o.get("layers", 2)):
                self.layer(b, l)
            self.final(b)
            self.P.barrier()


def build_nc(NB=2, opts=None):
    nc = bass.Bass("TRN2", target_bir_lowering=False)
    with contextlib.ExitStack() as st:
        P = Prog(nc, st)
        K = Kern(nc, P, NB, opts or {})
        K.run()
        P.emit(final_reads=["out"])
    return nc


_NC_CACHE = {}


def kernel(x, c, ctx, c_ctx, w_mod, b_mod, norm1_g, w_in, rg_conv_w, rg_conv_b, rg_w_a, rg_b_a,
           rg_w_x, rg_b_x, rg_lambda, ssd_conv_w, ssd_conv_b, ssd_dt_bias, ssd_a_log, ssd_d,
           ssd_norm_g, da_lambda, da_subln_g, w_out, norm2_g, w_gate, w_up, w_down, final_norm_g,
           _opts=None, _ncores=8):
    inp = dict(w_mod=w_mod, b_mod=b_mod, norm1_g=norm1_g, w_in=w_in, rg_conv_w=rg_conv_w, rg_conv_b=rg_conv_b,
               rg_w_a=rg_w_a, rg_b_a=rg_b_a, rg_w_x=rg_w_x, rg_b_x=rg_b_x, rg_lambda=rg_lambda,
               ssd_conv_w=ssd_conv_w, ssd_conv_b=ssd_conv_b, ssd_dt_bias=ssd_dt_bias, ssd_a_log=ssd_a_log,
               ssd_d=ssd_d, ssd_norm_g=ssd_norm_g, da_lambda=da_lambda, da_subln_g=da_subln_g, w_out=w_out,
               norm2_g=norm2_g, w_gate=w_gate, w_up=w_up, w_down=w_down, final_norm_g=final_norm_g)
    shared = pack_shared(inp)
    x = np.asarray(x, np.float32)
    ctx = np.asarray(ctx, np.float32)
    c = np.asarray(c, np.float32)
    c_ctx = np.asarray(c_ctx, np.float32)
    NB = 2
    in_maps = []
    for i in range(_ncores):
        cc = np.stack([c[2 * i], c[2 * i + 1], c_ctx], axis=0)
        ct = np.ascontiguousarray(cc.reshape(3, 8, 128).transpose(2, 1, 0))
        m = dict(shared)
        m["x"] = np.ascontiguousarray(x[2 * i:2 * i + 2])
        m["ctx"] = np.ascontiguousarray(ctx[2 * i:2 * i + 2])
        m["ct"] = ct
        in_maps.append(m)
    key = repr(sorted((_opts or {}).items()))
    if key not in _NC_CACHE:
        _NC_CACHE[key] = build_nc(NB, _opts)
    nc = _NC_CACHE[key]
    res = run_bass_kernel_spmd(nc, in_maps, core_ids=list(range(_ncores)))
    out = np.concatenate([np.asarray(r["out"], np.float32) for r in res.results], axis=0)
    return out
```
